# Optimizing a Trainium2 kernel written in Bass

```python
import math
import jax
import jax.numpy as jnp
from jax import lax
import numpy as np

D_MODEL = 1024
BATCH = 8
SEQ = 2048
DEPTH = 2

HEAD_DIM = 64
HGRN_WIDTH = D_MODEL // 4
HGRN_HEADS = HGRN_WIDTH // HEAD_DIM
HGRN_KEY_DIM = HEAD_DIM
HGRN_VAL_DIM = HEAD_DIM
HGRN_KEY_WIDTH = HGRN_HEADS * HGRN_KEY_DIM
HGRN_CHUNK = 64
ATTN_WIDTH = 3 * D_MODEL // 8
ATTN_HEADS = ATTN_WIDTH // HEAD_DIM
DILATED_PAIRS = ((128, 1), (512, 4), (2048, 16))
ATTN_BLOCK = 128
NUM_BUCKETS = 32
MAX_DISTANCE = 2048
SSM_WIDTH = D_MODEL - HGRN_WIDTH - ATTN_WIDTH
SSM_HEADS = SSM_WIDTH // HEAD_DIM
SSM_GROUPS = 2
SSM_STATE = 128
SSM_CONV = 4
SSM_CHUNK = 64
SSM_CONV_CH = SSM_WIDTH + 2 * SSM_GROUPS * SSM_STATE
MIX_WIDTH = HGRN_WIDTH + ATTN_WIDTH + SSM_WIDTH
IN_SIZES = (HGRN_KEY_WIDTH, HGRN_KEY_WIDTH, HGRN_WIDTH, HGRN_WIDTH, ATTN_WIDTH, ATTN_WIDTH, ATTN_WIDTH, SSM_WIDTH, SSM_CONV_CH, SSM_HEADS)
IN_COLS = 2 * HGRN_KEY_WIDTH + 2 * HGRN_WIDTH + 3 * ATTN_WIDTH + SSM_WIDTH + SSM_CONV_CH + SSM_HEADS
PEER_HEADS = 8
PEER_NKEYS = 128
PEER_EXPERTS = PEER_NKEYS * PEER_NKEYS
PEER_DKEY = 128
PEER_HALF = PEER_DKEY // 2
PEER_TOPK = 16
PEER_TOKEN_BLOCK = 128
NORM_EPS = 1e-6

kernel_name = 'hymba_style_hgrn2_dilated_ssd_peer'


def rms_norm(x, w):
    xf = x.astype(jnp.float32)
    xf = xf * lax.rsqrt(jnp.mean(xf * xf, axis=-1, keepdims=True) + NORM_EPS)
    return (xf * w.astype(jnp.float32)).astype(x.dtype)


def t5_bucket(dist):
    max_exact = NUM_BUCKETS // 2
    d = np.maximum(dist, 1).astype(np.float32)
    large = max_exact + (np.log(d / max_exact) / np.log(MAX_DISTANCE / max_exact) * (NUM_BUCKETS - max_exact)).astype(np.int32)
    large = np.minimum(large, NUM_BUCKETS - 1)
    return np.where(dist < max_exact, dist, large).astype(np.int32)


def hgrn2_mixer(q, f, i, g, lower_bound, norm_w):
    bsz, seq, _ = q.shape
    H, K, V, C = HGRN_HEADS, HGRN_KEY_DIM, HGRN_VAL_DIM, HGRN_CHUNK
    lb = lower_bound.astype(jnp.float32)
    forget = lb + (1.0 - lb) * jax.nn.sigmoid(f.astype(jnp.float32))
    log_f = jnp.log(forget)
    key = 1.0 - forget
    qs = q.astype(jnp.float32) * (K ** -0.5)

    def chunks(t, d):
        return t.reshape(bsz, seq // C, C, H, d).swapaxes(0, 1)

    causal = jnp.asarray(np.tril(np.ones((C, C), dtype=bool)))[None, :, :, None, None]

    def step(state, inp):
        qc, kc, vc, lc = inp
        b = jnp.cumsum(lc, axis=1)
        o = jnp.einsum('bthk,bhkv->bthv', qc * jnp.exp(b), state)
        decay = jnp.exp(jnp.where(causal, b[:, :, None] - b[:, None, :], -jnp.inf))
        a = jnp.einsum('bthk,bshk,btshk->btsh', qc, kc, decay)
        o = o + jnp.einsum('btsh,bshv->bthv', a, vc)
        b_end = b[:, -1]
        state = state * jnp.exp(b_end)[..., None] + jnp.einsum('bshk,bshv->bhkv', kc * jnp.exp(b_end[:, None] - b), vc)
        return state, o

    state0 = jnp.zeros((bsz, H, K, V), jnp.float32)
    _, o = lax.scan(step, state0, (chunks(qs, K), chunks(key, K), chunks(i.astype(jnp.float32), V), chunks(log_f, K)))
    o = o.swapaxes(0, 1).reshape(bsz, seq, H, V)
    o = rms_norm(o, norm_w) * jax.nn.silu(g.astype(jnp.float32).reshape(bsz, seq, H, V))
    return o.reshape(bsz, seq, H * V)


def dilated_branch(q, k, v, rel_bias, window, dil):
    bsz, seq, H, Dh = q.shape
    blk = ATTN_BLOCK
    n_back = window // dil
    L = seq // dil
    nb = -(-L // blk)
    Lp = nb * blk

    def residue(t):
        return t.reshape(bsz, L, dil, H, Dh).transpose(0, 2, 3, 1, 4)

    qb = jnp.pad(residue(q), ((0, 0), (0, 0), (0, 0), (0, Lp - L), (0, 0))).reshape(bsz, dil, H, nb, blk, Dh)

    def key_windows(t):
        tp = jnp.pad(residue(t), ((0, 0), (0, 0), (0, 0), (blk, Lp - L), (0, 0)))
        tb = tp.reshape(bsz, dil, H, nb + 1, blk, Dh)
        return jnp.concatenate([tb[:, :, :, :-1], tb[:, :, :, 1:]], axis=4)

    kw, vw = key_windows(k), key_windows(v)
    qi = np.arange(blk)[:, None]
    kj = np.arange(2 * blk)[None, :]
    delta = qi - kj + blk
    band = (delta >= 0) & (delta <= n_back)
    key_pos = (np.arange(nb)[:, None, None] - 1) * blk + kj[None]
    mask = jnp.asarray(band[None] & (key_pos >= 0))
    bucket = t5_bucket(np.maximum(delta, 0) * dil)
    bias = rel_bias.astype(jnp.float32)[bucket].transpose(2, 0, 1)
    s = jnp.einsum('bghnid,bghnjd->bghnij', qb, kw).astype(jnp.float32) + bias[None, None, :, None]
    s = jnp.where(mask[None, None, None], s, -jnp.inf)
    lse = jax.nn.logsumexp(s, axis=-1)
    p = jnp.exp(s - lse[..., None])
    o = jnp.einsum('bghnij,bghnjd->bghnid', p, vw.astype(jnp.float32))
    o = o.reshape(bsz, dil, H, Lp, Dh)[:, :, :, :L].transpose(0, 3, 1, 2, 4).reshape(bsz, seq, H, Dh)
    lse = lse.reshape(bsz, dil, H, Lp)[..., :L].transpose(0, 3, 1, 2).reshape(bsz, seq, H)
    return o, lse


def dilated_attention_mixer(q, k, v, q_norm_w, k_norm_w, rel_bias):
    bsz, seq, _ = q.shape
    H, Dh = ATTN_HEADS, HEAD_DIM
    q = rms_norm(q.reshape(bsz, seq, H, Dh), q_norm_w) * (Dh ** -0.5)
    k = rms_norm(k.reshape(bsz, seq, H, Dh), k_norm_w)
    v = v.reshape(bsz, seq, H, Dh)
    outs, lses = [], []
    for window, dil in DILATED_PAIRS:
        o, lse = dilated_branch(q, k, v, rel_bias, window, dil)
        outs.append(o)
        lses.append(lse)
    w = jax.nn.softmax(jnp.stack(lses, axis=0), axis=0)
    o = jnp.sum(w[..., None] * jnp.stack(outs, axis=0), axis=0)
    return o.reshape(bsz, seq, H * Dh)


def mamba2_mixer(z, xbc, dt, conv_w, conv_b, dt_bias, a_log, d_skip, norm_w):
    bsz, seq, _ = z.shape
    H, P, G, N, Q = SSM_HEADS, HEAD_DIM, SSM_GROUPS, SSM_STATE, SSM_CHUNK
    R = H // G
    xbc = lax.conv_general_dilated(xbc, conv_w[:, None, :], window_strides=(1,), padding=((SSM_CONV - 1, 0),),
                                   dimension_numbers=('NWC', 'WIO', 'NWC'), feature_group_count=SSM_CONV_CH) + conv_b
    xbc = jax.nn.silu(xbc.astype(jnp.float32))
    xs, bm, cm = jnp.split(xbc, [SSM_WIDTH, SSM_WIDTH + G * N], axis=-1)
    xs = xs.reshape(bsz, seq, H, P)
    bm = bm.reshape(bsz, seq, G, N)
    cm = cm.reshape(bsz, seq, G, N)
    dt = jax.nn.softplus(dt.astype(jnp.float32) + dt_bias.astype(jnp.float32))
    A = -jnp.exp(a_log.astype(jnp.float32))

    def chunks(t):
        return t.reshape((bsz, seq // Q, Q) + t.shape[2:]).swapaxes(0, 1)

    causal = jnp.asarray(np.tril(np.ones((Q, Q), dtype=bool)))[None, :, :, None]

    def step(state, inp):
        xc, dtc, bc, cc = inp
        bh = jnp.repeat(bc, R, axis=2)
        ch = jnp.repeat(cc, R, axis=2)
        cum = jnp.cumsum(dtc * A, axis=1)
        decay = jnp.exp(jnp.where(causal, cum[:, :, None, :] - cum[:, None, :, :], -jnp.inf))
        scores = jnp.einsum('bthn,bshn->btsh', ch, bh) * decay
        y = jnp.einsum('btsh,bshp->bthp', scores, xc * dtc[..., None])
        y = y + jnp.einsum('bthn,bhpn->bthp', ch, state) * jnp.exp(cum)[..., None]
        to_end = jnp.exp(cum[:, -1:] - cum) * dtc
        state = state * jnp.exp(cum[:, -1])[:, :, None, None] + jnp.einsum('bsh,bshn,bshp->bhpn', to_end, bh, xc)
        return state, y

    state0 = jnp.zeros((bsz, H, P, N), jnp.float32)
    _, y = lax.scan(step, state0, (chunks(xs), chunks(dt), chunks(bm), chunks(cm)))
    y = y.swapaxes(0, 1).reshape(bsz, seq, H, P)
    y = y + d_skip.astype(jnp.float32)[:, None] * xs
    y = y.reshape(bsz, seq, SSM_WIDTH) * jax.nn.silu(z.astype(jnp.float32))
    y = rms_norm(y.reshape(bsz, seq, G, SSM_WIDTH // G), norm_w.reshape(G, SSM_WIDTH // G))
    return y.reshape(bsz, seq, SSM_WIDTH)


def peer_ffn(x, w_query, sub_keys, down, up):
    bsz, seq, D = x.shape
    T = bsz * seq
    H, K, TB = PEER_HEADS, PEER_TOPK, PEER_TOKEN_BLOCK
    xt = x.reshape(T, D)
    q = (xt @ w_query).reshape(T, H, 2, PEER_HALF)
    s = jnp.einsum('thcd,cnd->thcn', q, sub_keys).astype(jnp.float32)
    sv, si = lax.top_k(s, K)
    cand_s = (sv[:, :, 0, :, None] + sv[:, :, 1, None, :]).reshape(T, H, K * K)
    cand_i = (si[:, :, 0, :, None] * PEER_NKEYS + si[:, :, 1, None, :]).reshape(T, H, K * K)
    top_s, pos = lax.top_k(cand_s, K)
    idx = jnp.take_along_axis(cand_i, pos, axis=-1)
    gate = jax.nn.softmax(top_s, axis=-1)

    def expert_block(args):
        xb, ib, gb = args
        a = jnp.einsum('thkd,td->thk', down[ib], xb).astype(jnp.float32)
        c = (gb * jax.nn.gelu(a)).astype(up.dtype)
        return jnp.einsum('thk,thkd->td', c, up[ib]).astype(x.dtype)

    nblk = T // TB
    out = lax.map(expert_block, (xt.reshape(nblk, TB, D), idx.reshape(nblk, TB, H, K), gate.reshape(nblk, TB, H, K)))
    return out.reshape(bsz, seq, D)


def hybrid_layer(x, attn_norm_w, w_in, lower_bound, hgrn_norm_w, q_norm_w, k_norm_w, rel_bias,
                 ssm_conv_w, ssm_conv_b, ssm_dt_bias, ssm_a_log, ssm_d, ssm_norm_w, w_out,
                 ffn_norm_w, peer_w_query, peer_sub_keys, peer_down, peer_up):
    u = rms_norm(x, attn_norm_w) @ w_in
    points = [int(p) for p in np.cumsum(IN_SIZES)[:-1]]
    hq, hf, hi, hg, aq, ak, av, sz, sxbc, sdt = jnp.split(u, points, axis=-1)
    mix = jnp.concatenate([
        hgrn2_mixer(hq, hf, hi, hg, lower_bound, hgrn_norm_w).astype(x.dtype),
        dilated_attention_mixer(aq, ak, av, q_norm_w, k_norm_w, rel_bias).astype(x.dtype),
        mamba2_mixer(sz, sxbc, sdt, ssm_conv_w, ssm_conv_b, ssm_dt_bias, ssm_a_log, ssm_d, ssm_norm_w).astype(x.dtype),
    ], axis=-1)
    x = x + (mix @ w_out).astype(x.dtype)
    x = x + peer_ffn(rms_norm(x, ffn_norm_w), peer_w_query, peer_sub_keys, peer_down, peer_up)
    return x


def setup_inputs(seed: int = 0) -> dict:
    key = jax.random.key(seed)
    ks = jax.random.split(key, 20)

    def nrm(k, shape, scale):
        return jax.random.normal(k, shape, jnp.float32) * scale

    dt0 = jnp.exp(jax.random.uniform(ks[10], (DEPTH, SSM_HEADS), jnp.float32, math.log(1e-3), math.log(1e-1)))
    return {
        'x': nrm(ks[0], (BATCH, SEQ, D_MODEL), 1.0),
        'attn_norm_w': 1.0 + nrm(ks[1], (DEPTH, D_MODEL), 0.02),
        'w_in': nrm(ks[2], (DEPTH, D_MODEL, IN_COLS), D_MODEL ** -0.5),
        'hgrn_lower_bounds': nrm(ks[3], (DEPTH, HGRN_KEY_WIDTH), 0.5),
        'hgrn_norm_w': 1.0 + nrm(ks[4], (DEPTH, HGRN_VAL_DIM), 0.02),
        'q_norm_w': 1.0 + nrm(ks[5], (DEPTH, HEAD_DIM), 0.02),
        'k_norm_w': 1.0 + nrm(ks[6], (DEPTH, HEAD_DIM), 0.02),
        'rel_bias': nrm(ks[7], (NUM_BUCKETS, ATTN_HEADS), 0.1),
        'ssm_conv_w': nrm(ks[8], (DEPTH, SSM_CONV, SSM_CONV_CH), SSM_CONV ** -0.5),
        'ssm_conv_b': nrm(ks[9], (DEPTH, SSM_CONV_CH), 0.02),
        'ssm_dt_bias': dt0 + jnp.log(-jnp.expm1(-dt0)),
        'ssm_a_log': jnp.log(jax.random.uniform(ks[11], (DEPTH, SSM_HEADS), jnp.float32, 1.0, 16.0)),
        'ssm_d': 1.0 + nrm(ks[12], (DEPTH, SSM_HEADS), 0.02),
        'ssm_norm_w': 1.0 + nrm(ks[13], (DEPTH, SSM_WIDTH), 0.02),
        'w_out': nrm(ks[14], (DEPTH, MIX_WIDTH, D_MODEL), MIX_WIDTH ** -0.5),
        'ffn_norm_w': 1.0 + nrm(ks[15], (DEPTH, D_MODEL), 0.02),
        'peer_w_query': nrm(ks[16], (DEPTH, D_MODEL, PEER_HEADS * PEER_DKEY), D_MODEL ** -0.5),
        'peer_sub_keys': nrm(ks[17], (DEPTH, 2, PEER_NKEYS, PEER_HALF), PEER_HALF ** -0.5),
        'peer_down': nrm(ks[18], (DEPTH, PEER_EXPERTS, D_MODEL), D_MODEL ** -0.5),
        'peer_up': nrm(ks[19], (DEPTH, PEER_EXPERTS, D_MODEL), D_MODEL ** -0.5),
    }


def reference(x, attn_norm_w, w_in, hgrn_lower_bounds, hgrn_norm_w, q_norm_w, k_norm_w, rel_bias,
              ssm_conv_w, ssm_conv_b, ssm_dt_bias, ssm_a_log, ssm_d, ssm_norm_w, w_out,
              ffn_norm_w, peer_w_query, peer_sub_keys, peer_down, peer_up):
    lb = jnp.cumsum(jax.nn.softmax(hgrn_lower_bounds.astype(jnp.float32), axis=0), axis=0)
    lb = lb - lb[0]
    for l in range(DEPTH):
        x = hybrid_layer(x, attn_norm_w[l], w_in[l], lb[l], hgrn_norm_w[l], q_norm_w[l], k_norm_w[l], rel_bias,
                         ssm_conv_w[l], ssm_conv_b[l], ssm_dt_bias[l], ssm_a_log[l], ssm_d[l], ssm_norm_w[l], w_out[l],
                         ffn_norm_w[l], peer_w_query[l], peer_sub_keys[l], peer_down[l], peer_up[l])
    return x
```

```python
from contextlib import ExitStack
import numpy as np
import concourse.bass as bass
import concourse.mybir as mybir
from concourse.bass_utils import run_bass_kernel_spmd

F32 = mybir.dt.float32
I32 = mybir.dt.int32
U32 = mybir.dt.uint32
ALU = mybir.AluOpType
AF = mybir.ActivationFunctionType
AX = mybir.AxisListType

NDMA = 24
EPOCH = 30000


def _prod(xs):
    r = 1
    for v in xs:
        r *= int(v)
    return r


def region(a):
    t = a.tensor
    name = t.name
    ap = a.ap
    off = int(a.offset)
    sp = str(a.space)
    if 'DRAM' in sp.upper():
        ext = sum((int(c) - 1) * abs(int(s)) for s, c in ap)
        return (name, 0, 1, off, off + ext + 1)
    row = _prod(t.shape[1:])
    p0 = off // row
    f0 = off % row
    pstep, pcnt = int(ap[0][0]), int(ap[0][1])
    if pstep == 0:
        pcnt = 1
    ext = sum((int(c) - 1) * abs(int(s)) for s, c in ap[1:])
    if 'PSUM' in sp.upper():
        return (name, (p0 // 32) * 32, ((p0 + pcnt + 31) // 32) * 32, 0, row)
    return (name, p0, p0 + pcnt, f0, f0 + ext + 1)


class Sched:
    def __init__(self, nc):
        self.nc = nc
        self.eng = {'pe': nc.tensor, 'dve': nc.vector, 'act': nc.scalar,
                    'pool': nc.gpsimd, 'sp': nc.sync}
        self.ops = {k: [] for k in self.eng}
        self.cnt = {k: 0 for k in self.eng}
        self.waited = {k: {} for k in self.eng}
        self.acc = {}
        self.dma_n = 0
        self.keys = set()
        self.ro = set()
        self.sems = {}
        self.semstack = None

    def _deps(self, eng, reads, writes):
        deps = {}

        def add(key, val):
            if deps.get(key, 0) < val:
                deps[key] = val

        for is_w, aps in ((False, reads), (True, writes)):
            for a in aps:
                if a.tensor.name in self.ro:
                    continue
                r = region(a)
                lst = self.acc.get(r[0], [])
                psum = 'PSUM' in str(a.space).upper()
                for (key, val, w2, e2, p0, p1, f0, f1) in lst:
                    if not (is_w or w2) and not (psum and e2 != eng):
                        continue
                    if p1 <= r[1] or r[2] <= p0 or f1 <= r[3] or r[4] <= f0:
                        continue
                    if e2 == eng and eng != 'dma':
                        if eng == 'pe':
                            continue
                    add(key, val)
        return deps

    def _record(self, eng, key, val, reads, writes):
        for is_w, aps in ((False, reads), (True, writes)):
            for a in aps:
                if a.tensor.name in self.ro:
                    continue
                r = region(a)
                lst = self.acc.setdefault(r[0], [])
                new = []
                for rec in lst:
                    (k2, v2, w2, e2, p0, p1, f0, f1) = rec
                    contained = (r[1] <= p0 and p1 <= r[2] and r[3] <= f0 and f1 <= r[4])
                    if contained and (is_w or (not w2 and e2 == eng and eng != 'dma')):
                        continue
                    new.append(rec)
                new.append((key, val, is_w, eng, r[1], r[2], r[3], r[4]))
                self.acc[r[0]] = new

    def _key(self, eng):
        idx = self.cnt[eng]
        return (eng, idx // EPOCH), idx % EPOCH + 1

    def op(self, eng, fn, reads=(), writes=()):
        deps = self._deps(eng, reads, writes)
        waits = []
        w = self.waited[eng]
        for key, val in deps.items():
            if w.get(key, 0) < val:
                w[key] = val
                waits.append((key, val))
        key, val = self._key(eng)
        self.cnt[eng] += 1
        self.keys.add(key)
        self.ops[eng].append((fn, waits, key, 1))
        self._record(eng, key, val, reads, writes)

    def dma(self, q, out, in_, **kw):
        n = self.dma_n
        self.dma_n += 1
        slot = n % NDMA
        key = ('dma', slot)
        val = 16 * (n // NDMA + 1)
        deps = self._deps('dma', [in_], [out])
        if n >= NDMA:
            deps[key] = max(deps.get(key, 0), val - 16)
        waits = []
        w = self.waited[q]
        for k, v in deps.items():
            if w.get(k, 0) < v:
                w[k] = v
                waits.append((k, v))
        self.keys.add(key)
        self.ops[q].append((lambda e: e.dma_start(out=out, in_=in_, **kw), waits, key, 16))
        self._record('dma', key, val, [in_], [out])

    def dma_custom(self, q, fn, reads, writes):
        n = self.dma_n
        self.dma_n += 1
        slot = n % NDMA
        key = ('dma', slot)
        val = 16 * (n // NDMA + 1)
        deps = self._deps('dma', reads, writes)
        if n >= NDMA:
            deps[key] = max(deps.get(key, 0), val - 16)
        waits = []
        w = self.waited[q]
        for k, v in deps.items():
            if w.get(k, 0) < v:
                w[k] = v
                waits.append((k, v))
        self.keys.add(key)
        self.ops[q].append((fn, waits, key, 16))
        self._record('dma', key, val, reads, writes)

    def barrier(self):
        targets = {}
        for e in self.eng:
            if self.cnt[e] > 0:
                idx = self.cnt[e] - 1
                targets[(e, idx // EPOCH)] = idx % EPOCH + 1
        for n in range(max(0, self.dma_n - NDMA), self.dma_n):
            targets[('dma', n % NDMA)] = 16 * (n // NDMA + 1)
        for e in self.eng:
            waits = []
            w = self.waited[e]
            for k, v in targets.items():
                if k[0] == e:
                    continue
                if w.get(k, 0) < v:
                    w[k] = v
                    waits.append((k, v))
            if waits:
                key, val = self._key(e)
                self.cnt[e] += 1
                self.keys.add(key)
                self.ops[e].append((None, waits, key, 1))
        self.acc = {}

    def finish(self):
        self.barrier()

    def flush(self):
        nc = self.nc
        for k in sorted(self.keys, key=str):
            if k not in self.sems:
                self.sems[k] = self.semstack.enter_context(nc.semaphore("s_%s_%d" % (k[0], k[1])))
        sems = self.sems
        ops = self.ops
        self.ops = {k: [] for k in self.eng}
        with nc.Block() as block:
            def mk(ename):
                def body(e):
                    for (fn, waits, key, inc) in ops[ename]:
                        for (k, v) in waits:
                            e.wait_ge(sems[k], v)
                        if fn is None:
                            ins = e.nop()
                        else:
                            ins = fn(e)
                        ins.then_inc(sems[key], inc)
                return body

            block.tensor(mk('pe'))
            block.vector(mk('dve'))
            block.scalar(mk('act'))
            block.gpsimd(mk('pool'))
            block.sync(mk('sp'))

    def emit(self):
        from contextlib import ExitStack
        with ExitStack() as st:
            self.semstack = st
            self.flush()


def _aps(*xs):
    return [x for x in xs if x is not None and not isinstance(x, (int, float))]


def _h_tt(self, eng, out, in0, in1, op):
    self.op(eng, lambda e: e.tensor_tensor(out=out, in0=in0, in1=in1, op=op), reads=[in0, in1], writes=[out])


def _h_ts(self, eng, out, in0, s1, s2, op0, op1=None):
    if op1 is None:
        self.op(eng, lambda e: e.tensor_scalar(out=out, in0=in0, scalar1=s1, scalar2=None, op0=op0),
                reads=_aps(in0, s1), writes=[out])
    else:
        self.op(eng, lambda e: e.tensor_scalar(out=out, in0=in0, scalar1=s1, scalar2=s2, op0=op0, op1=op1),
                reads=_aps(in0, s1, s2), writes=[out])


def _h_stt(self, eng, out, in0, scalar, in1, op0, op1):
    self.op(eng, lambda e: e.scalar_tensor_tensor(out=out, in0=in0, scalar=scalar, in1=in1, op0=op0, op1=op1),
            reads=_aps(in0, scalar, in1), writes=[out])


def _h_act(self, out, in_, func, bias=None, scale=None, accum_out=None):
    kw = {}
    if bias is not None:
        kw['bias'] = bias
    if scale is not None:
        kw['scale'] = scale
    if accum_out is not None:
        kw['accum_out'] = accum_out
    self.op('act', lambda e: e.activation(out=out, in_=in_, func=func, **kw),
            reads=_aps(in_, bias, scale), writes=_aps(out, accum_out))


def _h_copy(self, eng, out, in_):
    if eng == 'act':
        self.op('act', lambda e: e.copy(out=out, in_=in_), reads=[in_], writes=[out])
    else:
        self.op(eng, lambda e: e.tensor_copy(out=out, in_=in_), reads=[in_], writes=[out])


def _h_mm(self, out, lhsT, rhs, start=True, stop=True):
    self.op('pe', lambda e: e.matmul(out, lhsT=lhsT, rhs=rhs, start=start, stop=stop),
            reads=[lhsT, rhs], writes=[out])


def _h_tr(self, out, in_, ident):
    self.op('pe', lambda e: e.transpose(out=out, in_=in_, identity=ident), reads=[in_, ident], writes=[out])


def _h_red(self, eng, out, in_, op, axis=AX.X):
    self.op(eng, lambda e: e.tensor_reduce(out=out, in_=in_, axis=axis, op=op), reads=[in_], writes=[out])


def _h_memset(self, eng, ap, val):
    self.op(eng, lambda e: e.memset(ap, val), reads=[], writes=[ap])


def _h_recip(self, out, in_):
    self.op('dve', lambda e: e.reciprocal(out=out, in_=in_), reads=[in_], writes=[out])


Sched.tt = _h_tt
Sched.ts = _h_ts
Sched.stt = _h_stt
Sched.act = _h_act
Sched.copy = _h_copy
Sched.mm = _h_mm
Sched.tr = _h_tr
Sched.red = _h_red
Sched.memset = _h_memset
Sched.recip = _h_recip


def bcast(ap, shape):
    assert len(ap.shape) == len(shape), (ap.shape, shape)
    new = []
    for (st, cnt), tgt in zip(ap.ap, shape):
        if int(cnt) == int(tgt):
            new.append([int(st), int(cnt)])
        else:
            assert int(cnt) == 1
            new.append([0, int(tgt)])
    return bass.AP(ap.tensor, ap.offset, new)


T = 2048; D = 1024; NC = 3462
O_HQ, O_HF, O_HI, O_HG, O_AQ, O_AK, O_AV, O_SZ, O_XBC, O_DT = 0, 256, 512, 768, 1024, 1408, 1792, 2176, 2560, 3456

C_ID, C_TRI, C_TRI2, C_CO0, C_CO1, C_CI, C_I16, C_I128, C_END = 0, 128, 256, 384, 512, 640, 642, 658, 786


def make_consts():
    c = np.zeros((128, C_END), np.float32)
    s = np.arange(128)[:, None]; t = np.arange(128)[None, :]
    same = (s // 64) == (t // 64)
    c[:, C_ID:C_ID + 128] = np.eye(128)
    c[:, C_TRI:C_TRI + 128] = (same & (s <= t))
    c[:, C_TRI2:C_TRI2 + 128] = (same & (s > t))
    c[:, C_CO0:C_CO0 + 128] = (s < 64) * np.ones((1, 128))
    c[:, C_CO1:C_CO1 + 128] = (s >= 64) * np.ones((1, 128))
    c[:, C_CI] = (np.arange(128) < 64)
    c[:, C_CI + 1] = (np.arange(128) >= 64)
    c[:, C_I16:C_I16 + 16] = np.arange(16)[None, :]
    c[:, C_I128:C_I128 + 128] = np.arange(128)[None, :]
    return c


class Ctx:
    uid = 0

    def __init__(self, nc, st, pfx):
        Ctx.uid += 1
        self.nc, self.st, self.pfx = nc, st, "%s%d" % (pfx, Ctx.uid)
        self.n = 0

    def sb(self, shape, dt=F32, name=None):
        self.n += 1
        return self.st.enter_context(self.nc.sbuf_tensor("%s_%s%d" % (self.pfx, name or "t", self.n), list(shape), dt))

    def psum(self):
        return [self.st.enter_context(self.nc.psum_tensor("%s_ps%d" % (self.pfx, i), [128, 512], F32)) for i in range(8)]


def rstd_from_ss(s, out, tmp, ss, scale, eps=1e-6):
    s.act(tmp, ss, AF.Sqrt, bias=eps, scale=scale)
    s.recip(out, tmp)


def phaseB(s, nc, st, u_d, mix_d, hlb_d, hnw_d, consts_d, layer, ntiles=16):
    c = Ctx(nc, st, "B")
    P = c.psum()
    cst = c.sb([128, C_END], name="cst")
    s.dma('sp', cst[:], consts_d[:, :])
    ident = cst[:, C_ID:C_ID + 128]; tri = cst[:, C_TRI:C_TRI + 128]; tri2 = cst[:, C_TRI2:C_TRI2 + 128]
    chind = cst[:, C_CI:C_CI + 2]
    hnw = c.sb([128, 64], name="hnw")
    s.dma('sp', hnw[:], hnw_d.partition_broadcast(128))
    if layer > 0:
        h0 = c.sb([128, 256]); h1 = c.sb([128, 256]); lb = c.sb([128, 256]); oml = c.sb([128, 256])
        s.dma('sp', h0[:], hlb_d[0, :].partition_broadcast(128))
        s.dma('sp', h1[:], hlb_d[1, :].partition_broadcast(128))
        s.tt('dve', h1[:], h1[:], h0[:], ALU.subtract)
        s.act(lb[:], h1[:], AF.Sigmoid)
        s.ts('dve', oml[:], lb[:], -1.0, 1.0, ALU.mult, ALU.add)
    SA = c.sb([64, 4, 64], name="SA"); SB = c.sb([64, 4, 64], name="SB")
    s.memset('dve', SA[:], 0.0)
    NB = 2
    qTa = [c.sb([64, 4, 128]) for _ in range(NB)]; qTb = [c.sb([64, 4, 128]) for _ in range(NB)]
    kh0 = [c.sb([128, 256]) for _ in range(NB)]; kh1 = [c.sb([128, 256]) for _ in range(NB)]
    for t_ in qTa + qTb + kh0 + kh1:
        s.memset('dve', t_[:], 0.0)
    tin = [c.sb([128, 1024], name="tin") for _ in range(3)]
    fg = c.sb([128, 256]); logf = c.sb([128, 256]); key = c.sb([128, 256])
    eb = c.sb([128, 256]); enb = c.sb([128, 256]); er = c.sb([128, 256])
    qt = c.sb([128, 256]); kt = c.sb([128, 256])
    dec = [c.sb([64, 8]) for _ in range(NB)]; qT = [c.sb([64, 4, 128]) for _ in range(NB)]; kT = [c.sb([64, 4, 128]) for _ in range(NB)]
    sg = [c.sb([128, 256]) for _ in range(NB)]
    ATs = [c.sb([128, 128]) for _ in range(2)]
    osb = c.sb([128, 256]); sq = c.sb([128, 256]); ss = c.sb([128, 12])
    outt = [c.sb([128, 256]) for _ in range(2)]

    def pre(tt):
        p = tt % NB
        ti = tin[tt % 3]
        s.dma('sp', ti[:], u_d[tt * 128:(tt + 1) * 128, 0:1024])
        q = ti[:, 0:256]; f = ti[:, 256:512]; g = ti[:, 768:1024]
        s.act(fg[:], f, AF.Sigmoid)
        if layer > 0:
            s.tt('dve', fg[:], fg[:], oml[:], ALU.mult)
            s.tt('dve', fg[:], fg[:], lb[:], ALU.add)
        s.act(sg[p][:], g, AF.Silu)
        s.act(logf[:], fg[:], AF.Ln)
        yield
        s.ts('dve', key[:], fg[:], -1.0, 1.0, ALU.mult, ALU.add)
        s.mm(P[0][:, 0:256], lhsT=tri, rhs=logf[:])
        s.mm(P[0][:, 256:512], lhsT=tri2, rhs=logf[:])
        for h in range(4):
            s.mm(P[1][0:64, h * 2:h * 2 + 2], lhsT=logf[:, h * 64:(h + 1) * 64], rhs=chind)
        yield
        s.act(eb[:], P[0][:, 0:256], AF.Exp)
        s.act(enb[:], P[0][:, 0:256], AF.Exp, scale=-1.0)
        s.act(er[:], P[0][:, 256:512], AF.Exp)
        s.act(dec[p][:], P[1][0:64, 0:8], AF.Exp)
        yield
        s.stt('dve', qt[:], q, 0.125, eb[:], ALU.mult, ALU.mult)
        s.tt('pool', kt[:], key[:], enb[:], ALU.mult)
        s.tt('pool', kh0[p][0:64, :], key[0:64, :], er[0:64, :], ALU.mult)
        s.tt('pool', kh1[p][64:128, :], key[64:128, :], er[64:128, :], ALU.mult)
        s.tt('pool', sg[p][:].rearrange("p (h v) -> p h v", h=4), sg[p][:].rearrange("p (h v) -> p h v", h=4),
             bcast(hnw[:].rearrange("p (o v) -> p o v", o=1), [128, 4, 64]), ALU.mult)
        yield
        for h in range(4):
            s.tr(P[2][0:64, h * 128:(h + 1) * 128], qt[:, h * 64:(h + 1) * 64], ident)
            s.tr(P[3][0:64, h * 128:(h + 1) * 128], kt[:, h * 64:(h + 1) * 64], ident)
        yield
        p2v = P[2][0:64, :].rearrange("p (h t) -> p h t", h=4)
        p3v = P[3][0:64, :].rearrange("p (h t) -> p h t", h=4)
        s.copy('dve', qT[p][:], p2v)
        s.copy('dve', qTa[p][:, :, 0:64], p2v[:, :, 0:64])
        s.copy('dve', qTb[p][:, :, 64:128], p2v[:, :, 64:128])
        s.copy('act', kT[p][:], p3v)
        yield

    def heads(tt):
        p = tt % NB
        ti = tin[tt % 3]
        for h in range(4):
            hs = slice(h * 64, (h + 1) * 64)
            vs = slice(512 + h * 64, 512 + (h + 1) * 64)
            s.mm(P[6][0:64, (2 * h) * 64:(2 * h + 1) * 64], lhsT=kh0[p][:, hs], rhs=ti[:, vs])
            s.mm(P[6][0:64, (2 * h + 1) * 64:(2 * h + 2) * 64], lhsT=kh1[p][:, hs], rhs=ti[:, vs])
        yield
        for h in range(4):
            hs = slice(h * 64, (h + 1) * 64)
            vs = slice(512 + h * 64, 512 + (h + 1) * 64)
            AT = P[4 + h % 2][:, 0:128]
            s.mm(AT, lhsT=kT[p][:, h, :], rhs=qT[p][:, h, :])
            A = ATs[h % 2]
            s.tt('dve', A[:], AT, tri, ALU.mult)
            o = P[7][:, hs]
            s.mm(o, lhsT=A[:], rhs=ti[:, vs], start=True, stop=False)
            s.mm(o, lhsT=qTa[p][:, h, :], rhs=SA[:, h, :], start=False, stop=False)
            u0 = P[6][0:64, (2 * h) * 64:(2 * h + 1) * 64]
            u1 = P[6][0:64, (2 * h + 1) * 64:(2 * h + 2) * 64]
            s.stt('dve', SB[:, h, :], SA[:, h, :], dec[p][:, 2 * h:2 * h + 1], u0, ALU.mult, ALU.add)
            s.mm(o, lhsT=qTb[p][:, h, :], rhs=SB[:, h, :], start=False, stop=True)
            s.stt('dve', SA[:, h, :], SB[:, h, :], dec[p][:, 2 * h + 1:2 * h + 2], u1, ALU.mult, ALU.add)
            yield
        s.copy('act', osb[:], P[7][:, 0:256])
        s.tt('pool', sq[:], osb[:], osb[:], ALU.mult)
        s.red('dve', ss[:, 0:4], sq[:].rearrange("p (h v) -> p h v", h=4), ALU.add)
        rstd_from_ss(s, ss[:, 8:12], ss[:, 4:8], ss[:, 0:4], 1.0 / 64)
        ot = outt[tt % 2]
        s.tt('dve', ot[:].rearrange("p (h v) -> p h v", h=4), osb[:].rearrange("p (h v) -> p h v", h=4),
             bcast(ss[:, 8:12].rearrange("p (h o) -> p h o", o=1), [128, 4, 64]), ALU.mult)
        s.tt('dve', ot[:], ot[:], sg[p][:], ALU.mult)
        s.dma('sp', mix_d[tt * 128:(tt + 1) * 128, 0:256], ot[:])
        yield

    def drain(*gens):
        gens = [g for g in gens if g is not None]
        while gens:
            for g in list(gens):
                try:
                    next(g)
                except StopIteration:
                    gens.remove(g)

    drain(pre(0))
    for tt in range(ntiles):
        drain(heads(tt), pre(tt + 1) if tt + 1 < ntiles else None)


def phaseD(s, nc, st, u_d, mix_d, cw_d, cb_d, dtb_d, alog_d, dsk_d, nw_d, consts_d, ntiles=16):
    c = Ctx(nc, st, "D")
    BF16 = mybir.dt.bfloat16
    P = c.psum()
    cst = c.sb([128, C_END], name="cst")
    s.dma('sp', cst[:], consts_d[:, :])
    ident = cst[:, C_ID:C_ID + 128]; tri = cst[:, C_TRI:C_TRI + 128]; tri2 = cst[:, C_TRI2:C_TRI2 + 128]
    co0 = cst[:, C_CO0:C_CO0 + 128]; co1 = cst[:, C_CO1:C_CO1 + 128]
    cw = c.sb([128, 4, 896], name="cw"); cb = c.sb([128, 896], name="cb")
    for j in range(4):
        s.dma('sp', cw[:, j, :], cw_d[j, :].partition_broadcast(128))
    s.dma('sp', cb[:], cb_d.partition_broadcast(128))
    sm = c.sb([128, 32], name="sm")
    s.dma('sp', sm[:, 0:6], dtb_d.partition_broadcast(128))
    s.dma('sp', sm[:, 6:12], alog_d.partition_broadcast(128))
    s.dma('sp', sm[:, 18:24], dsk_d.partition_broadcast(128))
    nwb = c.sb([128, 384], name="nwb")
    s.dma('sp', nwb[:], nw_d.partition_broadcast(128))
    s.act(sm[:, 24:30], sm[:, 6:12], AF.Exp)
    s.ts('dve', sm[:, 12:18], sm[:, 24:30], -1.0, None, ALU.mult)
    dtbias = sm[:, 0:6]; Ab = sm[:, 12:18]
    R = [c.sb([128, 6, 64], name="R") for _ in range(6)]
    R16 = [c.sb([128, 6, 64], BF16, name="R16") for _ in range(6)]
    s.memset('dve', R[0][:], 0.0)
    s.memset('dve', R16[0][:], 0.0)
    Bz0 = c.sb([128, 256], BF16); Bz1 = c.sb([128, 256], BF16)
    s.memset('dve', Bz0[:], 0.0); s.memset('dve', Bz1[:], 0.0)
    NH = 3
    CTs0 = [c.sb([128, 128], BF16) for _ in range(NH)]; CTs1 = [c.sb([128, 128], BF16) for _ in range(NH)]
    for b_ in CTs0 + CTs1:
        s.memset('pool', b_[:], 0.0)
    M = [c.sb([128, 1286], name="M") for _ in range(2)]
    X = [[c.sb([128, 896], name="X") for _ in range(3)] for _ in range(2)]
    mj = [c.sb([128, 896]) for _ in range(4)]
    xcs = [c.sb([128, 896], name="xc") for _ in range(2)]; xc16s = [c.sb([128, 384], BF16) for _ in range(2)]
    v6s = [c.sb([128, 64], name="v6") for _ in range(2)]
    abcs = [c.sb([128, 6, 128]) for _ in range(2)]; BCTs = [c.sb([128, 4, 128], BF16) for _ in range(2)]; CBTs = [c.sb([128, 2, 128]) for _ in range(2)]
    xw = c.sb([128, 384], BF16)
    dsb = [c.sb([128, 128]) for _ in range(NH)]; ecm = [c.sb([128, 128]) for _ in range(NH)]
    esb = [c.sb([128, 128]) for _ in range(NH)]; msb = [c.sb([128, 128]) for _ in range(NH)]; sc = [c.sb([128, 128], BF16) for _ in range(NH)]
    ysb = c.sb([128, 384]); szbs = [c.sb([128, 384]) for _ in range(2)]; sq = c.sb([128, 384]); ss = c.sb([128, 8]); tmpS = c.sb([128, 384])
    outt = [c.sb([128, 384]) for _ in range(2)]
    DmB = [P[3], P[4]]
    def pre(tt):
        b = tt % 2
        Mt = M[b]; v6 = v6s[b]
        xc = xcs[b]; xc16 = xc16s[b]; abc = abcs[b]; BCT = BCTs[b]; CBT = CBTs[b]; szb = szbs[b]
        Sin, Smid, Sout = R[(2 * tt) % 6], R[(2 * tt + 1) % 6], R[(2 * tt + 2) % 6]
        Sin16, Smid16, Sout16 = R16[(2 * tt) % 6], R16[(2 * tt + 1) % 6], R16[(2 * tt + 2) % 6]
        s.dma('sp', Mt[:], u_d[tt * 128:(tt + 1) * 128, O_SZ:NC])
        for j in range(3):
            Xj = X[b][j]
            sh = 3 - j
            if tt == 0:
                s.memset('pool', Xj[:], 0.0)
                s.dma('sp', Xj[sh:128, :], u_d[0:128 - sh, O_XBC:O_XBC + 896])
            else:
                s.dma('sp', Xj[:], u_d[tt * 128 - sh:(tt + 1) * 128 - sh, O_XBC:O_XBC + 896])
        z = Mt[:, 0:384]; x3 = Mt[:, 384:1280]; dtr = Mt[:, 1280:1286]
        for j in range(3):
            s.tt('pool', mj[j][:], X[b][j][:], cw[:, j, :], ALU.mult)
        s.tt('pool', mj[3][:], x3, cw[:, 3, :], ALU.mult)
        s.tt('dve', mj[0][:], mj[0][:], mj[1][:], ALU.add)
        s.tt('dve', mj[2][:], mj[2][:], mj[3][:], ALU.add)
        s.tt('dve', mj[0][:], mj[0][:], mj[2][:], ALU.add)
        s.tt('dve', mj[0][:], mj[0][:], cb[:], ALU.add)
        yield
        s.act(xc[:], mj[0][:], AF.Silu)
        xs = xc[:, 0:384]
        s.copy('act', xc16[:], xs)
        s.act(szb[:], z, AF.Silu)
        dtt = v6[:, 0:6]; a_ = v6[:, 8:14]; cum = v6[:, 16:22]; wgt = v6[:, 24:30]; dAe = v6[:, 32:44]; tmp6 = v6[:, 48:54]
        s.tt('dve', tmp6, dtr, dtbias, ALU.add)
        s.act(tmp6, tmp6, AF.Exp)
        s.act(dtt, tmp6, AF.Ln, bias=1.0)
        s.tt('dve', a_, dtt, Ab, ALU.mult)
        s.copy('pool', abc[:], bcast(v6[:, 8:14].rearrange("p (h o) -> p h o", o=1), [128, 6, 128]))
        yield
        s.mm(P[0][:, 0:6], lhsT=tri, rhs=a_)
        s.mm(P[0][:, 8:14], lhsT=tri2, rhs=a_)
        s.mm(P[0][:, 16:22], lhsT=co0, rhs=a_)
        s.mm(P[0][:, 22:28], lhsT=co1, rhs=a_)
        s.copy('act', cum, P[0][:, 0:6])
        s.act(wgt, P[0][:, 8:14], AF.Exp)
        s.act(dAe, P[0][:, 16:28], AF.Exp)
        s.tt('dve', wgt, wgt, dtt, ALU.mult)
        s.tt('dve', xw[:].rearrange("p (h v) -> p h v", h=6), xs.rearrange("p (h v) -> p h v", h=6),
             bcast(v6[:, 24:30].rearrange("p (h o) -> p h o", o=1), [128, 6, 64]), ALU.mult)
        yield
        for k in range(4):
            s.tr(P[1][:, k * 128:(k + 1) * 128], xc[:, 384 + k * 128:384 + (k + 1) * 128], ident)
        s.copy('act', BCT[:], P[1][:, :].rearrange("p (k t) -> p k t", k=4))
        for g in range(2):
            s.mm(P[2][:, g * 128:(g + 1) * 128], lhsT=BCT[:, g, :], rhs=BCT[:, 2 + g, :])
        s.copy('dve', CBT[:], P[2][:, 0:256].rearrange("p (g t) -> p g t", g=2))
        s.copy('act', Bz0[0:64, :], xc[0:64, 384:640])
        s.copy('act', Bz1[64:128, :], xc[64:128, 384:640])
        yield
        for h in range(6):
            g = h // 3; hs = slice(h * 64, (h + 1) * 64)
            s.mm(P[7][:, hs], lhsT=Bz0[:, g * 128:(g + 1) * 128], rhs=xw[:, hs])
        for h in range(6):
            g = h // 3; hs = slice(h * 64, (h + 1) * 64)
            s.mm(P[1][:, hs], lhsT=Bz1[:, g * 128:(g + 1) * 128], rhs=xw[:, hs])
        s.tt('dve', tmpS[:].rearrange("p (h v) -> p h v", h=6), Sin[:], bcast(v6[:, 32:38].rearrange("p (h o) -> p h o", o=1), [128, 6, 64]), ALU.mult)
        s.tt('dve', Smid[:], tmpS[:].rearrange("p (h v) -> p h v", h=6), P[7][:, 0:384].rearrange("p (h v) -> p h v", h=6), ALU.add)
        s.copy('act', Smid16[:], Smid[:])
        s.tt('dve', tmpS[:].rearrange("p (h v) -> p h v", h=6), Smid[:], bcast(v6[:, 38:44].rearrange("p (h o) -> p h o", o=1), [128, 6, 64]), ALU.mult)
        s.tt('dve', Sout[:], tmpS[:].rearrange("p (h v) -> p h v", h=6), P[1][:, 0:384].rearrange("p (h v) -> p h v", h=6), ALU.add)
        s.copy('act', Sout16[:], Sout[:])

        yield

    def heads(tt):
        b = tt % 2
        Mt = M[b]; v6 = v6s[b]
        xc = xcs[b]; xc16 = xc16s[b]; abc = abcs[b]; BCT = BCTs[b]; CBT = CBTs[b]; szb = szbs[b]
        Sin16, Smid16 = R16[(2 * tt) % 6], R16[(2 * tt + 1) % 6]
        def st1(h):
            g = h // 3; hq = h % NH
            Dm = DmB[h % 2][:, 0:128]
            s.mm(Dm, lhsT=abc[:, h, :], rhs=tri)
            s.ts('dve', dsb[hq][:], Dm, v6[:, 16 + h:17 + h], 0.0, ALU.subtract, ALU.min)
            s.act(ecm[hq][:], Dm, AF.Exp)
            s.act(esb[hq][:], dsb[hq][:], AF.Exp)
            s.stt('dve', msb[hq][:], esb[hq][:], v6[:, h:h + 1], tri, ALU.mult, ALU.mult)
            s.tt('pool', sc[hq][:], msb[hq][:], CBT[:, g, :], ALU.mult)
            s.tt('pool', CTs0[hq][:, 0:64], BCT[:, 2 + g, 0:64], ecm[hq][:, 0:64], ALU.mult)
            s.tt('pool', CTs1[hq][:, 64:128], BCT[:, 2 + g, 64:128], ecm[hq][:, 64:128], ALU.mult)

        def st2(h):
            hq = h % NH; hs = slice(h * 64, (h + 1) * 64)
            y = P[5 + h % 2][:, 0:64]
            s.mm(y, lhsT=sc[hq][:], rhs=xc16[:, hs], start=True, stop=False)
            s.mm(y, lhsT=CTs0[hq][:], rhs=Sin16[:, h, :], start=False, stop=False)
            s.mm(y, lhsT=CTs1[hq][:], rhs=Smid16[:, h, :], start=False, stop=True)
            s.stt('dve', ysb[:, hs], xc[:, hs], sm[:, 18 + h:19 + h], y, ALU.mult, ALU.add)

        SK = 2
        for step in range(6 + SK):
            if step < 6:
                st1(step)
            if step >= SK:
                st2(step - SK)
            yield
        s.tt('pool', ysb[:], ysb[:], szb[:], ALU.mult)
        s.tt('pool', sq[:], ysb[:], ysb[:], ALU.mult)
        s.red('dve', ss[:, 0:2], sq[:].rearrange("p (g v) -> p g v", g=2), ALU.add)
        rstd_from_ss(s, ss[:, 4:6], ss[:, 2:4], ss[:, 0:2], 1.0 / 192)
        ot = outt[b]
        s.tt('dve', ot[:].rearrange("p (g v) -> p g v", g=2), ysb[:].rearrange("p (g v) -> p g v", g=2),
             bcast(ss[:, 4:6].rearrange("p (g o) -> p g o", o=1), [128, 2, 192]), ALU.mult)
        s.tt('pool', ot[:], ot[:], nwb[:], ALU.mult)
        s.dma('sp', mix_d[tt * 128:(tt + 1) * 128, 640:1024], ot[:])


    def drain(*gens):
        gens = [g for g in gens if g is not None]
        while gens:
            for g in list(gens):
                try:
                    next(g)
                except StopIteration:
                    gens.remove(g)

    drain(pre(0))
    for tt in range(ntiles):
        drain(heads(tt), pre(tt + 1) if tt + 1 < ntiles else None)


DILS = (1, 4, 16)


def t5_bucket_np(dist):
    max_exact = 16
    d = np.maximum(dist, 1).astype(np.float32)
    large = max_exact + (np.log(d / max_exact) / np.log(2048 / max_exact) * (32 - max_exact)).astype(np.int32)
    large = np.minimum(large, 31)
    return np.where(dist < max_exact, dist, large).astype(np.int32)


def make_attn_consts(rel_bias):
    j = np.arange(128)[:, None]; i = np.arange(128)[None, :]
    delta = np.concatenate([i - j + 128, i - j], axis=1)
    valid = (delta >= 0) & (delta <= 128)
    maskadd = np.where(valid, 0.0, -30000.0).astype(np.float32)
    biasg = np.zeros((18, 128, 256), np.float32)
    for bi, dil in enumerate(DILS):
        bucket = t5_bucket_np(np.maximum(delta, 0) * dil)
        for h in range(6):
            biasg[bi * 6 + h] = rel_bias[bucket, h]
    return biasg, maskadd


def phaseC(s, nc, st, u_d, mix_d, qk_d, ob_d, qw_d, kw_d, biasg_d, maskadd_d, consts_d, ntiles=16, branches=(0, 1, 2), do_c0=True, do_c2=True):
    c = Ctx(nc, st, "C")
    P = c.psum()
    cst = c.sb([128, C_END], name="cst")
    s.dma('sp', cst[:], consts_d[:, :])
    ident = cst[:, C_ID:C_ID + 128]
    qkw = c.sb([128, 128], name="qkw")
    s.dma('sp', qkw[:, 0:64], qw_d.partition_broadcast(128))
    s.dma('sp', qkw[:, 64:128], kw_d.partition_broadcast(128))
    biasT = c.sb([128, 18, 256], name="biasT")
    madd = c.sb([128, 256], name="madd")
    s.dma('sp', biasT[:], biasg_d.rearrange("n j c -> j n c"))
    s.dma('sp', madd[:], maskadd_d[:, :])
    s.tt('dve', biasT[:], biasT[:], bcast(madd[:].rearrange("p (o c) -> p o c", o=1), [128, 18, 256]), ALU.add)
    qk = [c.sb([128, 768]) for _ in range(2)]; sq = c.sb([128, 768]); ss = c.sb([128, 36]); qkn = [c.sb([128, 768]) for _ in range(2)]
    for tt in (range(ntiles) if do_c0 else []):
        b = tt % 2
        s.dma('sp', qk[b][:], u_d[tt * 128:(tt + 1) * 128, O_AQ:O_AQ + 768])
        s.tt('pool', sq[:], qk[b][:], qk[b][:], ALU.mult)
        s.red('dve', ss[:, 0:12], sq[:].rearrange("p (h v) -> p h v", h=12), ALU.add)
        rstd_from_ss(s, ss[:, 24:36], ss[:, 12:24], ss[:, 0:12], 1.0 / 64)
        s.tt('dve', qkn[b][:].rearrange("p (h v) -> p h v", h=12), qk[b][:].rearrange("p (h v) -> p h v", h=12),
             bcast(ss[:, 24:36].rearrange("p (h o) -> p h o", o=1), [128, 12, 64]), ALU.mult)
        s.stt('dve', qkn[b][:, 0:384].rearrange("p (h v) -> p h v", h=6), qkn[b][:, 0:384].rearrange("p (h v) -> p h v", h=6), 0.125,
              bcast(qkw[:, 0:64].rearrange("p (o v) -> p o v", o=1), [128, 6, 64]), ALU.mult, ALU.mult)
        s.tt('pool', qkn[b][:, 384:768].rearrange("p (h v) -> p h v", h=6), qkn[b][:, 384:768].rearrange("p (h v) -> p h v", h=6),
             bcast(qkw[:, 64:128].rearrange("p (o v) -> p o v", o=1), [128, 6, 64]), ALU.mult)
        s.dma('sp', qk_d[tt * 128:(tt + 1) * 128, :], qkn[b][:])
    QKb = [c.sb([128, 768]) for _ in range(2)]
    V32 = [c.sb([128, 384]) for _ in range(2)]
    BF16 = mybir.dt.bfloat16
    V1 = [c.sb([128, 6, 65], BF16) for _ in range(3)]
    for v in V1:
        s.memset('dve', v[:], 1.0)
    QT = [c.sb([128, 3, 128], BF16) for _ in range(2)]
    KTe = [c.sb([128, 3, 128], BF16) for _ in range(3)]; KTo = [c.sb([128, 3, 128], BF16) for _ in range(3)]
    for k_ in KTe + KTo:
        s.memset('pool', k_[:], 0.0)
    Esb = [c.sb([128, 256]) for _ in range(4)]; Psb = [c.sb([128, 256], BF16) for _ in range(4)]
    Osb = [c.sb([128, 390]) for _ in range(2)]
    blocks = []
    for bi in branches:
        dil = DILS[bi]
        L = (ntiles * 128) // dil
        nblk = L // 128
        for r in range(dil):
            for blk in range(nblk):
                blocks.append((bi, dil, r, blk))
    views = {}
    for bi in branches:
        dil = DILS[bi]
        views[bi] = (qk_d[0:ntiles * 128, :].rearrange("(b m d) c -> d b m c", m=128, d=dil),
                     u_d[0:ntiles * 128, O_AV:O_AV + 384].rearrange("(b m d) c -> d b m c", m=128, d=dil),
                     ob_d[bi, 0:ntiles * 128, :].rearrange("(b m d) c -> d b m c", m=128, d=dil))

    def pre(n):
        bi, dil, r, blk = blocks[n]
        pb = n % 2
        qv, vv, ov = views[bi]
        s.dma('sp', QKb[pb][:], qv[r, blk])
        p3 = n % 3
        s.dma('sp', V32[pb][:], vv[r, blk])
        s.copy('pool', V1[p3][:, :, 0:64], V32[pb][:].rearrange("m (h v) -> m h v", h=6))
        yield
        for pr in range(3):
            s.tr(P[0][:, pr * 128:(pr + 1) * 128], QKb[pb][:, pr * 128:(pr + 1) * 128], ident)
            s.tr(P[1][:, pr * 128:(pr + 1) * 128], QKb[pb][:, 384 + pr * 128:384 + (pr + 1) * 128], ident)
            yield
        s.copy('act', QT[pb][:], P[0][:, 0:384].rearrange("p (a t) -> p a t", a=3))
        p1v = P[1][:, 0:384].rearrange("p (a t) -> p a t", a=3)
        s.copy('dve', KTe[p3][0:64, :, :], p1v[0:64, :, :])
        s.copy('dve', KTo[p3][64:128, :, :], p1v[64:128, :, :])
        yield

    def heads(n):
        bi, dil, r, blk = blocks[n]
        pb = n % 2
        qv, vv, ov = views[bi]
        PO = P[6 + pb]
        c0 = 0 if blk > 0 else 128
        p3 = n % 3; q3 = (n - 1) % 3

        def S_(h):
            pr = h // 2; hq = h % 4
            KT = KTe if h % 2 == 0 else KTo
            ST = P[2 + hq]
            s.mm(ST[:, 128:256], lhsT=KT[p3][:, pr, :], rhs=QT[pb][:, pr, :])
            if blk > 0:
                s.mm(ST[:, 0:128], lhsT=KT[q3][:, pr, :], rhs=QT[pb][:, pr, :])
            s.tt('dve', Esb[hq][:, c0:256], ST[:, c0:256], biasT[:, bi * 6 + h, c0:256], ALU.add)
            s.act(Psb[hq][:, c0:256], Esb[hq][:, c0:256], AF.Exp)

        def PV_(h):
            hq = h % 4
            O = PO[:, h * 65:(h + 1) * 65]
            if blk > 0:
                s.mm(O, lhsT=Psb[hq][:, 0:128], rhs=V1[q3][:, h, :], start=True, stop=False)
            s.mm(O, lhsT=Psb[hq][:, 128:256], rhs=V1[p3][:, h, :], start=(blk == 0), stop=True)

        SK = 3
        for step in range(6 + SK):
            if step >= SK:
                PV_(step - SK)
            if step < 6:
                S_(step)
            yield
        s.copy('act', Osb[pb][:], PO[:, 0:390])
        s.dma('sp', ov[r, blk], Osb[pb][:])
        yield

    def drain(*gens):
        gens = [g for g in gens if g is not None]
        while gens:
            for g in list(gens):
                try:
                    next(g)
                except StopIteration:
                    gens.remove(g)

    drain(pre(0))
    for n in range(len(blocks)):
        drain(heads(n), pre(n + 1) if n + 1 < len(blocks) else None)
    ob = [[c.sb([128, 390]) for _ in range(3)] for _ in range(2)]
    rden = c.sb([128, 8]); outt = [c.sb([128, 384]) for _ in range(2)]
    for tt in (range(ntiles) if do_c2 else []):
        b = tt % 2
        for bi in range(3):
            s.dma('sp', ob[b][bi][:], ob_d[bi, tt * 128:(tt + 1) * 128, :])
        s.tt('pool', ob[b][0][:], ob[b][0][:], ob[b][1][:], ALU.add)
        s.tt('pool', ob[b][0][:], ob[b][0][:], ob[b][2][:], ALU.add)
        o3 = ob[b][0][:].rearrange("p (h v) -> p h v", h=6)
        s.recip(rden[:, 0:6].rearrange("p (h o) -> p h o", o=1), o3[:, :, 64:65])
        s.tt('dve', outt[b][:].rearrange("p (h v) -> p h v", h=6), o3[:, :, 0:64],
             bcast(rden[:, 0:6].rearrange("p (h o) -> p h o", o=1), [128, 6, 64]), ALU.mult)
        s.dma('sp', mix_d[tt * 128:(tt + 1) * 128, 256:640], outt[b][:])


def norm_transpose(s, xt, wb, junk, ss, xn, xnT, Pa, Pb, ident):
    s.act(junk[:], xt[:], AF.Square, accum_out=ss[:, 0:1])
    s.act(ss[:, 1:2], ss[:, 0:1], AF.Sqrt, bias=1e-6, scale=1.0 / D)
    s.recip(ss[:, 2:3], ss[:, 1:2])
    s.stt('dve', xn[:], xt[:], ss[:, 2:3], wb[:], ALU.mult, ALU.mult)
    transpose8(s, xn, xnT, Pa, Pb, ident)


def transpose8(s, xn, xnT, Pa, Pb, ident):
    for dc in range(8):
        pt = Pa if dc < 4 else Pb
        s.tr(pt[:, (dc % 4) * 128:(dc % 4 + 1) * 128], xn[:, dc * 128:(dc + 1) * 128], ident)
    s.copy('act', xnT[:, 0:4, :], Pa[:, :].rearrange("p (a c) -> p a c", a=4))
    s.copy('dve', xnT[:, 4:8, :], Pb[:, :].rearrange("p (a c) -> p a c", a=4))


def phaseA(s, nc, st, x_d, w_d, nw_d, u_d, consts_d, ntiles=16, qk_d=None, qw_d=None, kw_d=None):
    c = Ctx(nc, st, "A")
    BF16 = mybir.dt.bfloat16
    P = c.psum()
    W = c.sb([128, 8, NC], BF16, name="W")
    ident = c.sb([128, 128], name="ident")
    wb = c.sb([128, D], name="wb")
    s.dma('sp', ident[:], consts_d[:, C_ID:C_ID + 128])
    s.dma('sp', wb[:], nw_d.partition_broadcast(128))
    wst = [c.sb([128, NC], name="wst") for _ in range(2)]
    for dc in range(8):
        s.dma('sp', wst[dc % 2][:], w_d[dc * 128:(dc + 1) * 128, :])
        s.copy('pool' if dc % 2 == 0 else 'dve', W[:, dc, :], wst[dc % 2][:])
    xt = [c.sb([128, D]) for _ in range(2)]; xns = [c.sb([128, D]) for _ in range(2)]; junk = c.sb([128, D])
    if qk_d is not None:
        qkw = c.sb([128, 128], name="qkw")
        s.dma('sp', qkw[:, 0:64], qw_d.partition_broadcast(128))
        s.dma('sp', qkw[:, 64:128], kw_d.partition_broadcast(128))
        sq = c.sb([128, 768]); ssq = [c.sb([128, 36]) for _ in range(2)]; qkn = [c.sb([128, 768]) for _ in range(2)]
    sss = [c.sb([128, 4]) for _ in range(2)]; xnT = [c.sb([128, 8, 128], BF16) for _ in range(2)]; usb = [c.sb([128, NC]) for _ in range(2)]
    def pre(tt):
        b = tt % 2
        xn = xns[b]; ss = sss[b]
        s.dma('sp', xt[b][:], x_d[tt * 128:(tt + 1) * 128, :])
        s.act(junk[:], xt[b][:], AF.Square, accum_out=ss[:, 0:1])
        s.act(ss[:, 1:2], ss[:, 0:1], AF.Sqrt, bias=1e-6, scale=1.0 / D)
        s.recip(ss[:, 2:3], ss[:, 1:2])
        s.stt('dve', xn[:], xt[b][:], ss[:, 2:3], wb[:], ALU.mult, ALU.mult)
        yield
        for dc in range(8):
            pt = P[0] if dc < 4 else P[1]
            s.tr(pt[:, (dc % 4) * 128:(dc % 4 + 1) * 128], xn[:, dc * 128:(dc + 1) * 128], ident[:])
        yield
        s.copy('act', xnT[b][:, 0:4, :], P[0][:, :].rearrange("p (a c) -> p a c", a=4))
        s.copy('dve', xnT[b][:, 4:8, :], P[1][:, :].rearrange("p (a c) -> p a c", a=4))
        yield

    def main(tt):
        b = tt % 2
        nblk = (NC + 511) // 512
        for cb in range(nblk):
            c0 = cb * 512; c1 = min(NC, c0 + 512)
            pt = P[2 + cb % 6]
            for dc in range(8):
                s.mm(pt[:, 0:c1 - c0], lhsT=xnT[b][:, dc, :], rhs=W[:, dc, c0:c1], start=(dc == 0), stop=(dc == 7))
            s.copy('act' if cb % 2 == 0 else 'dve', usb[b][:, c0:c1], pt[:, 0:c1 - c0])
            yield
        s.dma('sp', u_d[tt * 128:(tt + 1) * 128, :], usb[b][:])
        if qk_d is not None:
            qk = usb[b][:, O_AQ:O_AQ + 768]; ss_ = ssq[b]
            s.tt('pool', sq[:], qk, qk, ALU.mult)
            s.red('dve', ss_[:, 0:12], sq[:].rearrange("p (h v) -> p h v", h=12), ALU.add)
            rstd_from_ss(s, ss_[:, 24:36], ss_[:, 12:24], ss_[:, 0:12], 1.0 / 64)
            s.tt('dve', qkn[b][:].rearrange("p (h v) -> p h v", h=12), qk.rearrange("p (h v) -> p h v", h=12),
                 bcast(ss_[:, 24:36].rearrange("p (h o) -> p h o", o=1), [128, 12, 64]), ALU.mult)
            s.stt('dve', qkn[b][:, 0:384].rearrange("p (h v) -> p h v", h=6), qkn[b][:, 0:384].rearrange("p (h v) -> p h v", h=6), 0.125,
                  bcast(qkw[:, 0:64].rearrange("p (o v) -> p o v", o=1), [128, 6, 64]), ALU.mult, ALU.mult)
            s.tt('pool', qkn[b][:, 384:768].rearrange("p (h v) -> p h v", h=6), qkn[b][:, 384:768].rearrange("p (h v) -> p h v", h=6),
                 bcast(qkw[:, 64:128].rearrange("p (o v) -> p o v", o=1), [128, 6, 64]), ALU.mult)
            s.dma('sp', qk_d[tt * 128:(tt + 1) * 128, :], qkn[b][:])
        yield

    def drain(*gens):
        gens = [g for g in gens if g is not None]
        while gens:
            for g in list(gens):
                try:
                    next(g)
                except StopIteration:
                    gens.remove(g)

    drain(pre(0))
    for tt in range(ntiles):
        drain(main(tt), pre(tt + 1) if tt + 1 < ntiles else None)


def phaseE(s, nc, st, mix_d, xin_d, w_d, x1_d, consts_d, ntiles=16, ob_d=None):
    c = Ctx(nc, st, "E")
    BF16 = mybir.dt.bfloat16
    P = c.psum()
    W = c.sb([128, 8, D], BF16, name="W")
    ident = c.sb([128, 128], name="ident")
    s.dma('sp', ident[:], consts_d[:, C_ID:C_ID + 128])
    wst = [c.sb([128, D], name="wst") for _ in range(2)]
    for dc in range(8):
        s.dma('sp', wst[dc % 2][:], w_d[dc * 128:(dc + 1) * 128, :])
        s.copy('pool' if dc % 2 == 0 else 'dve', W[:, dc, :], wst[dc % 2][:])
    mt = [c.sb([128, D]) for _ in range(2)]; xt = [c.sb([128, D]) for _ in range(2)]
    mT = [c.sb([128, 8, 128], BF16) for _ in range(2)]; ot = [c.sb([128, D]) for _ in range(2)]
    if ob_d is not None:
        ob = [[c.sb([128, 390]) for _ in range(3)] for _ in range(2)]; rden = [c.sb([128, 8]) for _ in range(2)]
    for tt in range(ntiles):
        b = tt % 2
        if ob_d is None:
            s.dma('sp', mt[b][:], mix_d[tt * 128:(tt + 1) * 128, :])
        else:
            s.dma('sp', mt[b][:, 0:256], mix_d[tt * 128:(tt + 1) * 128, 0:256])
            s.dma('sp', mt[b][:, 640:1024], mix_d[tt * 128:(tt + 1) * 128, 640:1024])
            for bi in range(3):
                s.dma('sp', ob[b][bi][:], ob_d[bi, tt * 128:(tt + 1) * 128, :])
            s.tt('pool', ob[b][0][:], ob[b][0][:], ob[b][1][:], ALU.add)
            s.tt('pool', ob[b][0][:], ob[b][0][:], ob[b][2][:], ALU.add)
            o3 = ob[b][0][:].rearrange("p (h v) -> p h v", h=6)
            s.recip(rden[b][:, 0:6].rearrange("p (h o) -> p h o", o=1), o3[:, :, 64:65])
            s.tt('dve', mt[b][:, 256:640].rearrange("p (h v) -> p h v", h=6), o3[:, :, 0:64],
                 bcast(rden[b][:, 0:6].rearrange("p (h o) -> p h o", o=1), [128, 6, 64]), ALU.mult)
        s.dma('sp', xt[b][:], xin_d[tt * 128:(tt + 1) * 128, :])
        transpose8(s, mt[b], mT[b], P[0], P[1], ident[:])
        for hb in range(2):
            pt = P[2 + 2 * b + hb]
            for dc in range(8):
                s.mm(pt[:, :], lhsT=mT[b][:, dc, :], rhs=W[:, dc, hb * 512:(hb + 1) * 512], start=(dc == 0), stop=(dc == 7))
            s.tt('dve', ot[b][:, hb * 512:(hb + 1) * 512], pt[:, :], xt[b][:, hb * 512:(hb + 1) * 512], ALU.add)
        s.dma('sp', x1_d[tt * 128:(tt + 1) * 128, :], ot[b][:])


GELU_C = 1.5957691216057308


def phaseF1(s, nc, st, x1_d, nw_d, wq_d, sk_d, G_d, xnT_all, consts_d, ntiles=16):
    c = Ctx(nc, st, "G")
    BF16 = mybir.dt.bfloat16
    P = c.psum()
    cst = c.sb([128, C_END], name="cst")
    s.dma('sp', cst[:], consts_d[:, :])
    ident = cst[:, C_ID:C_ID + 128]
    Wq = c.sb([128, 8, D], name="Wq")
    for dc in range(8):
        s.dma('sp', Wq[:, dc, :], wq_d[dc * 128:(dc + 1) * 128, :])
    wb = c.sb([128, D], name="wb")
    s.dma('sp', wb[:], nw_d.partition_broadcast(128))
    SKin = c.sb([128, 2, 128]); SKbd = c.sb([128, 256])
    s.memset('dve', SKin[:], 0.0)
    s.dma('sp', SKin[:, 0, 0:64], sk_d[0, :, :])
    s.dma('sp', SKin[:, 1, 64:128], sk_d[1, :, :])
    s.tr(P[0][:, 0:128], SKin[:, 0, :], ident)
    s.tr(P[0][:, 128:256], SKin[:, 1, :], ident)
    s.copy('dve', SKbd[:], P[0][:, 0:256])
    GT = 2 if ntiles % 2 == 0 else 1
    xt1 = c.sb([128, D]); xt = [xt1, xt1]; xn = c.sb([128, D]); ss = c.sb([128, 4])
    xnT4s = [c.sb([128, 8, GT * 128]) for _ in range(2)]; qT4s = [c.sb([128, 8, GT * 128]) for _ in range(2)]; ssbs = [c.sb([128, 8, 256]) for _ in range(2)]
    sv = c.sb([128, 16, 16]); si = c.sb([128, 16, 16], U32); sif = c.sb([128, 16, 16]); s2 = c.sb([128, 128])
    cand = c.sb([128, 8, 256]); c2 = c.sb([128, 256])
    tv = c.sb([128, 8, 16]); pos = c.sb([128, 8, 16], U32); pa = c.sb([128, 128], U32); pb = c.sb([128, 128], U32)
    af = c.sb([128, 128]); bf = c.sb([128, 128])
    ge = c.sb([128, 8, 16]); gs = c.sb([128, 16]); gate = c.sb([128, 8, 16])
    oh = c.sb([128, 128, 16]); ik = c.sb([128, 128]); jk = c.sb([128, 128])
    junk2 = c.sb([128, D])
    TQ = 64
    gikT = c.sb([128, 384])
    Abuf = c.sb([128, TQ, 128], BF16); Bbuf = c.sb([128, TQ, 128], BF16); Gs = [c.sb([128, 16, 128], BF16) for _ in range(2)]
    i128 = cst[:, C_I128:C_I128 + 128]
    svv = sv[:].rearrange("p (h c) a -> p h c a", c=2)
    sifv = sif[:].rearrange("p (h c) a -> p h c a", c=2)
    i16 = cst[:, C_I16:C_I16 + 16]
    gikTs = [gikT, c.sb([128, 384])]
    Ab = [Abuf[:, 0:32, :], Abuf[:, 32:64, :]]; Bb = [Bbuf[:, 0:32, :], Bbuf[:, 32:64, :]]
    TQ = 32

    def qgroup(grp):
        xnT4 = xnT4s[grp % 2]; qT4 = qT4s[grp % 2]
        for k4 in range(GT):
            tt = grp * GT + k4
            b = tt % 2
            s.dma('sp', xt[b][:], x1_d[tt * 128:(tt + 1) * 128, :])
            s.act(junk2[:], xt[b][:], AF.Square, accum_out=ss[:, 0:1])
            s.act(ss[:, 1:2], ss[:, 0:1], AF.Ln, bias=1e-6, scale=1.0 / D)
            s.act(ss[:, 2:3], ss[:, 1:2], AF.Exp, scale=-0.5)
            s.ts('pool', xn[:], xt[b][:], ss[:, 2:3], None, ALU.mult)
            s.tt('pool', xn[:], xn[:], wb[:], ALU.mult)
            yield
            for dc in range(8):
                pt = P[0] if dc < 4 else P[1]
                s.tr(pt[:, (dc % 4) * 128:(dc % 4 + 1) * 128], xn[:, dc * 128:(dc + 1) * 128], ident)
            s.copy('act', xnT4[:, 0:4, k4 * 128:(k4 + 1) * 128], P[0][:, :].rearrange("p (a c) -> p a c", a=4))
            s.copy('act', xnT4[:, 4:8, k4 * 128:(k4 + 1) * 128], P[1][:, :].rearrange("p (a c) -> p a c", a=4))
            s.copy('pool', xnT_all[:, :, tt * 128:(tt + 1) * 128], xnT4[:, :, k4 * 128:(k4 + 1) * 128])
            yield
        for h in range(8):
            pt = P[2 + h % 2]
            for dc in range(8):
                s.mm(pt[:, 0:GT * 128], lhsT=Wq[:, dc, h * 128:(h + 1) * 128], rhs=xnT4[:, dc, :], start=(dc == 0), stop=(dc == 7))
            s.copy('act', qT4[:, h, :], pt[:, 0:GT * 128])
            yield

    def scores(tt):
        k4 = tt % GT
        qT4 = qT4s[(tt // GT) % 2]
        ssb = ssbs[tt % 2]
        for h in range(8):
            s.mm(P[4 + (h // 2) % 2][:, (h % 2) * 256:(h % 2 + 1) * 256], lhsT=qT4[:, h, k4 * 128:(k4 + 1) * 128], rhs=SKbd[:])
            if h % 2 == 1:
                k = h // 2
                s.copy('act', ssb[:, 2 * k:2 * k + 2, :], P[4 + k % 2][:, :].rearrange("p (a c) -> p a c", a=2))

    def route(tt):
        ssb = ssbs[tt % 2]
        if tt == 0:
            scores(0)
        for half in range(2):
            grp_ = range(half * 8, half * 8 + 8)
            srcs = {g: ssb[:, g // 2, (g % 2) * 128:(g % 2 + 1) * 128] for g in grp_}
            for g in grp_:
                s.op('dve', lambda e, g=g, src=srcs[g]: e.max(out=sv[:, g, 0:8], in_=src), reads=[srcs[g]], writes=[sv[:, g, 0:8]])
            for g in grp_:
                s.op('dve', lambda e, g=g, src=srcs[g]: e.max_index(out=si[:, g, 0:8], in_max=sv[:, g, 0:8], in_values=src), reads=[srcs[g], sv[:, g, 0:8]], writes=[si[:, g, 0:8]])
            for g in grp_:
                s.op('dve', lambda e, g=g, src=srcs[g]: e.match_replace(out=src, in_to_replace=sv[:, g, 0:8], in_values=src, imm_value=-1e30), reads=[srcs[g], sv[:, g, 0:8]], writes=[srcs[g]])
            for g in grp_:
                s.op('dve', lambda e, g=g, src=srcs[g]: e.max(out=sv[:, g, 8:16], in_=src), reads=[srcs[g]], writes=[sv[:, g, 8:16]])
            for g in grp_:
                s.op('dve', lambda e, g=g, src=srcs[g]: e.max_index(out=si[:, g, 8:16], in_max=sv[:, g, 8:16], in_values=src), reads=[srcs[g], sv[:, g, 8:16]], writes=[si[:, g, 8:16]])
            if half == 0:
                yield
        s.copy('pool', sif[:], si[:])
        in0 = bcast(svv[:, :, 0:1, :].rearrange("p h o a -> p h a o"), [128, 8, 16, 16])
        in1 = bcast(svv[:, :, 1:2, :], [128, 8, 16, 16])
        s.tt('dve', cand[:].rearrange("p h (a b) -> p h a b", a=16), in0, in1, ALU.add)
        yield
        hs_ = range(8)
        for h in hs_:
            s.op('dve', lambda e, h=h: e.max(out=tv[:, h, 0:8], in_=cand[:, h, :]), reads=[cand[:, h, :]], writes=[tv[:, h, 0:8]])
        for h in hs_:
            s.op('dve', lambda e, h=h: e.max_index(out=pos[:, h, 0:8], in_max=tv[:, h, 0:8], in_values=cand[:, h, :]), reads=[cand[:, h, :], tv[:, h, 0:8]], writes=[pos[:, h, 0:8]])
        for h in hs_:
            s.op('dve', lambda e, h=h: e.match_replace(out=cand[:, h, :], in_to_replace=tv[:, h, 0:8], in_values=cand[:, h, :], imm_value=-1e30), reads=[cand[:, h, :], tv[:, h, 0:8]], writes=[cand[:, h, :]])
        for h in hs_:
            s.op('dve', lambda e, h=h: e.max(out=tv[:, h, 8:16], in_=cand[:, h, :]), reads=[cand[:, h, :]], writes=[tv[:, h, 8:16]])
        for h in hs_:
            s.op('dve', lambda e, h=h: e.max_index(out=pos[:, h, 8:16], in_max=tv[:, h, 8:16], in_values=cand[:, h, :]), reads=[cand[:, h, :], tv[:, h, 8:16]], writes=[pos[:, h, 8:16]])
        s.tt('dve', ge[:], tv[:], bcast(tv[:, :, 0:1], [128, 8, 16]), ALU.subtract)
        s.act(ge[:], ge[:], AF.Exp)
        yield
        s.red('dve', gs[:, 0:8], ge[:], ALU.add)
        s.recip(gs[:, 8:16], gs[:, 0:8])
        s.tt('dve', gate[:], ge[:], bcast(gs[:, 8:16].rearrange("p (h o) -> p h o", o=1), [128, 8, 16]), ALU.mult)
        posv = pos[:].rearrange("p h k -> p (h k)")
        s.ts('dve', pa[:], posv, 4, None, ALU.logical_shift_right)
        s.ts('dve', pb[:], posv, 15, None, ALU.bitwise_and)
        s.copy('dve', af[:], pa[:])
        s.copy('dve', bf[:], pb[:])
        for (sel, cidx, outk) in ((af, 0, ik), (bf, 1, jk)):
            s.tt('dve', oh[:], bcast(sel[:].rearrange("p (k o) -> p k o", o=1), [128, 128, 16]),
                 bcast(i16.rearrange("p (o a) -> p o a", o=1), [128, 128, 16]), ALU.is_equal)
            ohv = oh[:].rearrange("p (h k) a -> p h k a", h=8)
            s.tt('dve', ohv, ohv, bcast(sifv[:, :, cidx:cidx + 1, :], [128, 8, 16, 16]), ALU.mult)
            s.red('dve', outk[:], oh[:], ALU.add)
        s.tr(P[4][:, 0:128], gate[:].rearrange("p h k -> p (h k)"), ident)
        s.tr(P[4][:, 128:256], ik[:], ident)
        s.tr(P[4][:, 256:384], jk[:], ident)
        s.copy('act', gikTs[tt % 2][:], P[4][:, 0:384])
        if tt + 1 < ntiles:
            scores(tt + 1)
        yield

    def ggen(tt):
        gk = gikTs[tt % 2]
        for qt_ in range(4):
            t0 = qt_ * TQ
            A_ = Ab[qt_ % 2]; B_ = Bb[qt_ % 2]
            s.tt('dve', A_, bcast(gk[:, 128 + t0:128 + t0 + TQ].rearrange("p (t o) -> p t o", o=1), [128, TQ, 128]),
                 bcast(i128.rearrange("p (o a) -> p o a", o=1), [128, TQ, 128]), ALU.is_equal)
            s.tt('pool', A_, A_, bcast(gk[:, t0:t0 + TQ].rearrange("p (t o) -> p t o", o=1), [128, TQ, 128]), ALU.mult)
            s.tt('dve', B_, bcast(gk[:, 256 + t0:256 + t0 + TQ].rearrange("p (t o) -> p t o", o=1), [128, TQ, 128]),
                 bcast(i128.rearrange("p (o a) -> p o a", o=1), [128, TQ, 128]), ALU.is_equal)
            for g4 in range(TQ // 4):
                bank = P[6 + g4 % 2]
                for k in range(4):
                    t = g4 * 4 + k
                    s.mm(bank[:, k * 128:(k + 1) * 128], lhsT=A_[:, t, :], rhs=B_[:, t, :])
                gsb = Gs[(qt_ * 2 + g4 // 4) % 2]
                s.copy('act', gsb[:, (g4 % 4) * 4:(g4 % 4) * 4 + 4, :], bank[:, :].rearrange("p (t j) -> p t j", t=4))
                if g4 % 4 == 3:
                    tb = tt * 128 + t0 + (g4 // 4) * 16
                    s.dma('sp', G_d[tb:tb + 16, :].rearrange("t (i j) -> i t j", j=128), gsb[:])
            yield

    def drain(*gens):
        gens = [g for g in gens if g is not None]
        while gens:
            for g in list(gens):
                try:
                    next(g)
                except StopIteration:
                    gens.remove(g)

    QPRE = 4
    drain(qgroup(0))
    prev = None
    for tt in range(ntiles):
        qg = None
        if tt % GT == 0:
            g1 = tt // GT + 1
            qg = qgroup(g1) if g1 < ntiles // GT else None
        rt = route(tt); gg = ggen(prev) if prev is not None else None
        gens = [g for g in (rt, gg) if g is not None]
        for _ in range(QPRE):
            if qg is not None:
                try:
                    next(qg)
                except StopIteration:
                    qg = None
        while gens:
            for g in list(gens):
                try:
                    next(g)
                except StopIteration:
                    gens.remove(g)
                if g is rt and len(gens) and qg is not None:
                    pass
            for _ in range(2):
                if qg is not None:
                    try:
                        next(qg)
                    except StopIteration:
                        qg = None
        if qg is not None:
            drain(qg)
        prev = tt
    drain(ggen(prev))


def phaseF2(s, nc, st, x1_d, downT_d, up_d, x2_d, G_d, xnT_all, consts_d, ntiles=16, nslab=32):
    c = Ctx(nc, st, "H")
    BF16 = mybir.dt.bfloat16
    Pa = [st.enter_context(nc.psum_tensor("%s_pa%d" % (c.pfx, i), [128, 512], F32)) for i in range(2)]
    Pt = [st.enter_context(nc.psum_tensor("%s_pt%d" % (c.pfx, i), [128, 1024], BF16)) for i in range(2)]
    Po = [st.enter_context(nc.psum_tensor("%s_po%d" % (c.pfx, i), [128, 512], F32)) for i in range(4)]
    idf = c.sb([128, 128]); id16 = c.sb([128, 128], BF16)
    s.dma('sp', idf[:], consts_d[:, C_ID:C_ID + 128])
    s.copy('dve', id16[:], idf[:])
    acc = c.sb([128, ntiles, D], name="acc")
    dsl = [c.sb([128, 8, 512], BF16, name="dsl") for _ in range(2)]
    usl = [c.sb([128, 4, D], BF16, name="usl") for _ in range(2)]
    Gt = [c.sb([128, 512], BF16, name="Gt") for _ in range(4)]
    ga = [c.sb([128, 512], name="ga") for _ in range(2)]
    c16 = [c.sb([128, 512], BF16, name="c16") for _ in range(2)]
    cT = [c.sb([128, 4, 128], BF16, name="cT") for _ in range(2)]
    xt = [c.sb([128, D]) for _ in range(2)]

    dst32 = c.sb([128, 8, 512], name="dst32"); ust32 = c.sb([128, 4, D], name="ust32")

    def load_slab(sl):
        e0 = sl * 512
        s.dma('sp', dst32[:], downT_d[:, e0:e0 + 512].rearrange("(dc p) e -> p dc e", p=128))
        s.dma('sp', ust32[:], up_d[e0:e0 + 512, :].rearrange("(k p) d -> p k d", p=128))

    def cast_slab(sl, part):
        if part < 2:
            s.copy('dve', dsl[sl % 2][:, part * 4:(part + 1) * 4, :], dst32[:, part * 4:(part + 1) * 4, :])
        else:
            p2 = part - 2
            s.copy('dve', usl[sl % 2][:, p2 * 2:(p2 + 1) * 2, :], ust32[:, p2 * 2:(p2 + 1) * 2, :])

    load_slab(0)
    for part in range(4):
        cast_slab(0, part)
    it = 0
    for sl in range(nslab):
        if sl + 1 < nslab:
            load_slab(sl + 1)
        e0 = sl * 512
        D_ = dsl[sl % 2]; U_ = usl[sl % 2]

        def stage1(tt, n):
            s.dma('sp', Gt[n % 4][:], G_d[tt * 128:(tt + 1) * 128, e0:e0 + 512])
            for dc in range(8):
                s.mm(Pa[n % 2][:, :], lhsT=xnT_all[:, dc, tt * 128:(tt + 1) * 128], rhs=D_[:, dc, :], start=(dc == 0), stop=(dc == 7))
            s.act(ga[n % 2][:], Pa[n % 2][:, :], AF.Gelu_apprx_tanh)
            s.tt('pool', c16[n % 2][:], ga[n % 2][:], Gt[n % 4][:], ALU.mult)

        def stage2(tt, n):
            for k in range(4):
                s.tr(Pt[n % 2][:, k * 128:(k + 1) * 128], c16[n % 2][:, k * 128:(k + 1) * 128], id16[:])
            s.copy('act', cT[n % 2][:], Pt[n % 2][:, 0:512].rearrange("p (k t) -> p k t", k=4))

        def stage3(tt, n):
            for hb in range(2):
                po = Po[(n % 2) * 2 + hb]
                for k in range(4):
                    s.mm(po[:, :], lhsT=cT[n % 2][:, k, :], rhs=U_[:, k, hb * 512:(hb + 1) * 512], start=(k == 0), stop=(k == 3))
                if sl == 0:
                    s.copy('dve', acc[:, tt, hb * 512:(hb + 1) * 512], po[:, :])
                else:
                    s.tt('dve', acc[:, tt, hb * 512:(hb + 1) * 512], acc[:, tt, hb * 512:(hb + 1) * 512], po[:, :], ALU.add)

        for step in range(ntiles + 2):
            if sl + 1 < nslab and ntiles >= 12 and step in (5, 7, 9, 11):
                cast_slab(sl + 1, (step - 5) // 2)
            elif sl + 1 < nslab and ntiles < 12 and step == ntiles - 1:
                for part in range(4):
                    cast_slab(sl + 1, part)
            if step < ntiles:
                stage1(step, it + step)
            if 1 <= step <= ntiles:
                stage2(step - 1, it + step - 1)
            if 2 <= step:
                stage3(step - 2, it + step - 2)
        it += ntiles
    for tt in range(ntiles):
        b = tt % 2
        s.dma('sp', xt[b][:], x1_d[tt * 128:(tt + 1) * 128, :])
        s.tt('pool', xt[b][:], xt[b][:], acc[:, tt, :], ALU.add)
        s.dma('sp', x2_d[tt * 128:(tt + 1) * 128, :], xt[b][:])


def phaseFD(s, nc, st, x1_d, nw_d, wq_d, sk_d, downT_d, up_d, x2_d, G_d, consts_d, ntiles=16, nslab=32):
    BF16 = mybir.dt.bfloat16
    Ctx.uid += 1
    xnT_all = st.enter_context(nc.sbuf_tensor("xnTall%d" % Ctx.uid, [128, 8, ntiles * 128], BF16))
    with ExitStack() as st1:
        phaseF1(s, nc, st1, x1_d, nw_d, wq_d, sk_d, G_d, xnT_all, consts_d, ntiles)
        s.barrier(); s.flush()
    with ExitStack() as st2:
        phaseF2(s, nc, st2, x1_d, downT_d, up_d, x2_d, G_d, xnT_all, consts_d, ntiles, nslab)
        s.barrier(); s.flush()


NLAYER = 2
_CACHE = {}


def build_program():
    nc = bass.Bass("TRN2", target_bir_lowering=False)

    def din(name, shape):
        return nc.dram_tensor(name, list(shape), F32, kind="ExternalInput").ap()

    def dint(name, shape):
        return nc.dram_tensor(name, list(shape), F32, kind="Internal").ap()

    x_d = din("x", [T, D])
    anw = din("attn_norm_w", [NLAYER, D]); w_in = din("w_in", [NLAYER, D, NC])
    hlb = din("hlb", [NLAYER, 256]); hnw = din("hnw", [NLAYER, 64])
    qw = din("qw", [NLAYER, 64]); kw = din("kw", [NLAYER, 64])
    biasg = din("biasg", [18, 128, 256]); maskadd = din("maskadd", [128, 256])
    cw = din("cw", [NLAYER, 4, 896]); cb = din("cb", [NLAYER, 896]); dtb = din("dtb", [NLAYER, 6])
    alog = din("alog", [NLAYER, 6]); dsk = din("dsk", [NLAYER, 6]); snw = din("snw", [NLAYER, 384])
    w_out = din("w_out", [NLAYER, D, D]); fnw = din("fnw", [NLAYER, D]); wq = din("wq", [NLAYER, D, D])
    sk = din("sk", [NLAYER, 2, 128, 64])
    down = [din("downT%d" % l, [D, 16384]) for l in range(NLAYER)]
    up = [din("up%d" % l, [16384, D]) for l in range(NLAYER)]
    consts = din("consts", [128, C_END])
    out_d = nc.dram_tensor("out", [T, D], F32, kind="ExternalOutput").ap()
    U = dint("U", [T, NC]); QK = dint("QK", [T, 768]); OB = dint("OB", [3, T, 390]); MIX = dint("MIX", [T, D])
    X1 = dint("X1", [T, D]); X2 = dint("X2", [T, D]); G_d = nc.dram_tensor("Gd", [T, 16384], mybir.dt.bfloat16, kind="Internal").ap()

    s = Sched(nc)
    s.ro = {"x", "attn_norm_w", "w_in", "hlb", "hnw", "qw", "kw", "biasg", "maskadd", "cw", "cb", "dtb", "alog", "dsk",
            "snw", "w_out", "fnw", "wq", "sk", "downT0", "downT1", "up0", "up1", "consts"}
    with ExitStack() as semstack:
        s.semstack = semstack
        xin = x_d
        for l in range(NLAYER):
            xout = out_d if l == NLAYER - 1 else X2
            with ExitStack() as st:
                phaseA(s, nc, st, xin, w_in[l], anw[l], U, consts, qk_d=QK, qw_d=qw[l], kw_d=kw[l])
                s.barrier(); s.flush()
            with ExitStack() as st:
                phaseB(s, nc, st, U, MIX, hlb, hnw[l], consts, l)
                s.barrier(); s.flush()
            with ExitStack() as st:
                phaseC(s, nc, st, U, MIX, QK, OB, qw[l], kw[l], biasg, maskadd, consts, do_c0=False, do_c2=False)
                s.barrier(); s.flush()
            with ExitStack() as st:
                phaseD(s, nc, st, U, MIX, cw[l], cb[l], dtb[l], alog[l], dsk[l], snw[l], consts)
                s.barrier(); s.flush()
            with ExitStack() as st:
                phaseE(s, nc, st, MIX, xin, w_out[l], X1, consts, ob_d=OB)
                s.barrier(); s.flush()
            with ExitStack() as st:
                phaseFD(s, nc, st, X1, fnw[l], wq[l], sk[l], down[l], up[l], xout, G_d, consts)
            xin = X2
    return nc


def kernel(x, attn_norm_w, w_in, hgrn_lower_bounds, hgrn_norm_w, q_norm_w, k_norm_w, rel_bias,
           ssm_conv_w, ssm_conv_b, ssm_dt_bias, ssm_a_log, ssm_d, ssm_norm_w, w_out,
           ffn_norm_w, peer_w_query, peer_sub_keys, peer_down, peer_up):
    f = lambda a: np.ascontiguousarray(np.asarray(a, dtype=np.float32))
    if "nc" not in _CACHE:
        _CACHE["nc"] = build_program()
    nc = _CACHE["nc"]
    biasg, maskadd = make_attn_consts(f(rel_bias))
    shared = {
        "attn_norm_w": f(attn_norm_w), "w_in": f(w_in), "hlb": f(hgrn_lower_bounds), "hnw": f(hgrn_norm_w),
        "qw": f(q_norm_w), "kw": f(k_norm_w), "biasg": biasg, "maskadd": maskadd,
        "cw": f(ssm_conv_w), "cb": f(ssm_conv_b), "dtb": f(ssm_dt_bias), "alog": f(ssm_a_log), "dsk": f(ssm_d),
        "snw": f(ssm_norm_w), "w_out": f(w_out), "fnw": f(ffn_norm_w), "wq": f(peer_w_query), "sk": f(peer_sub_keys),
        "consts": make_consts(),
    }
    pd = f(peer_down); pu = f(peer_up)
    for l in range(NLAYER):
        shared["downT%d" % l] = np.ascontiguousarray(pd[l].T)
        shared["up%d" % l] = pu[l]
    xs = f(x)
    n = xs.shape[0]
    in_maps = [dict(shared, x=xs[i]) for i in range(n)]
    res = run_bass_kernel_spmd(nc, in_maps, core_ids=list(range(n)))
    return np.stack([np.asarray(r["out"]) for r in res.results], axis=0).astype(np.float32)
```

```python
from contextlib import ExitStack
import numpy as np
import concourse.bass as bass
import concourse.mybir as mybir
from concourse.bass_utils import run_bass_kernel_spmd

F32 = mybir.dt.float32
I32 = mybir.dt.int32
U32 = mybir.dt.uint32
ALU = mybir.AluOpType
AF = mybir.ActivationFunctionType
AX = mybir.AxisListType

NDMA = 24
EPOCH = 30000


def _prod(xs):
    r = 1
    for v in xs:
        r *= int(v)
    return r


def region(a):
    t = a.tensor
    name = t.name
    ap = a.ap
    off = int(a.offset)
    sp = str(a.space)
    if 'DRAM' in sp.upper():
        ext = sum((int(c) - 1) * abs(int(s)) for s, c in ap)
        return (name, 0, 1, off, off + ext + 1)
    row = _prod(t.shape[1:])
    p0 = off // row
    f0 = off % row
    pstep, pcnt = int(ap[0][0]), int(ap[0][1])
    if pstep == 0:
        pcnt = 1
    ext = sum((int(c) - 1) * abs(int(s)) for s, c in ap[1:])
    if 'PSUM' in sp.upper():
        return (name, (p0 // 32) * 32, ((p0 + pcnt + 31) // 32) * 32, 0, row)
    return (name, p0, p0 + pcnt, f0, f0 + ext + 1)


class Sched:
    def __init__(self, nc):
        self.nc = nc
        self.eng = {'pe': nc.tensor, 'dve': nc.vector, 'act': nc.scalar,
                    'pool': nc.gpsimd, 'sp': nc.sync}
        self.ops = {k: [] for k in self.eng}
        self.cnt = {k: 0 for k in self.eng}
        self.waited = {k: {} for k in self.eng}
        self.acc = {}
        self.dma_n = 0
        self.keys = set()
        self.ro = set()
        self.sems = {}
        self.semstack = None

    def _deps(self, eng, reads, writes):
        deps = {}

        def add(key, val):
            if deps.get(key, 0) < val:
                deps[key] = val

        for is_w, aps in ((False, reads), (True, writes)):
            for a in aps:
                if a.tensor.name in self.ro:
                    continue
                r = region(a)
                lst = self.acc.get(r[0], [])
                psum = 'PSUM' in str(a.space).upper()
                for (key, val, w2, e2, p0, p1, f0, f1) in lst:
                    if not (is_w or w2) and not (psum and e2 != eng):
                        continue
                    if p1 <= r[1] or r[2] <= p0 or f1 <= r[3] or r[4] <= f0:
                        continue
                    if e2 == eng and eng != 'dma':
                        if eng == 'pe':
                            continue
                    add(key, val)
        return deps

    def _record(self, eng, key, val, reads, writes):
        for is_w, aps in ((False, reads), (True, writes)):
            for a in aps:
                if a.tensor.name in self.ro:
                    continue
                r = region(a)
                lst = self.acc.setdefault(r[0], [])
                new = []
                for rec in lst:
                    (k2, v2, w2, e2, p0, p1, f0, f1) = rec
                    contained = (r[1] <= p0 and p1 <= r[2] and r[3] <= f0 and f1 <= r[4])
                    if contained and (is_w or (not w2 and e2 == eng and eng != 'dma')):
                        continue
                    new.append(rec)
                new.append((key, val, is_w, eng, r[1], r[2], r[3], r[4]))
                self.acc[r[0]] = new

    def _key(self, eng):
        idx = self.cnt[eng]
        return (eng, idx // EPOCH), idx % EPOCH + 1

    def op(self, eng, fn, reads=(), writes=()):
        deps = self._deps(eng, reads, writes)
        waits = []
        w = self.waited[eng]
        for key, val in deps.items():
            if w.get(key, 0) < val:
                w[key] = val
                waits.append((key, val))
        key, val = self._key(eng)
        self.cnt[eng] += 1
        self.keys.add(key)
        self.ops[eng].append((fn, waits, key, 1))
        self._record(eng, key, val, reads, writes)

    def dma(self, q, out, in_, **kw):
        n = self.dma_n
        self.dma_n += 1
        slot = n % NDMA
        key = ('dma', slot)
        val = 16 * (n // NDMA + 1)
        deps = self._deps('dma', [in_], [out])
        if n >= NDMA:
            deps[key] = max(deps.get(key, 0), val - 16)
        waits = []
        w = self.waited[q]
        for k, v in deps.items():
            if w.get(k, 0) < v:
                w[k] = v
                waits.append((k, v))
        self.keys.add(key)
        self.ops[q].append((lambda e: e.dma_start(out=out, in_=in_, **kw), waits, key, 16))
        self._record('dma', key, val, [in_], [out])

    def dma_custom(self, q, fn, reads, writes):
        n = self.dma_n
        self.dma_n += 1
        slot = n % NDMA
        key = ('dma', slot)
        val = 16 * (n // NDMA + 1)
        deps = self._deps('dma', reads, writes)
        if n >= NDMA:
            deps[key] = max(deps.get(key, 0), val - 16)
        waits = []
        w = self.waited[q]
        for k, v in deps.items():
            if w.get(k, 0) < v:
                w[k] = v
                waits.append((k, v))
        self.keys.add(key)
        self.ops[q].append((fn, waits, key, 16))
        self._record('dma', key, val, reads, writes)

    def barrier(self):
        targets = {}
        for e in self.eng:
            if self.cnt[e] > 0:
                idx = self.cnt[e] - 1
                targets[(e, idx // EPOCH)] = idx % EPOCH + 1
        for n in range(max(0, self.dma_n - NDMA), self.dma_n):
            targets[('dma', n % NDMA)] = 16 * (n // NDMA + 1)
        for e in self.eng:
            waits = []
            w = self.waited[e]
            for k, v in targets.items():
                if k[0] == e:
                    continue
                if w.get(k, 0) < v:
                    w[k] = v
                    waits.append((k, v))
            if waits:
                key, val = self._key(e)
                self.cnt[e] += 1
                self.keys.add(key)
                self.ops[e].append((None, waits, key, 1))
        self.acc = {}

    def finish(self):
        self.barrier()

    def flush(self):
        nc = self.nc
        for k in sorted(self.keys, key=str):
            if k not in self.sems:
                self.sems[k] = self.semstack.enter_context(nc.semaphore("s_%s_%d" % (k[0], k[1])))
        sems = self.sems
        ops = self.ops
        self.ops = {k: [] for k in self.eng}
        with nc.Block() as block:
            def mk(ename):
                def body(e):
                    for (fn, waits, key, inc) in ops[ename]:
                        for (k, v) in waits:
                            e.wait_ge(sems[k], v)
                        if fn is None:
                            ins = e.nop()
                        else:
                            ins = fn(e)
                        ins.then_inc(sems[key], inc)
                return body

            block.tensor(mk('pe'))
            block.vector(mk('dve'))
            block.scalar(mk('act'))
            block.gpsimd(mk('pool'))
            block.sync(mk('sp'))

    def emit(self):
        from contextlib import ExitStack
        with ExitStack() as st:
            self.semstack = st
            self.flush()


def _aps(*xs):
    return [x for x in xs if x is not None and not isinstance(x, (int, float))]


def _h_tt(self, eng, out, in0, in1, op):
    self.op(eng, lambda e: e.tensor_tensor(out=out, in0=in0, in1=in1, op=op), reads=[in0, in1], writes=[out])


def _h_ts(self, eng, out, in0, s1, s2, op0, op1=None):
    if op1 is None:
        self.op(eng, lambda e: e.tensor_scalar(out=out, in0=in0, scalar1=s1, scalar2=None, op0=op0),
                reads=_aps(in0, s1), writes=[out])
    else:
        self.op(eng, lambda e: e.tensor_scalar(out=out, in0=in0, scalar1=s1, scalar2=s2, op0=op0, op1=op1),
                reads=_aps(in0, s1, s2), writes=[out])


def _h_stt(self, eng, out, in0, scalar, in1, op0, op1):
    self.op(eng, lambda e: e.scalar_tensor_tensor(out=out, in0=in0, scalar=scalar, in1=in1, op0=op0, op1=op1),
            reads=_aps(in0, scalar, in1), writes=[out])


def _h_act(self, out, in_, func, bias=None, scale=None, accum_out=None):
    kw = {}
    if bias is not None:
        kw['bias'] = bias
    if scale is not None:
        kw['scale'] = scale
    if accum_out is not None:
        kw['accum_out'] = accum_out
    self.op('act', lambda e: e.activation(out=out, in_=in_, func=func, **kw),
            reads=_aps(in_, bias, scale), writes=_aps(out, accum_out))


def _h_copy(self, eng, out, in_):
    if eng == 'act':
        self.op('act', lambda e: e.copy(out=out, in_=in_), reads=[in_], writes=[out])
    else:
        self.op(eng, lambda e: e.tensor_copy(out=out, in_=in_), reads=[in_], writes=[out])


def _h_mm(self, out, lhsT, rhs, start=True, stop=True):
    self.op('pe', lambda e: e.matmul(out, lhsT=lhsT, rhs=rhs, start=start, stop=stop),
            reads=[lhsT, rhs], writes=[out])


def _h_tr(self, out, in_, ident):
    self.op('pe', lambda e: e.transpose(out=out, in_=in_, identity=ident), reads=[in_, ident], writes=[out])


def _h_red(self, eng, out, in_, op, axis=AX.X):
    self.op(eng, lambda e: e.tensor_reduce(out=out, in_=in_, axis=axis, op=op), reads=[in_], writes=[out])


def _h_memset(self, eng, ap, val):
    self.op(eng, lambda e: e.memset(ap, val), reads=[], writes=[ap])


def _h_recip(self, out, in_):
    self.op('dve', lambda e: e.reciprocal(out=out, in_=in_), reads=[in_], writes=[out])


Sched.tt = _h_tt
Sched.ts = _h_ts
Sched.stt = _h_stt
Sched.act = _h_act
Sched.copy = _h_copy
Sched.mm = _h_mm
Sched.tr = _h_tr
Sched.red = _h_red
Sched.memset = _h_memset
Sched.recip = _h_recip


def bcast(ap, shape):
    assert len(ap.shape) == len(shape), (ap.shape, shape)
    new = []
    for (st, cnt), tgt in zip(ap.ap, shape):
        if int(cnt) == int(tgt):
            new.append([int(st), int(cnt)])
        else:
            assert int(cnt) == 1
            new.append([0, int(tgt)])
    return bass.AP(ap.tensor, ap.offset, new)


T = 2048; D = 1024; NC = 3462
O_HQ, O_HF, O_HI, O_HG, O_AQ, O_AK, O_AV, O_SZ, O_XBC, O_DT = 0, 256, 512, 768, 1024, 1408, 1792, 2176, 2560, 3456

C_ID, C_TRI, C_TRI2, C_CO0, C_CO1, C_CI, C_I16, C_I128, C_END = 0, 128, 256, 384, 512, 640, 642, 658, 786


def make_consts():
    c = np.zeros((128, C_END), np.float32)
    s = np.arange(128)[:, None]; t = np.arange(128)[None, :]
    same = (s // 64) == (t // 64)
    c[:, C_ID:C_ID + 128] = np.eye(128)
    c[:, C_TRI:C_TRI + 128] = (same & (s <= t))
    c[:, C_TRI2:C_TRI2 + 128] = (same & (s > t))
    c[:, C_CO0:C_CO0 + 128] = (s < 64) * np.ones((1, 128))
    c[:, C_CO1:C_CO1 + 128] = (s >= 64) * np.ones((1, 128))
    c[:, C_CI] = (np.arange(128) < 64)
    c[:, C_CI + 1] = (np.arange(128) >= 64)
    c[:, C_I16:C_I16 + 16] = np.arange(16)[None, :]
    c[:, C_I128:C_I128 + 128] = np.arange(128)[None, :]
    return c


class Ctx:
    uid = 0

    def __init__(self, nc, st, pfx):
        Ctx.uid += 1
        self.nc, self.st, self.pfx = nc, st, "%s%d" % (pfx, Ctx.uid)
        self.n = 0

    def sb(self, shape, dt=F32, name=None):
        self.n += 1
        return self.st.enter_context(self.nc.sbuf_tensor("%s_%s%d" % (self.pfx, name or "t", self.n), list(shape), dt))

    def psum(self):
        return [self.st.enter_context(self.nc.psum_tensor("%s_ps%d" % (self.pfx, i), [128, 512], F32)) for i in range(8)]


def rstd_from_ss(s, out, tmp, ss, scale, eps=1e-6):
    s.act(tmp, ss, AF.Sqrt, bias=eps, scale=scale)
    s.recip(out, tmp)


def phaseB(s, nc, st, u_d, mix_d, hlb_d, hnw_d, consts_d, layer, ntiles=16):
    c = Ctx(nc, st, "B")
    P = c.psum()
    cst = c.sb([128, C_END], name="cst")
    s.dma('sp', cst[:], consts_d[:, :])
    ident = cst[:, C_ID:C_ID + 128]; tri = cst[:, C_TRI:C_TRI + 128]; tri2 = cst[:, C_TRI2:C_TRI2 + 128]
    chind = cst[:, C_CI:C_CI + 2]
    hnw = c.sb([128, 64], name="hnw")
    s.dma('sp', hnw[:], hnw_d.partition_broadcast(128))
    if layer > 0:
        h0 = c.sb([128, 256]); h1 = c.sb([128, 256]); lb = c.sb([128, 256]); oml = c.sb([128, 256])
        s.dma('sp', h0[:], hlb_d[0, :].partition_broadcast(128))
        s.dma('sp', h1[:], hlb_d[1, :].partition_broadcast(128))
        s.tt('dve', h1[:], h1[:], h0[:], ALU.subtract)
        s.act(lb[:], h1[:], AF.Sigmoid)
        s.ts('dve', oml[:], lb[:], -1.0, 1.0, ALU.mult, ALU.add)
    SA = c.sb([64, 4, 64], name="SA"); SB = c.sb([64, 4, 64], name="SB")
    s.memset('dve', SA[:], 0.0)
    NB = 2
    qTa = [c.sb([64, 4, 128]) for _ in range(NB)]; qTb = [c.sb([64, 4, 128]) for _ in range(NB)]
    kh0 = [c.sb([128, 256]) for _ in range(NB)]; kh1 = [c.sb([128, 256]) for _ in range(NB)]
    for t_ in qTa + qTb + kh0 + kh1:
        s.memset('dve', t_[:], 0.0)
    tin = [c.sb([128, 1024], name="tin") for _ in range(3)]
    fg = c.sb([128, 256]); logf = c.sb([128, 256]); key = c.sb([128, 256])
    eb = c.sb([128, 256]); enb = c.sb([128, 256]); er = c.sb([128, 256])
    qt = c.sb([128, 256]); kt = c.sb([128, 256])
    dec = [c.sb([64, 8]) for _ in range(NB)]; qT = [c.sb([64, 4, 128]) for _ in range(NB)]; kT = [c.sb([64, 4, 128]) for _ in range(NB)]
    sg = [c.sb([128, 256]) for _ in range(NB)]
    ATs = [c.sb([128, 128]) for _ in range(2)]
    osb = c.sb([128, 256]); sq = c.sb([128, 256]); ss = c.sb([128, 12])
    outt = [c.sb([128, 256]) for _ in range(2)]

    def pre(tt):
        p = tt % NB
        ti = tin[tt % 3]
        s.dma('sp', ti[:], u_d[tt * 128:(tt + 1) * 128, 0:1024])
        q = ti[:, 0:256]; f = ti[:, 256:512]; g = ti[:, 768:1024]
        s.act(fg[:], f, AF.Sigmoid)
        if layer > 0:
            s.tt('dve', fg[:], fg[:], oml[:], ALU.mult)
            s.tt('dve', fg[:], fg[:], lb[:], ALU.add)
        s.act(sg[p][:], g, AF.Silu)
        s.act(logf[:], fg[:], AF.Ln)
        yield
        s.ts('dve', key[:], fg[:], -1.0, 1.0, ALU.mult, ALU.add)
        s.mm(P[0][:, 0:256], lhsT=tri, rhs=logf[:])
        s.mm(P[0][:, 256:512], lhsT=tri2, rhs=logf[:])
        for h in range(4):
            s.mm(P[1][0:64, h * 2:h * 2 + 2], lhsT=logf[:, h * 64:(h + 1) * 64], rhs=chind)
        yield
        s.act(eb[:], P[0][:, 0:256], AF.Exp)
        s.act(enb[:], P[0][:, 0:256], AF.Exp, scale=-1.0)
        s.act(er[:], P[0][:, 256:512], AF.Exp)
        s.act(dec[p][:], P[1][0:64, 0:8], AF.Exp)
        yield
        s.stt('dve', qt[:], q, 0.125, eb[:], ALU.mult, ALU.mult)
        s.tt('pool', kt[:], key[:], enb[:], ALU.mult)
        s.tt('pool', kh0[p][0:64, :], key[0:64, :], er[0:64, :], ALU.mult)
        s.tt('pool', kh1[p][64:128, :], key[64:128, :], er[64:128, :], ALU.mult)
        s.tt('pool', sg[p][:].rearrange("p (h v) -> p h v", h=4), sg[p][:].rearrange("p (h v) -> p h v", h=4),
             bcast(hnw[:].rearrange("p (o v) -> p o v", o=1), [128, 4, 64]), ALU.mult)
        yield
        for h in range(4):
            s.tr(P[2][0:64, h * 128:(h + 1) * 128], qt[:, h * 64:(h + 1) * 64], ident)
            s.tr(P[3][0:64, h * 128:(h + 1) * 128], kt[:, h * 64:(h + 1) * 64], ident)
        yield
        p2v = P[2][0:64, :].rearrange("p (h t) -> p h t", h=4)
        p3v = P[3][0:64, :].rearrange("p (h t) -> p h t", h=4)
        s.copy('dve', qT[p][:], p2v)
        s.copy('dve', qTa[p][:, :, 0:64], p2v[:, :, 0:64])
        s.copy('dve', qTb[p][:, :, 64:128], p2v[:, :, 64:128])
        s.copy('act', kT[p][:], p3v)
        yield

    def heads(tt):
        p = tt % NB
        ti = tin[tt % 3]
        for h in range(4):
            hs = slice(h * 64, (h + 1) * 64)
            vs = slice(512 + h * 64, 512 + (h + 1) * 64)
            s.mm(P[6][0:64, (2 * h) * 64:(2 * h + 1) * 64], lhsT=kh0[p][:, hs], rhs=ti[:, vs])
            s.mm(P[6][0:64, (2 * h + 1) * 64:(2 * h + 2) * 64], lhsT=kh1[p][:, hs], rhs=ti[:, vs])
        yield
        for h in range(4):
            hs = slice(h * 64, (h + 1) * 64)
            vs = slice(512 + h * 64, 512 + (h + 1) * 64)
            AT = P[4 + h % 2][:, 0:128]
            s.mm(AT, lhsT=kT[p][:, h, :], rhs=qT[p][:, h, :])
            A = ATs[h % 2]
            s.tt('dve', A[:], AT, tri, ALU.mult)
            o = P[7][:, hs]
            s.mm(o, lhsT=A[:], rhs=ti[:, vs], start=True, stop=False)
            s.mm(o, lhsT=qTa[p][:, h, :], rhs=SA[:, h, :], start=False, stop=False)
            u0 = P[6][0:64, (2 * h) * 64:(2 * h + 1) * 64]
            u1 = P[6][0:64, (2 * h + 1) * 64:(2 * h + 2) * 64]
            s.stt('dve', SB[:, h, :], SA[:, h, :], dec[p][:, 2 * h:2 * h + 1], u0, ALU.mult, ALU.add)
            s.mm(o, lhsT=qTb[p][:, h, :], rhs=SB[:, h, :], start=False, stop=True)
            s.stt('dve', SA[:, h, :], SB[:, h, :], dec[p][:, 2 * h + 1:2 * h + 2], u1, ALU.mult, ALU.add)
            yield
        s.copy('act', osb[:], P[7][:, 0:256])
        s.tt('pool', sq[:], osb[:], osb[:], ALU.mult)
        s.red('dve', ss[:, 0:4], sq[:].rearrange("p (h v) -> p h v", h=4), ALU.add)
        rstd_from_ss(s, ss[:, 8:12], ss[:, 4:8], ss[:, 0:4], 1.0 / 64)
        ot = outt[tt % 2]
        s.tt('dve', ot[:].rearrange("p (h v) -> p h v", h=4), osb[:].rearrange("p (h v) -> p h v", h=4),
             bcast(ss[:, 8:12].rearrange("p (h o) -> p h o", o=1), [128, 4, 64]), ALU.mult)
        s.tt('dve', ot[:], ot[:], sg[p][:], ALU.mult)
        s.dma('sp', mix_d[tt * 128:(tt + 1) * 128, 0:256], ot[:])
        yield

    def drain(*gens):
        gens = [g for g in gens if g is not None]
        while gens:
            for g in list(gens):
                try:
                    next(g)
                except StopIteration:
                    gens.remove(g)

    drain(pre(0))
    for tt in range(ntiles):
        drain(heads(tt), pre(tt + 1) if tt + 1 < ntiles else None)


def phaseD(s, nc, st, u_d, mix_d, cw_d, cb_d, dtb_d, alog_d, dsk_d, nw_d, consts_d, ntiles=16):
    c = Ctx(nc, st, "D")
    BF16 = mybir.dt.bfloat16
    P = c.psum()
    cst = c.sb([128, C_END], name="cst")
    s.dma('sp', cst[:], consts_d[:, :])
    ident = cst[:, C_ID:C_ID + 128]; tri = cst[:, C_TRI:C_TRI + 128]; tri2 = cst[:, C_TRI2:C_TRI2 + 128]
    co0 = cst[:, C_CO0:C_CO0 + 128]; co1 = cst[:, C_CO1:C_CO1 + 128]
    cw = c.sb([128, 4, 896], name="cw"); cb = c.sb([128, 896], name="cb")
    for j in range(4):
        s.dma('sp', cw[:, j, :], cw_d[j, :].partition_broadcast(128))
    s.dma('sp', cb[:], cb_d.partition_broadcast(128))
    sm = c.sb([128, 32], name="sm")
    s.dma('sp', sm[:, 0:6], dtb_d.partition_broadcast(128))
    s.dma('sp', sm[:, 6:12], alog_d.partition_broadcast(128))
    s.dma('sp', sm[:, 18:24], dsk_d.partition_broadcast(128))
    nwb = c.sb([128, 384], name="nwb")
    s.dma('sp', nwb[:], nw_d.partition_broadcast(128))
    s.act(sm[:, 24:30], sm[:, 6:12], AF.Exp)
    s.ts('dve', sm[:, 12:18], sm[:, 24:30], -1.0, None, ALU.mult)
    dtbias = sm[:, 0:6]; Ab = sm[:, 12:18]
    R = [c.sb([128, 6, 64], name="R") for _ in range(6)]
    R16 = [c.sb([128, 6, 64], BF16, name="R16") for _ in range(6)]
    s.memset('dve', R[0][:], 0.0)
    s.memset('dve', R16[0][:], 0.0)
    Bz0 = c.sb([128, 256], BF16); Bz1 = c.sb([128, 256], BF16)
    s.memset('dve', Bz0[:], 0.0); s.memset('dve', Bz1[:], 0.0)
    NH = 3
    CTs0 = [c.sb([128, 128], BF16) for _ in range(NH)]; CTs1 = [c.sb([128, 128], BF16) for _ in range(NH)]
    for b_ in CTs0 + CTs1:
        s.memset('pool', b_[:], 0.0)
    M = [c.sb([128, 1286], name="M") for _ in range(3)]
    X = [[c.sb([128, 896], name="X") for _ in range(3)] for _ in range(3)]
    mj = [c.sb([128, 896]) for _ in range(4)]
    xcs = [c.sb([128, 896], name="xc") for _ in range(2)]; xc16s = [c.sb([128, 384], BF16) for _ in range(2)]
    v6s = [c.sb([128, 64], name="v6") for _ in range(2)]
    abcs = [c.sb([128, 6, 128]) for _ in range(2)]; BCTs = [c.sb([128, 4, 128], BF16) for _ in range(2)]; CBTs = [c.sb([128, 2, 128]) for _ in range(2)]
    xw = c.sb([128, 384], BF16)
    dsb = [c.sb([128, 128]) for _ in range(NH)]; ecm = [c.sb([128, 128]) for _ in range(NH)]
    esb = [c.sb([128, 128]) for _ in range(NH)]; msb = [c.sb([128, 128]) for _ in range(NH)]; sc = [c.sb([128, 128], BF16) for _ in range(NH)]
    ysb = c.sb([128, 384]); szbs = [c.sb([128, 384]) for _ in range(2)]; sq = c.sb([128, 384]); ss = c.sb([128, 8]); tmpS = c.sb([128, 384])
    outt = [c.sb([128, 384]) for _ in range(2)]
    DmB = [P[3], P[4]]
    def loads(tt):
        Mt = M[tt % 3]
        s.dma('sp', Mt[:], u_d[tt * 128:(tt + 1) * 128, O_SZ:NC])
        for j in range(3):
            Xj = X[tt % 3][j]
            sh = 3 - j
            if tt == 0:
                s.memset('pool', Xj[:], 0.0)
                s.dma('sp', Xj[sh:128, :], u_d[0:128 - sh, O_XBC:O_XBC + 896])
            else:
                s.dma('sp', Xj[:], u_d[tt * 128 - sh:(tt + 1) * 128 - sh, O_XBC:O_XBC + 896])

    def pre(tt):
        b = tt % 2
        Mt = M[tt % 3]; v6 = v6s[b]
        xc = xcs[b]; xc16 = xc16s[b]; abc = abcs[b]; BCT = BCTs[b]; CBT = CBTs[b]; szb = szbs[b]
        Sin, Smid, Sout = R[(2 * tt) % 6], R[(2 * tt + 1) % 6], R[(2 * tt + 2) % 6]
        Sin16, Smid16, Sout16 = R16[(2 * tt) % 6], R16[(2 * tt + 1) % 6], R16[(2 * tt + 2) % 6]
        if tt == 0:
            loads(0)
        if tt + 1 < ntiles:
            loads(tt + 1)
        z = Mt[:, 0:384]; x3 = Mt[:, 384:1280]; dtr = Mt[:, 1280:1286]
        xs = xc[:, 0:384]
        dtt = v6[:, 0:6]; a_ = v6[:, 8:14]; cum = v6[:, 16:22]; wgt = v6[:, 24:30]; dAe = v6[:, 32:44]; tmp6 = v6[:, 48:54]
        s.tt('dve', tmp6, dtr, dtbias, ALU.add)
        s.act(tmp6, tmp6, AF.Exp)
        s.act(dtt, tmp6, AF.Ln, bias=1.0)
        s.tt('dve', a_, dtt, Ab, ALU.mult)
        s.copy('pool', abc[:], bcast(v6[:, 8:14].rearrange("p (h o) -> p h o", o=1), [128, 6, 128]))
        yield
        for j in range(3):
            s.tt('pool', mj[j][:], X[tt % 3][j][:], cw[:, j, :], ALU.mult)
        s.tt('pool', mj[3][:], x3, cw[:, 3, :], ALU.mult)
        yield
        s.mm(P[0][:, 0:6], lhsT=tri, rhs=a_)
        s.mm(P[0][:, 8:14], lhsT=tri2, rhs=a_)
        s.mm(P[0][:, 16:22], lhsT=co0, rhs=a_)
        s.mm(P[0][:, 22:28], lhsT=co1, rhs=a_)
        s.copy('act', cum, P[0][:, 0:6])
        s.act(wgt, P[0][:, 8:14], AF.Exp)
        s.act(dAe, P[0][:, 16:28], AF.Exp)
        s.tt('dve', wgt, wgt, dtt, ALU.mult)
        s.tt('dve', mj[0][:], mj[0][:], mj[1][:], ALU.add)
        s.tt('dve', mj[2][:], mj[2][:], mj[3][:], ALU.add)
        s.tt('dve', mj[0][:], mj[0][:], mj[2][:], ALU.add)
        s.tt('dve', mj[0][:], mj[0][:], cb[:], ALU.add)
        yield
        s.act(xc[:], mj[0][:], AF.Silu)
        s.copy('act', xc16[:], xs)
        s.act(szb[:], z, AF.Silu)
        s.tt('dve', xw[:].rearrange("p (h v) -> p h v", h=6), xs.rearrange("p (h v) -> p h v", h=6),
             bcast(v6[:, 24:30].rearrange("p (h o) -> p h o", o=1), [128, 6, 64]), ALU.mult)
        yield
        for k in range(4):
            s.tr(P[1][:, k * 128:(k + 1) * 128], xc[:, 384 + k * 128:384 + (k + 1) * 128], ident)
        s.copy('act', BCT[:], P[1][:, :].rearrange("p (k t) -> p k t", k=4))
        for g in range(2):
            s.mm(P[2][:, g * 128:(g + 1) * 128], lhsT=BCT[:, g, :], rhs=BCT[:, 2 + g, :])
        s.copy('dve', CBT[:], P[2][:, 0:256].rearrange("p (g t) -> p g t", g=2))
        s.copy('act', Bz0[0:64, :], xc[0:64, 384:640])
        s.copy('act', Bz1[64:128, :], xc[64:128, 384:640])
        yield
        for h in range(6):
            g = h // 3; hs = slice(h * 64, (h + 1) * 64)
            s.mm(P[7][:, hs], lhsT=Bz0[:, g * 128:(g + 1) * 128], rhs=xw[:, hs])
        for h in range(6):
            g = h // 3; hs = slice(h * 64, (h + 1) * 64)
            s.mm(P[1][:, hs], lhsT=Bz1[:, g * 128:(g + 1) * 128], rhs=xw[:, hs])
        s.tt('dve', tmpS[:].rearrange("p (h v) -> p h v", h=6), Sin[:], bcast(v6[:, 32:38].rearrange("p (h o) -> p h o", o=1), [128, 6, 64]), ALU.mult)
        s.tt('dve', Smid[:], tmpS[:].rearrange("p (h v) -> p h v", h=6), P[7][:, 0:384].rearrange("p (h v) -> p h v", h=6), ALU.add)
        s.copy('act', Smid16[:], Smid[:])
        s.tt('dve', tmpS[:].rearrange("p (h v) -> p h v", h=6), Smid[:], bcast(v6[:, 38:44].rearrange("p (h o) -> p h o", o=1), [128, 6, 64]), ALU.mult)
        s.tt('dve', Sout[:], tmpS[:].rearrange("p (h v) -> p h v", h=6), P[1][:, 0:384].rearrange("p (h v) -> p h v", h=6), ALU.add)
        s.copy('act', Sout16[:], Sout[:])

        yield

    def heads(tt):
        b = tt % 2
        Mt = M[b]; v6 = v6s[b]
        xc = xcs[b]; xc16 = xc16s[b]; abc = abcs[b]; BCT = BCTs[b]; CBT = CBTs[b]; szb = szbs[b]
        Sin16, Smid16 = R16[(2 * tt) % 6], R16[(2 * tt + 1) % 6]
        def st1(h):
            g = h // 3; hq = h % NH
            Dm = DmB[h % 2][:, 0:128]
            s.mm(Dm, lhsT=abc[:, h, :], rhs=tri)
            s.ts('dve', dsb[hq][:], Dm, v6[:, 16 + h:17 + h], 0.0, ALU.subtract, ALU.min)
            s.act(ecm[hq][:], Dm, AF.Exp)
            s.act(esb[hq][:], dsb[hq][:], AF.Exp)
            s.stt('dve', msb[hq][:], esb[hq][:], v6[:, h:h + 1], tri, ALU.mult, ALU.mult)
            s.tt('pool', sc[hq][:], msb[hq][:], CBT[:, g, :], ALU.mult)
            s.tt('pool', CTs0[hq][:, 0:64], BCT[:, 2 + g, 0:64], ecm[hq][:, 0:64], ALU.mult)
            s.tt('pool', CTs1[hq][:, 64:128], BCT[:, 2 + g, 64:128], ecm[hq][:, 64:128], ALU.mult)

        def st2(h):
            hq = h % NH; hs = slice(h * 64, (h + 1) * 64)
            y = P[5 + h % 2][:, 0:64]
            s.mm(y, lhsT=sc[hq][:], rhs=xc16[:, hs], start=True, stop=False)
            s.mm(y, lhsT=CTs0[hq][:], rhs=Sin16[:, h, :], start=False, stop=False)
            s.mm(y, lhsT=CTs1[hq][:], rhs=Smid16[:, h, :], start=False, stop=True)
            s.stt('dve', ysb[:, hs], xc[:, hs], sm[:, 18 + h:19 + h], y, ALU.mult, ALU.add)

        SK = 2
        for step in range(6 + SK):
            if step < 6:
                st1(step)
            if step >= SK:
                st2(step - SK)
            yield
        s.tt('pool', ysb[:], ysb[:], szb[:], ALU.mult)
        s.tt('pool', sq[:], ysb[:], ysb[:], ALU.mult)
        s.red('dve', ss[:, 0:2], sq[:].rearrange("p (g v) -> p g v", g=2), ALU.add)
        rstd_from_ss(s, ss[:, 4:6], ss[:, 2:4], ss[:, 0:2], 1.0 / 192)
        ot = outt[b]
        s.tt('dve', ot[:].rearrange("p (g v) -> p g v", g=2), ysb[:].rearrange("p (g v) -> p g v", g=2),
             bcast(ss[:, 4:6].rearrange("p (g o) -> p g o", o=1), [128, 2, 192]), ALU.mult)
        s.tt('pool', ot[:], ot[:], nwb[:], ALU.mult)
        s.dma('sp', mix_d[tt * 128:(tt + 1) * 128, 640:1024], ot[:])


    def drain(*gens):
        gens = [g for g in gens if g is not None]
        while gens:
            for g in list(gens):
                try:
                    next(g)
                except StopIteration:
                    gens.remove(g)

    drain(pre(0))
    for tt in range(ntiles):
        drain(heads(tt), pre(tt + 1) if tt + 1 < ntiles else None)


DILS = (1, 4, 16)


def t5_bucket_np(dist):
    max_exact = 16
    d = np.maximum(dist, 1).astype(np.float32)
    large = max_exact + (np.log(d / max_exact) / np.log(2048 / max_exact) * (32 - max_exact)).astype(np.int32)
    large = np.minimum(large, 31)
    return np.where(dist < max_exact, dist, large).astype(np.int32)


def make_attn_consts(rel_bias):
    j = np.arange(128)[:, None]; i = np.arange(128)[None, :]
    delta = np.concatenate([i - j + 128, i - j], axis=1)
    valid = (delta >= 0) & (delta <= 128)
    maskadd = np.where(valid, 0.0, -30000.0).astype(np.float32)
    biasg = np.zeros((18, 128, 256), np.float32)
    for bi, dil in enumerate(DILS):
        bucket = t5_bucket_np(np.maximum(delta, 0) * dil)
        for h in range(6):
            biasg[bi * 6 + h] = rel_bias[bucket, h]
    return biasg, maskadd


def phaseC(s, nc, st, u_d, mix_d, qk_d, ob_d, qw_d, kw_d, biasg_d, maskadd_d, consts_d, ntiles=16, branches=(0, 1, 2), do_c0=True, do_c2=True):
    c = Ctx(nc, st, "C")
    P = c.psum()
    cst = c.sb([128, C_END], name="cst")
    s.dma('sp', cst[:], consts_d[:, :])
    ident = cst[:, C_ID:C_ID + 128]
    qkw = c.sb([128, 128], name="qkw")
    s.dma('sp', qkw[:, 0:64], qw_d.partition_broadcast(128))
    s.dma('sp', qkw[:, 64:128], kw_d.partition_broadcast(128))
    biasT = c.sb([128, 18, 256], name="biasT")
    madd = c.sb([128, 256], name="madd")
    s.dma('sp', biasT[:], biasg_d.rearrange("n j c -> j n c"))
    s.dma('sp', madd[:], maskadd_d[:, :])
    s.tt('dve', biasT[:], biasT[:], bcast(madd[:].rearrange("p (o c) -> p o c", o=1), [128, 18, 256]), ALU.add)
    qk = [c.sb([128, 768]) for _ in range(2)]; sq = c.sb([128, 768]); ss = c.sb([128, 36]); qkn = [c.sb([128, 768]) for _ in range(2)]
    for tt in (range(ntiles) if do_c0 else []):
        b = tt % 2
        s.dma('sp', qk[b][:], u_d[tt * 128:(tt + 1) * 128, O_AQ:O_AQ + 768])
        s.tt('pool', sq[:], qk[b][:], qk[b][:], ALU.mult)
        s.red('dve', ss[:, 0:12], sq[:].rearrange("p (h v) -> p h v", h=12), ALU.add)
        rstd_from_ss(s, ss[:, 24:36], ss[:, 12:24], ss[:, 0:12], 1.0 / 64)
        s.tt('dve', qkn[b][:].rearrange("p (h v) -> p h v", h=12), qk[b][:].rearrange("p (h v) -> p h v", h=12),
             bcast(ss[:, 24:36].rearrange("p (h o) -> p h o", o=1), [128, 12, 64]), ALU.mult)
        s.stt('dve', qkn[b][:, 0:384].rearrange("p (h v) -> p h v", h=6), qkn[b][:, 0:384].rearrange("p (h v) -> p h v", h=6), 0.125,
              bcast(qkw[:, 0:64].rearrange("p (o v) -> p o v", o=1), [128, 6, 64]), ALU.mult, ALU.mult)
        s.tt('pool', qkn[b][:, 384:768].rearrange("p (h v) -> p h v", h=6), qkn[b][:, 384:768].rearrange("p (h v) -> p h v", h=6),
             bcast(qkw[:, 64:128].rearrange("p (o v) -> p o v", o=1), [128, 6, 64]), ALU.mult)
        s.dma('sp', qk_d[tt * 128:(tt + 1) * 128, :], qkn[b][:])
    QKb = [c.sb([128, 768]) for _ in range(2)]
    V32 = [c.sb([128, 384]) for _ in range(2)]
    BF16 = mybir.dt.bfloat16
    V1 = [c.sb([128, 6, 65], BF16) for _ in range(3)]
    for v in V1:
        s.memset('dve', v[:], 1.0)
    QT = [c.sb([128, 3, 128], BF16) for _ in range(2)]
    KTe = [c.sb([128, 3, 128], BF16) for _ in range(3)]; KTo = [c.sb([128, 3, 128], BF16) for _ in range(3)]
    for k_ in KTe + KTo:
        s.memset('pool', k_[:], 0.0)
    Esb = [c.sb([128, 256]) for _ in range(4)]; Psb = [c.sb([128, 256], BF16) for _ in range(4)]
    Osb = [c.sb([128, 390]) for _ in range(2)]
    blocks = []
    for bi in branches:
        dil = DILS[bi]
        L = (ntiles * 128) // dil
        nblk = L // 128
        for r in range(dil):
            for blk in range(nblk):
                blocks.append((bi, dil, r, blk))
    views = {}
    for bi in branches:
        dil = DILS[bi]
        views[bi] = (qk_d[0:ntiles * 128, :].rearrange("(b m d) c -> d b m c", m=128, d=dil),
                     u_d[0:ntiles * 128, O_AV:O_AV + 384].rearrange("(b m d) c -> d b m c", m=128, d=dil),
                     ob_d[bi, 0:ntiles * 128, :].rearrange("(b m d) c -> d b m c", m=128, d=dil))

    def pre(n):
        bi, dil, r, blk = blocks[n]
        pb = n % 2
        qv, vv, ov = views[bi]
        s.dma('sp', QKb[pb][:], qv[r, blk])
        p3 = n % 3
        s.dma('sp', V32[pb][:], vv[r, blk])
        s.copy('pool', V1[p3][:, :, 0:64], V32[pb][:].rearrange("m (h v) -> m h v", h=6))
        yield
        for pr in range(3):
            s.tr(P[0][:, pr * 128:(pr + 1) * 128], QKb[pb][:, pr * 128:(pr + 1) * 128], ident)
            s.tr(P[1][:, pr * 128:(pr + 1) * 128], QKb[pb][:, 384 + pr * 128:384 + (pr + 1) * 128], ident)
            yield
        s.copy('act', QT[pb][:], P[0][:, 0:384].rearrange("p (a t) -> p a t", a=3))
        p1v = P[1][:, 0:384].rearrange("p (a t) -> p a t", a=3)
        s.copy('dve', KTe[p3][0:64, :, :], p1v[0:64, :, :])
        s.copy('dve', KTo[p3][64:128, :, :], p1v[64:128, :, :])
        yield

    def heads(n):
        bi, dil, r, blk = blocks[n]
        pb = n % 2
        qv, vv, ov = views[bi]
        PO = P[6 + pb]
        c0 = 0 if blk > 0 else 128
        p3 = n % 3; q3 = (n - 1) % 3

        def S_(h):
            pr = h // 2; hq = h % 4
            KT = KTe if h % 2 == 0 else KTo
            ST = P[2 + hq]
            s.mm(ST[:, 128:256], lhsT=KT[p3][:, pr, :], rhs=QT[pb][:, pr, :])
            if blk > 0:
                s.mm(ST[:, 0:128], lhsT=KT[q3][:, pr, :], rhs=QT[pb][:, pr, :])
            s.tt('dve', Esb[hq][:, c0:256], ST[:, c0:256], biasT[:, bi * 6 + h, c0:256], ALU.add)
            s.act(Psb[hq][:, c0:256], Esb[hq][:, c0:256], AF.Exp)

        def PV_(h):
            hq = h % 4
            O = PO[:, h * 65:(h + 1) * 65]
            if blk > 0:
                s.mm(O, lhsT=Psb[hq][:, 0:128], rhs=V1[q3][:, h, :], start=True, stop=False)
            s.mm(O, lhsT=Psb[hq][:, 128:256], rhs=V1[p3][:, h, :], start=(blk == 0), stop=True)

        SK = 3
        for step in range(6 + SK):
            if step >= SK:
                PV_(step - SK)
            if step < 6:
                S_(step)
            yield
        s.copy('act', Osb[pb][:], PO[:, 0:390])
        s.dma('sp', ov[r, blk], Osb[pb][:])
        yield

    def drain(*gens):
        gens = [g for g in gens if g is not None]
        while gens:
            for g in list(gens):
                try:
                    next(g)
                except StopIteration:
                    gens.remove(g)

    drain(pre(0))
    for n in range(len(blocks)):
        drain(heads(n), pre(n + 1) if n + 1 < len(blocks) else None)
    ob = [[c.sb([128, 390]) for _ in range(3)] for _ in range(2)]
    rden = c.sb([128, 8]); outt = [c.sb([128, 384]) for _ in range(2)]
    for tt in (range(ntiles) if do_c2 else []):
        b = tt % 2
        for bi in range(3):
            s.dma('sp', ob[b][bi][:], ob_d[bi, tt * 128:(tt + 1) * 128, :])
        s.tt('pool', ob[b][0][:], ob[b][0][:], ob[b][1][:], ALU.add)
        s.tt('pool', ob[b][0][:], ob[b][0][:], ob[b][2][:], ALU.add)
        o3 = ob[b][0][:].rearrange("p (h v) -> p h v", h=6)
        s.recip(rden[:, 0:6].rearrange("p (h o) -> p h o", o=1), o3[:, :, 64:65])
        s.tt('dve', outt[b][:].rearrange("p (h v) -> p h v", h=6), o3[:, :, 0:64],
             bcast(rden[:, 0:6].rearrange("p (h o) -> p h o", o=1), [128, 6, 64]), ALU.mult)
        s.dma('sp', mix_d[tt * 128:(tt + 1) * 128, 256:640], outt[b][:])


def norm_transpose(s, xt, wb, junk, ss, xn, xnT, Pa, Pb, ident):
    s.act(junk[:], xt[:], AF.Square, accum_out=ss[:, 0:1])
    s.act(ss[:, 1:2], ss[:, 0:1], AF.Sqrt, bias=1e-6, scale=1.0 / D)
    s.recip(ss[:, 2:3], ss[:, 1:2])
    s.stt('dve', xn[:], xt[:], ss[:, 2:3], wb[:], ALU.mult, ALU.mult)
    transpose8(s, xn, xnT, Pa, Pb, ident)


def transpose8(s, xn, xnT, Pa, Pb, ident):
    for dc in range(8):
        pt = Pa if dc < 4 else Pb
        s.tr(pt[:, (dc % 4) * 128:(dc % 4 + 1) * 128], xn[:, dc * 128:(dc + 1) * 128], ident)
    s.copy('act', xnT[:, 0:4, :], Pa[:, :].rearrange("p (a c) -> p a c", a=4))
    s.copy('dve', xnT[:, 4:8, :], Pb[:, :].rearrange("p (a c) -> p a c", a=4))


def phaseA(s, nc, st, x_d, w_d, nw_d, u_d, consts_d, ntiles=16, qk_d=None, qw_d=None, kw_d=None):
    c = Ctx(nc, st, "A")
    BF16 = mybir.dt.bfloat16
    P = c.psum()
    W = c.sb([128, 8, NC], BF16, name="W")
    ident = c.sb([128, 128], name="ident")
    wb = c.sb([128, D], name="wb")
    s.dma('sp', ident[:], consts_d[:, C_ID:C_ID + 128])
    s.dma('sp', wb[:], nw_d.partition_broadcast(128))
    wst = [c.sb([128, NC], name="wst") for _ in range(2)]
    for dc in range(8):
        s.dma('sp', wst[dc % 2][:], w_d[dc * 128:(dc + 1) * 128, :])
        s.copy('pool' if dc % 2 == 0 else 'dve', W[:, dc, :], wst[dc % 2][:])
    xt = [c.sb([128, D]) for _ in range(2)]; xns = [c.sb([128, D]) for _ in range(2)]; junk = c.sb([128, D])
    if qk_d is not None:
        qkw = c.sb([128, 128], name="qkw")
        s.dma('sp', qkw[:, 0:64], qw_d.partition_broadcast(128))
        s.dma('sp', qkw[:, 64:128], kw_d.partition_broadcast(128))
        sq = c.sb([128, 768]); ssq = [c.sb([128, 36]) for _ in range(2)]; qkn = [c.sb([128, 768]) for _ in range(2)]
    sss = [c.sb([128, 4]) for _ in range(2)]; xnT = [c.sb([128, 8, 128], BF16) for _ in range(2)]; usb = [c.sb([128, NC]) for _ in range(2)]
    def pre(tt):
        b = tt % 2
        xn = xns[b]; ss = sss[b]
        s.dma('sp', xt[b][:], x_d[tt * 128:(tt + 1) * 128, :])
        s.act(junk[:], xt[b][:], AF.Square, accum_out=ss[:, 0:1])
        s.act(ss[:, 1:2], ss[:, 0:1], AF.Sqrt, bias=1e-6, scale=1.0 / D)
        s.recip(ss[:, 2:3], ss[:, 1:2])
        s.stt('dve', xn[:], xt[b][:], ss[:, 2:3], wb[:], ALU.mult, ALU.mult)
        yield
        for dc in range(8):
            pt = P[0] if dc < 4 else P[1]
            s.tr(pt[:, (dc % 4) * 128:(dc % 4 + 1) * 128], xn[:, dc * 128:(dc + 1) * 128], ident[:])
        yield
        s.copy('act', xnT[b][:, 0:4, :], P[0][:, :].rearrange("p (a c) -> p a c", a=4))
        s.copy('dve', xnT[b][:, 4:8, :], P[1][:, :].rearrange("p (a c) -> p a c", a=4))
        yield

    def main(tt):
        b = tt % 2
        nblk = (NC + 511) // 512
        for cb in range(nblk):
            c0 = cb * 512; c1 = min(NC, c0 + 512)
            pt = P[2 + cb % 6]
            for dc in range(8):
                s.mm(pt[:, 0:c1 - c0], lhsT=xnT[b][:, dc, :], rhs=W[:, dc, c0:c1], start=(dc == 0), stop=(dc == 7))
            s.copy('act' if cb % 2 == 0 else 'dve', usb[b][:, c0:c1], pt[:, 0:c1 - c0])
            yield
        s.dma('sp', u_d[tt * 128:(tt + 1) * 128, :], usb[b][:])
        if qk_d is not None:
            qk = usb[b][:, O_AQ:O_AQ + 768]; ss_ = ssq[b]
            s.tt('pool', sq[:], qk, qk, ALU.mult)
            s.red('dve', ss_[:, 0:12], sq[:].rearrange("p (h v) -> p h v", h=12), ALU.add)
            rstd_from_ss(s, ss_[:, 24:36], ss_[:, 12:24], ss_[:, 0:12], 1.0 / 64)
            s.tt('dve', qkn[b][:].rearrange("p (h v) -> p h v", h=12), qk.rearrange("p (h v) -> p h v", h=12),
                 bcast(ss_[:, 24:36].rearrange("p (h o) -> p h o", o=1), [128, 12, 64]), ALU.mult)
            s.stt('dve', qkn[b][:, 0:384].rearrange("p (h v) -> p h v", h=6), qkn[b][:, 0:384].rearrange("p (h v) -> p h v", h=6), 0.125,
                  bcast(qkw[:, 0:64].rearrange("p (o v) -> p o v", o=1), [128, 6, 64]), ALU.mult, ALU.mult)
            s.tt('pool', qkn[b][:, 384:768].rearrange("p (h v) -> p h v", h=6), qkn[b][:, 384:768].rearrange("p (h v) -> p h v", h=6),
                 bcast(qkw[:, 64:128].rearrange("p (o v) -> p o v", o=1), [128, 6, 64]), ALU.mult)
            s.dma('sp', qk_d[tt * 128:(tt + 1) * 128, :], qkn[b][:])
        yield

    def drain(*gens):
        gens = [g for g in gens if g is not None]
        while gens:
            for g in list(gens):
                try:
                    next(g)
                except StopIteration:
                    gens.remove(g)

    drain(pre(0))
    for tt in range(ntiles):
        drain(main(tt), pre(tt + 1) if tt + 1 < ntiles else None)


def phaseE(s, nc, st, mix_d, xin_d, w_d, x1_d, consts_d, ntiles=16, ob_d=None):
    c = Ctx(nc, st, "E")
    BF16 = mybir.dt.bfloat16
    P = c.psum()
    W = c.sb([128, 8, D], BF16, name="W")
    ident = c.sb([128, 128], name="ident")
    s.dma('sp', ident[:], consts_d[:, C_ID:C_ID + 128])
    wst = [c.sb([128, D], name="wst") for _ in range(2)]
    for dc in range(8):
        s.dma('sp', wst[dc % 2][:], w_d[dc * 128:(dc + 1) * 128, :])
        s.copy('pool' if dc % 2 == 0 else 'dve', W[:, dc, :], wst[dc % 2][:])
    mt = [c.sb([128, D]) for _ in range(2)]; xt = [c.sb([128, D]) for _ in range(2)]
    mT = [c.sb([128, 8, 128], BF16) for _ in range(2)]; ot = [c.sb([128, D]) for _ in range(2)]
    if ob_d is not None:
        ob = [[c.sb([128, 390]) for _ in range(3)] for _ in range(2)]; rden = [c.sb([128, 8]) for _ in range(2)]
    for tt in range(ntiles):
        b = tt % 2
        if ob_d is None:
            s.dma('sp', mt[b][:], mix_d[tt * 128:(tt + 1) * 128, :])
        else:
            s.dma('sp', mt[b][:, 0:256], mix_d[tt * 128:(tt + 1) * 128, 0:256])
            s.dma('sp', mt[b][:, 640:1024], mix_d[tt * 128:(tt + 1) * 128, 640:1024])
            for bi in range(3):
                s.dma('sp', ob[b][bi][:], ob_d[bi, tt * 128:(tt + 1) * 128, :])
            s.tt('pool', ob[b][0][:], ob[b][0][:], ob[b][1][:], ALU.add)
            s.tt('pool', ob[b][0][:], ob[b][0][:], ob[b][2][:], ALU.add)
            o3 = ob[b][0][:].rearrange("p (h v) -> p h v", h=6)
            s.recip(rden[b][:, 0:6].rearrange("p (h o) -> p h o", o=1), o3[:, :, 64:65])
            s.tt('dve', mt[b][:, 256:640].rearrange("p (h v) -> p h v", h=6), o3[:, :, 0:64],
                 bcast(rden[b][:, 0:6].rearrange("p (h o) -> p h o", o=1), [128, 6, 64]), ALU.mult)
        s.dma('sp', xt[b][:], xin_d[tt * 128:(tt + 1) * 128, :])
        transpose8(s, mt[b], mT[b], P[0], P[1], ident[:])
        for hb in range(2):
            pt = P[2 + 2 * b + hb]
            for dc in range(8):
                s.mm(pt[:, :], lhsT=mT[b][:, dc, :], rhs=W[:, dc, hb * 512:(hb + 1) * 512], start=(dc == 0), stop=(dc == 7))
            s.tt('dve', ot[b][:, hb * 512:(hb + 1) * 512], pt[:, :], xt[b][:, hb * 512:(hb + 1) * 512], ALU.add)
        s.dma('sp', x1_d[tt * 128:(tt + 1) * 128, :], ot[b][:])


GELU_C = 1.5957691216057308


def phaseF1(s, nc, st, x1_d, nw_d, wq_d, sk_d, G_d, xnT_all, consts_d, ntiles=16):
    c = Ctx(nc, st, "G")
    BF16 = mybir.dt.bfloat16
    P = c.psum()
    cst = c.sb([128, C_END], name="cst")
    s.dma('sp', cst[:], consts_d[:, :])
    ident = cst[:, C_ID:C_ID + 128]
    Wq = c.sb([128, 8, D], name="Wq")
    for dc in range(8):
        s.dma('sp', Wq[:, dc, :], wq_d[dc * 128:(dc + 1) * 128, :])
    wb = c.sb([128, D], name="wb")
    s.dma('sp', wb[:], nw_d.partition_broadcast(128))
    SKin = c.sb([128, 2, 128]); SKbd = c.sb([128, 256])
    s.memset('dve', SKin[:], 0.0)
    s.dma('sp', SKin[:, 0, 0:64], sk_d[0, :, :])
    s.dma('sp', SKin[:, 1, 64:128], sk_d[1, :, :])
    s.tr(P[0][:, 0:128], SKin[:, 0, :], ident)
    s.tr(P[0][:, 128:256], SKin[:, 1, :], ident)
    s.copy('dve', SKbd[:], P[0][:, 0:256])
    GT = 2 if ntiles % 2 == 0 else 1
    xt1 = c.sb([128, D]); xt = [xt1, xt1]; xn = c.sb([128, D]); ss = c.sb([128, 4])
    xnT4s = [c.sb([128, 8, GT * 128]) for _ in range(2)]; qT4s = [c.sb([128, 8, GT * 128]) for _ in range(2)]; ssbs = [c.sb([128, 8, 256]) for _ in range(2)]
    sv = c.sb([128, 16, 16]); si = c.sb([128, 16, 16], U32); sif = c.sb([128, 16, 16]); s2 = c.sb([128, 128])
    cand = c.sb([128, 8, 256]); c2 = c.sb([128, 256])
    tv = c.sb([128, 8, 16]); pos = c.sb([128, 8, 16], U32); pa = c.sb([128, 128], U32); pb = c.sb([128, 128], U32)
    af = c.sb([128, 128]); bf = c.sb([128, 128])
    ge = c.sb([128, 8, 16]); gs = c.sb([128, 16]); gate = c.sb([128, 8, 16])
    oh = c.sb([128, 128, 16]); ik = c.sb([128, 128]); jk = c.sb([128, 128])
    junk2 = c.sb([128, D])
    TQ = 64
    gikT = c.sb([128, 384])
    Abuf = c.sb([128, TQ, 128], BF16); Bbuf = c.sb([128, TQ, 128], BF16); Gs = [c.sb([128, 16, 128], BF16) for _ in range(2)]
    i128 = cst[:, C_I128:C_I128 + 128]
    svv = sv[:].rearrange("p (h c) a -> p h c a", c=2)
    sifv = sif[:].rearrange("p (h c) a -> p h c a", c=2)
    i16 = cst[:, C_I16:C_I16 + 16]
    gikTs = [gikT, c.sb([128, 384])]
    Ab = [Abuf[:, 0:32, :], Abuf[:, 32:64, :]]; Bb = [Bbuf[:, 0:32, :], Bbuf[:, 32:64, :]]
    TQ = 32

    def qgroup(grp):
        xnT4 = xnT4s[grp % 2]; qT4 = qT4s[grp % 2]
        for k4 in range(GT):
            tt = grp * GT + k4
            b = tt % 2
            s.dma('sp', xt[b][:], x1_d[tt * 128:(tt + 1) * 128, :])
            s.act(junk2[:], xt[b][:], AF.Square, accum_out=ss[:, 0:1])
            s.act(ss[:, 1:2], ss[:, 0:1], AF.Ln, bias=1e-6, scale=1.0 / D)
            s.act(ss[:, 2:3], ss[:, 1:2], AF.Exp, scale=-0.5)
            s.ts('pool', xn[:], xt[b][:], ss[:, 2:3], None, ALU.mult)
            s.tt('pool', xn[:], xn[:], wb[:], ALU.mult)
            yield
            for dc in range(8):
                pt = P[0] if dc < 4 else P[1]
                s.tr(pt[:, (dc % 4) * 128:(dc % 4 + 1) * 128], xn[:, dc * 128:(dc + 1) * 128], ident)
            s.copy('act', xnT4[:, 0:4, k4 * 128:(k4 + 1) * 128], P[0][:, :].rearrange("p (a c) -> p a c", a=4))
            s.copy('act', xnT4[:, 4:8, k4 * 128:(k4 + 1) * 128], P[1][:, :].rearrange("p (a c) -> p a c", a=4))
            s.copy('pool', xnT_all[:, :, tt * 128:(tt + 1) * 128], xnT4[:, :, k4 * 128:(k4 + 1) * 128])
            yield
        for h in range(8):
            pt = P[2 + h % 2]
            for dc in range(8):
                s.mm(pt[:, 0:GT * 128], lhsT=Wq[:, dc, h * 128:(h + 1) * 128], rhs=xnT4[:, dc, :], start=(dc == 0), stop=(dc == 7))
            s.copy('act', qT4[:, h, :], pt[:, 0:GT * 128])
            yield

    def scores(tt):
        k4 = tt % GT
        qT4 = qT4s[(tt // GT) % 2]
        ssb = ssbs[tt % 2]
        for h in range(8):
            s.mm(P[4 + (h // 2) % 2][:, (h % 2) * 256:(h % 2 + 1) * 256], lhsT=qT4[:, h, k4 * 128:(k4 + 1) * 128], rhs=SKbd[:])
            if h % 2 == 1:
                k = h // 2
                s.copy('act', ssb[:, 2 * k:2 * k + 2, :], P[4 + k % 2][:, :].rearrange("p (a c) -> p a c", a=2))

    def route(tt):
        ssb = ssbs[tt % 2]
        if tt == 0:
            scores(0)
        for half in range(2):
            grp_ = range(half * 8, half * 8 + 8)
            srcs = {g: ssb[:, g // 2, (g % 2) * 128:(g % 2 + 1) * 128] for g in grp_}
            for g in grp_:
                s.op('dve', lambda e, g=g, src=srcs[g]: e.max(out=sv[:, g, 0:8], in_=src), reads=[srcs[g]], writes=[sv[:, g, 0:8]])
            for g in grp_:
                s.op('dve', lambda e, g=g, src=srcs[g]: e.max_index(out=si[:, g, 0:8], in_max=sv[:, g, 0:8], in_values=src), reads=[srcs[g], sv[:, g, 0:8]], writes=[si[:, g, 0:8]])
            for g in grp_:
                s.op('dve', lambda e, g=g, src=srcs[g]: e.match_replace(out=src, in_to_replace=sv[:, g, 0:8], in_values=src, imm_value=-1e30), reads=[srcs[g], sv[:, g, 0:8]], writes=[srcs[g]])
            for g in grp_:
                s.op('dve', lambda e, g=g, src=srcs[g]: e.max(out=sv[:, g, 8:16], in_=src), reads=[srcs[g]], writes=[sv[:, g, 8:16]])
            for g in grp_:
                s.op('dve', lambda e, g=g, src=srcs[g]: e.max_index(out=si[:, g, 8:16], in_max=sv[:, g, 8:16], in_values=src), reads=[srcs[g], sv[:, g, 8:16]], writes=[si[:, g, 8:16]])
            if half == 0:
                yield
        s.copy('pool', sif[:], si[:])
        in0 = bcast(svv[:, :, 0:1, :].rearrange("p h o a -> p h a o"), [128, 8, 16, 16])
        in1 = bcast(svv[:, :, 1:2, :], [128, 8, 16, 16])
        s.tt('dve', cand[:].rearrange("p h (a b) -> p h a b", a=16), in0, in1, ALU.add)
        yield
        hs_ = range(8)
        for h in hs_:
            s.op('dve', lambda e, h=h: e.max(out=tv[:, h, 0:8], in_=cand[:, h, :]), reads=[cand[:, h, :]], writes=[tv[:, h, 0:8]])
        for h in hs_:
            s.op('dve', lambda e, h=h: e.max_index(out=pos[:, h, 0:8], in_max=tv[:, h, 0:8], in_values=cand[:, h, :]), reads=[cand[:, h, :], tv[:, h, 0:8]], writes=[pos[:, h, 0:8]])
        for h in hs_:
            s.op('dve', lambda e, h=h: e.match_replace(out=cand[:, h, :], in_to_replace=tv[:, h, 0:8], in_values=cand[:, h, :], imm_value=-1e30), reads=[cand[:, h, :], tv[:, h, 0:8]], writes=[cand[:, h, :]])
        for h in hs_:
            s.op('dve', lambda e, h=h: e.max(out=tv[:, h, 8:16], in_=cand[:, h, :]), reads=[cand[:, h, :]], writes=[tv[:, h, 8:16]])
        for h in hs_:
            s.op('dve', lambda e, h=h: e.max_index(out=pos[:, h, 8:16], in_max=tv[:, h, 8:16], in_values=cand[:, h, :]), reads=[cand[:, h, :], tv[:, h, 8:16]], writes=[pos[:, h, 8:16]])
        s.tt('dve', ge[:], tv[:], bcast(tv[:, :, 0:1], [128, 8, 16]), ALU.subtract)
        s.act(ge[:], ge[:], AF.Exp)
        yield
        s.red('dve', gs[:, 0:8], ge[:], ALU.add)
        s.recip(gs[:, 8:16], gs[:, 0:8])
        s.tt('dve', gate[:], ge[:], bcast(gs[:, 8:16].rearrange("p (h o) -> p h o", o=1), [128, 8, 16]), ALU.mult)
        posv = pos[:].rearrange("p h k -> p (h k)")
        s.ts('dve', pa[:], posv, 4, None, ALU.logical_shift_right)
        s.ts('dve', pb[:], posv, 15, None, ALU.bitwise_and)
        s.copy('dve', af[:], pa[:])
        s.copy('dve', bf[:], pb[:])
        for (sel, cidx, outk) in ((af, 0, ik), (bf, 1, jk)):
            s.tt('dve', oh[:], bcast(sel[:].rearrange("p (k o) -> p k o", o=1), [128, 128, 16]),
                 bcast(i16.rearrange("p (o a) -> p o a", o=1), [128, 128, 16]), ALU.is_equal)
            ohv = oh[:].rearrange("p (h k) a -> p h k a", h=8)
            s.tt('dve', ohv, ohv, bcast(sifv[:, :, cidx:cidx + 1, :], [128, 8, 16, 16]), ALU.mult)
            s.red('dve', outk[:], oh[:], ALU.add)
        s.tr(P[4][:, 0:128], gate[:].rearrange("p h k -> p (h k)"), ident)
        s.tr(P[4][:, 128:256], ik[:], ident)
        s.tr(P[4][:, 256:384], jk[:], ident)
        s.copy('act', gikTs[tt % 2][:], P[4][:, 0:384])
        if tt + 1 < ntiles:
            scores(tt + 1)
        yield

    def ggen(tt):
        gk = gikTs[tt % 2]
        for qt_ in range(4):
            t0 = qt_ * TQ
            A_ = Ab[qt_ % 2]; B_ = Bb[qt_ % 2]
            s.tt('dve', A_, bcast(gk[:, 128 + t0:128 + t0 + TQ].rearrange("p (t o) -> p t o", o=1), [128, TQ, 128]),
                 bcast(i128.rearrange("p (o a) -> p o a", o=1), [128, TQ, 128]), ALU.is_equal)
            s.tt('pool', A_, A_, bcast(gk[:, t0:t0 + TQ].rearrange("p (t o) -> p t o", o=1), [128, TQ, 128]), ALU.mult)
            s.tt('dve', B_, bcast(gk[:, 256 + t0:256 + t0 + TQ].rearrange("p (t o) -> p t o", o=1), [128, TQ, 128]),
                 bcast(i128.rearrange("p (o a) -> p o a", o=1), [128, TQ, 128]), ALU.is_equal)
            for g4 in range(TQ // 4):
                bank = P[6 + g4 % 2]
                for k in range(4):
                    t = g4 * 4 + k
                    s.mm(bank[:, k * 128:(k + 1) * 128], lhsT=A_[:, t, :], rhs=B_[:, t, :])
                gsb = Gs[(qt_ * 2 + g4 // 4) % 2]
                s.copy('act', gsb[:, (g4 % 4) * 4:(g4 % 4) * 4 + 4, :], bank[:, :].rearrange("p (t j) -> p t j", t=4))
                if g4 % 4 == 3:
                    tb = tt * 128 + t0 + (g4 // 4) * 16
                    s.dma('sp', G_d[tb:tb + 16, :].rearrange("t (i j) -> i t j", j=128), gsb[:])
            yield

    def drain(*gens):
        gens = [g for g in gens if g is not None]
        while gens:
            for g in list(gens):
                try:
                    next(g)
                except StopIteration:
                    gens.remove(g)

    QPRE = 4
    drain(qgroup(0))
    prev = None
    for tt in range(ntiles):
        qg = None
        if tt % GT == 0:
            g1 = tt // GT + 1
            qg = qgroup(g1) if g1 < ntiles // GT else None
        rt = route(tt); gg = ggen(prev) if prev is not None else None
        gens = [g for g in (rt, gg) if g is not None]
        for _ in range(QPRE):
            if qg is not None:
                try:
                    next(qg)
                except StopIteration:
                    qg = None
        while gens:
            for g in list(gens):
                try:
                    next(g)
                except StopIteration:
                    gens.remove(g)
                if g is rt and len(gens) and qg is not None:
                    pass
            for _ in range(2):
                if qg is not None:
                    try:
                        next(qg)
                    except StopIteration:
                        qg = None
        if qg is not None:
            drain(qg)
        prev = tt
    drain(ggen(prev))


def phaseF2(s, nc, st, x1_d, downT_d, up_d, x2_d, G_d, xnT_all, consts_d, ntiles=16, nslab=32):
    c = Ctx(nc, st, "H")
    BF16 = mybir.dt.bfloat16
    Pa = [st.enter_context(nc.psum_tensor("%s_pa%d" % (c.pfx, i), [128, 512], F32)) for i in range(2)]
    Pt = [st.enter_context(nc.psum_tensor("%s_pt%d" % (c.pfx, i), [128, 1024], BF16)) for i in range(2)]
    Po = [st.enter_context(nc.psum_tensor("%s_po%d" % (c.pfx, i), [128, 512], F32)) for i in range(4)]
    idf = c.sb([128, 128]); id16 = c.sb([128, 128], BF16)
    s.dma('sp', idf[:], consts_d[:, C_ID:C_ID + 128])
    s.copy('dve', id16[:], idf[:])
    acc = c.sb([128, ntiles, D], name="acc")
    dsl = [c.sb([128, 8, 512], BF16, name="dsl") for _ in range(2)]
    usl = [c.sb([128, 4, D], BF16, name="usl") for _ in range(2)]
    Gt = [c.sb([128, 512], BF16, name="Gt") for _ in range(4)]
    ga = [c.sb([128, 512], name="ga") for _ in range(2)]
    c16 = [c.sb([128, 512], BF16, name="c16") for _ in range(2)]
    cT = [c.sb([128, 4, 128], BF16, name="cT") for _ in range(2)]
    xt = [c.sb([128, D]) for _ in range(2)]

    dst32 = c.sb([128, 8, 512], name="dst32"); ust32 = c.sb([128, 4, D], name="ust32")

    def load_slab(sl):
        e0 = sl * 512
        s.dma('sp', dst32[:], downT_d[:, e0:e0 + 512].rearrange("(dc p) e -> p dc e", p=128))
        s.dma('sp', ust32[:], up_d[e0:e0 + 512, :].rearrange("(k p) d -> p k d", p=128))

    def cast_slab(sl, part):
        if part < 2:
            s.copy('dve', dsl[sl % 2][:, part * 4:(part + 1) * 4, :], dst32[:, part * 4:(part + 1) * 4, :])
        else:
            p2 = part - 2
            s.copy('dve', usl[sl % 2][:, p2 * 2:(p2 + 1) * 2, :], ust32[:, p2 * 2:(p2 + 1) * 2, :])

    load_slab(0)
    for part in range(4):
        cast_slab(0, part)
    it = 0
    for sl in range(nslab):
        if sl + 1 < nslab:
            load_slab(sl + 1)
        e0 = sl * 512
        D_ = dsl[sl % 2]; U_ = usl[sl % 2]

        def stage1(tt, n):
            s.dma('sp', Gt[n % 4][:], G_d[tt * 128:(tt + 1) * 128, e0:e0 + 512])
            for dc in range(8):
                s.mm(Pa[n % 2][:, :], lhsT=xnT_all[:, dc, tt * 128:(tt + 1) * 128], rhs=D_[:, dc, :], start=(dc == 0), stop=(dc == 7))
            s.act(ga[n % 2][:], Pa[n % 2][:, :], AF.Gelu_apprx_tanh)
            s.tt('pool', c16[n % 2][:], ga[n % 2][:], Gt[n % 4][:], ALU.mult)

        def stage2(tt, n):
            for k in range(4):
                s.tr(Pt[n % 2][:, k * 128:(k + 1) * 128], c16[n % 2][:, k * 128:(k + 1) * 128], id16[:])
            s.copy('act', cT[n % 2][:], Pt[n % 2][:, 0:512].rearrange("p (k t) -> p k t", k=4))

        def stage3(tt, n):
            for hb in range(2):
                po = Po[(n % 2) * 2 + hb]
                for k in range(4):
                    s.mm(po[:, :], lhsT=cT[n % 2][:, k, :], rhs=U_[:, k, hb * 512:(hb + 1) * 512], start=(k == 0), stop=(k == 3))
                if sl == 0:
                    s.copy('dve', acc[:, tt, hb * 512:(hb + 1) * 512], po[:, :])
                else:
                    s.tt('dve', acc[:, tt, hb * 512:(hb + 1) * 512], acc[:, tt, hb * 512:(hb + 1) * 512], po[:, :], ALU.add)

        for step in range(ntiles + 2):
            if sl + 1 < nslab and ntiles >= 12 and step in (5, 7, 9, 11):
                cast_slab(sl + 1, (step - 5) // 2)
            elif sl + 1 < nslab and ntiles < 12 and step == ntiles - 1:
                for part in range(4):
                    cast_slab(sl + 1, part)
            if step < ntiles:
                stage1(step, it + step)
            if 1 <= step <= ntiles:
                stage2(step - 1, it + step - 1)
            if 2 <= step:
                stage3(step - 2, it + step - 2)
        it += ntiles
    for tt in range(ntiles):
        b = tt % 2
        s.dma('sp', xt[b][:], x1_d[tt * 128:(tt + 1) * 128, :])
        s.tt('pool', xt[b][:], xt[b][:], acc[:, tt, :], ALU.add)
        s.dma('sp', x2_d[tt * 128:(tt + 1) * 128, :], xt[b][:])


def phaseFD(s, nc, st, x1_d, nw_d, wq_d, sk_d, downT_d, up_d, x2_d, G_d, consts_d, ntiles=16, nslab=32):
    BF16 = mybir.dt.bfloat16
    Ctx.uid += 1
    xnT_all = st.enter_context(nc.sbuf_tensor("xnTall%d" % Ctx.uid, [128, 8, ntiles * 128], BF16))
    with ExitStack() as st1:
        phaseF1(s, nc, st1, x1_d, nw_d, wq_d, sk_d, G_d, xnT_all, consts_d, ntiles)
        s.barrier(); s.flush()
    with ExitStack() as st2:
        phaseF2(s, nc, st2, x1_d, downT_d, up_d, x2_d, G_d, xnT_all, consts_d, ntiles, nslab)
        s.barrier(); s.flush()


NLAYER = 2
_CACHE = {}


def build_program():
    nc = bass.Bass("TRN2", target_bir_lowering=False)

    def din(name, shape):
        return nc.dram_tensor(name, list(shape), F32, kind="ExternalInput").ap()

    def dint(name, shape):
        return nc.dram_tensor(name, list(shape), F32, kind="Internal").ap()

    x_d = din("x", [T, D])
    anw = din("attn_norm_w", [NLAYER, D]); w_in = din("w_in", [NLAYER, D, NC])
    hlb = din("hlb", [NLAYER, 256]); hnw = din("hnw", [NLAYER, 64])
    qw = din("qw", [NLAYER, 64]); kw = din("kw", [NLAYER, 64])
    biasg = din("biasg", [18, 128, 256]); maskadd = din("maskadd", [128, 256])
    cw = din("cw", [NLAYER, 4, 896]); cb = din("cb", [NLAYER, 896]); dtb = din("dtb", [NLAYER, 6])
    alog = din("alog", [NLAYER, 6]); dsk = din("dsk", [NLAYER, 6]); snw = din("snw", [NLAYER, 384])
    w_out = din("w_out", [NLAYER, D, D]); fnw = din("fnw", [NLAYER, D]); wq = din("wq", [NLAYER, D, D])
    sk = din("sk", [NLAYER, 2, 128, 64])
    down = [din("downT%d" % l, [D, 16384]) for l in range(NLAYER)]
    up = [din("up%d" % l, [16384, D]) for l in range(NLAYER)]
    consts = din("consts", [128, C_END])
    out_d = nc.dram_tensor("out", [T, D], F32, kind="ExternalOutput").ap()
    U = dint("U", [T, NC]); QK = dint("QK", [T, 768]); OB = dint("OB", [3, T, 390]); MIX = dint("MIX", [T, D])
    X1 = dint("X1", [T, D]); X2 = dint("X2", [T, D]); G_d = nc.dram_tensor("Gd", [T, 16384], mybir.dt.bfloat16, kind="Internal").ap()

    s = Sched(nc)
    s.ro = {"x", "attn_norm_w", "w_in", "hlb", "hnw", "qw", "kw", "biasg", "maskadd", "cw", "cb", "dtb", "alog", "dsk",
            "snw", "w_out", "fnw", "wq", "sk", "downT0", "downT1", "up0", "up1", "consts"}
    with ExitStack() as semstack:
        s.semstack = semstack
        xin = x_d
        for l in range(NLAYER):
            xout = out_d if l == NLAYER - 1 else X2
            with ExitStack() as st:
                phaseA(s, nc, st, xin, w_in[l], anw[l], U, consts, qk_d=QK, qw_d=qw[l], kw_d=kw[l])
                s.barrier(); s.flush()
            with ExitStack() as st:
                phaseB(s, nc, st, U, MIX, hlb, hnw[l], consts, l)
                s.barrier(); s.flush()
            with ExitStack() as st:
                phaseC(s, nc, st, U, MIX, QK, OB, qw[l], kw[l], biasg, maskadd, consts, do_c0=False, do_c2=False)
                s.barrier(); s.flush()
            with ExitStack() as st:
                phaseD(s, nc, st, U, MIX, cw[l], cb[l], dtb[l], alog[l], dsk[l], snw[l], consts)
                s.barrier(); s.flush()
            with ExitStack() as st:
                phaseE(s, nc, st, MIX, xin, w_out[l], X1, consts, ob_d=OB)
                s.barrier(); s.flush()
            with ExitStack() as st:
                phaseFD(s, nc, st, X1, fnw[l], wq[l], sk[l], down[l], up[l], xout, G_d, consts)
            xin = X2
    return nc


def kernel(x, attn_norm_w, w_in, hgrn_lower_bounds, hgrn_norm_w, q_norm_w, k_norm_w, rel_bias,
           ssm_conv_w, ssm_conv_b, ssm_dt_bias, ssm_a_log, ssm_d, ssm_norm_w, w_out,
           ffn_norm_w, peer_w_query, peer_sub_keys, peer_down, peer_up):
    f = lambda a: np.ascontiguousarray(np.asarray(a, dtype=np.float32))
    if "nc" not in _CACHE:
        _CACHE["nc"] = build_program()
    nc = _CACHE["nc"]
    biasg, maskadd = make_attn_consts(f(rel_bias))
    shared = {
        "attn_norm_w": f(attn_norm_w), "w_in": f(w_in), "hlb": f(hgrn_lower_bounds), "hnw": f(hgrn_norm_w),
        "qw": f(q_norm_w), "kw": f(k_norm_w), "biasg": biasg, "maskadd": maskadd,
        "cw": f(ssm_conv_w), "cb": f(ssm_conv_b), "dtb": f(ssm_dt_bias), "alog": f(ssm_a_log), "dsk": f(ssm_d),
        "snw": f(ssm_norm_w), "w_out": f(w_out), "fnw": f(ffn_norm_w), "wq": f(peer_w_query), "sk": f(peer_sub_keys),
        "consts": make_consts(),
    }
    pd = f(peer_down); pu = f(peer_up)
    for l in range(NLAYER):
        shared["downT%d" % l] = np.ascontiguousarray(pd[l].T)
        shared["up%d" % l] = pu[l]
    xs = f(x)
    n = xs.shape[0]
    in_maps = [dict(shared, x=xs[i]) for i in range(n)]
    res = run_bass_kernel_spmd(nc, in_maps, core_ids=list(range(n)))
    return np.stack([np.asarray(r["out"]) for r in res.results], axis=0).astype(np.float32)
```

```python
from contextlib import ExitStack
import numpy as np
import concourse.bass as bass
import concourse.mybir as mybir
from concourse.bass_utils import run_bass_kernel_spmd

F32 = mybir.dt.float32
I32 = mybir.dt.int32
U32 = mybir.dt.uint32
ALU = mybir.AluOpType
AF = mybir.ActivationFunctionType
AX = mybir.AxisListType

NDMA = 24
EPOCH = 30000


def _prod(xs):
    r = 1
    for v in xs:
        r *= int(v)
    return r


def region(a):
    t = a.tensor
    name = t.name
    ap = a.ap
    off = int(a.offset)
    sp = str(a.space)
    if 'DRAM' in sp.upper():
        ext = sum((int(c) - 1) * abs(int(s)) for s, c in ap)
        return (name, 0, 1, off, off + ext + 1)
    row = _prod(t.shape[1:])
    p0 = off // row
    f0 = off % row
    pstep, pcnt = int(ap[0][0]), int(ap[0][1])
    if pstep == 0:
        pcnt = 1
    ext = sum((int(c) - 1) * abs(int(s)) for s, c in ap[1:])
    if 'PSUM' in sp.upper():
        return (name, (p0 // 32) * 32, ((p0 + pcnt + 31) // 32) * 32, 0, row)
    return (name, p0, p0 + pcnt, f0, f0 + ext + 1)


class Sched:
    def __init__(self, nc):
        self.nc = nc
        self.eng = {'pe': nc.tensor, 'dve': nc.vector, 'act': nc.scalar,
                    'pool': nc.gpsimd, 'sp': nc.sync}
        self.ops = {k: [] for k in self.eng}
        self.cnt = {k: 0 for k in self.eng}
        self.waited = {k: {} for k in self.eng}
        self.acc = {}
        self.dma_n = 0
        self.keys = set()
        self.ro = set()
        self.sems = {}
        self.semstack = None

    def _deps(self, eng, reads, writes):
        deps = {}

        def add(key, val):
            if deps.get(key, 0) < val:
                deps[key] = val

        for is_w, aps in ((False, reads), (True, writes)):
            for a in aps:
                if a.tensor.name in self.ro:
                    continue
                r = region(a)
                lst = self.acc.get(r[0], [])
                psum = 'PSUM' in str(a.space).upper()
                for (key, val, w2, e2, p0, p1, f0, f1) in lst:
                    if not (is_w or w2) and not (psum and e2 != eng):
                        continue
                    if p1 <= r[1] or r[2] <= p0 or f1 <= r[3] or r[4] <= f0:
                        continue
                    if e2 == eng and eng != 'dma':
                        if eng == 'pe':
                            continue
                    add(key, val)
        return deps

    def _record(self, eng, key, val, reads, writes):
        for is_w, aps in ((False, reads), (True, writes)):
            for a in aps:
                if a.tensor.name in self.ro:
                    continue
                r = region(a)
                lst = self.acc.setdefault(r[0], [])
                new = []
                for rec in lst:
                    (k2, v2, w2, e2, p0, p1, f0, f1) = rec
                    contained = (r[1] <= p0 and p1 <= r[2] and r[3] <= f0 and f1 <= r[4])
                    if contained and (is_w or (not w2 and e2 == eng and eng != 'dma')):
                        continue
                    new.append(rec)
                new.append((key, val, is_w, eng, r[1], r[2], r[3], r[4]))
                self.acc[r[0]] = new

    def _key(self, eng):
        idx = self.cnt[eng]
        return (eng, idx // EPOCH), idx % EPOCH + 1

    def op(self, eng, fn, reads=(), writes=()):
        deps = self._deps(eng, reads, writes)
        waits = []
        w = self.waited[eng]
        for key, val in deps.items():
            if w.get(key, 0) < val:
                w[key] = val
                waits.append((key, val))
        key, val = self._key(eng)
        self.cnt[eng] += 1
        self.keys.add(key)
        self.ops[eng].append((fn, waits, key, 1))
        self._record(eng, key, val, reads, writes)

    def dma(self, q, out, in_, **kw):
        n = self.dma_n
        self.dma_n += 1
        slot = n % NDMA
        key = ('dma', slot)
        val = 16 * (n // NDMA + 1)
        deps = self._deps('dma', [in_], [out])
        if n >= NDMA:
            deps[key] = max(deps.get(key, 0), val - 16)
        waits = []
        w = self.waited[q]
        for k, v in deps.items():
            if w.get(k, 0) < v:
                w[k] = v
                waits.append((k, v))
        self.keys.add(key)
        self.ops[q].append((lambda e: e.dma_start(out=out, in_=in_, **kw), waits, key, 16))
        self._record('dma', key, val, [in_], [out])

    def dma_custom(self, q, fn, reads, writes):
        n = self.dma_n
        self.dma_n += 1
        slot = n % NDMA
        key = ('dma', slot)
        val = 16 * (n // NDMA + 1)
        deps = self._deps('dma', reads, writes)
        if n >= NDMA:
            deps[key] = max(deps.get(key, 0), val - 16)
        waits = []
        w = self.waited[q]
        for k, v in deps.items():
            if w.get(k, 0) < v:
                w[k] = v
                waits.append((k, v))
        self.keys.add(key)
        self.ops[q].append((fn, waits, key, 16))
        self._record('dma', key, val, reads, writes)

    def barrier(self):
        targets = {}
        for e in self.eng:
            if self.cnt[e] > 0:
                idx = self.cnt[e] - 1
                targets[(e, idx // EPOCH)] = idx % EPOCH + 1
        for n in range(max(0, self.dma_n - NDMA), self.dma_n):
            targets[('dma', n % NDMA)] = 16 * (n // NDMA + 1)
        for e in self.eng:
            waits = []
            w = self.waited[e]
            for k, v in targets.items():
                if k[0] == e:
                    continue
                if w.get(k, 0) < v:
                    w[k] = v
                    waits.append((k, v))
            if waits:
                key, val = self._key(e)
                self.cnt[e] += 1
                self.keys.add(key)
                self.ops[e].append((None, waits, key, 1))
        self.acc = {}

    def finish(self):
        self.barrier()

    def flush(self):
        nc = self.nc
        for k in sorted(self.keys, key=str):
            if k not in self.sems:
                self.sems[k] = self.semstack.enter_context(nc.semaphore("s_%s_%d" % (k[0], k[1])))
        sems = self.sems
        ops = self.ops
        self.ops = {k: [] for k in self.eng}
        with nc.Block() as block:
            def mk(ename):
                def body(e):
                    for (fn, waits, key, inc) in ops[ename]:
                        for (k, v) in waits:
                            e.wait_ge(sems[k], v)
                        if fn is None:
                            ins = e.nop()
                        else:
                            ins = fn(e)
                        ins.then_inc(sems[key], inc)
                return body

            block.tensor(mk('pe'))
            block.vector(mk('dve'))
            block.scalar(mk('act'))
            block.gpsimd(mk('pool'))
            block.sync(mk('sp'))

    def emit(self):
        from contextlib import ExitStack
        with ExitStack() as st:
            self.semstack = st
            self.flush()


def _aps(*xs):
    return [x for x in xs if x is not None and not isinstance(x, (int, float))]


def _h_tt(self, eng, out, in0, in1, op):
    self.op(eng, lambda e: e.tensor_tensor(out=out, in0=in0, in1=in1, op=op), reads=[in0, in1], writes=[out])


def _h_ts(self, eng, out, in0, s1, s2, op0, op1=None):
    if op1 is None:
        self.op(eng, lambda e: e.tensor_scalar(out=out, in0=in0, scalar1=s1, scalar2=None, op0=op0),
                reads=_aps(in0, s1), writes=[out])
    else:
        self.op(eng, lambda e: e.tensor_scalar(out=out, in0=in0, scalar1=s1, scalar2=s2, op0=op0, op1=op1),
                reads=_aps(in0, s1, s2), writes=[out])


def _h_stt(self, eng, out, in0, scalar, in1, op0, op1):
    self.op(eng, lambda e: e.scalar_tensor_tensor(out=out, in0=in0, scalar=scalar, in1=in1, op0=op0, op1=op1),
            reads=_aps(in0, scalar, in1), writes=[out])


def _h_act(self, out, in_, func, bias=None, scale=None, accum_out=None):
    kw = {}
    if bias is not None:
        kw['bias'] = bias
    if scale is not None:
        kw['scale'] = scale
    if accum_out is not None:
        kw['accum_out'] = accum_out
    self.op('act', lambda e: e.activation(out=out, in_=in_, func=func, **kw),
            reads=_aps(in_, bias, scale), writes=_aps(out, accum_out))


def _h_copy(self, eng, out, in_):
    if eng == 'act':
        self.op('act', lambda e: e.copy(out=out, in_=in_), reads=[in_], writes=[out])
    else:
        self.op(eng, lambda e: e.tensor_copy(out=out, in_=in_), reads=[in_], writes=[out])


def _h_mm(self, out, lhsT, rhs, start=True, stop=True):
    self.op('pe', lambda e: e.matmul(out, lhsT=lhsT, rhs=rhs, start=start, stop=stop),
            reads=[lhsT, rhs], writes=[out])


def _h_tr(self, out, in_, ident):
    self.op('pe', lambda e: e.transpose(out=out, in_=in_, identity=ident), reads=[in_, ident], writes=[out])


def _h_red(self, eng, out, in_, op, axis=AX.X):
    self.op(eng, lambda e: e.tensor_reduce(out=out, in_=in_, axis=axis, op=op), reads=[in_], writes=[out])


def _h_memset(self, eng, ap, val):
    self.op(eng, lambda e: e.memset(ap, val), reads=[], writes=[ap])


def _h_recip(self, out, in_):
    self.op('dve', lambda e: e.reciprocal(out=out, in_=in_), reads=[in_], writes=[out])


Sched.tt = _h_tt
Sched.ts = _h_ts
Sched.stt = _h_stt
Sched.act = _h_act
Sched.copy = _h_copy
Sched.mm = _h_mm
Sched.tr = _h_tr
Sched.red = _h_red
Sched.memset = _h_memset
Sched.recip = _h_recip


def bcast(ap, shape):
    assert len(ap.shape) == len(shape), (ap.shape, shape)
    new = []
    for (st, cnt), tgt in zip(ap.ap, shape):
        if int(cnt) == int(tgt):
            new.append([int(st), int(cnt)])
        else:
            assert int(cnt) == 1
            new.append([0, int(tgt)])
    return bass.AP(ap.tensor, ap.offset, new)


T = 2048; D = 1024; NC = 3462
O_HQ, O_HF, O_HI, O_HG, O_AQ, O_AK, O_AV, O_SZ, O_XBC, O_DT = 0, 256, 512, 768, 1024, 1408, 1792, 2176, 2560, 3456

C_ID, C_TRI, C_TRI2, C_CO0, C_CO1, C_CI, C_I16, C_I128, C_END = 0, 128, 256, 384, 512, 640, 642, 658, 786


def make_consts():
    c = np.zeros((128, C_END), np.float32)
    s = np.arange(128)[:, None]; t = np.arange(128)[None, :]
    same = (s // 64) == (t // 64)
    c[:, C_ID:C_ID + 128] = np.eye(128)
    c[:, C_TRI:C_TRI + 128] = (same & (s <= t))
    c[:, C_TRI2:C_TRI2 + 128] = (same & (s > t))
    c[:, C_CO0:C_CO0 + 128] = (s < 64) * np.ones((1, 128))
    c[:, C_CO1:C_CO1 + 128] = (s >= 64) * np.ones((1, 128))
    c[:, C_CI] = (np.arange(128) < 64)
    c[:, C_CI + 1] = (np.arange(128) >= 64)
    c[:, C_I16:C_I16 + 16] = np.arange(16)[None, :]
    c[:, C_I128:C_I128 + 128] = np.arange(128)[None, :]
    return c


class Ctx:
    uid = 0

    def __init__(self, nc, st, pfx):
        Ctx.uid += 1
        self.nc, self.st, self.pfx = nc, st, "%s%d" % (pfx, Ctx.uid)
        self.n = 0

    def sb(self, shape, dt=F32, name=None):
        self.n += 1
        return self.st.enter_context(self.nc.sbuf_tensor("%s_%s%d" % (self.pfx, name or "t", self.n), list(shape), dt))

    def psum(self):
        return [self.st.enter_context(self.nc.psum_tensor("%s_ps%d" % (self.pfx, i), [128, 512], F32)) for i in range(8)]


def rstd_from_ss(s, out, tmp, ss, scale, eps=1e-6):
    s.act(tmp, ss, AF.Sqrt, bias=eps, scale=scale)
    s.recip(out, tmp)


def phaseB(s, nc, st, u_d, mix_d, hlb_d, hnw_d, consts_d, layer, ntiles=16):
    c = Ctx(nc, st, "B")
    P = c.psum()
    cst = c.sb([128, C_END], name="cst")
    s.dma('sp', cst[:], consts_d[:, :])
    ident = cst[:, C_ID:C_ID + 128]; tri = cst[:, C_TRI:C_TRI + 128]; tri2 = cst[:, C_TRI2:C_TRI2 + 128]
    chind = cst[:, C_CI:C_CI + 2]
    hnw = c.sb([128, 64], name="hnw")
    s.dma('sp', hnw[:], hnw_d.partition_broadcast(128))
    if layer > 0:
        h0 = c.sb([128, 256]); h1 = c.sb([128, 256]); lb = c.sb([128, 256]); oml = c.sb([128, 256])
        s.dma('sp', h0[:], hlb_d[0, :].partition_broadcast(128))
        s.dma('sp', h1[:], hlb_d[1, :].partition_broadcast(128))
        s.tt('dve', h1[:], h1[:], h0[:], ALU.subtract)
        s.act(lb[:], h1[:], AF.Sigmoid)
        s.ts('dve', oml[:], lb[:], -1.0, 1.0, ALU.mult, ALU.add)
    SA = c.sb([64, 4, 64], name="SA"); SB = c.sb([64, 4, 64], name="SB")
    s.memset('dve', SA[:], 0.0)
    NB = 2
    qTa = [c.sb([64, 4, 128]) for _ in range(NB)]; qTb = [c.sb([64, 4, 128]) for _ in range(NB)]
    kh0 = [c.sb([128, 256]) for _ in range(NB)]; kh1 = [c.sb([128, 256]) for _ in range(NB)]
    for t_ in qTa + qTb + kh0 + kh1:
        s.memset('dve', t_[:], 0.0)
    tin = [c.sb([128, 1024], name="tin") for _ in range(4)]
    fg = c.sb([128, 256]); logf = c.sb([128, 256]); key = c.sb([128, 256])
    eb = c.sb([128, 256]); enb = c.sb([128, 256]); er = c.sb([128, 256])
    qt = c.sb([128, 256]); kt = c.sb([128, 256])
    dec = [c.sb([64, 8]) for _ in range(NB)]; qT = [c.sb([64, 4, 128]) for _ in range(NB)]; kT = [c.sb([64, 4, 128]) for _ in range(NB)]
    sg = [c.sb([128, 256]) for _ in range(NB)]
    ATs = [c.sb([128, 128]) for _ in range(2)]
    osb = c.sb([128, 256]); sq = c.sb([128, 256]); ss = c.sb([128, 12])
    outt = [c.sb([128, 256]) for _ in range(2)]

    def pre(tt):
        p = tt % NB
        ti = tin[tt % 4]
        if tt == 0:
            s.dma('sp', tin[0][:], u_d[0:128, 0:1024])
        if tt + 1 < ntiles:
            s.dma('sp', tin[(tt + 1) % 4][:], u_d[(tt + 1) * 128:(tt + 2) * 128, 0:1024])
        q = ti[:, 0:256]; f = ti[:, 256:512]; g = ti[:, 768:1024]
        s.act(fg[:], f, AF.Sigmoid)
        if layer > 0:
            s.tt('dve', fg[:], fg[:], oml[:], ALU.mult)
            s.tt('dve', fg[:], fg[:], lb[:], ALU.add)
        s.act(sg[p][:], g, AF.Silu)
        s.act(logf[:], fg[:], AF.Ln)
        yield
        s.ts('dve', key[:], fg[:], -1.0, 1.0, ALU.mult, ALU.add)
        s.mm(P[0][:, 0:256], lhsT=tri, rhs=logf[:])
        s.mm(P[0][:, 256:512], lhsT=tri2, rhs=logf[:])
        for h in range(4):
            s.mm(P[1][0:64, h * 2:h * 2 + 2], lhsT=logf[:, h * 64:(h + 1) * 64], rhs=chind)
        yield
        s.act(eb[:], P[0][:, 0:256], AF.Exp)
        s.act(enb[:], P[0][:, 0:256], AF.Exp, scale=-1.0)
        s.act(er[:], P[0][:, 256:512], AF.Exp)
        s.act(dec[p][:], P[1][0:64, 0:8], AF.Exp)
        yield
        s.stt('dve', qt[:], q, 0.125, eb[:], ALU.mult, ALU.mult)
        s.tt('pool', kt[:], key[:], enb[:], ALU.mult)
        s.tt('pool', kh0[p][0:64, :], key[0:64, :], er[0:64, :], ALU.mult)
        s.tt('pool', kh1[p][64:128, :], key[64:128, :], er[64:128, :], ALU.mult)
        s.tt('pool', sg[p][:].rearrange("p (h v) -> p h v", h=4), sg[p][:].rearrange("p (h v) -> p h v", h=4),
             bcast(hnw[:].rearrange("p (o v) -> p o v", o=1), [128, 4, 64]), ALU.mult)
        yield
        for h in range(4):
            s.tr(P[2][0:64, h * 128:(h + 1) * 128], qt[:, h * 64:(h + 1) * 64], ident)
            s.tr(P[3][0:64, h * 128:(h + 1) * 128], kt[:, h * 64:(h + 1) * 64], ident)
        yield
        p2v = P[2][0:64, :].rearrange("p (h t) -> p h t", h=4)
        p3v = P[3][0:64, :].rearrange("p (h t) -> p h t", h=4)
        s.copy('dve', qT[p][:], p2v)
        s.copy('dve', qTa[p][:, :, 0:64], p2v[:, :, 0:64])
        s.copy('dve', qTb[p][:, :, 64:128], p2v[:, :, 64:128])
        s.copy('act', kT[p][:], p3v)
        yield

    def heads(tt):
        p = tt % NB
        ti = tin[tt % 4]
        for h in range(4):
            hs = slice(h * 64, (h + 1) * 64)
            vs = slice(512 + h * 64, 512 + (h + 1) * 64)
            s.mm(P[6][0:64, (2 * h) * 64:(2 * h + 1) * 64], lhsT=kh0[p][:, hs], rhs=ti[:, vs])
            s.mm(P[6][0:64, (2 * h + 1) * 64:(2 * h + 2) * 64], lhsT=kh1[p][:, hs], rhs=ti[:, vs])
        yield
        for h in range(4):
            hs = slice(h * 64, (h + 1) * 64)
            vs = slice(512 + h * 64, 512 + (h + 1) * 64)
            AT = P[4 + h % 2][:, 0:128]
            s.mm(AT, lhsT=kT[p][:, h, :], rhs=qT[p][:, h, :])
            A = ATs[h % 2]
            s.tt('dve', A[:], AT, tri, ALU.mult)
            o = P[7][:, hs]
            s.mm(o, lhsT=A[:], rhs=ti[:, vs], start=True, stop=False)
            s.mm(o, lhsT=qTa[p][:, h, :], rhs=SA[:, h, :], start=False, stop=False)
            u0 = P[6][0:64, (2 * h) * 64:(2 * h + 1) * 64]
            u1 = P[6][0:64, (2 * h + 1) * 64:(2 * h + 2) * 64]
            s.stt('dve', SB[:, h, :], SA[:, h, :], dec[p][:, 2 * h:2 * h + 1], u0, ALU.mult, ALU.add)
            s.mm(o, lhsT=qTb[p][:, h, :], rhs=SB[:, h, :], start=False, stop=True)
            s.stt('dve', SA[:, h, :], SB[:, h, :], dec[p][:, 2 * h + 1:2 * h + 2], u1, ALU.mult, ALU.add)
            yield
        s.copy('act', osb[:], P[7][:, 0:256])
        s.tt('pool', sq[:], osb[:], osb[:], ALU.mult)
        s.red('dve', ss[:, 0:4], sq[:].rearrange("p (h v) -> p h v", h=4), ALU.add)
        rstd_from_ss(s, ss[:, 8:12], ss[:, 4:8], ss[:, 0:4], 1.0 / 64)
        ot = outt[tt % 2]
        s.tt('dve', ot[:].rearrange("p (h v) -> p h v", h=4), osb[:].rearrange("p (h v) -> p h v", h=4),
             bcast(ss[:, 8:12].rearrange("p (h o) -> p h o", o=1), [128, 4, 64]), ALU.mult)
        s.tt('dve', ot[:], ot[:], sg[p][:], ALU.mult)
        s.dma('sp', mix_d[tt * 128:(tt + 1) * 128, 0:256], ot[:])
        yield

    def drain(*gens):
        gens = [g for g in gens if g is not None]
        while gens:
            for g in list(gens):
                try:
                    next(g)
                except StopIteration:
                    gens.remove(g)

    drain(pre(0))
    for tt in range(ntiles):
        drain(heads(tt), pre(tt + 1) if tt + 1 < ntiles else None)


def phaseD(s, nc, st, u_d, mix_d, cw_d, cb_d, dtb_d, alog_d, dsk_d, nw_d, consts_d, ntiles=16):
    c = Ctx(nc, st, "D")
    BF16 = mybir.dt.bfloat16
    P = c.psum()
    cst = c.sb([128, C_END], name="cst")
    s.dma('sp', cst[:], consts_d[:, :])
    ident = cst[:, C_ID:C_ID + 128]; tri = cst[:, C_TRI:C_TRI + 128]; tri2 = cst[:, C_TRI2:C_TRI2 + 128]
    co0 = cst[:, C_CO0:C_CO0 + 128]; co1 = cst[:, C_CO1:C_CO1 + 128]
    cw = c.sb([128, 4, 896], name="cw"); cb = c.sb([128, 896], name="cb")
    for j in range(4):
        s.dma('sp', cw[:, j, :], cw_d[j, :].partition_broadcast(128))
    s.dma('sp', cb[:], cb_d.partition_broadcast(128))
    sm = c.sb([128, 32], name="sm")
    s.dma('sp', sm[:, 0:6], dtb_d.partition_broadcast(128))
    s.dma('sp', sm[:, 6:12], alog_d.partition_broadcast(128))
    s.dma('sp', sm[:, 18:24], dsk_d.partition_broadcast(128))
    nwb = c.sb([128, 384], name="nwb")
    s.dma('sp', nwb[:], nw_d.partition_broadcast(128))
    s.act(sm[:, 24:30], sm[:, 6:12], AF.Exp)
    s.ts('dve', sm[:, 12:18], sm[:, 24:30], -1.0, None, ALU.mult)
    dtbias = sm[:, 0:6]; Ab = sm[:, 12:18]
    R = [c.sb([128, 6, 64], name="R") for _ in range(6)]
    R16 = [c.sb([128, 6, 64], BF16, name="R16") for _ in range(6)]
    s.memset('dve', R[0][:], 0.0)
    s.memset('dve', R16[0][:], 0.0)
    Bz0 = c.sb([128, 256], BF16); Bz1 = c.sb([128, 256], BF16)
    s.memset('dve', Bz0[:], 0.0); s.memset('dve', Bz1[:], 0.0)
    NH = 3
    CTs0 = [c.sb([128, 128], BF16) for _ in range(NH)]; CTs1 = [c.sb([128, 128], BF16) for _ in range(NH)]
    for b_ in CTs0 + CTs1:
        s.memset('pool', b_[:], 0.0)
    M = [c.sb([128, 1286], name="M") for _ in range(3)]
    X = [[c.sb([128, 896], name="X") for _ in range(3)] for _ in range(3)]
    mj = [c.sb([128, 896]) for _ in range(4)]
    xcs = [c.sb([128, 896], name="xc") for _ in range(2)]; xc16s = [c.sb([128, 384], BF16) for _ in range(2)]
    v6s = [c.sb([128, 64], name="v6") for _ in range(2)]
    abcs = [c.sb([128, 6, 128]) for _ in range(2)]; BCTs = [c.sb([128, 4, 128], BF16) for _ in range(2)]; CBTs = [c.sb([128, 2, 128]) for _ in range(2)]
    xw = c.sb([128, 384], BF16)
    dsb = [c.sb([128, 128]) for _ in range(NH)]; ecm = [c.sb([128, 128]) for _ in range(NH)]
    esb = [c.sb([128, 128]) for _ in range(NH)]; msb = [c.sb([128, 128]) for _ in range(NH)]; sc = [c.sb([128, 128], BF16) for _ in range(NH)]
    ysb = c.sb([128, 384]); szbs = [c.sb([128, 384]) for _ in range(2)]; sq = c.sb([128, 384]); ss = c.sb([128, 8]); tmpS = c.sb([128, 384])
    outt = [c.sb([128, 384]) for _ in range(2)]
    DmB = [P[3], P[4]]
    def loads(tt):
        Mt = M[tt % 3]
        s.dma('sp', Mt[:], u_d[tt * 128:(tt + 1) * 128, O_SZ:NC])
        for j in range(3):
            Xj = X[tt % 3][j]
            sh = 3 - j
            if tt == 0:
                s.memset('pool', Xj[:], 0.0)
                s.dma('sp', Xj[sh:128, :], u_d[0:128 - sh, O_XBC:O_XBC + 896])
            else:
                s.dma('sp', Xj[:], u_d[tt * 128 - sh:(tt + 1) * 128 - sh, O_XBC:O_XBC + 896])

    def pre(tt):
        b = tt % 2
        Mt = M[tt % 3]; v6 = v6s[b]
        xc = xcs[b]; xc16 = xc16s[b]; abc = abcs[b]; BCT = BCTs[b]; CBT = CBTs[b]; szb = szbs[b]
        Sin, Smid, Sout = R[(2 * tt) % 6], R[(2 * tt + 1) % 6], R[(2 * tt + 2) % 6]
        Sin16, Smid16, Sout16 = R16[(2 * tt) % 6], R16[(2 * tt + 1) % 6], R16[(2 * tt + 2) % 6]
        if tt == 0:
            loads(0)
        if tt + 1 < ntiles:
            loads(tt + 1)
        z = Mt[:, 0:384]; x3 = Mt[:, 384:1280]; dtr = Mt[:, 1280:1286]
        xs = xc[:, 0:384]
        dtt = v6[:, 0:6]; a_ = v6[:, 8:14]; cum = v6[:, 16:22]; wgt = v6[:, 24:30]; dAe = v6[:, 32:44]; tmp6 = v6[:, 48:54]
        s.tt('dve', tmp6, dtr, dtbias, ALU.add)
        s.act(tmp6, tmp6, AF.Exp)
        s.act(dtt, tmp6, AF.Ln, bias=1.0)
        s.tt('dve', a_, dtt, Ab, ALU.mult)
        s.copy('pool', abc[:], bcast(v6[:, 8:14].rearrange("p (h o) -> p h o", o=1), [128, 6, 128]))
        yield
        for j in range(3):
            s.tt('pool', mj[j][:], X[tt % 3][j][:], cw[:, j, :], ALU.mult)
        s.tt('pool', mj[3][:], x3, cw[:, 3, :], ALU.mult)
        yield
        s.mm(P[0][:, 0:6], lhsT=tri, rhs=a_)
        s.mm(P[0][:, 8:14], lhsT=tri2, rhs=a_)
        s.mm(P[0][:, 16:22], lhsT=co0, rhs=a_)
        s.mm(P[0][:, 22:28], lhsT=co1, rhs=a_)
        s.copy('act', cum, P[0][:, 0:6])
        s.act(wgt, P[0][:, 8:14], AF.Exp)
        s.act(dAe, P[0][:, 16:28], AF.Exp)
        s.tt('dve', wgt, wgt, dtt, ALU.mult)
        s.tt('dve', mj[0][:], mj[0][:], mj[1][:], ALU.add)
        s.tt('dve', mj[2][:], mj[2][:], mj[3][:], ALU.add)
        s.tt('dve', mj[0][:], mj[0][:], mj[2][:], ALU.add)
        s.tt('dve', mj[0][:], mj[0][:], cb[:], ALU.add)
        yield
        s.act(xc[:], mj[0][:], AF.Silu)
        s.copy('act', xc16[:], xs)
        s.act(szb[:], z, AF.Silu)
        s.tt('dve', xw[:].rearrange("p (h v) -> p h v", h=6), xs.rearrange("p (h v) -> p h v", h=6),
             bcast(v6[:, 24:30].rearrange("p (h o) -> p h o", o=1), [128, 6, 64]), ALU.mult)
        yield
        for k in range(4):
            s.tr(P[1][:, k * 128:(k + 1) * 128], xc[:, 384 + k * 128:384 + (k + 1) * 128], ident)
        s.copy('act', BCT[:], P[1][:, :].rearrange("p (k t) -> p k t", k=4))
        for g in range(2):
            s.mm(P[2][:, g * 128:(g + 1) * 128], lhsT=BCT[:, g, :], rhs=BCT[:, 2 + g, :])
        s.copy('dve', CBT[:], P[2][:, 0:256].rearrange("p (g t) -> p g t", g=2))
        s.copy('act', Bz0[0:64, :], xc[0:64, 384:640])
        s.copy('act', Bz1[64:128, :], xc[64:128, 384:640])
        yield
        for h in range(6):
            g = h // 3; hs = slice(h * 64, (h + 1) * 64)
            s.mm(P[7][:, hs], lhsT=Bz0[:, g * 128:(g + 1) * 128], rhs=xw[:, hs])
        for h in range(6):
            g = h // 3; hs = slice(h * 64, (h + 1) * 64)
            s.mm(P[1][:, hs], lhsT=Bz1[:, g * 128:(g + 1) * 128], rhs=xw[:, hs])
        s.tt('dve', tmpS[:].rearrange("p (h v) -> p h v", h=6), Sin[:], bcast(v6[:, 32:38].rearrange("p (h o) -> p h o", o=1), [128, 6, 64]), ALU.mult)
        s.tt('dve', Smid[:], tmpS[:].rearrange("p (h v) -> p h v", h=6), P[7][:, 0:384].rearrange("p (h v) -> p h v", h=6), ALU.add)
        s.copy('act', Smid16[:], Smid[:])
        s.tt('dve', tmpS[:].rearrange("p (h v) -> p h v", h=6), Smid[:], bcast(v6[:, 38:44].rearrange("p (h o) -> p h o", o=1), [128, 6, 64]), ALU.mult)
        s.tt('dve', Sout[:], tmpS[:].rearrange("p (h v) -> p h v", h=6), P[1][:, 0:384].rearrange("p (h v) -> p h v", h=6), ALU.add)
        s.copy('act', Sout16[:], Sout[:])

        yield

    def heads(tt):
        b = tt % 2
        Mt = M[b]; v6 = v6s[b]
        xc = xcs[b]; xc16 = xc16s[b]; abc = abcs[b]; BCT = BCTs[b]; CBT = CBTs[b]; szb = szbs[b]
        Sin16, Smid16 = R16[(2 * tt) % 6], R16[(2 * tt + 1) % 6]
        def st1(h):
            g = h // 3; hq = h % NH
            Dm = DmB[h % 2][:, 0:128]
            s.mm(Dm, lhsT=abc[:, h, :], rhs=tri)
            s.ts('dve', dsb[hq][:], Dm, v6[:, 16 + h:17 + h], 0.0, ALU.subtract, ALU.min)
            s.act(ecm[hq][:], Dm, AF.Exp)
            s.act(esb[hq][:], dsb[hq][:], AF.Exp)
            s.stt('dve', msb[hq][:], esb[hq][:], v6[:, h:h + 1], tri, ALU.mult, ALU.mult)
            s.tt('pool', sc[hq][:], msb[hq][:], CBT[:, g, :], ALU.mult)
            s.tt('pool', CTs0[hq][:, 0:64], BCT[:, 2 + g, 0:64], ecm[hq][:, 0:64], ALU.mult)
            s.tt('pool', CTs1[hq][:, 64:128], BCT[:, 2 + g, 64:128], ecm[hq][:, 64:128], ALU.mult)

        def st2(h):
            hq = h % NH; hs = slice(h * 64, (h + 1) * 64)
            y = P[5 + h % 2][:, 0:64]
            s.mm(y, lhsT=sc[hq][:], rhs=xc16[:, hs], start=True, stop=False)
            s.mm(y, lhsT=CTs0[hq][:], rhs=Sin16[:, h, :], start=False, stop=False)
            s.mm(y, lhsT=CTs1[hq][:], rhs=Smid16[:, h, :], start=False, stop=True)
            s.stt('dve', ysb[:, hs], xc[:, hs], sm[:, 18 + h:19 + h], y, ALU.mult, ALU.add)

        SK = 2
        for step in range(6 + SK):
            if step < 6:
                st1(step)
            if step >= SK:
                st2(step - SK)
            yield
        s.tt('pool', ysb[:], ysb[:], szb[:], ALU.mult)
        s.tt('pool', sq[:], ysb[:], ysb[:], ALU.mult)
        s.red('dve', ss[:, 0:2], sq[:].rearrange("p (g v) -> p g v", g=2), ALU.add)
        rstd_from_ss(s, ss[:, 4:6], ss[:, 2:4], ss[:, 0:2], 1.0 / 192)
        ot = outt[b]
        s.tt('dve', ot[:].rearrange("p (g v) -> p g v", g=2), ysb[:].rearrange("p (g v) -> p g v", g=2),
             bcast(ss[:, 4:6].rearrange("p (g o) -> p g o", o=1), [128, 2, 192]), ALU.mult)
        s.tt('pool', ot[:], ot[:], nwb[:], ALU.mult)
        s.dma('sp', mix_d[tt * 128:(tt + 1) * 128, 640:1024], ot[:])


    def drain(*gens):
        gens = [g for g in gens if g is not None]
        while gens:
            for g in list(gens):
                try:
                    next(g)
                except StopIteration:
                    gens.remove(g)

    drain(pre(0))
    for tt in range(ntiles):
        drain(heads(tt), pre(tt + 1) if tt + 1 < ntiles else None)


DILS = (1, 4, 16)


def t5_bucket_np(dist):
    max_exact = 16
    d = np.maximum(dist, 1).astype(np.float32)
    large = max_exact + (np.log(d / max_exact) / np.log(2048 / max_exact) * (32 - max_exact)).astype(np.int32)
    large = np.minimum(large, 31)
    return np.where(dist < max_exact, dist, large).astype(np.int32)


def make_attn_consts(rel_bias):
    j = np.arange(128)[:, None]; i = np.arange(128)[None, :]
    delta = np.concatenate([i - j + 128, i - j], axis=1)
    valid = (delta >= 0) & (delta <= 128)
    maskadd = np.where(valid, 0.0, -30000.0).astype(np.float32)
    biasg = np.zeros((18, 128, 256), np.float32)
    for bi, dil in enumerate(DILS):
        bucket = t5_bucket_np(np.maximum(delta, 0) * dil)
        for h in range(6):
            biasg[bi * 6 + h] = rel_bias[bucket, h]
    return biasg, maskadd


def phaseC(s, nc, st, u_d, mix_d, qk_d, ob_d, qw_d, kw_d, biasg_d, maskadd_d, consts_d, ntiles=16, branches=(0, 1, 2), do_c0=True, do_c2=True):
    c = Ctx(nc, st, "C")
    P = c.psum()
    cst = c.sb([128, C_END], name="cst")
    s.dma('sp', cst[:], consts_d[:, :])
    ident = cst[:, C_ID:C_ID + 128]
    qkw = c.sb([128, 128], name="qkw")
    s.dma('sp', qkw[:, 0:64], qw_d.partition_broadcast(128))
    s.dma('sp', qkw[:, 64:128], kw_d.partition_broadcast(128))
    biasT = c.sb([128, 18, 256], name="biasT")
    madd = c.sb([128, 256], name="madd")
    s.dma('sp', biasT[:], biasg_d.rearrange("n j c -> j n c"))
    s.dma('sp', madd[:], maskadd_d[:, :])
    s.tt('dve', biasT[:], biasT[:], bcast(madd[:].rearrange("p (o c) -> p o c", o=1), [128, 18, 256]), ALU.add)
    qk = [c.sb([128, 768]) for _ in range(2)]; sq = c.sb([128, 768]); ss = c.sb([128, 36]); qkn = [c.sb([128, 768]) for _ in range(2)]
    for tt in (range(ntiles) if do_c0 else []):
        b = tt % 2
        s.dma('sp', qk[b][:], u_d[tt * 128:(tt + 1) * 128, O_AQ:O_AQ + 768])
        s.tt('pool', sq[:], qk[b][:], qk[b][:], ALU.mult)
        s.red('dve', ss[:, 0:12], sq[:].rearrange("p (h v) -> p h v", h=12), ALU.add)
        rstd_from_ss(s, ss[:, 24:36], ss[:, 12:24], ss[:, 0:12], 1.0 / 64)
        s.tt('dve', qkn[b][:].rearrange("p (h v) -> p h v", h=12), qk[b][:].rearrange("p (h v) -> p h v", h=12),
             bcast(ss[:, 24:36].rearrange("p (h o) -> p h o", o=1), [128, 12, 64]), ALU.mult)
        s.stt('dve', qkn[b][:, 0:384].rearrange("p (h v) -> p h v", h=6), qkn[b][:, 0:384].rearrange("p (h v) -> p h v", h=6), 0.125,
              bcast(qkw[:, 0:64].rearrange("p (o v) -> p o v", o=1), [128, 6, 64]), ALU.mult, ALU.mult)
        s.tt('pool', qkn[b][:, 384:768].rearrange("p (h v) -> p h v", h=6), qkn[b][:, 384:768].rearrange("p (h v) -> p h v", h=6),
             bcast(qkw[:, 64:128].rearrange("p (o v) -> p o v", o=1), [128, 6, 64]), ALU.mult)
        s.dma('sp', qk_d[tt * 128:(tt + 1) * 128, :], qkn[b][:])
    QKb = [c.sb([128, 768]) for _ in range(3)]
    V32 = [c.sb([128, 384]) for _ in range(3)]
    BF16 = mybir.dt.bfloat16
    V1 = [c.sb([128, 6, 65], BF16) for _ in range(3)]
    for v in V1:
        s.memset('dve', v[:], 1.0)
    QT = [c.sb([128, 3, 128], BF16) for _ in range(2)]
    KTe = [c.sb([128, 3, 128], BF16) for _ in range(3)]; KTo = [c.sb([128, 3, 128], BF16) for _ in range(3)]
    for k_ in KTe + KTo:
        s.memset('pool', k_[:], 0.0)
    Esb = [c.sb([128, 256]) for _ in range(4)]; Psb = [c.sb([128, 256], BF16) for _ in range(4)]
    Osb = [c.sb([128, 390]) for _ in range(2)]
    blocks = []
    for bi in branches:
        dil = DILS[bi]
        L = (ntiles * 128) // dil
        nblk = L // 128
        for r in range(dil):
            for blk in range(nblk):
                blocks.append((bi, dil, r, blk))
    views = {}
    for bi in branches:
        dil = DILS[bi]
        views[bi] = (qk_d[0:ntiles * 128, :].rearrange("(b m d) c -> d b m c", m=128, d=dil),
                     u_d[0:ntiles * 128, O_AV:O_AV + 384].rearrange("(b m d) c -> d b m c", m=128, d=dil),
                     ob_d[bi, 0:ntiles * 128, :].rearrange("(b m d) c -> d b m c", m=128, d=dil))

    def loads(n):
        bi, dil, r, blk = blocks[n]
        qv, vv, ov = views[bi]
        s.dma('sp', QKb[n % 3][:], qv[r, blk])
        s.dma('sp', V32[n % 3][:], vv[r, blk])

    def pre(n):
        bi, dil, r, blk = blocks[n]
        pb = n % 2
        qv, vv, ov = views[bi]
        p3 = n % 3
        if n == 0:
            loads(0)
        if n + 1 < len(blocks):
            loads(n + 1)
        s.copy('pool', V1[p3][:, :, 0:64], V32[p3][:].rearrange("m (h v) -> m h v", h=6))
        yield
        for pr in range(3):
            s.tr(P[0][:, pr * 128:(pr + 1) * 128], QKb[p3][:, pr * 128:(pr + 1) * 128], ident)
            s.tr(P[1][:, pr * 128:(pr + 1) * 128], QKb[p3][:, 384 + pr * 128:384 + (pr + 1) * 128], ident)
            yield
        s.copy('act', QT[pb][:], P[0][:, 0:384].rearrange("p (a t) -> p a t", a=3))
        p1v = P[1][:, 0:384].rearrange("p (a t) -> p a t", a=3)
        s.copy('dve', KTe[p3][0:64, :, :], p1v[0:64, :, :])
        s.copy('dve', KTo[p3][64:128, :, :], p1v[64:128, :, :])
        yield

    def heads(n):
        bi, dil, r, blk = blocks[n]
        pb = n % 2
        qv, vv, ov = views[bi]
        PO = P[6 + pb]
        c0 = 0 if blk > 0 else 128
        p3 = n % 3; q3 = (n - 1) % 3

        def S_(h):
            pr = h // 2; hq = h % 4
            KT = KTe if h % 2 == 0 else KTo
            ST = P[2 + hq]
            s.mm(ST[:, 128:256], lhsT=KT[p3][:, pr, :], rhs=QT[pb][:, pr, :])
            if blk > 0:
                s.mm(ST[:, 0:128], lhsT=KT[q3][:, pr, :], rhs=QT[pb][:, pr, :])
            s.tt('dve', Esb[hq][:, c0:256], ST[:, c0:256], biasT[:, bi * 6 + h, c0:256], ALU.add)
            s.act(Psb[hq][:, c0:256], Esb[hq][:, c0:256], AF.Exp)

        def PV_(h):
            hq = h % 4
            O = PO[:, h * 65:(h + 1) * 65]
            if blk > 0:
                s.mm(O, lhsT=Psb[hq][:, 0:128], rhs=V1[q3][:, h, :], start=True, stop=False)
            s.mm(O, lhsT=Psb[hq][:, 128:256], rhs=V1[p3][:, h, :], start=(blk == 0), stop=True)

        SK = 3
        for step in range(6 + SK):
            if step >= SK:
                PV_(step - SK)
            if step < 6:
                S_(step)
            yield
        s.copy('act', Osb[pb][:], PO[:, 0:390])
        s.dma('sp', ov[r, blk], Osb[pb][:])
        yield

    def drain(*gens):
        gens = [g for g in gens if g is not None]
        while gens:
            for g in list(gens):
                try:
                    next(g)
                except StopIteration:
                    gens.remove(g)

    drain(pre(0))
    for n in range(len(blocks)):
        drain(heads(n), pre(n + 1) if n + 1 < len(blocks) else None)
    ob = [[c.sb([128, 390]) for _ in range(3)] for _ in range(2)]
    rden = c.sb([128, 8]); outt = [c.sb([128, 384]) for _ in range(2)]
    for tt in (range(ntiles) if do_c2 else []):
        b = tt % 2
        for bi in range(3):
            s.dma('sp', ob[b][bi][:], ob_d[bi, tt * 128:(tt + 1) * 128, :])
        s.tt('pool', ob[b][0][:], ob[b][0][:], ob[b][1][:], ALU.add)
        s.tt('pool', ob[b][0][:], ob[b][0][:], ob[b][2][:], ALU.add)
        o3 = ob[b][0][:].rearrange("p (h v) -> p h v", h=6)
        s.recip(rden[:, 0:6].rearrange("p (h o) -> p h o", o=1), o3[:, :, 64:65])
        s.tt('dve', outt[b][:].rearrange("p (h v) -> p h v", h=6), o3[:, :, 0:64],
             bcast(rden[:, 0:6].rearrange("p (h o) -> p h o", o=1), [128, 6, 64]), ALU.mult)
        s.dma('sp', mix_d[tt * 128:(tt + 1) * 128, 256:640], outt[b][:])


def norm_transpose(s, xt, wb, junk, ss, xn, xnT, Pa, Pb, ident):
    s.act(junk[:], xt[:], AF.Square, accum_out=ss[:, 0:1])
    s.act(ss[:, 1:2], ss[:, 0:1], AF.Sqrt, bias=1e-6, scale=1.0 / D)
    s.recip(ss[:, 2:3], ss[:, 1:2])
    s.stt('dve', xn[:], xt[:], ss[:, 2:3], wb[:], ALU.mult, ALU.mult)
    transpose8(s, xn, xnT, Pa, Pb, ident)


def transpose8(s, xn, xnT, Pa, Pb, ident):
    for dc in range(8):
        pt = Pa if dc < 4 else Pb
        s.tr(pt[:, (dc % 4) * 128:(dc % 4 + 1) * 128], xn[:, dc * 128:(dc + 1) * 128], ident)
    s.copy('act', xnT[:, 0:4, :], Pa[:, :].rearrange("p (a c) -> p a c", a=4))
    s.copy('dve', xnT[:, 4:8, :], Pb[:, :].rearrange("p (a c) -> p a c", a=4))


def phaseA(s, nc, st, x_d, w_d, nw_d, u_d, consts_d, ntiles=16, qk_d=None, qw_d=None, kw_d=None):
    c = Ctx(nc, st, "A")
    BF16 = mybir.dt.bfloat16
    P = c.psum()
    W = c.sb([128, 8, NC], BF16, name="W")
    ident = c.sb([128, 128], name="ident")
    wb = c.sb([128, D], name="wb")
    s.dma('sp', ident[:], consts_d[:, C_ID:C_ID + 128])
    s.dma('sp', wb[:], nw_d.partition_broadcast(128))
    wst = [c.sb([128, NC], name="wst") for _ in range(2)]
    for dc in range(8):
        s.dma('sp', wst[dc % 2][:], w_d[dc * 128:(dc + 1) * 128, :])
        s.copy('pool' if dc % 2 == 0 else 'dve', W[:, dc, :], wst[dc % 2][:])
    xt = [c.sb([128, D]) for _ in range(3)]; xns = [c.sb([128, D]) for _ in range(2)]; junk = c.sb([128, D])
    if qk_d is not None:
        qkw = c.sb([128, 128], name="qkw")
        s.dma('sp', qkw[:, 0:64], qw_d.partition_broadcast(128))
        s.dma('sp', qkw[:, 64:128], kw_d.partition_broadcast(128))
        sq = c.sb([128, 768]); ssq = [c.sb([128, 36]) for _ in range(2)]; qkn = [c.sb([128, 768]) for _ in range(2)]
    sss = [c.sb([128, 4]) for _ in range(2)]; xnT = [c.sb([128, 8, 128], BF16) for _ in range(2)]; usb = [c.sb([128, NC]) for _ in range(2)]
    def pre(tt):
        b = tt % 2
        xn = xns[b]; ss = sss[b]
        if tt == 0:
            s.dma('sp', xt[0][:], x_d[0:128, :])
        if tt + 1 < ntiles:
            s.dma('sp', xt[(tt + 1) % 3][:], x_d[(tt + 1) * 128:(tt + 2) * 128, :])
        s.act(junk[:], xt[tt % 3][:], AF.Square, accum_out=ss[:, 0:1])
        s.act(ss[:, 1:2], ss[:, 0:1], AF.Sqrt, bias=1e-6, scale=1.0 / D)
        s.recip(ss[:, 2:3], ss[:, 1:2])
        s.stt('dve', xn[:], xt[tt % 3][:], ss[:, 2:3], wb[:], ALU.mult, ALU.mult)
        yield
        for dc in range(8):
            pt = P[0] if dc < 4 else P[1]
            s.tr(pt[:, (dc % 4) * 128:(dc % 4 + 1) * 128], xn[:, dc * 128:(dc + 1) * 128], ident[:])
        yield
        s.copy('act', xnT[b][:, 0:4, :], P[0][:, :].rearrange("p (a c) -> p a c", a=4))
        s.copy('dve', xnT[b][:, 4:8, :], P[1][:, :].rearrange("p (a c) -> p a c", a=4))
        yield

    def main(tt):
        b = tt % 2
        nblk = (NC + 511) // 512
        for cb in range(nblk):
            c0 = cb * 512; c1 = min(NC, c0 + 512)
            pt = P[2 + cb % 6]
            for dc in range(8):
                s.mm(pt[:, 0:c1 - c0], lhsT=xnT[b][:, dc, :], rhs=W[:, dc, c0:c1], start=(dc == 0), stop=(dc == 7))
            s.copy('act' if cb % 2 == 0 else 'dve', usb[b][:, c0:c1], pt[:, 0:c1 - c0])
            yield
        s.dma('sp', u_d[tt * 128:(tt + 1) * 128, :], usb[b][:])
        if qk_d is not None:
            qk = usb[b][:, O_AQ:O_AQ + 768]; ss_ = ssq[b]
            s.tt('pool', sq[:], qk, qk, ALU.mult)
            s.red('dve', ss_[:, 0:12], sq[:].rearrange("p (h v) -> p h v", h=12), ALU.add)
            rstd_from_ss(s, ss_[:, 24:36], ss_[:, 12:24], ss_[:, 0:12], 1.0 / 64)
            s.tt('dve', qkn[b][:].rearrange("p (h v) -> p h v", h=12), qk.rearrange("p (h v) -> p h v", h=12),
                 bcast(ss_[:, 24:36].rearrange("p (h o) -> p h o", o=1), [128, 12, 64]), ALU.mult)
            s.stt('dve', qkn[b][:, 0:384].rearrange("p (h v) -> p h v", h=6), qkn[b][:, 0:384].rearrange("p (h v) -> p h v", h=6), 0.125,
                  bcast(qkw[:, 0:64].rearrange("p (o v) -> p o v", o=1), [128, 6, 64]), ALU.mult, ALU.mult)
            s.tt('pool', qkn[b][:, 384:768].rearrange("p (h v) -> p h v", h=6), qkn[b][:, 384:768].rearrange("p (h v) -> p h v", h=6),
                 bcast(qkw[:, 64:128].rearrange("p (o v) -> p o v", o=1), [128, 6, 64]), ALU.mult)
            s.dma('sp', qk_d[tt * 128:(tt + 1) * 128, :], qkn[b][:])
        yield

    def drain(*gens):
        gens = [g for g in gens if g is not None]
        while gens:
            for g in list(gens):
                try:
                    next(g)
                except StopIteration:
                    gens.remove(g)

    drain(pre(0))
    for tt in range(ntiles):
        drain(main(tt), pre(tt + 1) if tt + 1 < ntiles else None)


def phaseE(s, nc, st, mix_d, xin_d, w_d, x1_d, consts_d, ntiles=16, ob_d=None):
    c = Ctx(nc, st, "E")
    BF16 = mybir.dt.bfloat16
    P = c.psum()
    W = c.sb([128, 8, D], BF16, name="W")
    ident = c.sb([128, 128], name="ident")
    s.dma('sp', ident[:], consts_d[:, C_ID:C_ID + 128])
    wst = [c.sb([128, D], name="wst") for _ in range(2)]
    for dc in range(8):
        s.dma('sp', wst[dc % 2][:], w_d[dc * 128:(dc + 1) * 128, :])
        s.copy('pool' if dc % 2 == 0 else 'dve', W[:, dc, :], wst[dc % 2][:])
    mt = [c.sb([128, D]) for _ in range(3)]; xt = [c.sb([128, D]) for _ in range(3)]
    mT = [c.sb([128, 8, 128], BF16) for _ in range(2)]; ot = [c.sb([128, D]) for _ in range(2)]
    if ob_d is not None:
        ob = [[c.sb([128, 390]) for _ in range(3)] for _ in range(3)]; rden = [c.sb([128, 8]) for _ in range(3)]
    for tt in range(ntiles):
        b = tt % 2; b3 = tt % 3
        if ob_d is None:
            s.dma('sp', mt[b3][:], mix_d[tt * 128:(tt + 1) * 128, :])
        else:
            s.dma('sp', mt[b3][:, 0:256], mix_d[tt * 128:(tt + 1) * 128, 0:256])
            s.dma('sp', mt[b3][:, 640:1024], mix_d[tt * 128:(tt + 1) * 128, 640:1024])
            for bi in range(3):
                s.dma('sp', ob[b3][bi][:], ob_d[bi, tt * 128:(tt + 1) * 128, :])
            s.tt('pool', ob[b3][0][:], ob[b3][0][:], ob[b3][1][:], ALU.add)
            s.tt('pool', ob[b3][0][:], ob[b3][0][:], ob[b3][2][:], ALU.add)
            o3 = ob[b3][0][:].rearrange("p (h v) -> p h v", h=6)
            s.recip(rden[b3][:, 0:6].rearrange("p (h o) -> p h o", o=1), o3[:, :, 64:65])
            s.tt('dve', mt[b3][:, 256:640].rearrange("p (h v) -> p h v", h=6), o3[:, :, 0:64],
                 bcast(rden[b3][:, 0:6].rearrange("p (h o) -> p h o", o=1), [128, 6, 64]), ALU.mult)
        s.dma('sp', xt[b3][:], xin_d[tt * 128:(tt + 1) * 128, :])
        transpose8(s, mt[b3], mT[b], P[0], P[1], ident[:])
        for hb in range(2):
            pt = P[2 + 2 * b + hb]
            for dc in range(8):
                s.mm(pt[:, :], lhsT=mT[b][:, dc, :], rhs=W[:, dc, hb * 512:(hb + 1) * 512], start=(dc == 0), stop=(dc == 7))
            s.tt('dve', ot[b][:, hb * 512:(hb + 1) * 512], pt[:, :], xt[b3][:, hb * 512:(hb + 1) * 512], ALU.add)
        s.dma('sp', x1_d[tt * 128:(tt + 1) * 128, :], ot[b][:])


GELU_C = 1.5957691216057308


def phaseF1(s, nc, st, x1_d, nw_d, wq_d, sk_d, G_d, xnT_all, consts_d, ntiles=16):
    c = Ctx(nc, st, "G")
    BF16 = mybir.dt.bfloat16
    P = c.psum()
    cst = c.sb([128, C_END], name="cst")
    s.dma('sp', cst[:], consts_d[:, :])
    ident = cst[:, C_ID:C_ID + 128]
    Wq = c.sb([128, 8, D], name="Wq")
    for dc in range(8):
        s.dma('sp', Wq[:, dc, :], wq_d[dc * 128:(dc + 1) * 128, :])
    wb = c.sb([128, D], name="wb")
    s.dma('sp', wb[:], nw_d.partition_broadcast(128))
    SKin = c.sb([128, 2, 128]); SKbd = c.sb([128, 256])
    s.memset('dve', SKin[:], 0.0)
    s.dma('sp', SKin[:, 0, 0:64], sk_d[0, :, :])
    s.dma('sp', SKin[:, 1, 64:128], sk_d[1, :, :])
    s.tr(P[0][:, 0:128], SKin[:, 0, :], ident)
    s.tr(P[0][:, 128:256], SKin[:, 1, :], ident)
    s.copy('dve', SKbd[:], P[0][:, 0:256])
    GT = 2 if ntiles % 2 == 0 else 1
    xt1 = c.sb([128, D]); xt = [xt1, xt1]; xn = c.sb([128, D]); ss = c.sb([128, 4])
    xnT4s = [c.sb([128, 8, GT * 128]) for _ in range(2)]; qT4s = [c.sb([128, 8, GT * 128]) for _ in range(2)]; ssbs = [c.sb([128, 8, 256]) for _ in range(2)]
    sv = c.sb([128, 16, 16]); si = c.sb([128, 16, 16], U32); sif = c.sb([128, 16, 16]); s2 = c.sb([128, 128])
    cand = c.sb([128, 8, 256]); c2 = c.sb([128, 256])
    tv = c.sb([128, 8, 16]); pos = c.sb([128, 8, 16], U32); pa = c.sb([128, 128], U32); pb = c.sb([128, 128], U32)
    af = c.sb([128, 128]); bf = c.sb([128, 128])
    ge = c.sb([128, 8, 16]); gs = c.sb([128, 16]); gate = c.sb([128, 8, 16])
    oh = c.sb([128, 128, 16]); ik = c.sb([128, 128]); jk = c.sb([128, 128])
    junk2 = c.sb([128, D])
    TQ = 64
    gikT = c.sb([128, 384])
    Abuf = c.sb([128, TQ, 128], BF16); Bbuf = c.sb([128, TQ, 128], BF16); Gs = [c.sb([128, 16, 128], BF16) for _ in range(2)]
    i128 = cst[:, C_I128:C_I128 + 128]
    svv = sv[:].rearrange("p (h c) a -> p h c a", c=2)
    sifv = sif[:].rearrange("p (h c) a -> p h c a", c=2)
    i16 = cst[:, C_I16:C_I16 + 16]
    gikTs = [gikT, c.sb([128, 384])]
    Ab = [Abuf[:, 0:32, :], Abuf[:, 32:64, :]]; Bb = [Bbuf[:, 0:32, :], Bbuf[:, 32:64, :]]
    TQ = 32

    def qgroup(grp):
        xnT4 = xnT4s[grp % 2]; qT4 = qT4s[grp % 2]
        for k4 in range(GT):
            tt = grp * GT + k4
            b = tt % 2
            s.dma('sp', xt[b][:], x1_d[tt * 128:(tt + 1) * 128, :])
            s.act(junk2[:], xt[b][:], AF.Square, accum_out=ss[:, 0:1])
            s.act(ss[:, 1:2], ss[:, 0:1], AF.Ln, bias=1e-6, scale=1.0 / D)
            s.act(ss[:, 2:3], ss[:, 1:2], AF.Exp, scale=-0.5)
            s.ts('pool', xn[:], xt[b][:], ss[:, 2:3], None, ALU.mult)
            s.tt('pool', xn[:], xn[:], wb[:], ALU.mult)
            yield
            for dc in range(8):
                pt = P[0] if dc < 4 else P[1]
                s.tr(pt[:, (dc % 4) * 128:(dc % 4 + 1) * 128], xn[:, dc * 128:(dc + 1) * 128], ident)
            s.copy('act', xnT4[:, 0:4, k4 * 128:(k4 + 1) * 128], P[0][:, :].rearrange("p (a c) -> p a c", a=4))
            s.copy('act', xnT4[:, 4:8, k4 * 128:(k4 + 1) * 128], P[1][:, :].rearrange("p (a c) -> p a c", a=4))
            s.copy('pool', xnT_all[:, :, tt * 128:(tt + 1) * 128], xnT4[:, :, k4 * 128:(k4 + 1) * 128])
            yield
        for h in range(8):
            pt = P[2 + h % 2]
            for dc in range(8):
                s.mm(pt[:, 0:GT * 128], lhsT=Wq[:, dc, h * 128:(h + 1) * 128], rhs=xnT4[:, dc, :], start=(dc == 0), stop=(dc == 7))
            s.copy('act', qT4[:, h, :], pt[:, 0:GT * 128])
            yield

    def scores(tt):
        k4 = tt % GT
        qT4 = qT4s[(tt // GT) % 2]
        ssb = ssbs[tt % 2]
        for h in range(8):
            s.mm(P[4 + (h // 2) % 2][:, (h % 2) * 256:(h % 2 + 1) * 256], lhsT=qT4[:, h, k4 * 128:(k4 + 1) * 128], rhs=SKbd[:])
            if h % 2 == 1:
                k = h // 2
                s.copy('act', ssb[:, 2 * k:2 * k + 2, :], P[4 + k % 2][:, :].rearrange("p (a c) -> p a c", a=2))

    def route(tt):
        ssb = ssbs[tt % 2]
        if tt == 0:
            scores(0)
        for half in range(2):
            grp_ = range(half * 8, half * 8 + 8)
            srcs = {g: ssb[:, g // 2, (g % 2) * 128:(g % 2 + 1) * 128] for g in grp_}
            for g in grp_:
                s.op('dve', lambda e, g=g, src=srcs[g]: e.max(out=sv[:, g, 0:8], in_=src), reads=[srcs[g]], writes=[sv[:, g, 0:8]])
            for g in grp_:
                s.op('dve', lambda e, g=g, src=srcs[g]: e.max_index(out=si[:, g, 0:8], in_max=sv[:, g, 0:8], in_values=src), reads=[srcs[g], sv[:, g, 0:8]], writes=[si[:, g, 0:8]])
            for g in grp_:
                s.op('dve', lambda e, g=g, src=srcs[g]: e.match_replace(out=src, in_to_replace=sv[:, g, 0:8], in_values=src, imm_value=-1e30), reads=[srcs[g], sv[:, g, 0:8]], writes=[srcs[g]])
            for g in grp_:
                s.op('dve', lambda e, g=g, src=srcs[g]: e.max(out=sv[:, g, 8:16], in_=src), reads=[srcs[g]], writes=[sv[:, g, 8:16]])
            for g in grp_:
                s.op('dve', lambda e, g=g, src=srcs[g]: e.max_index(out=si[:, g, 8:16], in_max=sv[:, g, 8:16], in_values=src), reads=[srcs[g], sv[:, g, 8:16]], writes=[si[:, g, 8:16]])
            if half == 0:
                yield
        s.copy('pool', sif[:], si[:])
        in0 = bcast(svv[:, :, 0:1, :].rearrange("p h o a -> p h a o"), [128, 8, 16, 16])
        in1 = bcast(svv[:, :, 1:2, :], [128, 8, 16, 16])
        s.tt('dve', cand[:].rearrange("p h (a b) -> p h a b", a=16), in0, in1, ALU.add)
        yield
        hs_ = range(8)
        for h in hs_:
            s.op('dve', lambda e, h=h: e.max(out=tv[:, h, 0:8], in_=cand[:, h, :]), reads=[cand[:, h, :]], writes=[tv[:, h, 0:8]])
        for h in hs_:
            s.op('dve', lambda e, h=h: e.max_index(out=pos[:, h, 0:8], in_max=tv[:, h, 0:8], in_values=cand[:, h, :]), reads=[cand[:, h, :], tv[:, h, 0:8]], writes=[pos[:, h, 0:8]])
        for h in hs_:
            s.op('dve', lambda e, h=h: e.match_replace(out=cand[:, h, :], in_to_replace=tv[:, h, 0:8], in_values=cand[:, h, :], imm_value=-1e30), reads=[cand[:, h, :], tv[:, h, 0:8]], writes=[cand[:, h, :]])
        for h in hs_:
            s.op('dve', lambda e, h=h: e.max(out=tv[:, h, 8:16], in_=cand[:, h, :]), reads=[cand[:, h, :]], writes=[tv[:, h, 8:16]])
        for h in hs_:
            s.op('dve', lambda e, h=h: e.max_index(out=pos[:, h, 8:16], in_max=tv[:, h, 8:16], in_values=cand[:, h, :]), reads=[cand[:, h, :], tv[:, h, 8:16]], writes=[pos[:, h, 8:16]])
        s.tt('dve', ge[:], tv[:], bcast(tv[:, :, 0:1], [128, 8, 16]), ALU.subtract)
        s.act(ge[:], ge[:], AF.Exp)
        yield
        s.red('dve', gs[:, 0:8], ge[:], ALU.add)
        s.recip(gs[:, 8:16], gs[:, 0:8])
        s.tt('dve', gate[:], ge[:], bcast(gs[:, 8:16].rearrange("p (h o) -> p h o", o=1), [128, 8, 16]), ALU.mult)
        posv = pos[:].rearrange("p h k -> p (h k)")
        s.ts('dve', pa[:], posv, 4, None, ALU.logical_shift_right)
        s.ts('dve', pb[:], posv, 15, None, ALU.bitwise_and)
        s.copy('dve', af[:], pa[:])
        s.copy('dve', bf[:], pb[:])
        for (sel, cidx, outk) in ((af, 0, ik), (bf, 1, jk)):
            s.tt('dve', oh[:], bcast(sel[:].rearrange("p (k o) -> p k o", o=1), [128, 128, 16]),
                 bcast(i16.rearrange("p (o a) -> p o a", o=1), [128, 128, 16]), ALU.is_equal)
            ohv = oh[:].rearrange("p (h k) a -> p h k a", h=8)
            s.tt('dve', ohv, ohv, bcast(sifv[:, :, cidx:cidx + 1, :], [128, 8, 16, 16]), ALU.mult)
            s.red('dve', outk[:], oh[:], ALU.add)
        s.tr(P[4][:, 0:128], gate[:].rearrange("p h k -> p (h k)"), ident)
        s.tr(P[4][:, 128:256], ik[:], ident)
        s.tr(P[4][:, 256:384], jk[:], ident)
        s.copy('act', gikTs[tt % 2][:], P[4][:, 0:384])
        if tt + 1 < ntiles:
            scores(tt + 1)
        yield

    def ggen(tt):
        gk = gikTs[tt % 2]
        for qt_ in range(4):
            t0 = qt_ * TQ
            A_ = Ab[qt_ % 2]; B_ = Bb[qt_ % 2]
            s.tt('dve', A_, bcast(gk[:, 128 + t0:128 + t0 + TQ].rearrange("p (t o) -> p t o", o=1), [128, TQ, 128]),
                 bcast(i128.rearrange("p (o a) -> p o a", o=1), [128, TQ, 128]), ALU.is_equal)
            s.tt('pool', A_, A_, bcast(gk[:, t0:t0 + TQ].rearrange("p (t o) -> p t o", o=1), [128, TQ, 128]), ALU.mult)
            s.tt('dve', B_, bcast(gk[:, 256 + t0:256 + t0 + TQ].rearrange("p (t o) -> p t o", o=1), [128, TQ, 128]),
                 bcast(i128.rearrange("p (o a) -> p o a", o=1), [128, TQ, 128]), ALU.is_equal)
            for g4 in range(TQ // 4):
                bank = P[6 + g4 % 2]
                for k in range(4):
                    t = g4 * 4 + k
                    s.mm(bank[:, k * 128:(k + 1) * 128], lhsT=A_[:, t, :], rhs=B_[:, t, :])
                gsb = Gs[(qt_ * 2 + g4 // 4) % 2]
                s.copy('act', gsb[:, (g4 % 4) * 4:(g4 % 4) * 4 + 4, :], bank[:, :].rearrange("p (t j) -> p t j", t=4))
                if g4 % 4 == 3:
                    tb = tt * 128 + t0 + (g4 // 4) * 16
                    s.dma('sp', G_d[tb:tb + 16, :].rearrange("t (i j) -> i t j", j=128), gsb[:])
            yield

    def drain(*gens):
        gens = [g for g in gens if g is not None]
        while gens:
            for g in list(gens):
                try:
                    next(g)
                except StopIteration:
                    gens.remove(g)

    QPRE = 4
    drain(qgroup(0))
    prev = None
    for tt in range(ntiles):
        qg = None
        if tt % GT == 0:
            g1 = tt // GT + 1
            qg = qgroup(g1) if g1 < ntiles // GT else None
        rt = route(tt); gg = ggen(prev) if prev is not None else None
        gens = [g for g in (rt, gg) if g is not None]
        for _ in range(QPRE):
            if qg is not None:
                try:
                    next(qg)
                except StopIteration:
                    qg = None
        while gens:
            for g in list(gens):
                try:
                    next(g)
                except StopIteration:
                    gens.remove(g)
                if g is rt and len(gens) and qg is not None:
                    pass
            for _ in range(2):
                if qg is not None:
                    try:
                        next(qg)
                    except StopIteration:
                        qg = None
        if qg is not None:
            drain(qg)
        prev = tt
    drain(ggen(prev))


def phaseF2(s, nc, st, x1_d, downT_d, up_d, x2_d, G_d, xnT_all, consts_d, ntiles=16, nslab=32):
    c = Ctx(nc, st, "H")
    BF16 = mybir.dt.bfloat16
    Pa = [st.enter_context(nc.psum_tensor("%s_pa%d" % (c.pfx, i), [128, 512], F32)) for i in range(2)]
    Pt = [st.enter_context(nc.psum_tensor("%s_pt%d" % (c.pfx, i), [128, 1024], BF16)) for i in range(2)]
    Po = [st.enter_context(nc.psum_tensor("%s_po%d" % (c.pfx, i), [128, 512], F32)) for i in range(4)]
    idf = c.sb([128, 128]); id16 = c.sb([128, 128], BF16)
    s.dma('sp', idf[:], consts_d[:, C_ID:C_ID + 128])
    s.copy('dve', id16[:], idf[:])
    acc = c.sb([128, ntiles, D], name="acc")
    dsl = [c.sb([128, 8, 512], BF16, name="dsl") for _ in range(2)]
    usl = [c.sb([128, 4, D], BF16, name="usl") for _ in range(2)]
    Gt = [c.sb([128, 512], BF16, name="Gt") for _ in range(4)]
    ga = [c.sb([128, 512], name="ga") for _ in range(2)]
    c16 = [c.sb([128, 512], BF16, name="c16") for _ in range(2)]
    cT = [c.sb([128, 4, 128], BF16, name="cT") for _ in range(2)]
    xt = [c.sb([128, D]) for _ in range(2)]

    dst32 = c.sb([128, 8, 512], name="dst32"); ust32 = c.sb([128, 4, D], name="ust32")

    def load_slab(sl):
        e0 = sl * 512
        s.dma('sp', dst32[:], downT_d[:, e0:e0 + 512].rearrange("(dc p) e -> p dc e", p=128))
        s.dma('sp', ust32[:], up_d[e0:e0 + 512, :].rearrange("(k p) d -> p k d", p=128))

    def cast_slab(sl, part):
        if part < 2:
            s.copy('dve', dsl[sl % 2][:, part * 4:(part + 1) * 4, :], dst32[:, part * 4:(part + 1) * 4, :])
        else:
            p2 = part - 2
            s.copy('dve', usl[sl % 2][:, p2 * 2:(p2 + 1) * 2, :], ust32[:, p2 * 2:(p2 + 1) * 2, :])

    load_slab(0)
    for part in range(4):
        cast_slab(0, part)
    it = 0
    for sl in range(nslab):
        if sl + 1 < nslab:
            load_slab(sl + 1)
        e0 = sl * 512
        D_ = dsl[sl % 2]; U_ = usl[sl % 2]

        def stage1(tt, n):
            s.dma('sp', Gt[n % 4][:], G_d[tt * 128:(tt + 1) * 128, e0:e0 + 512])
            for dc in range(8):
                s.mm(Pa[n % 2][:, :], lhsT=xnT_all[:, dc, tt * 128:(tt + 1) * 128], rhs=D_[:, dc, :], start=(dc == 0), stop=(dc == 7))
            s.act(ga[n % 2][:], Pa[n % 2][:, :], AF.Gelu_apprx_tanh)
            s.tt('pool', c16[n % 2][:], ga[n % 2][:], Gt[n % 4][:], ALU.mult)

        def stage2(tt, n):
            for k in range(4):
                s.tr(Pt[n % 2][:, k * 128:(k + 1) * 128], c16[n % 2][:, k * 128:(k + 1) * 128], id16[:])
            s.copy('act', cT[n % 2][:], Pt[n % 2][:, 0:512].rearrange("p (k t) -> p k t", k=4))

        def stage3(tt, n):
            for hb in range(2):
                po = Po[(n % 2) * 2 + hb]
                for k in range(4):
                    s.mm(po[:, :], lhsT=cT[n % 2][:, k, :], rhs=U_[:, k, hb * 512:(hb + 1) * 512], start=(k == 0), stop=(k == 3))
                if sl == 0:
                    s.copy('dve', acc[:, tt, hb * 512:(hb + 1) * 512], po[:, :])
                else:
                    s.tt('dve', acc[:, tt, hb * 512:(hb + 1) * 512], acc[:, tt, hb * 512:(hb + 1) * 512], po[:, :], ALU.add)

        for step in range(ntiles + 2):
            if sl + 1 < nslab and ntiles >= 12 and step in (5, 7, 9, 11):
                cast_slab(sl + 1, (step - 5) // 2)
            elif sl + 1 < nslab and ntiles < 12 and step == ntiles - 1:
                for part in range(4):
                    cast_slab(sl + 1, part)
            if step < ntiles:
                stage1(step, it + step)
            if 1 <= step <= ntiles:
                stage2(step - 1, it + step - 1)
            if 2 <= step:
                stage3(step - 2, it + step - 2)
        it += ntiles
    for tt in range(ntiles):
        b = tt % 2
        s.dma('sp', xt[b][:], x1_d[tt * 128:(tt + 1) * 128, :])
        s.tt('pool', xt[b][:], xt[b][:], acc[:, tt, :], ALU.add)
        s.dma('sp', x2_d[tt * 128:(tt + 1) * 128, :], xt[b][:])


def phaseFD(s, nc, st, x1_d, nw_d, wq_d, sk_d, downT_d, up_d, x2_d, G_d, consts_d, ntiles=16, nslab=32):
    BF16 = mybir.dt.bfloat16
    Ctx.uid += 1
    xnT_all = st.enter_context(nc.sbuf_tensor("xnTall%d" % Ctx.uid, [128, 8, ntiles * 128], BF16))
    with ExitStack() as st1:
        phaseF1(s, nc, st1, x1_d, nw_d, wq_d, sk_d, G_d, xnT_all, consts_d, ntiles)
        s.barrier(); s.flush()
    with ExitStack() as st2:
        phaseF2(s, nc, st2, x1_d, downT_d, up_d, x2_d, G_d, xnT_all, consts_d, ntiles, nslab)
        s.barrier(); s.flush()


NLAYER = 2
_CACHE = {}


def build_program():
    nc = bass.Bass("TRN2", target_bir_lowering=False)

    def din(name, shape):
        return nc.dram_tensor(name, list(shape), F32, kind="ExternalInput").ap()

    def dint(name, shape):
        return nc.dram_tensor(name, list(shape), F32, kind="Internal").ap()

    x_d = din("x", [T, D])
    anw = din("attn_norm_w", [NLAYER, D]); w_in = din("w_in", [NLAYER, D, NC])
    hlb = din("hlb", [NLAYER, 256]); hnw = din("hnw", [NLAYER, 64])
    qw = din("qw", [NLAYER, 64]); kw = din("kw", [NLAYER, 64])
    biasg = din("biasg", [18, 128, 256]); maskadd = din("maskadd", [128, 256])
    cw = din("cw", [NLAYER, 4, 896]); cb = din("cb", [NLAYER, 896]); dtb = din("dtb", [NLAYER, 6])
    alog = din("alog", [NLAYER, 6]); dsk = din("dsk", [NLAYER, 6]); snw = din("snw", [NLAYER, 384])
    w_out = din("w_out", [NLAYER, D, D]); fnw = din("fnw", [NLAYER, D]); wq = din("wq", [NLAYER, D, D])
    sk = din("sk", [NLAYER, 2, 128, 64])
    down = [din("downT%d" % l, [D, 16384]) for l in range(NLAYER)]
    up = [din("up%d" % l, [16384, D]) for l in range(NLAYER)]
    consts = din("consts", [128, C_END])
    out_d = nc.dram_tensor("out", [T, D], F32, kind="ExternalOutput").ap()
    U = dint("U", [T, NC]); QK = dint("QK", [T, 768]); OB = dint("OB", [3, T, 390]); MIX = dint("MIX", [T, D])
    X1 = dint("X1", [T, D]); X2 = dint("X2", [T, D]); G_d = nc.dram_tensor("Gd", [T, 16384], mybir.dt.bfloat16, kind="Internal").ap()

    s = Sched(nc)
    s.ro = {"x", "attn_norm_w", "w_in", "hlb", "hnw", "qw", "kw", "biasg", "maskadd", "cw", "cb", "dtb", "alog", "dsk",
            "snw", "w_out", "fnw", "wq", "sk", "downT0", "downT1", "up0", "up1", "consts"}
    with ExitStack() as semstack:
        s.semstack = semstack
        xin = x_d
        for l in range(NLAYER):
            xout = out_d if l == NLAYER - 1 else X2
            with ExitStack() as st:
                phaseA(s, nc, st, xin, w_in[l], anw[l], U, consts, qk_d=QK, qw_d=qw[l], kw_d=kw[l])
                s.barrier(); s.flush()
            with ExitStack() as st:
                phaseB(s, nc, st, U, MIX, hlb, hnw[l], consts, l)
                s.barrier(); s.flush()
            with ExitStack() as st:
                phaseC(s, nc, st, U, MIX, QK, OB, qw[l], kw[l], biasg, maskadd, consts, do_c0=False, do_c2=False)
                s.barrier(); s.flush()
            with ExitStack() as st:
                phaseD(s, nc, st, U, MIX, cw[l], cb[l], dtb[l], alog[l], dsk[l], snw[l], consts)
                s.barrier(); s.flush()
            with ExitStack() as st:
                phaseE(s, nc, st, MIX, xin, w_out[l], X1, consts, ob_d=OB)
                s.barrier(); s.flush()
            with ExitStack() as st:
                phaseFD(s, nc, st, X1, fnw[l], wq[l], sk[l], down[l], up[l], xout, G_d, consts)
            xin = X2
    return nc


def kernel(x, attn_norm_w, w_in, hgrn_lower_bounds, hgrn_norm_w, q_norm_w, k_norm_w, rel_bias,
           ssm_conv_w, ssm_conv_b, ssm_dt_bias, ssm_a_log, ssm_d, ssm_norm_w, w_out,
           ffn_norm_w, peer_w_query, peer_sub_keys, peer_down, peer_up):
    f = lambda a: np.ascontiguousarray(np.asarray(a, dtype=np.float32))
    if "nc" not in _CACHE:
        _CACHE["nc"] = build_program()
    nc = _CACHE["nc"]
    biasg, maskadd = make_attn_consts(f(rel_bias))
    shared = {
        "attn_norm_w": f(attn_norm_w), "w_in": f(w_in), "hlb": f(hgrn_lower_bounds), "hnw": f(hgrn_norm_w),
        "qw": f(q_norm_w), "kw": f(k_norm_w), "biasg": biasg, "maskadd": maskadd,
        "cw": f(ssm_conv_w), "cb": f(ssm_conv_b), "dtb": f(ssm_dt_bias), "alog": f(ssm_a_log), "dsk": f(ssm_d),
        "snw": f(ssm_norm_w), "w_out": f(w_out), "fnw": f(ffn_norm_w), "wq": f(peer_w_query), "sk": f(peer_sub_keys),
        "consts": make_consts(),
    }
    pd = f(peer_down); pu = f(peer_up)
    for l in range(NLAYER):
        shared["downT%d" % l] = np.ascontiguousarray(pd[l].T)
        shared["up%d" % l] = pu[l]
    xs = f(x)
    n = xs.shape[0]
    in_maps = [dict(shared, x=xs[i]) for i in range(n)]
    res = run_bass_kernel_spmd(nc, in_maps, core_ids=list(range(n)))
    return np.stack([np.asarray(r["out"]) for r in res.results], axis=0).astype(np.float32)
```

```python
from contextlib import ExitStack
import numpy as np
import concourse.bass as bass
import concourse.mybir as mybir
from concourse.bass_utils import run_bass_kernel_spmd

F32 = mybir.dt.float32
I32 = mybir.dt.int32
U32 = mybir.dt.uint32
ALU = mybir.AluOpType
AF = mybir.ActivationFunctionType
AX = mybir.AxisListType

NDMA = 24
EPOCH = 30000


def _prod(xs):
    r = 1
    for v in xs:
        r *= int(v)
    return r


def region(a):
    t = a.tensor
    name = t.name
    ap = a.ap
    off = int(a.offset)
    sp = str(a.space)
    if 'DRAM' in sp.upper():
        ext = sum((int(c) - 1) * abs(int(s)) for s, c in ap)
        return (name, 0, 1, off, off + ext + 1)
    row = _prod(t.shape[1:])
    p0 = off // row
    f0 = off % row
    pstep, pcnt = int(ap[0][0]), int(ap[0][1])
    if pstep == 0:
        pcnt = 1
    ext = sum((int(c) - 1) * abs(int(s)) for s, c in ap[1:])
    if 'PSUM' in sp.upper():
        return (name, (p0 // 32) * 32, ((p0 + pcnt + 31) // 32) * 32, 0, row)
    return (name, p0, p0 + pcnt, f0, f0 + ext + 1)


class Sched:
    def __init__(self, nc):
        self.nc = nc
        self.eng = {'pe': nc.tensor, 'dve': nc.vector, 'act': nc.scalar,
                    'pool': nc.gpsimd, 'sp': nc.sync}
        self.ops = {k: [] for k in self.eng}
        self.cnt = {k: 0 for k in self.eng}
        self.waited = {k: {} for k in self.eng}
        self.acc = {}
        self.dma_n = 0
        self.keys = set()
        self.ro = set()
        self.sems = {}
        self.semstack = None

    def _deps(self, eng, reads, writes):
        deps = {}

        def add(key, val):
            if deps.get(key, 0) < val:
                deps[key] = val

        for is_w, aps in ((False, reads), (True, writes)):
            for a in aps:
                if a.tensor.name in self.ro:
                    continue
                r = region(a)
                lst = self.acc.get(r[0], [])
                psum = 'PSUM' in str(a.space).upper()
                for (key, val, w2, e2, p0, p1, f0, f1) in lst:
                    if not (is_w or w2) and not (psum and e2 != eng):
                        continue
                    if p1 <= r[1] or r[2] <= p0 or f1 <= r[3] or r[4] <= f0:
                        continue
                    if e2 == eng and eng != 'dma':
                        if eng == 'pe':
                            continue
                    add(key, val)
        return deps

    def _record(self, eng, key, val, reads, writes):
        for is_w, aps in ((False, reads), (True, writes)):
            for a in aps:
                if a.tensor.name in self.ro:
                    continue
                r = region(a)
                lst = self.acc.setdefault(r[0], [])
                new = []
                for rec in lst:
                    (k2, v2, w2, e2, p0, p1, f0, f1) = rec
                    contained = (r[1] <= p0 and p1 <= r[2] and r[3] <= f0 and f1 <= r[4])
                    if contained and (is_w or (not w2 and e2 == eng and eng != 'dma')):
                        continue
                    new.append(rec)
                new.append((key, val, is_w, eng, r[1], r[2], r[3], r[4]))
                self.acc[r[0]] = new

    def _key(self, eng):
        idx = self.cnt[eng]
        return (eng, idx // EPOCH), idx % EPOCH + 1

    def op(self, eng, fn, reads=(), writes=()):
        deps = self._deps(eng, reads, writes)
        waits = []
        w = self.waited[eng]
        for key, val in deps.items():
            if w.get(key, 0) < val:
                w[key] = val
                waits.append((key, val))
        key, val = self._key(eng)
        self.cnt[eng] += 1
        self.keys.add(key)
        self.ops[eng].append((fn, waits, key, 1))
        self._record(eng, key, val, reads, writes)

    def dma(self, q, out, in_, **kw):
        n = self.dma_n
        self.dma_n += 1
        slot = n % NDMA
        key = ('dma', slot)
        val = 16 * (n // NDMA + 1)
        deps = self._deps('dma', [in_], [out])
        if n >= NDMA:
            deps[key] = max(deps.get(key, 0), val - 16)
        waits = []
        w = self.waited[q]
        for k, v in deps.items():
            if w.get(k, 0) < v:
                w[k] = v
                waits.append((k, v))
        self.keys.add(key)
        self.ops[q].append((lambda e: e.dma_start(out=out, in_=in_, **kw), waits, key, 16))
        self._record('dma', key, val, [in_], [out])

    def dma_custom(self, q, fn, reads, writes):
        n = self.dma_n
        self.dma_n += 1
        slot = n % NDMA
        key = ('dma', slot)
        val = 16 * (n // NDMA + 1)
        deps = self._deps('dma', reads, writes)
        if n >= NDMA:
            deps[key] = max(deps.get(key, 0), val - 16)
        waits = []
        w = self.waited[q]
        for k, v in deps.items():
            if w.get(k, 0) < v:
                w[k] = v
                waits.append((k, v))
        self.keys.add(key)
        self.ops[q].append((fn, waits, key, 16))
        self._record('dma', key, val, reads, writes)

    def barrier(self):
        targets = {}
        for e in self.eng:
            if self.cnt[e] > 0:
                idx = self.cnt[e] - 1
                targets[(e, idx // EPOCH)] = idx % EPOCH + 1
        for n in range(max(0, self.dma_n - NDMA), self.dma_n):
            targets[('dma', n % NDMA)] = 16 * (n // NDMA + 1)
        for e in self.eng:
            waits = []
            w = self.waited[e]
            for k, v in targets.items():
                if k[0] == e:
                    continue
                if w.get(k, 0) < v:
                    w[k] = v
                    waits.append((k, v))
            if waits:
                key, val = self._key(e)
                self.cnt[e] += 1
                self.keys.add(key)
                self.ops[e].append((None, waits, key, 1))
        self.acc = {}

    def finish(self):
        self.barrier()

    def flush(self):
        nc = self.nc
        for k in sorted(self.keys, key=str):
            if k not in self.sems:
                self.sems[k] = self.semstack.enter_context(nc.semaphore("s_%s_%d" % (k[0], k[1])))
        sems = self.sems
        ops = self.ops
        self.ops = {k: [] for k in self.eng}
        with nc.Block() as block:
            def mk(ename):
                def body(e):
                    for (fn, waits, key, inc) in ops[ename]:
                        for (k, v) in waits:
                            e.wait_ge(sems[k], v)
                        if fn is None:
                            ins = e.nop()
                        else:
                            ins = fn(e)
                        ins.then_inc(sems[key], inc)
                return body

            block.tensor(mk('pe'))
            block.vector(mk('dve'))
            block.scalar(mk('act'))
            block.gpsimd(mk('pool'))
            block.sync(mk('sp'))

    def emit(self):
        from contextlib import ExitStack
        with ExitStack() as st:
            self.semstack = st
            self.flush()


def _aps(*xs):
    return [x for x in xs if x is not None and not isinstance(x, (int, float))]


def _h_tt(self, eng, out, in0, in1, op):
    self.op(eng, lambda e: e.tensor_tensor(out=out, in0=in0, in1=in1, op=op), reads=[in0, in1], writes=[out])


def _h_ts(self, eng, out, in0, s1, s2, op0, op1=None):
    if op1 is None:
        self.op(eng, lambda e: e.tensor_scalar(out=out, in0=in0, scalar1=s1, scalar2=None, op0=op0),
                reads=_aps(in0, s1), writes=[out])
    else:
        self.op(eng, lambda e: e.tensor_scalar(out=out, in0=in0, scalar1=s1, scalar2=s2, op0=op0, op1=op1),
                reads=_aps(in0, s1, s2), writes=[out])


def _h_stt(self, eng, out, in0, scalar, in1, op0, op1):
    self.op(eng, lambda e: e.scalar_tensor_tensor(out=out, in0=in0, scalar=scalar, in1=in1, op0=op0, op1=op1),
            reads=_aps(in0, scalar, in1), writes=[out])


def _h_act(self, out, in_, func, bias=None, scale=None, accum_out=None):
    kw = {}
    if bias is not None:
        kw['bias'] = bias
    if scale is not None:
        kw['scale'] = scale
    if accum_out is not None:
        kw['accum_out'] = accum_out
    self.op('act', lambda e: e.activation(out=out, in_=in_, func=func, **kw),
            reads=_aps(in_, bias, scale), writes=_aps(out, accum_out))


def _h_copy(self, eng, out, in_):
    if eng == 'act':
        self.op('act', lambda e: e.copy(out=out, in_=in_), reads=[in_], writes=[out])
    else:
        self.op(eng, lambda e: e.tensor_copy(out=out, in_=in_), reads=[in_], writes=[out])


def _h_mm(self, out, lhsT, rhs, start=True, stop=True):
    self.op('pe', lambda e: e.matmul(out, lhsT=lhsT, rhs=rhs, start=start, stop=stop),
            reads=[lhsT, rhs], writes=[out])


def _h_tr(self, out, in_, ident):
    self.op('pe', lambda e: e.transpose(out=out, in_=in_, identity=ident), reads=[in_, ident], writes=[out])


def _h_red(self, eng, out, in_, op, axis=AX.X):
    self.op(eng, lambda e: e.tensor_reduce(out=out, in_=in_, axis=axis, op=op), reads=[in_], writes=[out])


def _h_memset(self, eng, ap, val):
    self.op(eng, lambda e: e.memset(ap, val), reads=[], writes=[ap])


def _h_recip(self, out, in_):
    self.op('dve', lambda e: e.reciprocal(out=out, in_=in_), reads=[in_], writes=[out])


Sched.tt = _h_tt
Sched.ts = _h_ts
Sched.stt = _h_stt
Sched.act = _h_act
Sched.copy = _h_copy
Sched.mm = _h_mm
Sched.tr = _h_tr
Sched.red = _h_red
Sched.memset = _h_memset
Sched.recip = _h_recip


def bcast(ap, shape):
    assert len(ap.shape) == len(shape), (ap.shape, shape)
    new = []
    for (st, cnt), tgt in zip(ap.ap, shape):
        if int(cnt) == int(tgt):
            new.append([int(st), int(cnt)])
        else:
            assert int(cnt) == 1
            new.append([0, int(tgt)])
    return bass.AP(ap.tensor, ap.offset, new)


T = 2048; D = 1024; NC = 3462
O_HQ, O_HF, O_HI, O_HG, O_AQ, O_AK, O_AV, O_SZ, O_XBC, O_DT = 0, 256, 512, 768, 1024, 1408, 1792, 2176, 2560, 3456

C_ID, C_TRI, C_TRI2, C_CO0, C_CO1, C_CI, C_I16, C_I128, C_END = 0, 128, 256, 384, 512, 640, 642, 658, 786


def make_consts():
    c = np.zeros((128, C_END), np.float32)
    s = np.arange(128)[:, None]; t = np.arange(128)[None, :]
    same = (s // 64) == (t // 64)
    c[:, C_ID:C_ID + 128] = np.eye(128)
    c[:, C_TRI:C_TRI + 128] = (same & (s <= t))
    c[:, C_TRI2:C_TRI2 + 128] = (same & (s > t))
    c[:, C_CO0:C_CO0 + 128] = (s < 64) * np.ones((1, 128))
    c[:, C_CO1:C_CO1 + 128] = (s >= 64) * np.ones((1, 128))
    c[:, C_CI] = (np.arange(128) < 64)
    c[:, C_CI + 1] = (np.arange(128) >= 64)
    c[:, C_I16:C_I16 + 16] = np.arange(16)[None, :]
    c[:, C_I128:C_I128 + 128] = np.arange(128)[None, :]
    return c


class Ctx:
    uid = 0

    def __init__(self, nc, st, pfx):
        Ctx.uid += 1
        self.nc, self.st, self.pfx = nc, st, "%s%d" % (pfx, Ctx.uid)
        self.n = 0

    def sb(self, shape, dt=F32, name=None):
        self.n += 1
        return self.st.enter_context(self.nc.sbuf_tensor("%s_%s%d" % (self.pfx, name or "t", self.n), list(shape), dt))

    def psum(self):
        return [self.st.enter_context(self.nc.psum_tensor("%s_ps%d" % (self.pfx, i), [128, 512], F32)) for i in range(8)]


def rstd_from_ss(s, out, tmp, ss, scale, eps=1e-6):
    s.act(tmp, ss, AF.Sqrt, bias=eps, scale=scale)
    s.recip(out, tmp)


def phaseB(s, nc, st, u_d, mix_d, hlb_d, hnw_d, consts_d, layer, ntiles=16):
    c = Ctx(nc, st, "B")
    P = c.psum()
    cst = c.sb([128, C_END], name="cst")
    s.dma('sp', cst[:], consts_d[:, :])
    ident = cst[:, C_ID:C_ID + 128]; tri = cst[:, C_TRI:C_TRI + 128]; tri2 = cst[:, C_TRI2:C_TRI2 + 128]
    chind = cst[:, C_CI:C_CI + 2]
    hnw = c.sb([128, 64], name="hnw")
    s.dma('sp', hnw[:], hnw_d.partition_broadcast(128))
    if layer > 0:
        h0 = c.sb([128, 256]); h1 = c.sb([128, 256]); lb = c.sb([128, 256]); oml = c.sb([128, 256])
        s.dma('sp', h0[:], hlb_d[0, :].partition_broadcast(128))
        s.dma('sp', h1[:], hlb_d[1, :].partition_broadcast(128))
        s.tt('dve', h1[:], h1[:], h0[:], ALU.subtract)
        s.act(lb[:], h1[:], AF.Sigmoid)
        s.ts('dve', oml[:], lb[:], -1.0, 1.0, ALU.mult, ALU.add)
    SA = c.sb([64, 4, 64], name="SA"); SB = c.sb([64, 4, 64], name="SB")
    s.memset('dve', SA[:], 0.0)
    NB = 2
    qTa = [c.sb([64, 4, 128]) for _ in range(NB)]; qTb = [c.sb([64, 4, 128]) for _ in range(NB)]
    kh0 = [c.sb([128, 256]) for _ in range(NB)]; kh1 = [c.sb([128, 256]) for _ in range(NB)]
    for t_ in qTa + qTb + kh0 + kh1:
        s.memset('dve', t_[:], 0.0)
    tin = [c.sb([128, 1024], name="tin") for _ in range(4)]
    fg = c.sb([128, 256]); logf = c.sb([128, 256]); key = c.sb([128, 256])
    eb = c.sb([128, 256]); enb = c.sb([128, 256]); er = c.sb([128, 256])
    qt = c.sb([128, 256]); kt = c.sb([128, 256])
    dec = [c.sb([64, 8]) for _ in range(NB)]; qT = [c.sb([64, 4, 128]) for _ in range(NB)]; kT = [c.sb([64, 4, 128]) for _ in range(NB)]
    sg = [c.sb([128, 256]) for _ in range(NB)]
    ATs = [c.sb([128, 128]) for _ in range(2)]
    osb = c.sb([128, 256]); sq = c.sb([128, 256]); ss = c.sb([128, 12])
    outt = [c.sb([128, 256]) for _ in range(2)]

    def pre(tt):
        p = tt % NB
        ti = tin[tt % 4]
        if tt == 0:
            s.dma('sp', tin[0][:], u_d[0:128, 0:1024])
        if tt + 1 < ntiles:
            s.dma('sp', tin[(tt + 1) % 4][:], u_d[(tt + 1) * 128:(tt + 2) * 128, 0:1024])
        q = ti[:, 0:256]; f = ti[:, 256:512]; g = ti[:, 768:1024]
        s.act(fg[:], f, AF.Sigmoid)
        if layer > 0:
            s.tt('dve', fg[:], fg[:], oml[:], ALU.mult)
            s.tt('dve', fg[:], fg[:], lb[:], ALU.add)
        s.act(sg[p][:], g, AF.Silu)
        s.act(logf[:], fg[:], AF.Ln)
        yield
        s.ts('dve', key[:], fg[:], -1.0, 1.0, ALU.mult, ALU.add)
        s.mm(P[0][:, 0:256], lhsT=tri, rhs=logf[:])
        s.mm(P[0][:, 256:512], lhsT=tri2, rhs=logf[:])
        for h in range(4):
            s.mm(P[1][0:64, h * 2:h * 2 + 2], lhsT=logf[:, h * 64:(h + 1) * 64], rhs=chind)
        yield
        s.act(eb[:], P[0][:, 0:256], AF.Exp)
        s.act(enb[:], P[0][:, 0:256], AF.Exp, scale=-1.0)
        s.act(er[:], P[0][:, 256:512], AF.Exp)
        s.act(dec[p][:], P[1][0:64, 0:8], AF.Exp)
        yield
        s.stt('dve', qt[:], q, 0.125, eb[:], ALU.mult, ALU.mult)
        s.tt('pool', kt[:], key[:], enb[:], ALU.mult)
        s.tt('pool', kh0[p][0:64, :], key[0:64, :], er[0:64, :], ALU.mult)
        s.tt('pool', kh1[p][64:128, :], key[64:128, :], er[64:128, :], ALU.mult)
        s.tt('pool', sg[p][:].rearrange("p (h v) -> p h v", h=4), sg[p][:].rearrange("p (h v) -> p h v", h=4),
             bcast(hnw[:].rearrange("p (o v) -> p o v", o=1), [128, 4, 64]), ALU.mult)
        yield
        for h in range(4):
            s.tr(P[2][0:64, h * 128:(h + 1) * 128], qt[:, h * 64:(h + 1) * 64], ident)
            s.tr(P[3][0:64, h * 128:(h + 1) * 128], kt[:, h * 64:(h + 1) * 64], ident)
        yield
        p2v = P[2][0:64, :].rearrange("p (h t) -> p h t", h=4)
        p3v = P[3][0:64, :].rearrange("p (h t) -> p h t", h=4)
        s.copy('dve', qT[p][:], p2v)
        s.copy('dve', qTa[p][:, :, 0:64], p2v[:, :, 0:64])
        s.copy('dve', qTb[p][:, :, 64:128], p2v[:, :, 64:128])
        s.copy('act', kT[p][:], p3v)
        yield

    def heads(tt):
        p = tt % NB
        ti = tin[tt % 4]
        for h in range(4):
            hs = slice(h * 64, (h + 1) * 64)
            vs = slice(512 + h * 64, 512 + (h + 1) * 64)
            s.mm(P[6][0:64, (2 * h) * 64:(2 * h + 1) * 64], lhsT=kh0[p][:, hs], rhs=ti[:, vs])
            s.mm(P[6][0:64, (2 * h + 1) * 64:(2 * h + 2) * 64], lhsT=kh1[p][:, hs], rhs=ti[:, vs])
        yield
        for h in range(4):
            hs = slice(h * 64, (h + 1) * 64)
            vs = slice(512 + h * 64, 512 + (h + 1) * 64)
            AT = P[4 + h % 2][:, 0:128]
            s.mm(AT, lhsT=kT[p][:, h, :], rhs=qT[p][:, h, :])
            A = ATs[h % 2]
            s.tt('dve', A[:], AT, tri, ALU.mult)
            o = P[7][:, hs]
            s.mm(o, lhsT=A[:], rhs=ti[:, vs], start=True, stop=False)
            s.mm(o, lhsT=qTa[p][:, h, :], rhs=SA[:, h, :], start=False, stop=False)
            u0 = P[6][0:64, (2 * h) * 64:(2 * h + 1) * 64]
            u1 = P[6][0:64, (2 * h + 1) * 64:(2 * h + 2) * 64]
            s.stt('dve', SB[:, h, :], SA[:, h, :], dec[p][:, 2 * h:2 * h + 1], u0, ALU.mult, ALU.add)
            s.mm(o, lhsT=qTb[p][:, h, :], rhs=SB[:, h, :], start=False, stop=True)
            s.stt('dve', SA[:, h, :], SB[:, h, :], dec[p][:, 2 * h + 1:2 * h + 2], u1, ALU.mult, ALU.add)
            yield
        s.copy('act', osb[:], P[7][:, 0:256])
        s.tt('pool', sq[:], osb[:], osb[:], ALU.mult)
        s.red('dve', ss[:, 0:4], sq[:].rearrange("p (h v) -> p h v", h=4), ALU.add)
        rstd_from_ss(s, ss[:, 8:12], ss[:, 4:8], ss[:, 0:4], 1.0 / 64)
        ot = outt[tt % 2]
        s.tt('dve', ot[:].rearrange("p (h v) -> p h v", h=4), osb[:].rearrange("p (h v) -> p h v", h=4),
             bcast(ss[:, 8:12].rearrange("p (h o) -> p h o", o=1), [128, 4, 64]), ALU.mult)
        s.tt('dve', ot[:], ot[:], sg[p][:], ALU.mult)
        s.dma('sp', mix_d[tt * 128:(tt + 1) * 128, 0:256], ot[:])
        yield

    def drain(*gens):
        gens = [g for g in gens if g is not None]
        while gens:
            for g in list(gens):
                try:
                    next(g)
                except StopIteration:
                    gens.remove(g)

    drain(pre(0))
    for tt in range(ntiles):
        drain(heads(tt), pre(tt + 1) if tt + 1 < ntiles else None)


def phaseD(s, nc, st, u_d, mix_d, cw_d, cb_d, dtb_d, alog_d, dsk_d, nw_d, consts_d, ntiles=16):
    c = Ctx(nc, st, "D")
    BF16 = mybir.dt.bfloat16
    P = c.psum()
    cst = c.sb([128, C_END], name="cst")
    s.dma('sp', cst[:], consts_d[:, :])
    ident = cst[:, C_ID:C_ID + 128]; tri = cst[:, C_TRI:C_TRI + 128]; tri2 = cst[:, C_TRI2:C_TRI2 + 128]
    co0 = cst[:, C_CO0:C_CO0 + 128]; co1 = cst[:, C_CO1:C_CO1 + 128]
    cw = c.sb([128, 4, 896], name="cw"); cb = c.sb([128, 896], name="cb")
    for j in range(4):
        s.dma('sp', cw[:, j, :], cw_d[j, :].partition_broadcast(128))
    s.dma('sp', cb[:], cb_d.partition_broadcast(128))
    sm = c.sb([128, 32], name="sm")
    s.dma('sp', sm[:, 0:6], dtb_d.partition_broadcast(128))
    s.dma('sp', sm[:, 6:12], alog_d.partition_broadcast(128))
    s.dma('sp', sm[:, 18:24], dsk_d.partition_broadcast(128))
    nwb = c.sb([128, 384], name="nwb")
    s.dma('sp', nwb[:], nw_d.partition_broadcast(128))
    s.act(sm[:, 24:30], sm[:, 6:12], AF.Exp)
    s.ts('dve', sm[:, 12:18], sm[:, 24:30], -1.0, None, ALU.mult)
    dtbias = sm[:, 0:6]; Ab = sm[:, 12:18]
    R = [c.sb([128, 6, 64], name="R") for _ in range(6)]
    R16 = [c.sb([128, 6, 64], BF16, name="R16") for _ in range(6)]
    s.memset('dve', R[0][:], 0.0)
    s.memset('dve', R16[0][:], 0.0)
    Bz0 = c.sb([128, 256], BF16); Bz1 = c.sb([128, 256], BF16)
    s.memset('dve', Bz0[:], 0.0); s.memset('dve', Bz1[:], 0.0)
    NH = 3
    CTs0 = [c.sb([128, 128], BF16) for _ in range(NH)]; CTs1 = [c.sb([128, 128], BF16) for _ in range(NH)]
    for b_ in CTs0 + CTs1:
        s.memset('pool', b_[:], 0.0)
    M = [c.sb([128, 1286], name="M") for _ in range(3)]
    X = [[c.sb([128, 896], name="X") for _ in range(3)] for _ in range(3)]
    mj = [c.sb([128, 896]) for _ in range(4)]
    xcs = [c.sb([128, 896], name="xc") for _ in range(2)]; xc16s = [c.sb([128, 384], BF16) for _ in range(2)]
    v6s = [c.sb([128, 64], name="v6") for _ in range(2)]
    abcs = [c.sb([128, 6, 128]) for _ in range(2)]; BCTs = [c.sb([128, 4, 128], BF16) for _ in range(2)]; CBTs = [c.sb([128, 2, 128]) for _ in range(2)]
    xw = c.sb([128, 384], BF16)
    dsb = [c.sb([128, 128]) for _ in range(NH)]; ecm = [c.sb([128, 128]) for _ in range(NH)]
    esb = [c.sb([128, 128]) for _ in range(NH)]; msb = [c.sb([128, 128]) for _ in range(NH)]; sc = [c.sb([128, 128], BF16) for _ in range(NH)]
    ysb = c.sb([128, 384]); szbs = [c.sb([128, 384]) for _ in range(2)]; sq = c.sb([128, 384]); ss = c.sb([128, 8]); tmpS = c.sb([128, 384])
    outt = [c.sb([128, 384]) for _ in range(2)]
    DmB = [P[3], P[4]]
    def loads(tt):
        Mt = M[tt % 3]
        s.dma('sp', Mt[:], u_d[tt * 128:(tt + 1) * 128, O_SZ:NC])
        for j in range(3):
            Xj = X[tt % 3][j]
            sh = 3 - j
            if tt == 0:
                s.memset('pool', Xj[:], 0.0)
                s.dma('sp', Xj[sh:128, :], u_d[0:128 - sh, O_XBC:O_XBC + 896])
            else:
                s.dma('sp', Xj[:], u_d[tt * 128 - sh:(tt + 1) * 128 - sh, O_XBC:O_XBC + 896])

    def pre(tt):
        b = tt % 2
        Mt = M[tt % 3]; v6 = v6s[b]
        xc = xcs[b]; xc16 = xc16s[b]; abc = abcs[b]; BCT = BCTs[b]; CBT = CBTs[b]; szb = szbs[b]
        Sin, Smid, Sout = R[(2 * tt) % 6], R[(2 * tt + 1) % 6], R[(2 * tt + 2) % 6]
        Sin16, Smid16, Sout16 = R16[(2 * tt) % 6], R16[(2 * tt + 1) % 6], R16[(2 * tt + 2) % 6]
        if tt == 0:
            loads(0)
        if tt + 1 < ntiles:
            loads(tt + 1)
        z = Mt[:, 0:384]; x3 = Mt[:, 384:1280]; dtr = Mt[:, 1280:1286]
        xs = xc[:, 0:384]
        dtt = v6[:, 0:6]; a_ = v6[:, 8:14]; cum = v6[:, 16:22]; wgt = v6[:, 24:30]; dAe = v6[:, 32:44]; tmp6 = v6[:, 48:54]
        s.tt('dve', tmp6, dtr, dtbias, ALU.add)
        s.act(tmp6, tmp6, AF.Exp)
        s.act(dtt, tmp6, AF.Ln, bias=1.0)
        s.tt('dve', a_, dtt, Ab, ALU.mult)
        s.copy('pool', abc[:], bcast(v6[:, 8:14].rearrange("p (h o) -> p h o", o=1), [128, 6, 128]))
        yield
        for j in range(3):
            s.tt('pool', mj[j][:], X[tt % 3][j][:], cw[:, j, :], ALU.mult)
        s.tt('pool', mj[3][:], x3, cw[:, 3, :], ALU.mult)
        yield
        s.mm(P[0][:, 0:6], lhsT=tri, rhs=a_)
        s.mm(P[0][:, 8:14], lhsT=tri2, rhs=a_)
        s.mm(P[0][:, 16:22], lhsT=co0, rhs=a_)
        s.mm(P[0][:, 22:28], lhsT=co1, rhs=a_)
        s.copy('act', cum, P[0][:, 0:6])
        s.act(wgt, P[0][:, 8:14], AF.Exp)
        s.act(dAe, P[0][:, 16:28], AF.Exp)
        s.tt('dve', wgt, wgt, dtt, ALU.mult)
        s.tt('dve', mj[0][:], mj[0][:], mj[1][:], ALU.add)
        s.tt('dve', mj[2][:], mj[2][:], mj[3][:], ALU.add)
        s.tt('dve', mj[0][:], mj[0][:], mj[2][:], ALU.add)
        s.tt('dve', mj[0][:], mj[0][:], cb[:], ALU.add)
        yield
        s.act(xc[:], mj[0][:], AF.Silu)
        s.copy('act', xc16[:], xs)
        s.act(szb[:], z, AF.Silu)
        s.tt('dve', xw[:].rearrange("p (h v) -> p h v", h=6), xs.rearrange("p (h v) -> p h v", h=6),
             bcast(v6[:, 24:30].rearrange("p (h o) -> p h o", o=1), [128, 6, 64]), ALU.mult)
        yield
        for k in range(4):
            s.tr(P[1][:, k * 128:(k + 1) * 128], xc[:, 384 + k * 128:384 + (k + 1) * 128], ident)
        s.copy('act', BCT[:], P[1][:, :].rearrange("p (k t) -> p k t", k=4))
        for g in range(2):
            s.mm(P[2][:, g * 128:(g + 1) * 128], lhsT=BCT[:, g, :], rhs=BCT[:, 2 + g, :])
        s.copy('dve', CBT[:], P[2][:, 0:256].rearrange("p (g t) -> p g t", g=2))
        s.copy('act', Bz0[0:64, :], xc[0:64, 384:640])
        s.copy('act', Bz1[64:128, :], xc[64:128, 384:640])
        yield
        for h in range(6):
            g = h // 3; hs = slice(h * 64, (h + 1) * 64)
            s.mm(P[7][:, hs], lhsT=Bz0[:, g * 128:(g + 1) * 128], rhs=xw[:, hs])
        for h in range(6):
            g = h // 3; hs = slice(h * 64, (h + 1) * 64)
            s.mm(P[1][:, hs], lhsT=Bz1[:, g * 128:(g + 1) * 128], rhs=xw[:, hs])
        s.tt('dve', tmpS[:].rearrange("p (h v) -> p h v", h=6), Sin[:], bcast(v6[:, 32:38].rearrange("p (h o) -> p h o", o=1), [128, 6, 64]), ALU.mult)
        s.tt('dve', Smid[:], tmpS[:].rearrange("p (h v) -> p h v", h=6), P[7][:, 0:384].rearrange("p (h v) -> p h v", h=6), ALU.add)
        s.copy('act', Smid16[:], Smid[:])
        s.tt('dve', tmpS[:].rearrange("p (h v) -> p h v", h=6), Smid[:], bcast(v6[:, 38:44].rearrange("p (h o) -> p h o", o=1), [128, 6, 64]), ALU.mult)
        s.tt('dve', Sout[:], tmpS[:].rearrange("p (h v) -> p h v", h=6), P[1][:, 0:384].rearrange("p (h v) -> p h v", h=6), ALU.add)
        s.copy('act', Sout16[:], Sout[:])

        yield

    def heads(tt):
        b = tt % 2
        Mt = M[b]; v6 = v6s[b]
        xc = xcs[b]; xc16 = xc16s[b]; abc = abcs[b]; BCT = BCTs[b]; CBT = CBTs[b]; szb = szbs[b]
        Sin16, Smid16 = R16[(2 * tt) % 6], R16[(2 * tt + 1) % 6]
        def st1(h):
            g = h // 3; hq = h % NH
            Dm = DmB[h % 2][:, 0:128]
            s.mm(Dm, lhsT=abc[:, h, :], rhs=tri)
            s.ts('dve', dsb[hq][:], Dm, v6[:, 16 + h:17 + h], 0.0, ALU.subtract, ALU.min)
            s.act(ecm[hq][:], Dm, AF.Exp)
            s.act(esb[hq][:], dsb[hq][:], AF.Exp)
            s.stt('dve', msb[hq][:], esb[hq][:], v6[:, h:h + 1], tri, ALU.mult, ALU.mult)
            s.tt('pool', sc[hq][:], msb[hq][:], CBT[:, g, :], ALU.mult)
            s.tt('pool', CTs0[hq][:, 0:64], BCT[:, 2 + g, 0:64], ecm[hq][:, 0:64], ALU.mult)
            s.tt('pool', CTs1[hq][:, 64:128], BCT[:, 2 + g, 64:128], ecm[hq][:, 64:128], ALU.mult)

        def st2(h):
            hq = h % NH; hs = slice(h * 64, (h + 1) * 64)
            y = P[5 + h % 2][:, 0:64]
            s.mm(y, lhsT=sc[hq][:], rhs=xc16[:, hs], start=True, stop=False)
            s.mm(y, lhsT=CTs0[hq][:], rhs=Sin16[:, h, :], start=False, stop=False)
            s.mm(y, lhsT=CTs1[hq][:], rhs=Smid16[:, h, :], start=False, stop=True)
            s.stt('dve', ysb[:, hs], xc[:, hs], sm[:, 18 + h:19 + h], y, ALU.mult, ALU.add)

        SK = 2
        for step in range(6 + SK):
            if step < 6:
                st1(step)
            if step >= SK:
                st2(step - SK)
            yield
        s.tt('pool', ysb[:], ysb[:], szb[:], ALU.mult)
        s.tt('pool', sq[:], ysb[:], ysb[:], ALU.mult)
        s.red('dve', ss[:, 0:2], sq[:].rearrange("p (g v) -> p g v", g=2), ALU.add)
        rstd_from_ss(s, ss[:, 4:6], ss[:, 2:4], ss[:, 0:2], 1.0 / 192)
        ot = outt[b]
        s.tt('dve', ot[:].rearrange("p (g v) -> p g v", g=2), ysb[:].rearrange("p (g v) -> p g v", g=2),
             bcast(ss[:, 4:6].rearrange("p (g o) -> p g o", o=1), [128, 2, 192]), ALU.mult)
        s.tt('pool', ot[:], ot[:], nwb[:], ALU.mult)
        s.dma('sp', mix_d[tt * 128:(tt + 1) * 128, 640:1024], ot[:])


    def drain(*gens):
        gens = [g for g in gens if g is not None]
        while gens:
            for g in list(gens):
                try:
                    next(g)
                except StopIteration:
                    gens.remove(g)

    drain(pre(0))
    for tt in range(ntiles):
        drain(heads(tt), pre(tt + 1) if tt + 1 < ntiles else None)


DILS = (1, 4, 16)


def t5_bucket_np(dist):
    max_exact = 16
    d = np.maximum(dist, 1).astype(np.float32)
    large = max_exact + (np.log(d / max_exact) / np.log(2048 / max_exact) * (32 - max_exact)).astype(np.int32)
    large = np.minimum(large, 31)
    return np.where(dist < max_exact, dist, large).astype(np.int32)


def make_attn_consts(rel_bias):
    j = np.arange(128)[:, None]; i = np.arange(128)[None, :]
    delta = np.concatenate([i - j + 128, i - j], axis=1)
    valid = (delta >= 0) & (delta <= 128)
    maskadd = np.where(valid, 0.0, -30000.0).astype(np.float32)
    biasg = np.zeros((18, 128, 256), np.float32)
    for bi, dil in enumerate(DILS):
        bucket = t5_bucket_np(np.maximum(delta, 0) * dil)
        for h in range(6):
            biasg[bi * 6 + h] = rel_bias[bucket, h]
    return biasg, maskadd


def phaseC(s, nc, st, u_d, mix_d, qk_d, ob_d, qw_d, kw_d, biasg_d, maskadd_d, consts_d, ntiles=16, branches=(0, 1, 2), do_c0=True, do_c2=True):
    c = Ctx(nc, st, "C")
    P = c.psum()
    cst = c.sb([128, C_END], name="cst")
    s.dma('sp', cst[:], consts_d[:, :])
    ident = cst[:, C_ID:C_ID + 128]
    qkw = c.sb([128, 128], name="qkw")
    s.dma('sp', qkw[:, 0:64], qw_d.partition_broadcast(128))
    s.dma('sp', qkw[:, 64:128], kw_d.partition_broadcast(128))
    biasT = c.sb([128, 18, 256], name="biasT")
    madd = c.sb([128, 256], name="madd")
    s.dma('sp', biasT[:], biasg_d.rearrange("n j c -> j n c"))
    s.dma('sp', madd[:], maskadd_d[:, :])
    s.tt('dve', biasT[:], biasT[:], bcast(madd[:].rearrange("p (o c) -> p o c", o=1), [128, 18, 256]), ALU.add)
    qk = [c.sb([128, 768]) for _ in range(2)]; sq = c.sb([128, 768]); ss = c.sb([128, 36]); qkn = [c.sb([128, 768]) for _ in range(2)]
    for tt in (range(ntiles) if do_c0 else []):
        b = tt % 2
        s.dma('sp', qk[b][:], u_d[tt * 128:(tt + 1) * 128, O_AQ:O_AQ + 768])
        s.tt('pool', sq[:], qk[b][:], qk[b][:], ALU.mult)
        s.red('dve', ss[:, 0:12], sq[:].rearrange("p (h v) -> p h v", h=12), ALU.add)
        rstd_from_ss(s, ss[:, 24:36], ss[:, 12:24], ss[:, 0:12], 1.0 / 64)
        s.tt('dve', qkn[b][:].rearrange("p (h v) -> p h v", h=12), qk[b][:].rearrange("p (h v) -> p h v", h=12),
             bcast(ss[:, 24:36].rearrange("p (h o) -> p h o", o=1), [128, 12, 64]), ALU.mult)
        s.stt('dve', qkn[b][:, 0:384].rearrange("p (h v) -> p h v", h=6), qkn[b][:, 0:384].rearrange("p (h v) -> p h v", h=6), 0.125,
              bcast(qkw[:, 0:64].rearrange("p (o v) -> p o v", o=1), [128, 6, 64]), ALU.mult, ALU.mult)
        s.tt('pool', qkn[b][:, 384:768].rearrange("p (h v) -> p h v", h=6), qkn[b][:, 384:768].rearrange("p (h v) -> p h v", h=6),
             bcast(qkw[:, 64:128].rearrange("p (o v) -> p o v", o=1), [128, 6, 64]), ALU.mult)
        s.dma('sp', qk_d[tt * 128:(tt + 1) * 128, :], qkn[b][:])
    QKb = [c.sb([128, 768]) for _ in range(3)]
    V32 = [c.sb([128, 384]) for _ in range(3)]
    BF16 = mybir.dt.bfloat16
    V1 = [c.sb([128, 6, 65], BF16) for _ in range(3)]
    for v in V1:
        s.memset('dve', v[:], 1.0)
    QT = [c.sb([128, 3, 128], BF16) for _ in range(2)]
    KTe = [c.sb([128, 3, 128], BF16) for _ in range(3)]; KTo = [c.sb([128, 3, 128], BF16) for _ in range(3)]
    for k_ in KTe + KTo:
        s.memset('pool', k_[:], 0.0)
    Esb = [c.sb([128, 256]) for _ in range(4)]; Psb = [c.sb([128, 256], BF16) for _ in range(4)]
    Osb = [c.sb([128, 390]) for _ in range(2)]
    blocks = []
    for bi in branches:
        dil = DILS[bi]
        L = (ntiles * 128) // dil
        nblk = L // 128
        for r in range(dil):
            for blk in range(nblk):
                blocks.append((bi, dil, r, blk))
    views = {}
    for bi in branches:
        dil = DILS[bi]
        views[bi] = (qk_d[0:ntiles * 128, :].rearrange("(b m d) c -> d b m c", m=128, d=dil),
                     u_d[0:ntiles * 128, O_AV:O_AV + 384].rearrange("(b m d) c -> d b m c", m=128, d=dil),
                     ob_d[bi, 0:ntiles * 128, :].rearrange("(b m d) c -> d b m c", m=128, d=dil))

    def loads(n):
        bi, dil, r, blk = blocks[n]
        qv, vv, ov = views[bi]
        s.dma('sp', QKb[n % 3][:], qv[r, blk])
        s.dma('sp', V32[n % 3][:], vv[r, blk])

    def pre(n):
        bi, dil, r, blk = blocks[n]
        pb = n % 2
        qv, vv, ov = views[bi]
        p3 = n % 3
        if n == 0:
            loads(0)
        if n + 1 < len(blocks):
            loads(n + 1)
        s.copy('pool', V1[p3][:, :, 0:64], V32[p3][:].rearrange("m (h v) -> m h v", h=6))
        yield
        for pr in range(3):
            s.tr(P[0][:, pr * 128:(pr + 1) * 128], QKb[p3][:, pr * 128:(pr + 1) * 128], ident)
            s.tr(P[1][:, pr * 128:(pr + 1) * 128], QKb[p3][:, 384 + pr * 128:384 + (pr + 1) * 128], ident)
            yield
        s.copy('act', QT[pb][:], P[0][:, 0:384].rearrange("p (a t) -> p a t", a=3))
        p1v = P[1][:, 0:384].rearrange("p (a t) -> p a t", a=3)
        s.copy('dve', KTe[p3][0:64, :, :], p1v[0:64, :, :])
        s.copy('dve', KTo[p3][64:128, :, :], p1v[64:128, :, :])
        yield

    def heads(n):
        bi, dil, r, blk = blocks[n]
        pb = n % 2
        qv, vv, ov = views[bi]
        PO = P[6 + pb]
        c0 = 0 if blk > 0 else 128
        p3 = n % 3; q3 = (n - 1) % 3

        def S_(h):
            pr = h // 2; hq = h % 4
            KT = KTe if h % 2 == 0 else KTo
            ST = P[2 + hq]
            s.mm(ST[:, 128:256], lhsT=KT[p3][:, pr, :], rhs=QT[pb][:, pr, :])
            if blk > 0:
                s.mm(ST[:, 0:128], lhsT=KT[q3][:, pr, :], rhs=QT[pb][:, pr, :])
            s.tt('dve', Esb[hq][:, c0:256], ST[:, c0:256], biasT[:, bi * 6 + h, c0:256], ALU.add)
            s.act(Psb[hq][:, c0:256], Esb[hq][:, c0:256], AF.Exp)

        def PV_(h):
            hq = h % 4
            O = PO[:, h * 65:(h + 1) * 65]
            if blk > 0:
                s.mm(O, lhsT=Psb[hq][:, 0:128], rhs=V1[q3][:, h, :], start=True, stop=False)
            s.mm(O, lhsT=Psb[hq][:, 128:256], rhs=V1[p3][:, h, :], start=(blk == 0), stop=True)

        SK = 3
        for step in range(6 + SK):
            if step >= SK:
                PV_(step - SK)
            if step < 6:
                S_(step)
            yield
        s.copy('act', Osb[pb][:], PO[:, 0:390])
        s.dma('sp', ov[r, blk], Osb[pb][:])
        yield

    def drain(*gens):
        gens = [g for g in gens if g is not None]
        while gens:
            for g in list(gens):
                try:
                    next(g)
                except StopIteration:
                    gens.remove(g)

    drain(pre(0))
    for n in range(len(blocks)):
        drain(heads(n), pre(n + 1) if n + 1 < len(blocks) else None)
    ob = [[c.sb([128, 390]) for _ in range(3)] for _ in range(2)]
    rden = c.sb([128, 8]); outt = [c.sb([128, 384]) for _ in range(2)]
    for tt in (range(ntiles) if do_c2 else []):
        b = tt % 2
        for bi in range(3):
            s.dma('sp', ob[b][bi][:], ob_d[bi, tt * 128:(tt + 1) * 128, :])
        s.tt('pool', ob[b][0][:], ob[b][0][:], ob[b][1][:], ALU.add)
        s.tt('pool', ob[b][0][:], ob[b][0][:], ob[b][2][:], ALU.add)
        o3 = ob[b][0][:].rearrange("p (h v) -> p h v", h=6)
        s.recip(rden[:, 0:6].rearrange("p (h o) -> p h o", o=1), o3[:, :, 64:65])
        s.tt('dve', outt[b][:].rearrange("p (h v) -> p h v", h=6), o3[:, :, 0:64],
             bcast(rden[:, 0:6].rearrange("p (h o) -> p h o", o=1), [128, 6, 64]), ALU.mult)
        s.dma('sp', mix_d[tt * 128:(tt + 1) * 128, 256:640], outt[b][:])


def norm_transpose(s, xt, wb, junk, ss, xn, xnT, Pa, Pb, ident):
    s.act(junk[:], xt[:], AF.Square, accum_out=ss[:, 0:1])
    s.act(ss[:, 1:2], ss[:, 0:1], AF.Sqrt, bias=1e-6, scale=1.0 / D)
    s.recip(ss[:, 2:3], ss[:, 1:2])
    s.stt('dve', xn[:], xt[:], ss[:, 2:3], wb[:], ALU.mult, ALU.mult)
    transpose8(s, xn, xnT, Pa, Pb, ident)


def transpose8(s, xn, xnT, Pa, Pb, ident):
    for dc in range(8):
        pt = Pa if dc < 4 else Pb
        s.tr(pt[:, (dc % 4) * 128:(dc % 4 + 1) * 128], xn[:, dc * 128:(dc + 1) * 128], ident)
    s.copy('act', xnT[:, 0:4, :], Pa[:, :].rearrange("p (a c) -> p a c", a=4))
    s.copy('dve', xnT[:, 4:8, :], Pb[:, :].rearrange("p (a c) -> p a c", a=4))


def phaseA(s, nc, st, x_d, w_d, nw_d, u_d, consts_d, ntiles=16, qk_d=None, qw_d=None, kw_d=None):
    c = Ctx(nc, st, "A")
    BF16 = mybir.dt.bfloat16
    P = c.psum()
    W = c.sb([128, 8, NC], BF16, name="W")
    ident = c.sb([128, 128], name="ident")
    wb = c.sb([128, D], name="wb")
    s.dma('sp', ident[:], consts_d[:, C_ID:C_ID + 128])
    s.dma('sp', wb[:], nw_d.partition_broadcast(128))
    wst = [c.sb([128, NC], name="wst") for _ in range(2)]
    for dc in range(8):
        s.dma('sp', wst[dc % 2][:], w_d[dc * 128:(dc + 1) * 128, :])
        s.copy('pool' if dc % 2 == 0 else 'dve', W[:, dc, :], wst[dc % 2][:])
    xt = [c.sb([128, D]) for _ in range(3)]; xns = [c.sb([128, D]) for _ in range(2)]; junk = c.sb([128, D])
    if qk_d is not None:
        qkw = c.sb([128, 128], name="qkw")
        s.dma('sp', qkw[:, 0:64], qw_d.partition_broadcast(128))
        s.dma('sp', qkw[:, 64:128], kw_d.partition_broadcast(128))
        sq = c.sb([128, 768]); ssq = [c.sb([128, 36]) for _ in range(2)]; qkn = [c.sb([128, 768]) for _ in range(2)]
    sss = [c.sb([128, 4]) for _ in range(2)]; xnT = [c.sb([128, 8, 128], BF16) for _ in range(2)]; usb = [c.sb([128, NC]) for _ in range(2)]
    def pre(tt):
        b = tt % 2
        xn = xns[b]; ss = sss[b]
        if tt == 0:
            s.dma('sp', xt[0][:], x_d[0:128, :])
        if tt + 1 < ntiles:
            s.dma('sp', xt[(tt + 1) % 3][:], x_d[(tt + 1) * 128:(tt + 2) * 128, :])
        s.act(junk[:], xt[tt % 3][:], AF.Square, accum_out=ss[:, 0:1])
        s.act(ss[:, 1:2], ss[:, 0:1], AF.Sqrt, bias=1e-6, scale=1.0 / D)
        s.recip(ss[:, 2:3], ss[:, 1:2])
        s.stt('dve', xn[:], xt[tt % 3][:], ss[:, 2:3], wb[:], ALU.mult, ALU.mult)
        yield
        for dc in range(8):
            pt = P[0] if dc < 4 else P[1]
            s.tr(pt[:, (dc % 4) * 128:(dc % 4 + 1) * 128], xn[:, dc * 128:(dc + 1) * 128], ident[:])
        yield
        s.copy('act', xnT[b][:, 0:4, :], P[0][:, :].rearrange("p (a c) -> p a c", a=4))
        s.copy('dve', xnT[b][:, 4:8, :], P[1][:, :].rearrange("p (a c) -> p a c", a=4))
        yield

    def main(tt):
        b = tt % 2
        nblk = (NC + 511) // 512
        for cb in range(nblk):
            c0 = cb * 512; c1 = min(NC, c0 + 512)
            pt = P[2 + cb % 6]
            for dc in range(8):
                s.mm(pt[:, 0:c1 - c0], lhsT=xnT[b][:, dc, :], rhs=W[:, dc, c0:c1], start=(dc == 0), stop=(dc == 7))
            s.copy('act' if cb % 2 == 0 else 'dve', usb[b][:, c0:c1], pt[:, 0:c1 - c0])
            yield
        s.dma('sp', u_d[tt * 128:(tt + 1) * 128, :], usb[b][:])
        if qk_d is not None:
            qk = usb[b][:, O_AQ:O_AQ + 768]; ss_ = ssq[b]
            s.tt('pool', sq[:], qk, qk, ALU.mult)
            s.red('dve', ss_[:, 0:12], sq[:].rearrange("p (h v) -> p h v", h=12), ALU.add)
            rstd_from_ss(s, ss_[:, 24:36], ss_[:, 12:24], ss_[:, 0:12], 1.0 / 64)
            s.tt('dve', qkn[b][:].rearrange("p (h v) -> p h v", h=12), qk.rearrange("p (h v) -> p h v", h=12),
                 bcast(ss_[:, 24:36].rearrange("p (h o) -> p h o", o=1), [128, 12, 64]), ALU.mult)
            s.stt('dve', qkn[b][:, 0:384].rearrange("p (h v) -> p h v", h=6), qkn[b][:, 0:384].rearrange("p (h v) -> p h v", h=6), 0.125,
                  bcast(qkw[:, 0:64].rearrange("p (o v) -> p o v", o=1), [128, 6, 64]), ALU.mult, ALU.mult)
            s.tt('pool', qkn[b][:, 384:768].rearrange("p (h v) -> p h v", h=6), qkn[b][:, 384:768].rearrange("p (h v) -> p h v", h=6),
                 bcast(qkw[:, 64:128].rearrange("p (o v) -> p o v", o=1), [128, 6, 64]), ALU.mult)
            s.dma('sp', qk_d[tt * 128:(tt + 1) * 128, :], qkn[b][:])
        yield

    def drain(*gens):
        gens = [g for g in gens if g is not None]
        while gens:
            for g in list(gens):
                try:
                    next(g)
                except StopIteration:
                    gens.remove(g)

    drain(pre(0))
    for tt in range(ntiles):
        drain(main(tt), pre(tt + 1) if tt + 1 < ntiles else None)


def phaseE(s, nc, st, mix_d, xin_d, w_d, x1_d, consts_d, ntiles=16, ob_d=None):
    c = Ctx(nc, st, "E")
    BF16 = mybir.dt.bfloat16
    P = c.psum()
    W = c.sb([128, 8, D], BF16, name="W")
    ident = c.sb([128, 128], name="ident")
    s.dma('sp', ident[:], consts_d[:, C_ID:C_ID + 128])
    wst = [c.sb([128, D], name="wst") for _ in range(2)]
    for dc in range(8):
        s.dma('sp', wst[dc % 2][:], w_d[dc * 128:(dc + 1) * 128, :])
        s.copy('pool' if dc % 2 == 0 else 'dve', W[:, dc, :], wst[dc % 2][:])
    mt = [c.sb([128, D]) for _ in range(3)]; xt = [c.sb([128, D]) for _ in range(3)]
    mT = [c.sb([128, 8, 128], BF16) for _ in range(2)]; ot = [c.sb([128, D]) for _ in range(2)]
    if ob_d is not None:
        ob = [[c.sb([128, 390]) for _ in range(3)] for _ in range(3)]; rden = [c.sb([128, 8]) for _ in range(3)]
    for tt in range(ntiles):
        b = tt % 2; b3 = tt % 3
        if ob_d is None:
            s.dma('sp', mt[b3][:], mix_d[tt * 128:(tt + 1) * 128, :])
        else:
            s.dma('sp', mt[b3][:, 0:256], mix_d[tt * 128:(tt + 1) * 128, 0:256])
            s.dma('sp', mt[b3][:, 640:1024], mix_d[tt * 128:(tt + 1) * 128, 640:1024])
            for bi in range(3):
                s.dma('sp', ob[b3][bi][:], ob_d[bi, tt * 128:(tt + 1) * 128, :])
            s.tt('pool', ob[b3][0][:], ob[b3][0][:], ob[b3][1][:], ALU.add)
            s.tt('pool', ob[b3][0][:], ob[b3][0][:], ob[b3][2][:], ALU.add)
            o3 = ob[b3][0][:].rearrange("p (h v) -> p h v", h=6)
            s.recip(rden[b3][:, 0:6].rearrange("p (h o) -> p h o", o=1), o3[:, :, 64:65])
            s.tt('dve', mt[b3][:, 256:640].rearrange("p (h v) -> p h v", h=6), o3[:, :, 0:64],
                 bcast(rden[b3][:, 0:6].rearrange("p (h o) -> p h o", o=1), [128, 6, 64]), ALU.mult)
        s.dma('sp', xt[b3][:], xin_d[tt * 128:(tt + 1) * 128, :])
        transpose8(s, mt[b3], mT[b], P[0], P[1], ident[:])
        for hb in range(2):
            pt = P[2 + 2 * b + hb]
            for dc in range(8):
                s.mm(pt[:, :], lhsT=mT[b][:, dc, :], rhs=W[:, dc, hb * 512:(hb + 1) * 512], start=(dc == 0), stop=(dc == 7))
            s.tt('dve', ot[b][:, hb * 512:(hb + 1) * 512], pt[:, :], xt[b3][:, hb * 512:(hb + 1) * 512], ALU.add)
        s.dma('sp', x1_d[tt * 128:(tt + 1) * 128, :], ot[b][:])


GELU_C = 1.5957691216057308


def phaseF1(s, nc, st, x1_d, nw_d, wq_d, sk_d, G_d, xnT_all, consts_d, ntiles=16):
    c = Ctx(nc, st, "G")
    BF16 = mybir.dt.bfloat16
    P = c.psum()
    cst = c.sb([128, C_END], name="cst")
    s.dma('sp', cst[:], consts_d[:, :])
    ident = cst[:, C_ID:C_ID + 128]
    Wq = c.sb([128, 8, D], name="Wq")
    for dc in range(8):
        s.dma('sp', Wq[:, dc, :], wq_d[dc * 128:(dc + 1) * 128, :])
    wb = c.sb([128, D], name="wb")
    s.dma('sp', wb[:], nw_d.partition_broadcast(128))
    SKin = c.sb([128, 2, 128]); SKbd = c.sb([128, 256])
    s.memset('dve', SKin[:], 0.0)
    s.dma('sp', SKin[:, 0, 0:64], sk_d[0, :, :])
    s.dma('sp', SKin[:, 1, 64:128], sk_d[1, :, :])
    s.tr(P[0][:, 0:128], SKin[:, 0, :], ident)
    s.tr(P[0][:, 128:256], SKin[:, 1, :], ident)
    s.copy('dve', SKbd[:], P[0][:, 0:256])
    GT = 2 if ntiles % 2 == 0 else 1
    xt1 = c.sb([128, D]); xt = [xt1, xt1]; xn = c.sb([128, D]); ss = c.sb([128, 4])
    xnT4s = [c.sb([128, 8, GT * 128]) for _ in range(2)]; qT4s = [c.sb([128, 8, GT * 128]) for _ in range(2)]; ssbs = [c.sb([128, 8, 256]) for _ in range(2)]
    sv = c.sb([128, 16, 16]); si = c.sb([128, 16, 16], U32); sif = c.sb([128, 16, 16]); s2 = c.sb([128, 128])
    cand = c.sb([128, 8, 256]); c2 = c.sb([128, 256])
    tv = c.sb([128, 8, 16]); pos = c.sb([128, 8, 16], U32); pa = c.sb([128, 128], U32); pb = c.sb([128, 128], U32)
    af = c.sb([128, 128]); bf = c.sb([128, 128])
    ge = c.sb([128, 8, 16]); gs = c.sb([128, 16]); gate = c.sb([128, 8, 16])
    oh = c.sb([128, 128, 16]); ik = c.sb([128, 128]); jk = c.sb([128, 128])
    junk2 = c.sb([128, D])
    TQ = 64
    gikT = c.sb([128, 384])
    Abuf = c.sb([128, TQ, 128], BF16); Bbuf = c.sb([128, TQ, 128], BF16); Gs = [c.sb([128, 16, 128], BF16) for _ in range(2)]
    i128 = cst[:, C_I128:C_I128 + 128]
    svv = sv[:].rearrange("p (h c) a -> p h c a", c=2)
    sifv = sif[:].rearrange("p (h c) a -> p h c a", c=2)
    i16 = cst[:, C_I16:C_I16 + 16]
    gikTs = [gikT, c.sb([128, 384])]
    Ab = [Abuf[:, 0:32, :], Abuf[:, 32:64, :]]; Bb = [Bbuf[:, 0:32, :], Bbuf[:, 32:64, :]]
    TQ = 32

    def qgroup(grp):
        xnT4 = xnT4s[grp % 2]; qT4 = qT4s[grp % 2]
        for k4 in range(GT):
            tt = grp * GT + k4
            b = tt % 2
            s.dma('sp', xt[b][:], x1_d[tt * 128:(tt + 1) * 128, :])
            s.act(junk2[:], xt[b][:], AF.Square, accum_out=ss[:, 0:1])
            s.act(ss[:, 1:2], ss[:, 0:1], AF.Ln, bias=1e-6, scale=1.0 / D)
            s.act(ss[:, 2:3], ss[:, 1:2], AF.Exp, scale=-0.5)
            s.ts('pool', xn[:], xt[b][:], ss[:, 2:3], None, ALU.mult)
            s.tt('pool', xn[:], xn[:], wb[:], ALU.mult)
            yield
            for dc in range(8):
                pt = P[0] if dc < 4 else P[1]
                s.tr(pt[:, (dc % 4) * 128:(dc % 4 + 1) * 128], xn[:, dc * 128:(dc + 1) * 128], ident)
            s.copy('act', xnT4[:, 0:4, k4 * 128:(k4 + 1) * 128], P[0][:, :].rearrange("p (a c) -> p a c", a=4))
            s.copy('act', xnT4[:, 4:8, k4 * 128:(k4 + 1) * 128], P[1][:, :].rearrange("p (a c) -> p a c", a=4))
            s.copy('pool', xnT_all[:, :, tt * 128:(tt + 1) * 128], xnT4[:, :, k4 * 128:(k4 + 1) * 128])
            yield
        for h in range(8):
            pt = P[2 + h % 2]
            for dc in range(8):
                s.mm(pt[:, 0:GT * 128], lhsT=Wq[:, dc, h * 128:(h + 1) * 128], rhs=xnT4[:, dc, :], start=(dc == 0), stop=(dc == 7))
            s.copy('act', qT4[:, h, :], pt[:, 0:GT * 128])
            yield

    def scores(tt):
        k4 = tt % GT
        qT4 = qT4s[(tt // GT) % 2]
        ssb = ssbs[tt % 2]
        for h in range(8):
            s.mm(P[4 + (h // 2) % 2][:, (h % 2) * 256:(h % 2 + 1) * 256], lhsT=qT4[:, h, k4 * 128:(k4 + 1) * 128], rhs=SKbd[:])
            if h % 2 == 1:
                k = h // 2
                s.copy('act', ssb[:, 2 * k:2 * k + 2, :], P[4 + k % 2][:, :].rearrange("p (a c) -> p a c", a=2))

    def route(tt):
        ssb = ssbs[tt % 2]
        if tt == 0:
            scores(0)
        for half in range(2):
            grp_ = range(half * 8, half * 8 + 8)
            srcs = {g: ssb[:, g // 2, (g % 2) * 128:(g % 2 + 1) * 128] for g in grp_}
            for g in grp_:
                s.op('dve', lambda e, g=g, src=srcs[g]: e.max(out=sv[:, g, 0:8], in_=src), reads=[srcs[g]], writes=[sv[:, g, 0:8]])
            for g in grp_:
                s.op('dve', lambda e, g=g, src=srcs[g]: e.max_index(out=si[:, g, 0:8], in_max=sv[:, g, 0:8], in_values=src), reads=[srcs[g], sv[:, g, 0:8]], writes=[si[:, g, 0:8]])
            for g in grp_:
                s.op('dve', lambda e, g=g, src=srcs[g]: e.match_replace(out=src, in_to_replace=sv[:, g, 0:8], in_values=src, imm_value=-1e30), reads=[srcs[g], sv[:, g, 0:8]], writes=[srcs[g]])
            for g in grp_:
                s.op('dve', lambda e, g=g, src=srcs[g]: e.max(out=sv[:, g, 8:16], in_=src), reads=[srcs[g]], writes=[sv[:, g, 8:16]])
            for g in grp_:
                s.op('dve', lambda e, g=g, src=srcs[g]: e.max_index(out=si[:, g, 8:16], in_max=sv[:, g, 8:16], in_values=src), reads=[srcs[g], sv[:, g, 8:16]], writes=[si[:, g, 8:16]])
            if half == 0:
                yield
        s.copy('pool', sif[:], si[:])
        in0 = bcast(svv[:, :, 0:1, :].rearrange("p h o a -> p h a o"), [128, 8, 16, 16])
        in1 = bcast(svv[:, :, 1:2, :], [128, 8, 16, 16])
        s.tt('dve', cand[:].rearrange("p h (a b) -> p h a b", a=16), in0, in1, ALU.add)
        yield
        hs_ = range(8)
        for h in hs_:
            s.op('dve', lambda e, h=h: e.max(out=tv[:, h, 0:8], in_=cand[:, h, :]), reads=[cand[:, h, :]], writes=[tv[:, h, 0:8]])
        for h in hs_:
            s.op('dve', lambda e, h=h: e.max_index(out=pos[:, h, 0:8], in_max=tv[:, h, 0:8], in_values=cand[:, h, :]), reads=[cand[:, h, :], tv[:, h, 0:8]], writes=[pos[:, h, 0:8]])
        for h in hs_:
            s.op('dve', lambda e, h=h: e.match_replace(out=cand[:, h, :], in_to_replace=tv[:, h, 0:8], in_values=cand[:, h, :], imm_value=-1e30), reads=[cand[:, h, :], tv[:, h, 0:8]], writes=[cand[:, h, :]])
        for h in hs_:
            s.op('dve', lambda e, h=h: e.max(out=tv[:, h, 8:16], in_=cand[:, h, :]), reads=[cand[:, h, :]], writes=[tv[:, h, 8:16]])
        for h in hs_:
            s.op('dve', lambda e, h=h: e.max_index(out=pos[:, h, 8:16], in_max=tv[:, h, 8:16], in_values=cand[:, h, :]), reads=[cand[:, h, :], tv[:, h, 8:16]], writes=[pos[:, h, 8:16]])
        s.tt('dve', ge[:], tv[:], bcast(tv[:, :, 0:1], [128, 8, 16]), ALU.subtract)
        s.act(ge[:], ge[:], AF.Exp)
        yield
        posv = pos[:].rearrange("p h k -> p (h k)")
        s.ts('dve', pa[:], posv, 4, None, ALU.logical_shift_right)
        s.ts('dve', pb[:], posv, 15, None, ALU.bitwise_and)
        s.copy('dve', af[:], pa[:])
        s.copy('dve', bf[:], pb[:])
        for (sel, cidx, outk) in ((af, 0, ik), (bf, 1, jk)):
            s.tt('dve', oh[:], bcast(sel[:].rearrange("p (k o) -> p k o", o=1), [128, 128, 16]),
                 bcast(i16.rearrange("p (o a) -> p o a", o=1), [128, 128, 16]), ALU.is_equal)
            ohv = oh[:].rearrange("p (h k) a -> p h k a", h=8)
            s.tt('dve', ohv, ohv, bcast(sifv[:, :, cidx:cidx + 1, :], [128, 8, 16, 16]), ALU.mult)
            s.red('dve', outk[:], oh[:], ALU.add)
        s.tr(P[4][:, 128:256], ik[:], ident)
        s.tr(P[4][:, 256:384], jk[:], ident)
        s.copy('act', gikTs[tt % 2][:, 128:384], P[4][:, 128:384])
        s.red('dve', gs[:, 0:8], ge[:], ALU.add)
        s.recip(gs[:, 8:16], gs[:, 0:8])
        s.tt('dve', gate[:], ge[:], bcast(gs[:, 8:16].rearrange("p (h o) -> p h o", o=1), [128, 8, 16]), ALU.mult)
        s.tr(P[5][:, 0:128], gate[:].rearrange("p h k -> p (h k)"), ident)
        s.copy('act', gikTs[tt % 2][:, 0:128], P[5][:, 0:128])
        if tt + 1 < ntiles:
            scores(tt + 1)
        yield

    def ggen(tt):
        gk = gikTs[tt % 2]
        for qt_ in range(4):
            t0 = qt_ * TQ
            A_ = Ab[qt_ % 2]; B_ = Bb[qt_ % 2]
            s.tt('dve', A_, bcast(gk[:, 128 + t0:128 + t0 + TQ].rearrange("p (t o) -> p t o", o=1), [128, TQ, 128]),
                 bcast(i128.rearrange("p (o a) -> p o a", o=1), [128, TQ, 128]), ALU.is_equal)
            s.tt('pool', A_, A_, bcast(gk[:, t0:t0 + TQ].rearrange("p (t o) -> p t o", o=1), [128, TQ, 128]), ALU.mult)
            s.tt('dve', B_, bcast(gk[:, 256 + t0:256 + t0 + TQ].rearrange("p (t o) -> p t o", o=1), [128, TQ, 128]),
                 bcast(i128.rearrange("p (o a) -> p o a", o=1), [128, TQ, 128]), ALU.is_equal)
            for g4 in range(TQ // 4):
                bank = P[6 + g4 % 2]
                for k in range(4):
                    t = g4 * 4 + k
                    s.mm(bank[:, k * 128:(k + 1) * 128], lhsT=A_[:, t, :], rhs=B_[:, t, :])
                gsb = Gs[(qt_ * 2 + g4 // 4) % 2]
                s.copy('act', gsb[:, (g4 % 4) * 4:(g4 % 4) * 4 + 4, :], bank[:, :].rearrange("p (t j) -> p t j", t=4))
                if g4 % 4 == 3:
                    tb = tt * 128 + t0 + (g4 // 4) * 16
                    s.dma('sp', G_d[tb:tb + 16, :].rearrange("t (i j) -> i t j", j=128), gsb[:])
            yield

    def drain(*gens):
        gens = [g for g in gens if g is not None]
        while gens:
            for g in list(gens):
                try:
                    next(g)
                except StopIteration:
                    gens.remove(g)

    QPRE = 4
    drain(qgroup(0))
    prev = None
    for tt in range(ntiles):
        qg = None
        if tt % GT == 0:
            g1 = tt // GT + 1
            qg = qgroup(g1) if g1 < ntiles // GT else None
        rt = route(tt); gg = ggen(prev) if prev is not None else None
        gens = [g for g in (rt, gg) if g is not None]
        for _ in range(QPRE):
            if qg is not None:
                try:
                    next(qg)
                except StopIteration:
                    qg = None
        while gens:
            for g in list(gens):
                try:
                    next(g)
                except StopIteration:
                    gens.remove(g)
                if g is rt and len(gens) and qg is not None:
                    pass
            for _ in range(2):
                if qg is not None:
                    try:
                        next(qg)
                    except StopIteration:
                        qg = None
        if qg is not None:
            drain(qg)
        prev = tt
    drain(ggen(prev))


def phaseF2(s, nc, st, x1_d, downT_d, up_d, x2_d, G_d, xnT_all, consts_d, ntiles=16, nslab=32):
    c = Ctx(nc, st, "H")
    BF16 = mybir.dt.bfloat16
    Pa = [st.enter_context(nc.psum_tensor("%s_pa%d" % (c.pfx, i), [128, 512], F32)) for i in range(2)]
    Pt = [st.enter_context(nc.psum_tensor("%s_pt%d" % (c.pfx, i), [128, 1024], BF16)) for i in range(2)]
    Po = [st.enter_context(nc.psum_tensor("%s_po%d" % (c.pfx, i), [128, 512], F32)) for i in range(4)]
    idf = c.sb([128, 128]); id16 = c.sb([128, 128], BF16)
    s.dma('sp', idf[:], consts_d[:, C_ID:C_ID + 128])
    s.copy('dve', id16[:], idf[:])
    acc = c.sb([128, ntiles, D], name="acc")
    dsl = [c.sb([128, 8, 512], BF16, name="dsl") for _ in range(2)]
    usl = [c.sb([128, 4, D], BF16, name="usl") for _ in range(2)]
    Gt = [c.sb([128, 512], BF16, name="Gt") for _ in range(6)]
    ga = [c.sb([128, 512], name="ga") for _ in range(3)]
    c16 = [c.sb([128, 512], BF16, name="c16") for _ in range(3)]
    cT = [c.sb([128, 4, 128], BF16, name="cT") for _ in range(3)]
    xt = [c.sb([128, D]) for _ in range(2)]

    dst32 = c.sb([128, 8, 512], name="dst32"); ust32 = c.sb([128, 4, D], name="ust32")

    def load_slab(sl):
        e0 = sl * 512
        s.dma('sp', dst32[:], downT_d[:, e0:e0 + 512].rearrange("(dc p) e -> p dc e", p=128))
        s.dma('sp', ust32[:], up_d[e0:e0 + 512, :].rearrange("(k p) d -> p k d", p=128))

    def cast_slab(sl, part):
        if part < 2:
            s.copy('dve', dsl[sl % 2][:, part * 4:(part + 1) * 4, :], dst32[:, part * 4:(part + 1) * 4, :])
        else:
            p2 = part - 2
            s.copy('dve', usl[sl % 2][:, p2 * 2:(p2 + 1) * 2, :], ust32[:, p2 * 2:(p2 + 1) * 2, :])

    load_slab(0)
    for part in range(4):
        cast_slab(0, part)
    it = 0
    for sl in range(nslab):
        if sl + 1 < nslab:
            load_slab(sl + 1)
        e0 = sl * 512
        D_ = dsl[sl % 2]; U_ = usl[sl % 2]

        def stage1(tt, n):
            s.dma('sp', Gt[n % 6][:], G_d[tt * 128:(tt + 1) * 128, e0:e0 + 512])
            for dc in range(8):
                s.mm(Pa[n % 2][:, :], lhsT=xnT_all[:, dc, tt * 128:(tt + 1) * 128], rhs=D_[:, dc, :], start=(dc == 0), stop=(dc == 7))
            s.act(ga[n % 3][:], Pa[n % 2][:, :], AF.Gelu_apprx_tanh)
            s.tt('pool', c16[n % 3][:], ga[n % 3][:], Gt[n % 6][:], ALU.mult)

        def stage2(tt, n):
            for k in range(4):
                s.tr(Pt[n % 2][:, k * 128:(k + 1) * 128], c16[n % 3][:, k * 128:(k + 1) * 128], id16[:])
            s.copy('act', cT[n % 3][:], Pt[n % 2][:, 0:512].rearrange("p (k t) -> p k t", k=4))

        def stage3(tt, n):
            for hb in range(2):
                po = Po[(n % 2) * 2 + hb]
                for k in range(4):
                    s.mm(po[:, :], lhsT=cT[n % 3][:, k, :], rhs=U_[:, k, hb * 512:(hb + 1) * 512], start=(k == 0), stop=(k == 3))
                if sl == 0:
                    s.copy('dve', acc[:, tt, hb * 512:(hb + 1) * 512], po[:, :])
                else:
                    s.tt('dve', acc[:, tt, hb * 512:(hb + 1) * 512], acc[:, tt, hb * 512:(hb + 1) * 512], po[:, :], ALU.add)

        for step in range(ntiles + 2):
            if sl + 1 < nslab and ntiles >= 12 and step in (5, 7, 9, 11):
                cast_slab(sl + 1, (step - 5) // 2)
            elif sl + 1 < nslab and ntiles < 12 and step == ntiles - 1:
                for part in range(4):
                    cast_slab(sl + 1, part)
            if step < ntiles:
                stage1(step, it + step)
            if 1 <= step <= ntiles:
                stage2(step - 1, it + step - 1)
            if 2 <= step:
                stage3(step - 2, it + step - 2)
        it += ntiles
    for tt in range(ntiles):
        b = tt % 2
        s.dma('sp', xt[b][:], x1_d[tt * 128:(tt + 1) * 128, :])
        s.tt('pool', xt[b][:], xt[b][:], acc[:, tt, :], ALU.add)
        s.dma('sp', x2_d[tt * 128:(tt + 1) * 128, :], xt[b][:])


def phaseFD(s, nc, st, x1_d, nw_d, wq_d, sk_d, downT_d, up_d, x2_d, G_d, consts_d, ntiles=16, nslab=32):
    BF16 = mybir.dt.bfloat16
    Ctx.uid += 1
    xnT_all = st.enter_context(nc.sbuf_tensor("xnTall%d" % Ctx.uid, [128, 8, ntiles * 128], BF16))
    with ExitStack() as st1:
        phaseF1(s, nc, st1, x1_d, nw_d, wq_d, sk_d, G_d, xnT_all, consts_d, ntiles)
        s.barrier(); s.flush()
    with ExitStack() as st2:
        phaseF2(s, nc, st2, x1_d, downT_d, up_d, x2_d, G_d, xnT_all, consts_d, ntiles, nslab)
        s.barrier(); s.flush()


NLAYER = 2
_CACHE = {}


def build_program():
    nc = bass.Bass("TRN2", target_bir_lowering=False)

    def din(name, shape):
        return nc.dram_tensor(name, list(shape), F32, kind="ExternalInput").ap()

    def dint(name, shape):
        return nc.dram_tensor(name, list(shape), F32, kind="Internal").ap()

    x_d = din("x", [T, D])
    anw = din("attn_norm_w", [NLAYER, D]); w_in = din("w_in", [NLAYER, D, NC])
    hlb = din("hlb", [NLAYER, 256]); hnw = din("hnw", [NLAYER, 64])
    qw = din("qw", [NLAYER, 64]); kw = din("kw", [NLAYER, 64])
    biasg = din("biasg", [18, 128, 256]); maskadd = din("maskadd", [128, 256])
    cw = din("cw", [NLAYER, 4, 896]); cb = din("cb", [NLAYER, 896]); dtb = din("dtb", [NLAYER, 6])
    alog = din("alog", [NLAYER, 6]); dsk = din("dsk", [NLAYER, 6]); snw = din("snw", [NLAYER, 384])
    w_out = din("w_out", [NLAYER, D, D]); fnw = din("fnw", [NLAYER, D]); wq = din("wq", [NLAYER, D, D])
    sk = din("sk", [NLAYER, 2, 128, 64])
    down = [din("downT%d" % l, [D, 16384]) for l in range(NLAYER)]
    up = [din("up%d" % l, [16384, D]) for l in range(NLAYER)]
    consts = din("consts", [128, C_END])
    out_d = nc.dram_tensor("out", [T, D], F32, kind="ExternalOutput").ap()
    U = dint("U", [T, NC]); QK = dint("QK", [T, 768]); OB = dint("OB", [3, T, 390]); MIX = dint("MIX", [T, D])
    X1 = dint("X1", [T, D]); X2 = dint("X2", [T, D]); G_d = nc.dram_tensor("Gd", [T, 16384], mybir.dt.bfloat16, kind="Internal").ap()

    s = Sched(nc)
    s.ro = {"x", "attn_norm_w", "w_in", "hlb", "hnw", "qw", "kw", "biasg", "maskadd", "cw", "cb", "dtb", "alog", "dsk",
            "snw", "w_out", "fnw", "wq", "sk", "downT0", "downT1", "up0", "up1", "consts"}
    with ExitStack() as semstack:
        s.semstack = semstack
        xin = x_d
        for l in range(NLAYER):
            xout = out_d if l == NLAYER - 1 else X2
            with ExitStack() as st:
                phaseA(s, nc, st, xin, w_in[l], anw[l], U, consts, qk_d=QK, qw_d=qw[l], kw_d=kw[l])
                s.barrier(); s.flush()
            with ExitStack() as st:
                phaseB(s, nc, st, U, MIX, hlb, hnw[l], consts, l)
                s.barrier(); s.flush()
            with ExitStack() as st:
                phaseC(s, nc, st, U, MIX, QK, OB, qw[l], kw[l], biasg, maskadd, consts, do_c0=False, do_c2=False)
                s.barrier(); s.flush()
            with ExitStack() as st:
                phaseD(s, nc, st, U, MIX, cw[l], cb[l], dtb[l], alog[l], dsk[l], snw[l], consts)
                s.barrier(); s.flush()
            with ExitStack() as st:
                phaseE(s, nc, st, MIX, xin, w_out[l], X1, consts, ob_d=OB)
                s.barrier(); s.flush()
            with ExitStack() as st:
                phaseFD(s, nc, st, X1, fnw[l], wq[l], sk[l], down[l], up[l], xout, G_d, consts)
            xin = X2
    return nc


def kernel(x, attn_norm_w, w_in, hgrn_lower_bounds, hgrn_norm_w, q_norm_w, k_norm_w, rel_bias,
           ssm_conv_w, ssm_conv_b, ssm_dt_bias, ssm_a_log, ssm_d, ssm_norm_w, w_out,
           ffn_norm_w, peer_w_query, peer_sub_keys, peer_down, peer_up):
    f = lambda a: np.ascontiguousarray(np.asarray(a, dtype=np.float32))
    if "nc" not in _CACHE:
        _CACHE["nc"] = build_program()
    nc = _CACHE["nc"]
    biasg, maskadd = make_attn_consts(f(rel_bias))
    shared = {
        "attn_norm_w": f(attn_norm_w), "w_in": f(w_in), "hlb": f(hgrn_lower_bounds), "hnw": f(hgrn_norm_w),
        "qw": f(q_norm_w), "kw": f(k_norm_w), "biasg": biasg, "maskadd": maskadd,
        "cw": f(ssm_conv_w), "cb": f(ssm_conv_b), "dtb": f(ssm_dt_bias), "alog": f(ssm_a_log), "dsk": f(ssm_d),
        "snw": f(ssm_norm_w), "w_out": f(w_out), "fnw": f(ffn_norm_w), "wq": f(peer_w_query), "sk": f(peer_sub_keys),
        "consts": make_consts(),
    }
    pd = f(peer_down); pu = f(peer_up)
    for l in range(NLAYER):
        shared["downT%d" % l] = np.ascontiguousarray(pd[l].T)
        shared["up%d" % l] = pu[l]
    xs = f(x)
    n = xs.shape[0]
    in_maps = [dict(shared, x=xs[i]) for i in range(n)]
    res = run_bass_kernel_spmd(nc, in_maps, core_ids=list(range(n)))
    return np.stack([np.asarray(r["out"]) for r in res.results], axis=0).astype(np.float32)
```

```python
from contextlib import ExitStack
import numpy as np
import concourse.bass as bass
import concourse.mybir as mybir
from concourse.bass_utils import run_bass_kernel_spmd

F32 = mybir.dt.float32
I32 = mybir.dt.int32
U32 = mybir.dt.uint32
ALU = mybir.AluOpType
AF = mybir.ActivationFunctionType
AX = mybir.AxisListType

NDMA = 24
EPOCH = 30000


def _prod(xs):
    r = 1
    for v in xs:
        r *= int(v)
    return r


def region(a):
    t = a.tensor
    name = t.name
    ap = a.ap
    off = int(a.offset)
    sp = str(a.space)
    if 'DRAM' in sp.upper():
        ext = sum((int(c) - 1) * abs(int(s)) for s, c in ap)
        return (name, 0, 1, off, off + ext + 1)
    row = _prod(t.shape[1:])
    p0 = off // row
    f0 = off % row
    pstep, pcnt = int(ap[0][0]), int(ap[0][1])
    if pstep == 0:
        pcnt = 1
    ext = sum((int(c) - 1) * abs(int(s)) for s, c in ap[1:])
    if 'PSUM' in sp.upper():
        return (name, (p0 // 32) * 32, ((p0 + pcnt + 31) // 32) * 32, 0, row)
    return (name, p0, p0 + pcnt, f0, f0 + ext + 1)


class Sched:
    def __init__(self, nc):
        self.nc = nc
        self.eng = {'pe': nc.tensor, 'dve': nc.vector, 'act': nc.scalar,
                    'pool': nc.gpsimd, 'sp': nc.sync}
        self.ops = {k: [] for k in self.eng}
        self.cnt = {k: 0 for k in self.eng}
        self.waited = {k: {} for k in self.eng}
        self.acc = {}
        self.dma_n = 0
        self.keys = set()
        self.ro = set()
        self.sems = {}
        self.semstack = None

    def _deps(self, eng, reads, writes):
        deps = {}

        def add(key, val):
            if deps.get(key, 0) < val:
                deps[key] = val

        for is_w, aps in ((False, reads), (True, writes)):
            for a in aps:
                if a.tensor.name in self.ro:
                    continue
                r = region(a)
                lst = self.acc.get(r[0], [])
                psum = 'PSUM' in str(a.space).upper()
                for (key, val, w2, e2, p0, p1, f0, f1) in lst:
                    if not (is_w or w2) and not (psum and e2 != eng):
                        continue
                    if p1 <= r[1] or r[2] <= p0 or f1 <= r[3] or r[4] <= f0:
                        continue
                    if e2 == eng and eng != 'dma':
                        if eng == 'pe':
                            continue
                    add(key, val)
        return deps

    def _record(self, eng, key, val, reads, writes):
        for is_w, aps in ((False, reads), (True, writes)):
            for a in aps:
                if a.tensor.name in self.ro:
                    continue
                r = region(a)
                lst = self.acc.setdefault(r[0], [])
                new = []
                for rec in lst:
                    (k2, v2, w2, e2, p0, p1, f0, f1) = rec
                    contained = (r[1] <= p0 and p1 <= r[2] and r[3] <= f0 and f1 <= r[4])
                    if contained and (is_w or (not w2 and e2 == eng and eng != 'dma')):
                        continue
                    new.append(rec)
                new.append((key, val, is_w, eng, r[1], r[2], r[3], r[4]))
                self.acc[r[0]] = new

    def _key(self, eng):
        idx = self.cnt[eng]
        return (eng, idx // EPOCH), idx % EPOCH + 1

    def op(self, eng, fn, reads=(), writes=()):
        deps = self._deps(eng, reads, writes)
        waits = []
        w = self.waited[eng]
        for key, val in deps.items():
            if w.get(key, 0) < val:
                w[key] = val
                waits.append((key, val))
        key, val = self._key(eng)
        self.cnt[eng] += 1
        self.keys.add(key)
        self.ops[eng].append((fn, waits, key, 1))
        self._record(eng, key, val, reads, writes)

    def dma(self, q, out, in_, **kw):
        n = self.dma_n
        self.dma_n += 1
        slot = n % NDMA
        key = ('dma', slot)
        val = 16 * (n // NDMA + 1)
        deps = self._deps('dma', [in_], [out])
        if n >= NDMA:
            deps[key] = max(deps.get(key, 0), val - 16)
        waits = []
        w = self.waited[q]
        for k, v in deps.items():
            if w.get(k, 0) < v:
                w[k] = v
                waits.append((k, v))
        self.keys.add(key)
        self.ops[q].append((lambda e: e.dma_start(out=out, in_=in_, **kw), waits, key, 16))
        self._record('dma', key, val, [in_], [out])

    def dma_custom(self, q, fn, reads, writes):
        n = self.dma_n
        self.dma_n += 1
        slot = n % NDMA
        key = ('dma', slot)
        val = 16 * (n // NDMA + 1)
        deps = self._deps('dma', reads, writes)
        if n >= NDMA:
            deps[key] = max(deps.get(key, 0), val - 16)
        waits = []
        w = self.waited[q]
        for k, v in deps.items():
            if w.get(k, 0) < v:
                w[k] = v
                waits.append((k, v))
        self.keys.add(key)
        self.ops[q].append((fn, waits, key, 16))
        self._record('dma', key, val, reads, writes)

    def barrier(self):
        targets = {}
        for e in self.eng:
            if self.cnt[e] > 0:
                idx = self.cnt[e] - 1
                targets[(e, idx // EPOCH)] = idx % EPOCH + 1
        for n in range(max(0, self.dma_n - NDMA), self.dma_n):
            targets[('dma', n % NDMA)] = 16 * (n // NDMA + 1)
        for e in self.eng:
            waits = []
            w = self.waited[e]
            for k, v in targets.items():
                if k[0] == e:
                    continue
                if w.get(k, 0) < v:
                    w[k] = v
                    waits.append((k, v))
            if waits:
                key, val = self._key(e)
                self.cnt[e] += 1
                self.keys.add(key)
                self.ops[e].append((None, waits, key, 1))
        self.acc = {}

    def finish(self):
        self.barrier()

    def flush(self):
        nc = self.nc
        for k in sorted(self.keys, key=str):
            if k not in self.sems:
                self.sems[k] = self.semstack.enter_context(nc.semaphore("s_%s_%d" % (k[0], k[1])))
        sems = self.sems
        ops = self.ops
        self.ops = {k: [] for k in self.eng}
        with nc.Block() as block:
            def mk(ename):
                def body(e):
                    for (fn, waits, key, inc) in ops[ename]:
                        for (k, v) in waits:
                            e.wait_ge(sems[k], v)
                        if fn is None:
                            ins = e.nop()
                        else:
                            ins = fn(e)
                        ins.then_inc(sems[key], inc)
                return body

            block.tensor(mk('pe'))
            block.vector(mk('dve'))
            block.scalar(mk('act'))
            block.gpsimd(mk('pool'))
            block.sync(mk('sp'))

    def emit(self):
        from contextlib import ExitStack
        with ExitStack() as st:
            self.semstack = st
            self.flush()


def _aps(*xs):
    return [x for x in xs if x is not None and not isinstance(x, (int, float))]


def _h_tt(self, eng, out, in0, in1, op):
    self.op(eng, lambda e: e.tensor_tensor(out=out, in0=in0, in1=in1, op=op), reads=[in0, in1], writes=[out])


def _h_ts(self, eng, out, in0, s1, s2, op0, op1=None):
    if op1 is None:
        self.op(eng, lambda e: e.tensor_scalar(out=out, in0=in0, scalar1=s1, scalar2=None, op0=op0),
                reads=_aps(in0, s1), writes=[out])
    else:
        self.op(eng, lambda e: e.tensor_scalar(out=out, in0=in0, scalar1=s1, scalar2=s2, op0=op0, op1=op1),
                reads=_aps(in0, s1, s2), writes=[out])


def _h_stt(self, eng, out, in0, scalar, in1, op0, op1):
    self.op(eng, lambda e: e.scalar_tensor_tensor(out=out, in0=in0, scalar=scalar, in1=in1, op0=op0, op1=op1),
            reads=_aps(in0, scalar, in1), writes=[out])


def _h_act(self, out, in_, func, bias=None, scale=None, accum_out=None):
    kw = {}
    if bias is not None:
        kw['bias'] = bias
    if scale is not None:
        kw['scale'] = scale
    if accum_out is not None:
        kw['accum_out'] = accum_out
    self.op('act', lambda e: e.activation(out=out, in_=in_, func=func, **kw),
            reads=_aps(in_, bias, scale), writes=_aps(out, accum_out))


def _h_copy(self, eng, out, in_):
    if eng == 'act':
        self.op('act', lambda e: e.copy(out=out, in_=in_), reads=[in_], writes=[out])
    else:
        self.op(eng, lambda e: e.tensor_copy(out=out, in_=in_), reads=[in_], writes=[out])


def _h_mm(self, out, lhsT, rhs, start=True, stop=True):
    self.op('pe', lambda e: e.matmul(out, lhsT=lhsT, rhs=rhs, start=start, stop=stop),
            reads=[lhsT, rhs], writes=[out])


def _h_tr(self, out, in_, ident):
    self.op('pe', lambda e: e.transpose(out=out, in_=in_, identity=ident), reads=[in_, ident], writes=[out])


def _h_red(self, eng, out, in_, op, axis=AX.X):
    self.op(eng, lambda e: e.tensor_reduce(out=out, in_=in_, axis=axis, op=op), reads=[in_], writes=[out])


def _h_memset(self, eng, ap, val):
    self.op(eng, lambda e: e.memset(ap, val), reads=[], writes=[ap])


def _h_recip(self, out, in_):
    self.op('dve', lambda e: e.reciprocal(out=out, in_=in_), reads=[in_], writes=[out])


Sched.tt = _h_tt
Sched.ts = _h_ts
Sched.stt = _h_stt
Sched.act = _h_act
Sched.copy = _h_copy
Sched.mm = _h_mm
Sched.tr = _h_tr
Sched.red = _h_red
Sched.memset = _h_memset
Sched.recip = _h_recip


def bcast(ap, shape):
    assert len(ap.shape) == len(shape), (ap.shape, shape)
    new = []
    for (st, cnt), tgt in zip(ap.ap, shape):
        if int(cnt) == int(tgt):
            new.append([int(st), int(cnt)])
        else:
            assert int(cnt) == 1
            new.append([0, int(tgt)])
    return bass.AP(ap.tensor, ap.offset, new)


T = 2048; D = 1024; NC = 3462
O_HQ, O_HF, O_HI, O_HG, O_AQ, O_AK, O_AV, O_SZ, O_XBC, O_DT = 0, 256, 512, 768, 1024, 1408, 1792, 2176, 2560, 3456

C_ID, C_TRI, C_TRI2, C_CO0, C_CO1, C_CI, C_I16, C_I128, C_END = 0, 128, 256, 384, 512, 640, 642, 658, 786


def make_consts():
    c = np.zeros((128, C_END), np.float32)
    s = np.arange(128)[:, None]; t = np.arange(128)[None, :]
    same = (s // 64) == (t // 64)
    c[:, C_ID:C_ID + 128] = np.eye(128)
    c[:, C_TRI:C_TRI + 128] = (same & (s <= t))
    c[:, C_TRI2:C_TRI2 + 128] = (same & (s > t))
    c[:, C_CO0:C_CO0 + 128] = (s < 64) * np.ones((1, 128))
    c[:, C_CO1:C_CO1 + 128] = (s >= 64) * np.ones((1, 128))
    c[:, C_CI] = (np.arange(128) < 64)
    c[:, C_CI + 1] = (np.arange(128) >= 64)
    c[:, C_I16:C_I16 + 16] = np.arange(16)[None, :]
    c[:, C_I128:C_I128 + 128] = np.arange(128)[None, :]
    return c


class Ctx:
    uid = 0

    def __init__(self, nc, st, pfx):
        Ctx.uid += 1
        self.nc, self.st, self.pfx = nc, st, "%s%d" % (pfx, Ctx.uid)
        self.n = 0

    def sb(self, shape, dt=F32, name=None):
        self.n += 1
        return self.st.enter_context(self.nc.sbuf_tensor("%s_%s%d" % (self.pfx, name or "t", self.n), list(shape), dt))

    def psum(self):
        return [self.st.enter_context(self.nc.psum_tensor("%s_ps%d" % (self.pfx, i), [128, 512], F32)) for i in range(8)]


def rstd_from_ss(s, out, tmp, ss, scale, eps=1e-6):
    s.act(tmp, ss, AF.Sqrt, bias=eps, scale=scale)
    s.recip(out, tmp)


def phaseB(s, nc, st, u_d, mix_d, hlb_d, hnw_d, consts_d, layer, ntiles=16):
    c = Ctx(nc, st, "B")
    P = c.psum()
    cst = c.sb([128, C_END], name="cst")
    s.dma('sp', cst[:], consts_d[:, :])
    ident = cst[:, C_ID:C_ID + 128]; tri = cst[:, C_TRI:C_TRI + 128]; tri2 = cst[:, C_TRI2:C_TRI2 + 128]
    chind = cst[:, C_CI:C_CI + 2]
    hnw = c.sb([128, 64], name="hnw")
    s.dma('sp', hnw[:], hnw_d.partition_broadcast(128))
    if layer > 0:
        h0 = c.sb([128, 256]); h1 = c.sb([128, 256]); lb = c.sb([128, 256]); oml = c.sb([128, 256])
        s.dma('sp', h0[:], hlb_d[0, :].partition_broadcast(128))
        s.dma('sp', h1[:], hlb_d[1, :].partition_broadcast(128))
        s.tt('dve', h1[:], h1[:], h0[:], ALU.subtract)
        s.act(lb[:], h1[:], AF.Sigmoid)
        s.ts('dve', oml[:], lb[:], -1.0, 1.0, ALU.mult, ALU.add)
    SA = c.sb([64, 4, 64], name="SA"); SB = c.sb([64, 4, 64], name="SB")
    s.memset('dve', SA[:], 0.0)
    NB = 2
    qTa = [c.sb([64, 4, 128]) for _ in range(NB)]; qTb = [c.sb([64, 4, 128]) for _ in range(NB)]
    kh0 = [c.sb([128, 256]) for _ in range(NB)]; kh1 = [c.sb([128, 256]) for _ in range(NB)]
    for t_ in qTa + qTb + kh0 + kh1:
        s.memset('dve', t_[:], 0.0)
    tin = [c.sb([128, 1024], name="tin") for _ in range(4)]
    fg = c.sb([128, 256]); logf = c.sb([128, 256]); key = c.sb([128, 256])
    eb = c.sb([128, 256]); enb = c.sb([128, 256]); er = c.sb([128, 256])
    qt = c.sb([128, 256]); kt = c.sb([128, 256])
    dec = [c.sb([64, 8]) for _ in range(NB)]; qT = [c.sb([64, 4, 128]) for _ in range(NB)]; kT = [c.sb([64, 4, 128]) for _ in range(NB)]
    sg = [c.sb([128, 256]) for _ in range(NB)]
    ATs = [c.sb([128, 128]) for _ in range(2)]
    osb = c.sb([128, 256]); sq = c.sb([128, 256]); ss = c.sb([128, 12])
    outt = [c.sb([128, 256]) for _ in range(2)]

    def pre(tt):
        p = tt % NB
        ti = tin[tt % 4]
        if tt == 0:
            s.dma('sp', tin[0][:], u_d[0:128, 0:1024])
        if tt + 1 < ntiles:
            s.dma('sp', tin[(tt + 1) % 4][:], u_d[(tt + 1) * 128:(tt + 2) * 128, 0:1024])
        q = ti[:, 0:256]; f = ti[:, 256:512]; g = ti[:, 768:1024]
        s.act(fg[:], f, AF.Sigmoid)
        if layer > 0:
            s.tt('dve', fg[:], fg[:], oml[:], ALU.mult)
            s.tt('dve', fg[:], fg[:], lb[:], ALU.add)
        s.act(sg[p][:], g, AF.Silu)
        s.act(logf[:], fg[:], AF.Ln)
        yield
        s.ts('dve', key[:], fg[:], -1.0, 1.0, ALU.mult, ALU.add)
        s.mm(P[0][:, 0:256], lhsT=tri, rhs=logf[:])
        s.mm(P[0][:, 256:512], lhsT=tri2, rhs=logf[:])
        for h in range(4):
            s.mm(P[1][0:64, h * 2:h * 2 + 2], lhsT=logf[:, h * 64:(h + 1) * 64], rhs=chind)
        yield
        s.act(eb[:], P[0][:, 0:256], AF.Exp)
        s.act(enb[:], P[0][:, 0:256], AF.Exp, scale=-1.0)
        s.act(er[:], P[0][:, 256:512], AF.Exp)
        s.act(dec[p][:], P[1][0:64, 0:8], AF.Exp)
        yield
        s.stt('dve', qt[:], q, 0.125, eb[:], ALU.mult, ALU.mult)
        s.tt('pool', kt[:], key[:], enb[:], ALU.mult)
        s.tt('pool', kh0[p][0:64, :], key[0:64, :], er[0:64, :], ALU.mult)
        s.tt('pool', kh1[p][64:128, :], key[64:128, :], er[64:128, :], ALU.mult)
        s.tt('pool', sg[p][:].rearrange("p (h v) -> p h v", h=4), sg[p][:].rearrange("p (h v) -> p h v", h=4),
             bcast(hnw[:].rearrange("p (o v) -> p o v", o=1), [128, 4, 64]), ALU.mult)
        yield
        for h in range(4):
            s.tr(P[2][0:64, h * 128:(h + 1) * 128], qt[:, h * 64:(h + 1) * 64], ident)
            s.tr(P[3][0:64, h * 128:(h + 1) * 128], kt[:, h * 64:(h + 1) * 64], ident)
        yield
        p2v = P[2][0:64, :].rearrange("p (h t) -> p h t", h=4)
        p3v = P[3][0:64, :].rearrange("p (h t) -> p h t", h=4)
        s.copy('dve', qT[p][:], p2v)
        s.copy('dve', qTa[p][:, :, 0:64], p2v[:, :, 0:64])
        s.copy('dve', qTb[p][:, :, 64:128], p2v[:, :, 64:128])
        s.copy('act', kT[p][:], p3v)
        yield

    def heads(tt):
        p = tt % NB
        ti = tin[tt % 4]
        for h in range(4):
            hs = slice(h * 64, (h + 1) * 64)
            vs = slice(512 + h * 64, 512 + (h + 1) * 64)
            s.mm(P[6][0:64, (2 * h) * 64:(2 * h + 1) * 64], lhsT=kh0[p][:, hs], rhs=ti[:, vs])
            s.mm(P[6][0:64, (2 * h + 1) * 64:(2 * h + 2) * 64], lhsT=kh1[p][:, hs], rhs=ti[:, vs])
        yield
        for h in range(4):
            hs = slice(h * 64, (h + 1) * 64)
            vs = slice(512 + h * 64, 512 + (h + 1) * 64)
            AT = P[4 + h % 2][:, 0:128]
            s.mm(AT, lhsT=kT[p][:, h, :], rhs=qT[p][:, h, :])
            A = ATs[h % 2]
            s.tt('dve', A[:], AT, tri, ALU.mult)
            o = P[7][:, hs]
            s.mm(o, lhsT=A[:], rhs=ti[:, vs], start=True, stop=False)
            s.mm(o, lhsT=qTa[p][:, h, :], rhs=SA[:, h, :], start=False, stop=False)
            u0 = P[6][0:64, (2 * h) * 64:(2 * h + 1) * 64]
            u1 = P[6][0:64, (2 * h + 1) * 64:(2 * h + 2) * 64]
            s.stt('dve', SB[:, h, :], SA[:, h, :], dec[p][:, 2 * h:2 * h + 1], u0, ALU.mult, ALU.add)
            s.mm(o, lhsT=qTb[p][:, h, :], rhs=SB[:, h, :], start=False, stop=True)
            s.stt('dve', SA[:, h, :], SB[:, h, :], dec[p][:, 2 * h + 1:2 * h + 2], u1, ALU.mult, ALU.add)
            yield
        s.copy('act', osb[:], P[7][:, 0:256])
        s.tt('pool', sq[:], osb[:], osb[:], ALU.mult)
        s.red('dve', ss[:, 0:4], sq[:].rearrange("p (h v) -> p h v", h=4), ALU.add)
        rstd_from_ss(s, ss[:, 8:12], ss[:, 4:8], ss[:, 0:4], 1.0 / 64)
        ot = outt[tt % 2]
        s.tt('dve', ot[:].rearrange("p (h v) -> p h v", h=4), osb[:].rearrange("p (h v) -> p h v", h=4),
             bcast(ss[:, 8:12].rearrange("p (h o) -> p h o", o=1), [128, 4, 64]), ALU.mult)
        s.tt('dve', ot[:], ot[:], sg[p][:], ALU.mult)
        s.dma('sp', mix_d[tt * 128:(tt + 1) * 128, 0:256], ot[:])
        yield

    def drain(*gens):
        gens = [g for g in gens if g is not None]
        while gens:
            for g in list(gens):
                try:
                    next(g)
                except StopIteration:
                    gens.remove(g)

    drain(pre(0))
    for tt in range(ntiles):
        drain(heads(tt), pre(tt + 1) if tt + 1 < ntiles else None)


def phaseD(s, nc, st, u_d, mix_d, cw_d, cb_d, dtb_d, alog_d, dsk_d, nw_d, consts_d, ntiles=16):
    c = Ctx(nc, st, "D")
    BF16 = mybir.dt.bfloat16
    P = c.psum()
    cst = c.sb([128, C_END], name="cst")
    s.dma('sp', cst[:], consts_d[:, :])
    ident = cst[:, C_ID:C_ID + 128]; tri = cst[:, C_TRI:C_TRI + 128]; tri2 = cst[:, C_TRI2:C_TRI2 + 128]
    co0 = cst[:, C_CO0:C_CO0 + 128]; co1 = cst[:, C_CO1:C_CO1 + 128]
    cw = c.sb([128, 4, 896], name="cw"); cb = c.sb([128, 896], name="cb")
    for j in range(4):
        s.dma('sp', cw[:, j, :], cw_d[j, :].partition_broadcast(128))
    s.dma('sp', cb[:], cb_d.partition_broadcast(128))
    sm = c.sb([128, 32], name="sm")
    s.dma('sp', sm[:, 0:6], dtb_d.partition_broadcast(128))
    s.dma('sp', sm[:, 6:12], alog_d.partition_broadcast(128))
    s.dma('sp', sm[:, 18:24], dsk_d.partition_broadcast(128))
    nwb = c.sb([128, 384], name="nwb")
    s.dma('sp', nwb[:], nw_d.partition_broadcast(128))
    s.act(sm[:, 24:30], sm[:, 6:12], AF.Exp)
    s.ts('dve', sm[:, 12:18], sm[:, 24:30], -1.0, None, ALU.mult)
    dtbias = sm[:, 0:6]; Ab = sm[:, 12:18]
    R = [c.sb([128, 6, 64], name="R") for _ in range(6)]
    R16 = [c.sb([128, 6, 64], BF16, name="R16") for _ in range(6)]
    s.memset('dve', R[0][:], 0.0)
    s.memset('dve', R16[0][:], 0.0)
    Bz0 = c.sb([128, 256], BF16); Bz1 = c.sb([128, 256], BF16)
    s.memset('dve', Bz0[:], 0.0); s.memset('dve', Bz1[:], 0.0)
    NH = 3
    CTs0 = [c.sb([128, 128], BF16) for _ in range(NH)]; CTs1 = [c.sb([128, 128], BF16) for _ in range(NH)]
    for b_ in CTs0 + CTs1:
        s.memset('pool', b_[:], 0.0)
    M = [c.sb([128, 1286], name="M") for _ in range(3)]
    X = [[c.sb([128, 896], name="X") for _ in range(3)] for _ in range(3)]
    mj = [c.sb([128, 896]) for _ in range(4)]
    xcs = [c.sb([128, 896], name="xc") for _ in range(2)]; xc16s = [c.sb([128, 384], BF16) for _ in range(2)]
    v6s = [c.sb([128, 64], name="v6") for _ in range(2)]
    abcs = [c.sb([128, 6, 128]) for _ in range(2)]; BCTs = [c.sb([128, 4, 128], BF16) for _ in range(2)]; CBTs = [c.sb([128, 2, 128]) for _ in range(2)]
    xw = c.sb([128, 384], BF16)
    dsb = [c.sb([128, 128]) for _ in range(NH)]; ecm = [c.sb([128, 128]) for _ in range(NH)]
    esb = [c.sb([128, 128]) for _ in range(NH)]; msb = [c.sb([128, 128]) for _ in range(NH)]; sc = [c.sb([128, 128], BF16) for _ in range(NH)]
    ysb = c.sb([128, 384]); szbs = [c.sb([128, 384]) for _ in range(2)]; sq = c.sb([128, 384]); ss = c.sb([128, 8]); tmpS = c.sb([128, 384])
    outt = [c.sb([128, 384]) for _ in range(2)]
    DmB = [P[3], P[4]]
    def loads(tt):
        Mt = M[tt % 3]
        s.dma('sp', Mt[:], u_d[tt * 128:(tt + 1) * 128, O_SZ:NC])
        for j in range(3):
            Xj = X[tt % 3][j]
            sh = 3 - j
            if tt == 0:
                s.memset('pool', Xj[:], 0.0)
                s.dma('sp', Xj[sh:128, :], u_d[0:128 - sh, O_XBC:O_XBC + 896])
            else:
                s.dma('sp', Xj[:], u_d[tt * 128 - sh:(tt + 1) * 128 - sh, O_XBC:O_XBC + 896])

    def pre(tt):
        b = tt % 2
        Mt = M[tt % 3]; v6 = v6s[b]
        xc = xcs[b]; xc16 = xc16s[b]; abc = abcs[b]; BCT = BCTs[b]; CBT = CBTs[b]; szb = szbs[b]
        Sin, Smid, Sout = R[(2 * tt) % 6], R[(2 * tt + 1) % 6], R[(2 * tt + 2) % 6]
        Sin16, Smid16, Sout16 = R16[(2 * tt) % 6], R16[(2 * tt + 1) % 6], R16[(2 * tt + 2) % 6]
        if tt == 0:
            loads(0)
        if tt + 1 < ntiles:
            loads(tt + 1)
        z = Mt[:, 0:384]; x3 = Mt[:, 384:1280]; dtr = Mt[:, 1280:1286]
        xs = xc[:, 0:384]
        dtt = v6[:, 0:6]; a_ = v6[:, 8:14]; cum = v6[:, 16:22]; wgt = v6[:, 24:30]; dAe = v6[:, 32:44]; tmp6 = v6[:, 48:54]
        s.tt('dve', tmp6, dtr, dtbias, ALU.add)
        s.act(tmp6, tmp6, AF.Exp)
        s.act(dtt, tmp6, AF.Ln, bias=1.0)
        s.tt('dve', a_, dtt, Ab, ALU.mult)
        s.copy('pool', abc[:], bcast(v6[:, 8:14].rearrange("p (h o) -> p h o", o=1), [128, 6, 128]))
        yield
        for j in range(3):
            s.tt('pool', mj[j][:], X[tt % 3][j][:], cw[:, j, :], ALU.mult)
        s.tt('pool', mj[3][:], x3, cw[:, 3, :], ALU.mult)
        yield
        s.mm(P[0][:, 0:6], lhsT=tri, rhs=a_)
        s.mm(P[0][:, 8:14], lhsT=tri2, rhs=a_)
        s.mm(P[0][:, 16:22], lhsT=co0, rhs=a_)
        s.mm(P[0][:, 22:28], lhsT=co1, rhs=a_)
        s.copy('act', cum, P[0][:, 0:6])
        s.act(wgt, P[0][:, 8:14], AF.Exp)
        s.act(dAe, P[0][:, 16:28], AF.Exp)
        s.tt('dve', wgt, wgt, dtt, ALU.mult)
        s.tt('dve', mj[0][:], mj[0][:], mj[1][:], ALU.add)
        s.tt('dve', mj[2][:], mj[2][:], mj[3][:], ALU.add)
        s.tt('dve', mj[0][:], mj[0][:], mj[2][:], ALU.add)
        s.tt('dve', mj[0][:], mj[0][:], cb[:], ALU.add)
        yield
        s.act(xc[:], mj[0][:], AF.Silu)
        s.copy('act', xc16[:], xs)
        s.act(szb[:], z, AF.Silu)
        s.tt('dve', xw[:].rearrange("p (h v) -> p h v", h=6), xs.rearrange("p (h v) -> p h v", h=6),
             bcast(v6[:, 24:30].rearrange("p (h o) -> p h o", o=1), [128, 6, 64]), ALU.mult)
        yield
        for k in range(4):
            s.tr(P[1][:, k * 128:(k + 1) * 128], xc[:, 384 + k * 128:384 + (k + 1) * 128], ident)
        s.copy('act', BCT[:], P[1][:, :].rearrange("p (k t) -> p k t", k=4))
        for g in range(2):
            s.mm(P[2][:, g * 128:(g + 1) * 128], lhsT=BCT[:, g, :], rhs=BCT[:, 2 + g, :])
        s.copy('dve', CBT[:], P[2][:, 0:256].rearrange("p (g t) -> p g t", g=2))
        s.copy('act', Bz0[0:64, :], xc[0:64, 384:640])
        s.copy('act', Bz1[64:128, :], xc[64:128, 384:640])
        yield
        for h in range(6):
            g = h // 3; hs = slice(h * 64, (h + 1) * 64)
            s.mm(P[7][:, hs], lhsT=Bz0[:, g * 128:(g + 1) * 128], rhs=xw[:, hs])
        for h in range(6):
            g = h // 3; hs = slice(h * 64, (h + 1) * 64)
            s.mm(P[1][:, hs], lhsT=Bz1[:, g * 128:(g + 1) * 128], rhs=xw[:, hs])
        s.tt('dve', tmpS[:].rearrange("p (h v) -> p h v", h=6), Sin[:], bcast(v6[:, 32:38].rearrange("p (h o) -> p h o", o=1), [128, 6, 64]), ALU.mult)
        s.tt('dve', Smid[:], tmpS[:].rearrange("p (h v) -> p h v", h=6), P[7][:, 0:384].rearrange("p (h v) -> p h v", h=6), ALU.add)
        s.copy('act', Smid16[:], Smid[:])
        s.tt('dve', tmpS[:].rearrange("p (h v) -> p h v", h=6), Smid[:], bcast(v6[:, 38:44].rearrange("p (h o) -> p h o", o=1), [128, 6, 64]), ALU.mult)
        s.tt('dve', Sout[:], tmpS[:].rearrange("p (h v) -> p h v", h=6), P[1][:, 0:384].rearrange("p (h v) -> p h v", h=6), ALU.add)
        s.copy('act', Sout16[:], Sout[:])

        yield

    def heads(tt):
        b = tt % 2
        Mt = M[b]; v6 = v6s[b]
        xc = xcs[b]; xc16 = xc16s[b]; abc = abcs[b]; BCT = BCTs[b]; CBT = CBTs[b]; szb = szbs[b]
        Sin16, Smid16 = R16[(2 * tt) % 6], R16[(2 * tt + 1) % 6]
        def st1(h):
            g = h // 3; hq = h % NH
            Dm = DmB[h % 2][:, 0:128]
            s.mm(Dm, lhsT=abc[:, h, :], rhs=tri)
            s.ts('dve', dsb[hq][:], Dm, v6[:, 16 + h:17 + h], 0.0, ALU.subtract, ALU.min)
            s.act(ecm[hq][:], Dm, AF.Exp)
            s.act(esb[hq][:], dsb[hq][:], AF.Exp)
            s.stt('dve', msb[hq][:], esb[hq][:], v6[:, h:h + 1], tri, ALU.mult, ALU.mult)
            s.tt('pool', sc[hq][:], msb[hq][:], CBT[:, g, :], ALU.mult)
            s.tt('pool', CTs0[hq][:, 0:64], BCT[:, 2 + g, 0:64], ecm[hq][:, 0:64], ALU.mult)
            s.tt('pool', CTs1[hq][:, 64:128], BCT[:, 2 + g, 64:128], ecm[hq][:, 64:128], ALU.mult)

        def st2(h):
            hq = h % NH; hs = slice(h * 64, (h + 1) * 64)
            y = P[5 + h % 2][:, 0:64]
            s.mm(y, lhsT=sc[hq][:], rhs=xc16[:, hs], start=True, stop=False)
            s.mm(y, lhsT=CTs0[hq][:], rhs=Sin16[:, h, :], start=False, stop=False)
            s.mm(y, lhsT=CTs1[hq][:], rhs=Smid16[:, h, :], start=False, stop=True)
            s.stt('dve', ysb[:, hs], xc[:, hs], sm[:, 18 + h:19 + h], y, ALU.mult, ALU.add)

        SK = 2
        for step in range(6 + SK):
            if step < 6:
                st1(step)
            if step >= SK:
                st2(step - SK)
            yield
        s.tt('pool', ysb[:], ysb[:], szb[:], ALU.mult)
        s.tt('pool', sq[:], ysb[:], ysb[:], ALU.mult)
        s.red('dve', ss[:, 0:2], sq[:].rearrange("p (g v) -> p g v", g=2), ALU.add)
        rstd_from_ss(s, ss[:, 4:6], ss[:, 2:4], ss[:, 0:2], 1.0 / 192)
        ot = outt[b]
        s.tt('dve', ot[:].rearrange("p (g v) -> p g v", g=2), ysb[:].rearrange("p (g v) -> p g v", g=2),
             bcast(ss[:, 4:6].rearrange("p (g o) -> p g o", o=1), [128, 2, 192]), ALU.mult)
        s.tt('pool', ot[:], ot[:], nwb[:], ALU.mult)
        s.dma('sp', mix_d[tt * 128:(tt + 1) * 128, 640:1024], ot[:])


    def drain(*gens):
        gens = [g for g in gens if g is not None]
        while gens:
            for g in list(gens):
                try:
                    next(g)
                except StopIteration:
                    gens.remove(g)

    drain(pre(0))
    for tt in range(ntiles):
        drain(heads(tt), pre(tt + 1) if tt + 1 < ntiles else None)


DILS = (1, 4, 16)


def t5_bucket_np(dist):
    max_exact = 16
    d = np.maximum(dist, 1).astype(np.float32)
    large = max_exact + (np.log(d / max_exact) / np.log(2048 / max_exact) * (32 - max_exact)).astype(np.int32)
    large = np.minimum(large, 31)
    return np.where(dist < max_exact, dist, large).astype(np.int32)


def make_attn_consts(rel_bias):
    j = np.arange(128)[:, None]; i = np.arange(128)[None, :]
    delta = np.concatenate([i - j + 128, i - j], axis=1)
    valid = (delta >= 0) & (delta <= 128)
    maskadd = np.where(valid, 0.0, -30000.0).astype(np.float32)
    biasg = np.zeros((18, 128, 256), np.float32)
    for bi, dil in enumerate(DILS):
        bucket = t5_bucket_np(np.maximum(delta, 0) * dil)
        for h in range(6):
            biasg[bi * 6 + h] = rel_bias[bucket, h]
    return biasg, maskadd


def phaseC(s, nc, st, u_d, mix_d, qk_d, ob_d, qw_d, kw_d, biasg_d, maskadd_d, consts_d, ntiles=16, branches=(0, 1, 2), do_c0=True, do_c2=True):
    c = Ctx(nc, st, "C")
    P = c.psum()
    cst = c.sb([128, C_END], name="cst")
    s.dma('sp', cst[:], consts_d[:, :])
    ident = cst[:, C_ID:C_ID + 128]
    qkw = c.sb([128, 128], name="qkw")
    s.dma('sp', qkw[:, 0:64], qw_d.partition_broadcast(128))
    s.dma('sp', qkw[:, 64:128], kw_d.partition_broadcast(128))
    biasT = c.sb([128, 18, 256], name="biasT")
    madd = c.sb([128, 256], name="madd")
    s.dma('sp', biasT[:], biasg_d.rearrange("n j c -> j n c"))
    s.dma('sp', madd[:], maskadd_d[:, :])
    s.tt('dve', biasT[:], biasT[:], bcast(madd[:].rearrange("p (o c) -> p o c", o=1), [128, 18, 256]), ALU.add)
    qk = [c.sb([128, 768]) for _ in range(2)]; sq = c.sb([128, 768]); ss = c.sb([128, 36]); qkn = [c.sb([128, 768]) for _ in range(2)]
    for tt in (range(ntiles) if do_c0 else []):
        b = tt % 2
        s.dma('sp', qk[b][:], u_d[tt * 128:(tt + 1) * 128, O_AQ:O_AQ + 768])
        s.tt('pool', sq[:], qk[b][:], qk[b][:], ALU.mult)
        s.red('dve', ss[:, 0:12], sq[:].rearrange("p (h v) -> p h v", h=12), ALU.add)
        rstd_from_ss(s, ss[:, 24:36], ss[:, 12:24], ss[:, 0:12], 1.0 / 64)
        s.tt('dve', qkn[b][:].rearrange("p (h v) -> p h v", h=12), qk[b][:].rearrange("p (h v) -> p h v", h=12),
             bcast(ss[:, 24:36].rearrange("p (h o) -> p h o", o=1), [128, 12, 64]), ALU.mult)
        s.stt('dve', qkn[b][:, 0:384].rearrange("p (h v) -> p h v", h=6), qkn[b][:, 0:384].rearrange("p (h v) -> p h v", h=6), 0.125,
              bcast(qkw[:, 0:64].rearrange("p (o v) -> p o v", o=1), [128, 6, 64]), ALU.mult, ALU.mult)
        s.tt('pool', qkn[b][:, 384:768].rearrange("p (h v) -> p h v", h=6), qkn[b][:, 384:768].rearrange("p (h v) -> p h v", h=6),
             bcast(qkw[:, 64:128].rearrange("p (o v) -> p o v", o=1), [128, 6, 64]), ALU.mult)
        s.dma('sp', qk_d[tt * 128:(tt + 1) * 128, :], qkn[b][:])
    QKb = [c.sb([128, 768]) for _ in range(3)]
    V32 = [c.sb([128, 384]) for _ in range(3)]
    BF16 = mybir.dt.bfloat16
    V1 = [c.sb([128, 6, 65], BF16) for _ in range(3)]
    for v in V1:
        s.memset('dve', v[:], 1.0)
    QT = [c.sb([128, 3, 128], BF16) for _ in range(2)]
    KTe = [c.sb([128, 3, 128], BF16) for _ in range(3)]; KTo = [c.sb([128, 3, 128], BF16) for _ in range(3)]
    for k_ in KTe + KTo:
        s.memset('pool', k_[:], 0.0)
    Esb = [c.sb([128, 256]) for _ in range(4)]; Psb = [c.sb([128, 256], BF16) for _ in range(4)]
    Osb = [c.sb([128, 390]) for _ in range(2)]
    blocks = []
    for bi in branches:
        dil = DILS[bi]
        L = (ntiles * 128) // dil
        nblk = L // 128
        for r in range(dil):
            for blk in range(nblk):
                blocks.append((bi, dil, r, blk))
    views = {}
    for bi in branches:
        dil = DILS[bi]
        views[bi] = (qk_d[0:ntiles * 128, :].rearrange("(b m d) c -> d b m c", m=128, d=dil),
                     u_d[0:ntiles * 128, O_AV:O_AV + 384].rearrange("(b m d) c -> d b m c", m=128, d=dil),
                     ob_d[bi, 0:ntiles * 128, :].rearrange("(b m d) c -> d b m c", m=128, d=dil))

    def loads(n):
        bi, dil, r, blk = blocks[n]
        qv, vv, ov = views[bi]
        s.dma('sp', QKb[n % 3][:], qv[r, blk])
        s.dma('sp', V32[n % 3][:], vv[r, blk])

    def pre(n):
        bi, dil, r, blk = blocks[n]
        pb = n % 2
        qv, vv, ov = views[bi]
        p3 = n % 3
        if n == 0:
            loads(0)
        if n + 1 < len(blocks):
            loads(n + 1)
        s.copy('pool', V1[p3][:, :, 0:64], V32[p3][:].rearrange("m (h v) -> m h v", h=6))
        yield
        for pr in range(3):
            s.tr(P[0][:, pr * 128:(pr + 1) * 128], QKb[p3][:, pr * 128:(pr + 1) * 128], ident)
            s.tr(P[1][:, pr * 128:(pr + 1) * 128], QKb[p3][:, 384 + pr * 128:384 + (pr + 1) * 128], ident)
            yield
        s.copy('act', QT[pb][:], P[0][:, 0:384].rearrange("p (a t) -> p a t", a=3))
        p1v = P[1][:, 0:384].rearrange("p (a t) -> p a t", a=3)
        s.copy('dve', KTe[p3][0:64, :, :], p1v[0:64, :, :])
        s.copy('dve', KTo[p3][64:128, :, :], p1v[64:128, :, :])
        yield

    def heads(n):
        bi, dil, r, blk = blocks[n]
        pb = n % 2
        qv, vv, ov = views[bi]
        PO = P[6 + pb]
        c0 = 0 if blk > 0 else 128
        p3 = n % 3; q3 = (n - 1) % 3

        def S_(h):
            pr = h // 2; hq = h % 4
            KT = KTe if h % 2 == 0 else KTo
            ST = P[2 + hq]
            s.mm(ST[:, 128:256], lhsT=KT[p3][:, pr, :], rhs=QT[pb][:, pr, :])
            if blk > 0:
                s.mm(ST[:, 0:128], lhsT=KT[q3][:, pr, :], rhs=QT[pb][:, pr, :])
            s.tt('dve', Esb[hq][:, c0:256], ST[:, c0:256], biasT[:, bi * 6 + h, c0:256], ALU.add)
            s.act(Psb[hq][:, c0:256], Esb[hq][:, c0:256], AF.Exp)

        def PV_(h):
            hq = h % 4
            O = PO[:, h * 65:(h + 1) * 65]
            if blk > 0:
                s.mm(O, lhsT=Psb[hq][:, 0:128], rhs=V1[q3][:, h, :], start=True, stop=False)
            s.mm(O, lhsT=Psb[hq][:, 128:256], rhs=V1[p3][:, h, :], start=(blk == 0), stop=True)

        SK = 3
        for step in range(6 + SK):
            if step >= SK:
                PV_(step - SK)
            if step < 6:
                S_(step)
            yield
        s.copy('act', Osb[pb][:], PO[:, 0:390])
        s.dma('sp', ov[r, blk], Osb[pb][:])
        yield

    def drain(*gens):
        gens = [g for g in gens if g is not None]
        while gens:
            for g in list(gens):
                try:
                    next(g)
                except StopIteration:
                    gens.remove(g)

    drain(pre(0))
    for n in range(len(blocks)):
        drain(heads(n), pre(n + 1) if n + 1 < len(blocks) else None)
    ob = [[c.sb([128, 390]) for _ in range(3)] for _ in range(2)]
    rden = c.sb([128, 8]); outt = [c.sb([128, 384]) for _ in range(2)]
    for tt in (range(ntiles) if do_c2 else []):
        b = tt % 2
        for bi in range(3):
            s.dma('sp', ob[b][bi][:], ob_d[bi, tt * 128:(tt + 1) * 128, :])
        s.tt('pool', ob[b][0][:], ob[b][0][:], ob[b][1][:], ALU.add)
        s.tt('pool', ob[b][0][:], ob[b][0][:], ob[b][2][:], ALU.add)
        o3 = ob[b][0][:].rearrange("p (h v) -> p h v", h=6)
        s.recip(rden[:, 0:6].rearrange("p (h o) -> p h o", o=1), o3[:, :, 64:65])
        s.tt('dve', outt[b][:].rearrange("p (h v) -> p h v", h=6), o3[:, :, 0:64],
             bcast(rden[:, 0:6].rearrange("p (h o) -> p h o", o=1), [128, 6, 64]), ALU.mult)
        s.dma('sp', mix_d[tt * 128:(tt + 1) * 128, 256:640], outt[b][:])


def norm_transpose(s, xt, wb, junk, ss, xn, xnT, Pa, Pb, ident):
    s.act(junk[:], xt[:], AF.Square, accum_out=ss[:, 0:1])
    s.act(ss[:, 1:2], ss[:, 0:1], AF.Sqrt, bias=1e-6, scale=1.0 / D)
    s.recip(ss[:, 2:3], ss[:, 1:2])
    s.stt('dve', xn[:], xt[:], ss[:, 2:3], wb[:], ALU.mult, ALU.mult)
    transpose8(s, xn, xnT, Pa, Pb, ident)


def transpose8(s, xn, xnT, Pa, Pb, ident):
    for dc in range(8):
        pt = Pa if dc < 4 else Pb
        s.tr(pt[:, (dc % 4) * 128:(dc % 4 + 1) * 128], xn[:, dc * 128:(dc + 1) * 128], ident)
    s.copy('act', xnT[:, 0:4, :], Pa[:, :].rearrange("p (a c) -> p a c", a=4))
    s.copy('dve', xnT[:, 4:8, :], Pb[:, :].rearrange("p (a c) -> p a c", a=4))


def phaseA(s, nc, st, x_d, w_d, nw_d, u_d, consts_d, ntiles=16, qk_d=None, qw_d=None, kw_d=None):
    c = Ctx(nc, st, "A")
    BF16 = mybir.dt.bfloat16
    P = c.psum()
    W = c.sb([128, 8, NC], BF16, name="W")
    ident = c.sb([128, 128], name="ident")
    wb = c.sb([128, D], name="wb")
    s.dma('sp', ident[:], consts_d[:, C_ID:C_ID + 128])
    s.dma('sp', wb[:], nw_d.partition_broadcast(128))
    wst = [c.sb([128, NC], name="wst") for _ in range(2)]
    for dc in range(8):
        s.dma('sp', wst[dc % 2][:], w_d[dc * 128:(dc + 1) * 128, :])
        s.copy('pool' if dc % 2 == 0 else 'dve', W[:, dc, :], wst[dc % 2][:])
    xt = [c.sb([128, D]) for _ in range(3)]; xns = [c.sb([128, D]) for _ in range(2)]; junk = c.sb([128, D])
    if qk_d is not None:
        qkw = c.sb([128, 128], name="qkw")
        s.dma('sp', qkw[:, 0:64], qw_d.partition_broadcast(128))
        s.dma('sp', qkw[:, 64:128], kw_d.partition_broadcast(128))
        sq = c.sb([128, 768]); ssq = [c.sb([128, 36]) for _ in range(2)]; qkn = [c.sb([128, 768]) for _ in range(2)]
    sss = [c.sb([128, 4]) for _ in range(2)]; xnT = [c.sb([128, 8, 128], BF16) for _ in range(2)]; usb = [c.sb([128, NC]) for _ in range(2)]
    def pre(tt):
        b = tt % 2
        xn = xns[b]; ss = sss[b]
        if tt == 0:
            s.dma('sp', xt[0][:], x_d[0:128, :])
        if tt + 1 < ntiles:
            s.dma('sp', xt[(tt + 1) % 3][:], x_d[(tt + 1) * 128:(tt + 2) * 128, :])
        s.act(junk[:], xt[tt % 3][:], AF.Square, accum_out=ss[:, 0:1])
        s.act(ss[:, 1:2], ss[:, 0:1], AF.Sqrt, bias=1e-6, scale=1.0 / D)
        s.recip(ss[:, 2:3], ss[:, 1:2])
        s.stt('dve', xn[:], xt[tt % 3][:], ss[:, 2:3], wb[:], ALU.mult, ALU.mult)
        yield
        for dc in range(8):
            pt = P[0] if dc < 4 else P[1]
            s.tr(pt[:, (dc % 4) * 128:(dc % 4 + 1) * 128], xn[:, dc * 128:(dc + 1) * 128], ident[:])
        yield
        s.copy('act', xnT[b][:, 0:4, :], P[0][:, :].rearrange("p (a c) -> p a c", a=4))
        s.copy('dve', xnT[b][:, 4:8, :], P[1][:, :].rearrange("p (a c) -> p a c", a=4))
        yield

    def main(tt):
        b = tt % 2
        nblk = (NC + 511) // 512
        for cb in range(nblk):
            c0 = cb * 512; c1 = min(NC, c0 + 512)
            pt = P[2 + cb % 6]
            for dc in range(8):
                s.mm(pt[:, 0:c1 - c0], lhsT=xnT[b][:, dc, :], rhs=W[:, dc, c0:c1], start=(dc == 0), stop=(dc == 7))
            s.copy('act' if cb % 2 == 0 else 'dve', usb[b][:, c0:c1], pt[:, 0:c1 - c0])
            yield
        s.dma('sp', u_d[tt * 128:(tt + 1) * 128, :], usb[b][:])
        if qk_d is not None:
            qk = usb[b][:, O_AQ:O_AQ + 768]; ss_ = ssq[b]
            s.tt('pool', sq[:], qk, qk, ALU.mult)
            s.red('dve', ss_[:, 0:12], sq[:].rearrange("p (h v) -> p h v", h=12), ALU.add)
            rstd_from_ss(s, ss_[:, 24:36], ss_[:, 12:24], ss_[:, 0:12], 1.0 / 64)
            s.tt('dve', qkn[b][:].rearrange("p (h v) -> p h v", h=12), qk.rearrange("p (h v) -> p h v", h=12),
                 bcast(ss_[:, 24:36].rearrange("p (h o) -> p h o", o=1), [128, 12, 64]), ALU.mult)
            s.stt('dve', qkn[b][:, 0:384].rearrange("p (h v) -> p h v", h=6), qkn[b][:, 0:384].rearrange("p (h v) -> p h v", h=6), 0.125,
                  bcast(qkw[:, 0:64].rearrange("p (o v) -> p o v", o=1), [128, 6, 64]), ALU.mult, ALU.mult)
            s.tt('pool', qkn[b][:, 384:768].rearrange("p (h v) -> p h v", h=6), qkn[b][:, 384:768].rearrange("p (h v) -> p h v", h=6),
                 bcast(qkw[:, 64:128].rearrange("p (o v) -> p o v", o=1), [128, 6, 64]), ALU.mult)
            s.dma('sp', qk_d[tt * 128:(tt + 1) * 128, :], qkn[b][:])
        yield

    def drain(*gens):
        gens = [g for g in gens if g is not None]
        while gens:
            for g in list(gens):
                try:
                    next(g)
                except StopIteration:
                    gens.remove(g)

    drain(pre(0))
    for tt in range(ntiles):
        drain(main(tt), pre(tt + 1) if tt + 1 < ntiles else None)


def phaseE(s, nc, st, mix_d, xin_d, w_d, x1_d, consts_d, ntiles=16, ob_d=None):
    c = Ctx(nc, st, "E")
    BF16 = mybir.dt.bfloat16
    P = c.psum()
    W = c.sb([128, 8, D], BF16, name="W")
    ident = c.sb([128, 128], name="ident")
    s.dma('sp', ident[:], consts_d[:, C_ID:C_ID + 128])
    wst = [c.sb([128, D], name="wst") for _ in range(2)]
    for dc in range(8):
        s.dma('sp', wst[dc % 2][:], w_d[dc * 128:(dc + 1) * 128, :])
        s.copy('pool' if dc % 2 == 0 else 'dve', W[:, dc, :], wst[dc % 2][:])
    mt = [c.sb([128, D]) for _ in range(3)]; xt = [c.sb([128, D]) for _ in range(3)]
    mT = [c.sb([128, 8, 128], BF16) for _ in range(2)]; ot = [c.sb([128, D]) for _ in range(2)]
    if ob_d is not None:
        ob = [[c.sb([128, 390]) for _ in range(3)] for _ in range(3)]; rden = [c.sb([128, 8]) for _ in range(3)]
    for tt in range(ntiles):
        b = tt % 2; b3 = tt % 3
        if ob_d is None:
            s.dma('sp', mt[b3][:], mix_d[tt * 128:(tt + 1) * 128, :])
        else:
            s.dma('sp', mt[b3][:, 0:256], mix_d[tt * 128:(tt + 1) * 128, 0:256])
            s.dma('sp', mt[b3][:, 640:1024], mix_d[tt * 128:(tt + 1) * 128, 640:1024])
            for bi in range(3):
                s.dma('sp', ob[b3][bi][:], ob_d[bi, tt * 128:(tt + 1) * 128, :])
            s.tt('pool', ob[b3][0][:], ob[b3][0][:], ob[b3][1][:], ALU.add)
            s.tt('pool', ob[b3][0][:], ob[b3][0][:], ob[b3][2][:], ALU.add)
            o3 = ob[b3][0][:].rearrange("p (h v) -> p h v", h=6)
            s.recip(rden[b3][:, 0:6].rearrange("p (h o) -> p h o", o=1), o3[:, :, 64:65])
            s.tt('dve', mt[b3][:, 256:640].rearrange("p (h v) -> p h v", h=6), o3[:, :, 0:64],
                 bcast(rden[b3][:, 0:6].rearrange("p (h o) -> p h o", o=1), [128, 6, 64]), ALU.mult)
        s.dma('sp', xt[b3][:], xin_d[tt * 128:(tt + 1) * 128, :])
        transpose8(s, mt[b3], mT[b], P[0], P[1], ident[:])
        for hb in range(2):
            pt = P[2 + 2 * b + hb]
            for dc in range(8):
                s.mm(pt[:, :], lhsT=mT[b][:, dc, :], rhs=W[:, dc, hb * 512:(hb + 1) * 512], start=(dc == 0), stop=(dc == 7))
            s.tt('dve', ot[b][:, hb * 512:(hb + 1) * 512], pt[:, :], xt[b3][:, hb * 512:(hb + 1) * 512], ALU.add)
        s.dma('sp', x1_d[tt * 128:(tt + 1) * 128, :], ot[b][:])


GELU_C = 1.5957691216057308


def phaseF1(s, nc, st, x1_d, nw_d, wq_d, sk_d, G_d, xnT_all, consts_d, ntiles=16):
    c = Ctx(nc, st, "G")
    BF16 = mybir.dt.bfloat16
    P = c.psum()
    cst = c.sb([128, C_END], name="cst")
    s.dma('sp', cst[:], consts_d[:, :])
    ident = cst[:, C_ID:C_ID + 128]
    Wq = c.sb([128, 8, D], name="Wq")
    for dc in range(8):
        s.dma('sp', Wq[:, dc, :], wq_d[dc * 128:(dc + 1) * 128, :])
    wb = c.sb([128, D], name="wb")
    s.dma('sp', wb[:], nw_d.partition_broadcast(128))
    SKin = c.sb([128, 2, 128]); SKbd = c.sb([128, 256])
    s.memset('dve', SKin[:], 0.0)
    s.dma('sp', SKin[:, 0, 0:64], sk_d[0, :, :])
    s.dma('sp', SKin[:, 1, 64:128], sk_d[1, :, :])
    s.tr(P[0][:, 0:128], SKin[:, 0, :], ident)
    s.tr(P[0][:, 128:256], SKin[:, 1, :], ident)
    s.copy('dve', SKbd[:], P[0][:, 0:256])
    GT = 2 if ntiles % 2 == 0 else 1
    xt1 = c.sb([128, D]); xt = [xt1, xt1]; xn = c.sb([128, D]); ss = c.sb([128, 4])
    xnT4s = [c.sb([128, 8, GT * 128]) for _ in range(2)]; qT4s = [c.sb([128, 8, GT * 128]) for _ in range(2)]; ssbs = [c.sb([128, 8, 256]) for _ in range(2)]
    sv = c.sb([128, 16, 16]); si = c.sb([128, 16, 16], U32); sif = c.sb([128, 16, 16]); s2 = c.sb([128, 128])
    cand = c.sb([128, 8, 256]); c2 = c.sb([128, 256])
    tv = c.sb([128, 8, 16]); pos = c.sb([128, 8, 16], U32); pa = c.sb([128, 128], U32); pb = c.sb([128, 128], U32)
    af = c.sb([128, 128]); bf = c.sb([128, 128])
    ge = c.sb([128, 8, 16]); gs = c.sb([128, 16]); gate = c.sb([128, 8, 16])
    oh = c.sb([128, 128, 16]); ik = c.sb([128, 128]); jk = c.sb([128, 128])
    junk2 = c.sb([128, D])
    TQ = 64
    gikT = c.sb([128, 384])
    Abuf = c.sb([128, TQ, 128], BF16); Bbuf = c.sb([128, TQ, 128], BF16); Gs = [c.sb([128, 16, 128], BF16) for _ in range(2)]
    i128 = cst[:, C_I128:C_I128 + 128]
    svv = sv[:].rearrange("p (h c) a -> p h c a", c=2)
    sifv = sif[:].rearrange("p (h c) a -> p h c a", c=2)
    i16 = cst[:, C_I16:C_I16 + 16]
    gikTs = [gikT, c.sb([128, 384])]
    Ab = [Abuf[:, 0:32, :], Abuf[:, 32:64, :]]; Bb = [Bbuf[:, 0:32, :], Bbuf[:, 32:64, :]]
    TQ = 32

    def qgroup(grp):
        xnT4 = xnT4s[grp % 2]; qT4 = qT4s[grp % 2]
        for k4 in range(GT):
            tt = grp * GT + k4
            b = tt % 2
            s.dma('sp', xt[b][:], x1_d[tt * 128:(tt + 1) * 128, :])
            s.act(junk2[:], xt[b][:], AF.Square, accum_out=ss[:, 0:1])
            s.act(ss[:, 1:2], ss[:, 0:1], AF.Ln, bias=1e-6, scale=1.0 / D)
            s.act(ss[:, 2:3], ss[:, 1:2], AF.Exp, scale=-0.5)
            s.ts('pool', xn[:], xt[b][:], ss[:, 2:3], None, ALU.mult)
            s.tt('pool', xn[:], xn[:], wb[:], ALU.mult)
            yield
            for dc in range(8):
                pt = P[0] if dc < 4 else P[1]
                s.tr(pt[:, (dc % 4) * 128:(dc % 4 + 1) * 128], xn[:, dc * 128:(dc + 1) * 128], ident)
            s.copy('act', xnT4[:, 0:4, k4 * 128:(k4 + 1) * 128], P[0][:, :].rearrange("p (a c) -> p a c", a=4))
            s.copy('act', xnT4[:, 4:8, k4 * 128:(k4 + 1) * 128], P[1][:, :].rearrange("p (a c) -> p a c", a=4))
            s.copy('pool', xnT_all[:, :, tt * 128:(tt + 1) * 128], xnT4[:, :, k4 * 128:(k4 + 1) * 128])
            yield
        for h in range(8):
            pt = P[2 + h % 2]
            for dc in range(8):
                s.mm(pt[:, 0:GT * 128], lhsT=Wq[:, dc, h * 128:(h + 1) * 128], rhs=xnT4[:, dc, :], start=(dc == 0), stop=(dc == 7))
            s.copy('act', qT4[:, h, :], pt[:, 0:GT * 128])
            yield

    def scores(tt):
        k4 = tt % GT
        qT4 = qT4s[(tt // GT) % 2]
        ssb = ssbs[tt % 2]
        for h in range(8):
            s.mm(P[4 + (h // 2) % 2][:, (h % 2) * 256:(h % 2 + 1) * 256], lhsT=qT4[:, h, k4 * 128:(k4 + 1) * 128], rhs=SKbd[:])
            if h % 2 == 1:
                k = h // 2
                s.copy('act', ssb[:, 2 * k:2 * k + 2, :], P[4 + k % 2][:, :].rearrange("p (a c) -> p a c", a=2))

    def route(tt):
        ssb = ssbs[tt % 2]
        if tt == 0:
            scores(0)
        for half in range(2):
            grp_ = range(half * 8, half * 8 + 8)
            srcs = {g: ssb[:, g // 2, (g % 2) * 128:(g % 2 + 1) * 128] for g in grp_}
            for g in grp_:
                s.op('dve', lambda e, g=g, src=srcs[g]: e.max(out=sv[:, g, 0:8], in_=src), reads=[srcs[g]], writes=[sv[:, g, 0:8]])
            for g in grp_:
                s.op('dve', lambda e, g=g, src=srcs[g]: e.max_index(out=si[:, g, 0:8], in_max=sv[:, g, 0:8], in_values=src), reads=[srcs[g], sv[:, g, 0:8]], writes=[si[:, g, 0:8]])
            for g in grp_:
                s.op('dve', lambda e, g=g, src=srcs[g]: e.match_replace(out=src, in_to_replace=sv[:, g, 0:8], in_values=src, imm_value=-1e30), reads=[srcs[g], sv[:, g, 0:8]], writes=[srcs[g]])
            for g in grp_:
                s.op('dve', lambda e, g=g, src=srcs[g]: e.max(out=sv[:, g, 8:16], in_=src), reads=[srcs[g]], writes=[sv[:, g, 8:16]])
            for g in grp_:
                s.op('dve', lambda e, g=g, src=srcs[g]: e.max_index(out=si[:, g, 8:16], in_max=sv[:, g, 8:16], in_values=src), reads=[srcs[g], sv[:, g, 8:16]], writes=[si[:, g, 8:16]])
            if half == 0:
                yield
        s.copy('pool', sif[:], si[:])
        in0 = bcast(svv[:, :, 0:1, :].rearrange("p h o a -> p h a o"), [128, 8, 16, 16])
        in1 = bcast(svv[:, :, 1:2, :], [128, 8, 16, 16])
        s.tt('dve', cand[:].rearrange("p h (a b) -> p h a b", a=16), in0, in1, ALU.add)
        yield
        hs_ = range(8)
        for h in hs_:
            s.op('dve', lambda e, h=h: e.max(out=tv[:, h, 0:8], in_=cand[:, h, :]), reads=[cand[:, h, :]], writes=[tv[:, h, 0:8]])
        for h in hs_:
            s.op('dve', lambda e, h=h: e.max_index(out=pos[:, h, 0:8], in_max=tv[:, h, 0:8], in_values=cand[:, h, :]), reads=[cand[:, h, :], tv[:, h, 0:8]], writes=[pos[:, h, 0:8]])
        for h in hs_:
            s.op('dve', lambda e, h=h: e.match_replace(out=cand[:, h, :], in_to_replace=tv[:, h, 0:8], in_values=cand[:, h, :], imm_value=-1e30), reads=[cand[:, h, :], tv[:, h, 0:8]], writes=[cand[:, h, :]])
        for h in hs_:
            s.op('dve', lambda e, h=h: e.max(out=tv[:, h, 8:16], in_=cand[:, h, :]), reads=[cand[:, h, :]], writes=[tv[:, h, 8:16]])
        for h in hs_:
            s.op('dve', lambda e, h=h: e.max_index(out=pos[:, h, 8:16], in_max=tv[:, h, 8:16], in_values=cand[:, h, :]), reads=[cand[:, h, :], tv[:, h, 8:16]], writes=[pos[:, h, 8:16]])
        s.tt('dve', ge[:], tv[:], bcast(tv[:, :, 0:1], [128, 8, 16]), ALU.subtract)
        s.act(ge[:], ge[:], AF.Exp)
        yield
        posv = pos[:].rearrange("p h k -> p (h k)")
        s.ts('dve', pa[:], posv, 4, None, ALU.logical_shift_right)
        s.ts('dve', pb[:], posv, 15, None, ALU.bitwise_and)
        s.copy('dve', af[:], pa[:])
        s.copy('dve', bf[:], pb[:])
        for (sel, cidx, outk) in ((af, 0, ik), (bf, 1, jk)):
            s.tt('dve', oh[:], bcast(sel[:].rearrange("p (k o) -> p k o", o=1), [128, 128, 16]),
                 bcast(i16.rearrange("p (o a) -> p o a", o=1), [128, 128, 16]), ALU.is_equal)
            ohv = oh[:].rearrange("p (h k) a -> p h k a", h=8)
            s.tt('dve', ohv, ohv, bcast(sifv[:, :, cidx:cidx + 1, :], [128, 8, 16, 16]), ALU.mult)
            s.red('dve', outk[:], oh[:], ALU.add)
        s.tr(P[4][:, 128:256], ik[:], ident)
        s.tr(P[4][:, 256:384], jk[:], ident)
        s.copy('act', gikTs[tt % 2][:, 128:384], P[4][:, 128:384])
        s.red('dve', gs[:, 0:8], ge[:], ALU.add)
        s.recip(gs[:, 8:16], gs[:, 0:8])
        s.tt('dve', gate[:], ge[:], bcast(gs[:, 8:16].rearrange("p (h o) -> p h o", o=1), [128, 8, 16]), ALU.mult)
        s.tr(P[5][:, 0:128], gate[:].rearrange("p h k -> p (h k)"), ident)
        s.copy('act', gikTs[tt % 2][:, 0:128], P[5][:, 0:128])
        if tt + 1 < ntiles:
            scores(tt + 1)
        yield

    def ggen(tt):
        gk = gikTs[tt % 2]
        for qt_ in range(4):
            t0 = qt_ * TQ
            A_ = Ab[qt_ % 2]; B_ = Bb[qt_ % 2]
            s.tt('dve', A_, bcast(gk[:, 128 + t0:128 + t0 + TQ].rearrange("p (t o) -> p t o", o=1), [128, TQ, 128]),
                 bcast(i128.rearrange("p (o a) -> p o a", o=1), [128, TQ, 128]), ALU.is_equal)
            s.tt('pool', A_, A_, bcast(gk[:, t0:t0 + TQ].rearrange("p (t o) -> p t o", o=1), [128, TQ, 128]), ALU.mult)
            s.tt('dve', B_, bcast(gk[:, 256 + t0:256 + t0 + TQ].rearrange("p (t o) -> p t o", o=1), [128, TQ, 128]),
                 bcast(i128.rearrange("p (o a) -> p o a", o=1), [128, TQ, 128]), ALU.is_equal)
            for g4 in range(TQ // 4):
                bank = P[6 + g4 % 2]
                for k in range(4):
                    t = g4 * 4 + k
                    s.mm(bank[:, k * 128:(k + 1) * 128], lhsT=A_[:, t, :], rhs=B_[:, t, :])
                gsb = Gs[(qt_ * 2 + g4 // 4) % 2]
                s.copy('act', gsb[:, (g4 % 4) * 4:(g4 % 4) * 4 + 4, :], bank[:, :].rearrange("p (t j) -> p t j", t=4))
                if g4 % 4 == 3:
                    tb = tt * 128 + t0 + (g4 // 4) * 16
                    s.dma('sp', G_d[tb:tb + 16, :].rearrange("t (i j) -> i t j", j=128), gsb[:])
            yield

    def drain(*gens):
        gens = [g for g in gens if g is not None]
        while gens:
            for g in list(gens):
                try:
                    next(g)
                except StopIteration:
                    gens.remove(g)

    QPRE = 4
    drain(qgroup(0))
    prev = None
    for tt in range(ntiles):
        qg = None
        if tt % GT == 0:
            g1 = tt // GT + 1
            qg = qgroup(g1) if g1 < ntiles // GT else None
        rt = route(tt); gg = ggen(prev) if prev is not None else None
        gens = [g for g in (rt, gg) if g is not None]
        for _ in range(QPRE):
            if qg is not None:
                try:
                    next(qg)
                except StopIteration:
                    qg = None
        while gens:
            for g in list(gens):
                try:
                    next(g)
                except StopIteration:
                    gens.remove(g)
                if g is rt and len(gens) and qg is not None:
                    pass
            for _ in range(2):
                if qg is not None:
                    try:
                        next(qg)
                    except StopIteration:
                        qg = None
        if qg is not None:
            drain(qg)
        prev = tt
    drain(ggen(prev))


def phaseF2(s, nc, st, x1_d, downT_d, up_d, x2_d, G_d, xnT_all, consts_d, ntiles=16, nslab=32):
    c = Ctx(nc, st, "H")
    BF16 = mybir.dt.bfloat16
    Pa = [st.enter_context(nc.psum_tensor("%s_pa%d" % (c.pfx, i), [128, 512], F32)) for i in range(2)]
    Pt = [st.enter_context(nc.psum_tensor("%s_pt%d" % (c.pfx, i), [128, 1024], BF16)) for i in range(2)]
    Po = [st.enter_context(nc.psum_tensor("%s_po%d" % (c.pfx, i), [128, 512], F32)) for i in range(4)]
    idf = c.sb([128, 128]); id16 = c.sb([128, 128], BF16)
    s.dma('sp', idf[:], consts_d[:, C_ID:C_ID + 128])
    s.copy('dve', id16[:], idf[:])
    acc = c.sb([128, ntiles, D], name="acc")
    dsl = [c.sb([128, 8, 512], BF16, name="dsl") for _ in range(2)]
    usl = [c.sb([128, 4, D], BF16, name="usl") for _ in range(2)]
    Gt = [c.sb([128, 512], BF16, name="Gt") for _ in range(6)]
    ga = [c.sb([128, 512], name="ga") for _ in range(3)]
    c16 = [c.sb([128, 512], BF16, name="c16") for _ in range(3)]
    cT = [c.sb([128, 4, 128], BF16, name="cT") for _ in range(3)]
    xt = [c.sb([128, D]) for _ in range(2)]

    dst32 = c.sb([128, 8, 512], name="dst32"); ust32 = c.sb([128, 4, D], name="ust32")

    def load_slab(sl):
        e0 = sl * 512
        s.dma('sp', dst32[:], downT_d[:, e0:e0 + 512].rearrange("(dc p) e -> p dc e", p=128))
        s.dma('sp', ust32[:], up_d[e0:e0 + 512, :].rearrange("(k p) d -> p k d", p=128))

    def cast_slab(sl, part):
        if part < 2:
            s.copy('dve', dsl[sl % 2][:, part * 4:(part + 1) * 4, :], dst32[:, part * 4:(part + 1) * 4, :])
        else:
            p2 = part - 2
            s.copy('dve', usl[sl % 2][:, p2 * 2:(p2 + 1) * 2, :], ust32[:, p2 * 2:(p2 + 1) * 2, :])

    load_slab(0)
    for part in range(4):
        cast_slab(0, part)
    it = 0
    for sl in range(nslab):
        if sl + 1 < nslab:
            load_slab(sl + 1)
        e0 = sl * 512
        D_ = dsl[sl % 2]; U_ = usl[sl % 2]

        def stage1(tt, n):
            s.dma('sp', Gt[n % 6][:], G_d[tt * 128:(tt + 1) * 128, e0:e0 + 512])
            for dc in range(8):
                s.mm(Pa[n % 2][:, :], lhsT=xnT_all[:, dc, tt * 128:(tt + 1) * 128], rhs=D_[:, dc, :], start=(dc == 0), stop=(dc == 7))
            s.act(ga[n % 3][:], Pa[n % 2][:, :], AF.Gelu_apprx_tanh)
            s.tt('pool', c16[n % 3][:], ga[n % 3][:], Gt[n % 6][:], ALU.mult)

        def stage2(tt, n):
            for k in range(4):
                s.tr(Pt[n % 2][:, k * 128:(k + 1) * 128], c16[n % 3][:, k * 128:(k + 1) * 128], id16[:])
            s.copy('act', cT[n % 3][:], Pt[n % 2][:, 0:512].rearrange("p (k t) -> p k t", k=4))

        def stage3(tt, n):
            for hb in range(2):
                po = Po[(n % 2) * 2 + hb]
                for k in range(4):
                    s.mm(po[:, :], lhsT=cT[n % 3][:, k, :], rhs=U_[:, k, hb * 512:(hb + 1) * 512], start=(k == 0), stop=(k == 3))
                if sl == 0:
                    s.copy('dve', acc[:, tt, hb * 512:(hb + 1) * 512], po[:, :])
                else:
                    s.tt('dve', acc[:, tt, hb * 512:(hb + 1) * 512], acc[:, tt, hb * 512:(hb + 1) * 512], po[:, :], ALU.add)
            if sl == nslab - 1:
                b_ = tt % 2
                s.dma('sp', xt[b_][:], x1_d[tt * 128:(tt + 1) * 128, :])
                s.tt('pool', xt[b_][:], xt[b_][:], acc[:, tt, :], ALU.add)
                s.dma('sp', x2_d[tt * 128:(tt + 1) * 128, :], xt[b_][:])

        for step in range(ntiles + 2):
            if sl + 1 < nslab and ntiles >= 12 and step in (5, 7, 9, 11):
                cast_slab(sl + 1, (step - 5) // 2)
            elif sl + 1 < nslab and ntiles < 12 and step == ntiles - 1:
                for part in range(4):
                    cast_slab(sl + 1, part)
            if step < ntiles:
                stage1(step, it + step)
            if 1 <= step <= ntiles:
                stage2(step - 1, it + step - 1)
            if 2 <= step:
                stage3(step - 2, it + step - 2)
        it += ntiles


def phaseFD(s, nc, st, x1_d, nw_d, wq_d, sk_d, downT_d, up_d, x2_d, G_d, consts_d, ntiles=16, nslab=32):
    BF16 = mybir.dt.bfloat16
    Ctx.uid += 1
    xnT_all = st.enter_context(nc.sbuf_tensor("xnTall%d" % Ctx.uid, [128, 8, ntiles * 128], BF16))
    with ExitStack() as st1:
        phaseF1(s, nc, st1, x1_d, nw_d, wq_d, sk_d, G_d, xnT_all, consts_d, ntiles)
        s.barrier(); s.flush()
    with ExitStack() as st2:
        phaseF2(s, nc, st2, x1_d, downT_d, up_d, x2_d, G_d, xnT_all, consts_d, ntiles, nslab)
        s.barrier(); s.flush()


NLAYER = 2
_CACHE = {}


def build_program():
    nc = bass.Bass("TRN2", target_bir_lowering=False)

    def din(name, shape):
        return nc.dram_tensor(name, list(shape), F32, kind="ExternalInput").ap()

    def dint(name, shape):
        return nc.dram_tensor(name, list(shape), F32, kind="Internal").ap()

    x_d = din("x", [T, D])
    anw = din("attn_norm_w", [NLAYER, D]); w_in = din("w_in", [NLAYER, D, NC])
    hlb = din("hlb", [NLAYER, 256]); hnw = din("hnw", [NLAYER, 64])
    qw = din("qw", [NLAYER, 64]); kw = din("kw", [NLAYER, 64])
    biasg = din("biasg", [18, 128, 256]); maskadd = din("maskadd", [128, 256])
    cw = din("cw", [NLAYER, 4, 896]); cb = din("cb", [NLAYER, 896]); dtb = din("dtb", [NLAYER, 6])
    alog = din("alog", [NLAYER, 6]); dsk = din("dsk", [NLAYER, 6]); snw = din("snw", [NLAYER, 384])
    w_out = din("w_out", [NLAYER, D, D]); fnw = din("fnw", [NLAYER, D]); wq = din("wq", [NLAYER, D, D])
    sk = din("sk", [NLAYER, 2, 128, 64])
    down = [din("downT%d" % l, [D, 16384]) for l in range(NLAYER)]
    up = [din("up%d" % l, [16384, D]) for l in range(NLAYER)]
    consts = din("consts", [128, C_END])
    out_d = nc.dram_tensor("out", [T, D], F32, kind="ExternalOutput").ap()
    U = dint("U", [T, NC]); QK = dint("QK", [T, 768]); OB = dint("OB", [3, T, 390]); MIX = dint("MIX", [T, D])
    X1 = dint("X1", [T, D]); X2 = dint("X2", [T, D]); G_d = nc.dram_tensor("Gd", [T, 16384], mybir.dt.bfloat16, kind="Internal").ap()

    s = Sched(nc)
    s.ro = {"x", "attn_norm_w", "w_in", "hlb", "hnw", "qw", "kw", "biasg", "maskadd", "cw", "cb", "dtb", "alog", "dsk",
            "snw", "w_out", "fnw", "wq", "sk", "downT0", "downT1", "up0", "up1", "consts"}
    with ExitStack() as semstack:
        s.semstack = semstack
        xin = x_d
        for l in range(NLAYER):
            xout = out_d if l == NLAYER - 1 else X2
            with ExitStack() as st:
                phaseA(s, nc, st, xin, w_in[l], anw[l], U, consts, qk_d=QK, qw_d=qw[l], kw_d=kw[l])
                s.barrier(); s.flush()
            with ExitStack() as st:
                phaseB(s, nc, st, U, MIX, hlb, hnw[l], consts, l)
                s.barrier(); s.flush()
            with ExitStack() as st:
                phaseC(s, nc, st, U, MIX, QK, OB, qw[l], kw[l], biasg, maskadd, consts, do_c0=False, do_c2=False)
                s.barrier(); s.flush()
            with ExitStack() as st:
                phaseD(s, nc, st, U, MIX, cw[l], cb[l], dtb[l], alog[l], dsk[l], snw[l], consts)
                s.barrier(); s.flush()
            with ExitStack() as st:
                phaseE(s, nc, st, MIX, xin, w_out[l], X1, consts, ob_d=OB)
                s.barrier(); s.flush()
            with ExitStack() as st:
                phaseFD(s, nc, st, X1, fnw[l], wq[l], sk[l], down[l], up[l], xout, G_d, consts)
            xin = X2
    return nc


def kernel(x, attn_norm_w, w_in, hgrn_lower_bounds, hgrn_norm_w, q_norm_w, k_norm_w, rel_bias,
           ssm_conv_w, ssm_conv_b, ssm_dt_bias, ssm_a_log, ssm_d, ssm_norm_w, w_out,
           ffn_norm_w, peer_w_query, peer_sub_keys, peer_down, peer_up):
    f = lambda a: np.ascontiguousarray(np.asarray(a, dtype=np.float32))
    if "nc" not in _CACHE:
        _CACHE["nc"] = build_program()
    nc = _CACHE["nc"]
    biasg, maskadd = make_attn_consts(f(rel_bias))
    shared = {
        "attn_norm_w": f(attn_norm_w), "w_in": f(w_in), "hlb": f(hgrn_lower_bounds), "hnw": f(hgrn_norm_w),
        "qw": f(q_norm_w), "kw": f(k_norm_w), "biasg": biasg, "maskadd": maskadd,
        "cw": f(ssm_conv_w), "cb": f(ssm_conv_b), "dtb": f(ssm_dt_bias), "alog": f(ssm_a_log), "dsk": f(ssm_d),
        "snw": f(ssm_norm_w), "w_out": f(w_out), "fnw": f(ffn_norm_w), "wq": f(peer_w_query), "sk": f(peer_sub_keys),
        "consts": make_consts(),
    }
    pd = f(peer_down); pu = f(peer_up)
    for l in range(NLAYER):
        shared["downT%d" % l] = np.ascontiguousarray(pd[l].T)
        shared["up%d" % l] = pu[l]
    xs = f(x)
    n = xs.shape[0]
    in_maps = [dict(shared, x=xs[i]) for i in range(n)]
    res = run_bass_kernel_spmd(nc, in_maps, core_ids=list(range(n)))
    return np.stack([np.asarray(r["out"]) for r in res.results], axis=0).astype(np.float32)
```
